# Optimizing a Trainium2 kernel written in Bass

```python
import math
import jax, jax.numpy as jnp
from jax import lax
import numpy as np

D_MODEL = 1024
BATCH = 2
SEQ = 8192
DEPTH = 4
DEC_BATCH = 32
DEC_SEQ = 4
PAST_LEN = 8192
PAGE_SIZE = 128

CONV_DIM = D_MODEL // 4
ATT_HEADS = 4
ATT_HDIM = D_MODEL // 16
ATT_DIM = ATT_HEADS * ATT_HDIM
HG_HEADS = 4
HG_DIM = D_MODEL - CONV_DIM - ATT_DIM
HG_KDIM = HG_DIM // HG_HEADS
HG_VDIM = HG_DIM // HG_HEADS
HG_CHUNK = 64
CONV_WIDTH = 31
DILATED_BRANCHES = ((128, 1), (512, 4), (2048, 16))
ATT_MAX_WIN = 2048
FFN_DIM = ((8 * D_MODEL // 3 + 255) // 256) * 256
EPS = 1e-6
NEG_BIG = -1e30
N_IN = 2 * CONV_DIM + 3 * ATT_DIM + 4 * HG_DIM
SPLITS = (2 * CONV_DIM,
          2 * CONV_DIM + ATT_DIM,
          2 * CONV_DIM + 2 * ATT_DIM,
          2 * CONV_DIM + 3 * ATT_DIM,
          2 * CONV_DIM + 3 * ATT_DIM + HG_DIM,
          2 * CONV_DIM + 3 * ATT_DIM + 2 * HG_DIM,
          2 * CONV_DIM + 3 * ATT_DIM + 3 * HG_DIM)

kernel_name = "hymba_conv_dilatedattn_hgrn2_decode_step"

F32 = jnp.float32


def _rmsnorm(x, g):
    x32 = x.astype(F32)
    y = x32 * lax.rsqrt(jnp.mean(x32 * x32, axis=-1, keepdims=True) + EPS)
    return (y * g.astype(F32)).astype(x.dtype)


def _layernorm(x, g, b):
    x32 = x.astype(F32)
    xc = x32 - jnp.mean(x32, axis=-1, keepdims=True)
    y = xc * lax.rsqrt(jnp.mean(xc * xc, axis=-1, keepdims=True) + EPS)
    return (y * g.astype(F32) + b.astype(F32)).astype(x.dtype)


def _swiglu(h, w_gate, w_up, w_down):
    return (jax.nn.silu(h @ w_gate) * (h @ w_up)) @ w_down


def _alibi_slopes():
    return jnp.asarray([2.0 ** (-8.0 * (h + 1) / ATT_HEADS) for h in range(ATT_HEADS)], dtype=F32)


def _conv_group(z, conv_state, dw_w, dw_b, ln_g, ln_b):
    a, gate = jnp.split(z, 2, axis=-1)
    u = a * jax.nn.sigmoid(gate)
    full = jnp.concatenate([conv_state.astype(u.dtype), u], axis=1)
    y = lax.conv_general_dilated(full, dw_w[:, None, :].astype(u.dtype), window_strides=(1,),
                                 padding="VALID", dimension_numbers=("NWC", "WIO", "NWC"),
                                 feature_group_count=CONV_DIM)
    y = jax.nn.silu(_layernorm(y + dw_b, ln_g, ln_b))
    return y, full[:, -(CONV_WIDTH - 1):]


def _dilated_branch_prompt(q, k, v, window, dil, slopes):
    B, S, H, Dh = q.shape
    blk = window // dil
    span = window
    Sp = -(-S // span) * span
    nb = Sp // span

    def to_blocks(t):
        t = jnp.pad(t.astype(F32), ((0, 0), (0, Sp - S), (0, 0), (0, 0)))
        t = t.reshape(B, Sp // dil, dil, H, Dh).transpose(0, 2, 3, 1, 4)
        return t.reshape(B, dil, H, nb, blk, Dh)

    def with_prev(t):
        prev = jnp.pad(t, ((0, 0), (0, 0), (0, 0), (1, 0), (0, 0), (0, 0)))[:, :, :, :-1]
        return jnp.concatenate([prev, t], axis=4)

    qb = to_blocks(q)
    kk = with_prev(to_blocks(k))
    vv = with_prev(to_blocks(v))
    s = jnp.einsum("brhnqd,brhnkd->brhnqk", qb, kk)
    qi = jnp.arange(blk)[:, None]
    ki = jnp.arange(2 * blk)[None, :]
    j = qi - ki + blk
    n = jnp.arange(nb)[:, None, None]
    valid = (j >= 0) & (j <= blk) & ((n > 0) | (ki >= blk))
    bias = -slopes[:, None, None, None] * (j * dil).astype(F32)
    s = jnp.where(valid, s + bias, NEG_BIG)
    m = jnp.max(s, axis=-1, keepdims=True)
    p = jnp.exp(s - m)
    l = jnp.sum(p, axis=-1)
    o = jnp.einsum("brhnqk,brhnkd->brhnqd", p, vv) / l[..., None]
    lse = m[..., 0] + jnp.log(l)
    o = o.reshape(B, dil, H, Sp // dil, Dh).transpose(0, 3, 1, 2, 4).reshape(B, Sp, H, Dh)[:, :S]
    lse = lse.reshape(B, dil, H, Sp // dil).transpose(0, 3, 1, 2).reshape(B, Sp, H)[:, :S]
    return o, lse


def _dilated_branch_sample(q, kf, vf, n_past, window, dil, slopes):
    T = q.shape[1]
    jj = jnp.arange(window // dil + 1)
    idx = n_past + jnp.arange(T)[:, None] - jj[None, :] * dil
    valid = idx >= 0
    idx = jnp.maximum(idx, 0)
    kg = kf[:, idx].astype(F32)
    vg = vf[:, idx].astype(F32)
    s = jnp.einsum("bthd,btjhd->bthj", q, kg)
    bias = -slopes[:, None] * (jj * dil).astype(F32)
    s = jnp.where(valid[:, None, :], s + bias, NEG_BIG)
    m = jnp.max(s, axis=-1, keepdims=True)
    p = jnp.exp(s - m)
    l = jnp.sum(p, axis=-1)
    o = jnp.einsum("bthj,btjhd->bthd", p, vg) / l[..., None]
    lse = m[..., 0] + jnp.log(l)
    return o, lse


def _att_group(q, k, v, k_buf, v_buf, slopes):
    qs = q.astype(F32) * (ATT_HDIM ** -0.5)
    outs, lses = [], []
    if k_buf is None:
        for window, dil in DILATED_BRANCHES:
            o, lse = _dilated_branch_prompt(qs, k, v, window, dil, slopes)
            outs.append(o)
            lses.append(lse)
    else:
        kf = jnp.concatenate([k_buf.astype(k.dtype), k], axis=1)
        vf = jnp.concatenate([v_buf.astype(v.dtype), v], axis=1)
        n_past = k_buf.shape[1]
        for window, dil in DILATED_BRANCHES:
            o, lse = _dilated_branch_sample(qs, kf, vf, n_past, window, dil, slopes)
            outs.append(o)
            lses.append(lse)
    w = jax.nn.softmax(jnp.stack(lses, axis=0), axis=0)
    o = jnp.sum(w[..., None] * jnp.stack(outs, axis=0), axis=0)
    return o.astype(q.dtype)


def _gated_recurrence(q, k, v, logf, s0, chunk):
    B, T, H, dk = q.shape
    dv = v.shape[-1]
    nc = T // chunk

    def to_chunks(t):
        return t.reshape(B, nc, chunk, H, t.shape[-1]).transpose(1, 0, 3, 2, 4)

    causal = jnp.tril(jnp.ones((chunk, chunk), dtype=bool))

    def step(S, inp):
        qc, kc, vc, gc = inp
        b = jnp.cumsum(gc, axis=2)
        diff = b[:, :, :, None, :] - b[:, :, None, :, :]
        decay = jnp.exp(jnp.where(causal[:, :, None], diff, NEG_BIG))
        a = jnp.einsum("bhtd,bhsd,bhtsd->bhts", qc, kc, decay)
        o = jnp.einsum("bhts,bhsv->bhtv", a, vc) + jnp.einsum("bhtd,bhdv->bhtv", qc * jnp.exp(b), S)
        b_last = b[:, :, -1:, :]
        S = jnp.exp(b_last[:, :, 0, :])[..., None] * S + jnp.einsum(
            "bhsd,bhsv->bhdv", kc * jnp.exp(b_last - b), vc)
        return S, o

    s_fin, o = lax.scan(step, s0, (to_chunks(q), to_chunks(k), to_chunks(v), to_chunks(logf)))
    return o.transpose(1, 0, 3, 2, 4).reshape(B, T, H, dv), s_fin


def _hgrn2_group(zq, zf, zi, zg, s0, lb, norm_g):
    B, T, _ = zq.shape
    q = jax.nn.silu(zq.astype(F32)).reshape(B, T, HG_HEADS, HG_KDIM)
    lbv = lb.astype(F32)
    f = lbv + (1.0 - lbv) * jax.nn.sigmoid(zf.astype(F32))
    logf = jnp.log(f).reshape(B, T, HG_HEADS, HG_KDIM)
    k = (1.0 - f).reshape(B, T, HG_HEADS, HG_KDIM)
    v = zi.astype(F32).reshape(B, T, HG_HEADS, HG_VDIM)
    chunk = HG_CHUNK if T % HG_CHUNK == 0 else T
    o, s_fin = _gated_recurrence(q, k, v, logf, s0.astype(F32), chunk)
    o = _rmsnorm(o, norm_g) * jax.nn.silu(zg.astype(F32)).reshape(B, T, HG_HEADS, HG_VDIM)
    return o.reshape(B, T, HG_DIM).astype(zq.dtype), s_fin


def _layer(x, conv_state, k_buf, v_buf, hg_state, lb, slopes,
           ln1, wg1, wu1, wd1, lnm, wi, dww, dwb, clg, clb, hgn, wo, ln2, wg2, wu2, wd2):
    B, T, _ = x.shape
    x = x + 0.5 * _swiglu(_rmsnorm(x, ln1), wg1, wu1, wd1)
    z = _rmsnorm(x, lnm) @ wi
    z_conv, zq, zk, zv, hq, hf, hi, hg = jnp.split(z, SPLITS, axis=-1)
    y_conv, conv_new = _conv_group(z_conv, conv_state, dww, dwb, clg, clb)
    q = zq.reshape(B, T, ATT_HEADS, ATT_HDIM)
    k = zk.reshape(B, T, ATT_HEADS, ATT_HDIM)
    v = zv.reshape(B, T, ATT_HEADS, ATT_HDIM)
    y_att = _att_group(q, k, v, k_buf, v_buf, slopes).reshape(B, T, ATT_DIM)
    y_hg, hg_new = _hgrn2_group(hq, hf, hi, hg, hg_state, lb, hgn)
    x = x + jnp.concatenate([y_conv, y_att, y_hg], axis=-1) @ wo
    x = x + 0.5 * _swiglu(_rmsnorm(x, ln2), wg2, wu2, wd2)
    if k_buf is None:
        keep = min(ATT_MAX_WIN, T)
        k_new, v_new = k[:, -keep:], v[:, -keep:]
    else:
        k_new, v_new = k, v
    return x, conv_new, k_new, v_new, hg_new.astype(x.dtype)


def setup_inputs(seed: int = 0) -> dict:
    key = jax.random.key(seed)
    ks = jax.random.split(key, 32)
    d = D_MODEL
    win = min(ATT_MAX_WIN, PAST_LEN)

    def nrm(k, shape, scale):
        return scale * jax.random.normal(k, shape, F32)

    return {
        "x_prompt": nrm(ks[0], (BATCH, SEQ, d), 1.0),
        "x_sample": nrm(ks[1], (DEC_BATCH, DEC_SEQ, d), 1.0),
        "state_conv": nrm(ks[2], (DEPTH, DEC_BATCH, CONV_WIDTH - 1, CONV_DIM), 0.5),
        "cache_k_win": nrm(ks[3], (DEPTH, DEC_BATCH, win, ATT_HEADS, ATT_HDIM), 1.0),
        "cache_v_win": nrm(ks[4], (DEPTH, DEC_BATCH, win, ATT_HEADS, ATT_HDIM), 1.0),
        "state_hgrn": nrm(ks[5], (DEPTH, DEC_BATCH, HG_HEADS, HG_KDIM, HG_VDIM), 0.5),
        "ln_ffn1": 1.0 + nrm(ks[6], (DEPTH, d), 0.01),
        "w_ffn1_gate": nrm(ks[7], (DEPTH, d, FFN_DIM), d ** -0.5),
        "w_ffn1_up": nrm(ks[8], (DEPTH, d, FFN_DIM), d ** -0.5),
        "w_ffn1_down": nrm(ks[9], (DEPTH, FFN_DIM, d), FFN_DIM ** -0.5),
        "ln_mix": 1.0 + nrm(ks[10], (DEPTH, d), 0.01),
        "w_in": nrm(ks[11], (DEPTH, d, N_IN), d ** -0.5),
        "conv_dw_w": nrm(ks[12], (DEPTH, CONV_WIDTH, CONV_DIM), CONV_WIDTH ** -0.5),
        "conv_dw_b": nrm(ks[13], (DEPTH, CONV_DIM), 0.01),
        "conv_ln_g": 1.0 + nrm(ks[14], (DEPTH, CONV_DIM), 0.01),
        "conv_ln_b": nrm(ks[15], (DEPTH, CONV_DIM), 0.01),
        "hg_lower_bounds": nrm(ks[16], (DEPTH, HG_DIM), 0.1),
        "hg_norm_g": 1.0 + nrm(ks[17], (DEPTH, HG_VDIM), 0.01),
        "w_out": nrm(ks[18], (DEPTH, d, d), d ** -0.5),
        "ln_ffn2": 1.0 + nrm(ks[19], (DEPTH, d), 0.01),
        "w_ffn2_gate": nrm(ks[20], (DEPTH, d, FFN_DIM), d ** -0.5),
        "w_ffn2_up": nrm(ks[21], (DEPTH, d, FFN_DIM), d ** -0.5),
        "w_ffn2_down": nrm(ks[22], (DEPTH, FFN_DIM, d), FFN_DIM ** -0.5),
        "ln_final": 1.0 + nrm(ks[23], (d,), 0.01),
    }


def reference(x_prompt, x_sample, state_conv, cache_k_win, cache_v_win, state_hgrn,
              ln_ffn1, w_ffn1_gate, w_ffn1_up, w_ffn1_down,
              ln_mix, w_in, conv_dw_w, conv_dw_b, conv_ln_g, conv_ln_b,
              hg_lower_bounds, hg_norm_g, w_out,
              ln_ffn2, w_ffn2_gate, w_ffn2_up, w_ffn2_down, ln_final):
    slopes = _alibi_slopes()
    sm = jax.nn.softmax(hg_lower_bounds.astype(F32), axis=0)
    lower = jnp.cumsum(sm, axis=0) - sm[0]
    Bp = x_prompt.shape[0]
    hp, hs = x_prompt, x_sample
    cp_l, cs_l, kp_l, vp_l, ks_l, vs_l, sp_l, ss_l = [], [], [], [], [], [], [], []
    for l in range(DEPTH):
        wl = (ln_ffn1[l], w_ffn1_gate[l], w_ffn1_up[l], w_ffn1_down[l],
              ln_mix[l], w_in[l], conv_dw_w[l], conv_dw_b[l], conv_ln_g[l], conv_ln_b[l],
              hg_norm_g[l], w_out[l], ln_ffn2[l], w_ffn2_gate[l], w_ffn2_up[l], w_ffn2_down[l])
        conv0 = jnp.zeros((Bp, CONV_WIDTH - 1, CONV_DIM), hp.dtype)
        hg0 = jnp.zeros((Bp, HG_HEADS, HG_KDIM, HG_VDIM), F32)
        hp, cp, kp, vp, sp = _layer(hp, conv0, None, None, hg0, lower[l], slopes, *wl)
        hs, cs, ksn, vsn, ss = _layer(hs, state_conv[l], cache_k_win[l], cache_v_win[l],
                                      state_hgrn[l], lower[l], slopes, *wl)
        cp_l.append(cp); cs_l.append(cs); kp_l.append(kp); vp_l.append(vp)
        ks_l.append(ksn); vs_l.append(vsn); sp_l.append(sp); ss_l.append(ss)
    y_prompt = _rmsnorm(hp, ln_final)
    y_sample = _rmsnorm(hs, ln_final)
    return (y_prompt, y_sample,
            jnp.stack(cp_l), jnp.stack(cs_l),
            jnp.stack(kp_l), jnp.stack(vp_l),
            jnp.stack(ks_l), jnp.stack(vs_l),
            jnp.stack(sp_l), jnp.stack(ss_l))
```

```python
import contextlib
import os
import numpy as np
import concourse.bass as bass
import concourse.mybir as mybir
from concourse.bass_utils import run_bass_kernel_spmd

F32 = mybir.dt.float32
BF16 = mybir.dt.bfloat16
AF = mybir.ActivationFunctionType
ALU = mybir.AluOpType

D = 1024
NCH = 8
FFN = 2816
NFC = 22
N_IN = 3328
NIC = 26
CONV_DIM = 256
CW = 31
ATT_H = 4
HD = 64
HG_H = 4
HGC = 64
WIN = 2048
TT = 512
EPS = 1e-6
DEC_T = 4


class Sched:
    COMPUTE = ("pe", "act", "dve", "pool")

    def __init__(self, nc, stack, n_dma_sems=24):
        self.nc = nc
        self.streams = {e: [] for e in ("pe", "act", "dve", "pool", "sp")}
        self.sems = {}
        for e in self.COMPUTE:
            self.sems[e] = stack.enter_context(nc.semaphore("s_" + e))
        self.count = {e: 0 for e in self.COMPUTE}
        self.dma_sems = {}
        self.dma_val = {}
        self.dma_rr = {}
        for q in ("sp", "pool"):
            self.dma_sems[q] = []
            for i in range(n_dma_sems):
                key = "d_%s_%d" % (q, i)
                self.sems[key] = stack.enter_context(nc.semaphore(key))
                self.dma_sems[q].append(key)
                self.dma_val[key] = 0
            self.dma_rr[q] = 0
        self.waited = {}
        self.last_write = {}
        self.readers = {}
        self.out_tokens = []
        self.n_inst = {e: 0 for e in self.streams}

    def _need(self, eng, token, needs):
        if token is None:
            return
        key, val = token
        if self.waited.get((eng, key), 0) >= val:
            return
        if needs.get(key, 0) < val:
            needs[key] = val

    def _collect(self, eng, reads, writes):
        needs = {}
        for t in reads:
            self._need(eng, self.last_write.get(t), needs)
        for t in writes:
            self._need(eng, self.last_write.get(t), needs)
            for tok in self.readers.get(t, ()):
                self._need(eng, tok, needs)
        return needs

    def _emit_waits(self, eng, needs, is_dma=False):
        for key, val in needs.items():
            if key == eng and not is_dma:
                continue
            sem = self.sems[key]
            self.streams[eng].append(lambda e, sem=sem, val=val: e.wait_ge(sem, val))
            self.waited[(eng, key)] = val
            self.n_inst[eng] += 1

    def _commit(self, token, reads, writes):
        for t in reads:
            self.readers.setdefault(t, []).append(token)
        for t in writes:
            self.last_write[t] = token
            self.readers[t] = []

    def op(self, eng, fn, reads=(), writes=()):
        needs = self._collect(eng, reads, writes)
        if eng in needs:
            own = needs.pop(eng)
            val = own if eng != "pe" else 0
            if val > self.waited.get((eng, eng), 0):
                sem = self.sems[eng]
                self.streams[eng].append(lambda e, sem=sem, val=val: e.wait_ge(sem, val))
                self.waited[(eng, eng)] = val
        self._emit_waits(eng, needs)
        self.count[eng] += 1
        token = (eng, self.count[eng])
        sem = self.sems[eng]
        self.streams[eng].append(lambda e, fn=fn, sem=sem: fn(e).then_inc(sem, 1))
        self.n_inst[eng] += 1
        self._commit(token, reads, writes)
        return token

    def dma(self, q, out, in_, reads=(), writes=(), is_output=False):
        needs = self._collect(q, reads, writes)
        rr = self.dma_rr[q]
        self.dma_rr[q] = (rr + 1) % len(self.dma_sems[q])
        key = self.dma_sems[q][rr]
        prev = self.dma_val[key]
        if prev > 0 and self.waited.get((q, key), 0) < prev:
            needs[key] = max(needs.get(key, 0), prev)
        self._emit_waits(q, needs, is_dma=True)
        val = prev + 16
        self.dma_val[key] = val
        sem = self.sems[key]
        self.streams[q].append(lambda e, out=out, in_=in_, sem=sem: e.dma_start(out=out, in_=in_).then_inc(sem, 16))
        self.n_inst[q] += 1
        token = (key, val)
        self._commit(token, reads, writes)
        if is_output:
            self.out_tokens.append(token)
        return token

    def barrier(self):
        for eng in self.streams:
            for x in self.COMPUTE:
                if x != eng and self.count[x] > 0:
                    sem, val = self.sems[x], self.count[x]
                    self.streams[eng].append(lambda e, sem=sem, val=val: e.wait_ge(sem, val))
                    self.waited[(eng, x)] = val
            for key, val in self.dma_val.items():
                if val > 0:
                    sem = self.sems[key]
                    self.streams[eng].append(lambda e, sem=sem, val=val: e.wait_ge(sem, val))
                    self.waited[(eng, key)] = val
        self.last_write = {}
        self.readers = {}

    def finish(self):
        finals = {}
        for key, val in self.out_tokens:
            finals[key] = max(finals.get(key, 0), val)
        for key, val in finals.items():
            sem = self.sems[key]
            self.streams["sp"].append(lambda e, sem=sem, val=val: e.wait_ge(sem, val))

    def replay(self, block):
        nc = self.nc
        streams = self.streams

        @block.sync
        def _(e):
            for f in streams["sp"]:
                f(e)

        @block.tensor
        def _(e):
            for f in streams["pe"]:
                f(e)

        @block.scalar
        def _(e):
            for f in streams["act"]:
                f(e)

        @block.vector
        def _(e):
            for f in streams["dve"]:
                f(e)

        @block.gpsimd
        def _(e):
            for f in streams["pool"]:
                f(e)


class Cfg:
    def __init__(self, seq=8192, depth=4, nsamp=4, n_cores=8, nseq=2, stages=99):
        self.seq = seq
        self.depth = depth
        self.nsamp = nsamp
        self.n_cores = n_cores
        self.nseq = nseq
        self.stages = stages
        self.ntile = seq // TT
        self.ns_tok = nsamp * DEC_T
        self.keep = min(WIN, seq)
        self.debug = False


def prm_layout(depth):
    off = {}
    r = 0
    for name, rows in (("ln1", depth * 8), ("lnm", depth * 8), ("ln2", depth * 8), ("lnf", 8),
                       ("dww", depth * CW * 2), ("dwb", depth * 2), ("clg", depth * 2), ("clb", depth * 2),
                       ("hlb", depth * 4), ("hgn", depth)):
        off[name] = r
        r += rows
    return off, ((r + 127) // 128) * 128


NM = 17


def att_weight_table():
    k = np.arange(128)[:, None, None]
    m = np.arange(NM)[None, :, None]
    q = np.arange(128)[None, None, :]
    d = 128 * m + q - k
    mult = ((d >= 0) & (d <= 128)).astype(np.float64) + ((d >= 0) & (d <= 512) & (d % 4 == 0)) + ((d >= 0) & (d <= 2048) & (d % 16 == 0))
    out = np.zeros((128, NM, ATT_H, 128), np.float32)
    for h in range(ATT_H):
        slope = 2.0 ** (-8.0 * (h + 1) / ATT_H)
        out[:, :, h, :] = mult * np.exp(-slope * np.maximum(d, 0))
    return out


def const_tables(nsamp):
    c = {}
    c["ident"] = np.eye(128, dtype=np.float32)
    c["wtab"] = att_weight_table().reshape(128, NM * ATT_H * 128)
    p = np.arange(128)
    bd = ((p[:, None] // HGC) == (p[None, :] // HGC)) & (p[:, None] <= p[None, :])
    c["bdmask"] = bd.astype(np.float32)
    bds = np.zeros((128, 128), np.float32)
    ns = nsamp * DEC_T
    ps = np.arange(ns)
    bds[:ns, :ns] = (((ps[:, None] // DEC_T) == (ps[None, :] // DEC_T)) & (ps[:, None] <= ps[None, :]))
    c["bdmask_s"] = bds
    rm = np.zeros((128, 8), np.float32)
    for j in range(4):
        rm[:, j] = (p // HGC == j)
        rm[:ns, 4 + j] = (ps // DEC_T == j)
    c["rowmask"] = rm
    rs = np.ones((128, TT), np.float32)
    rs[:, ::HGC] = 0.0
    c["rsmask"] = rs
    rss = np.ones((128, 128), np.float32)
    rss[:, 0:ns:DEC_T] = 0.0
    c["rsmask_s"] = rss
    wsm = np.zeros((128, nsamp, ATT_H, DEC_T), np.float32)
    for kk in range(ns):
        kb, kt = divmod(kk, DEC_T)
        for t in range(DEC_T):
            if t >= kt:
                d = t - kt
                mult = 1 + (d % 4 == 0) + (d % 16 == 0)
                for h in range(ATT_H):
                    slope = 2.0 ** (-8.0 * (h + 1) / ATT_H)
                    wsm[kk, kb, h, t] = mult * np.exp(-slope * d)
    c["wsm"] = wsm.reshape(128, nsamp * ATT_H * DEC_T)
    return c


def build_program(cfg):
    nc = bass.Bass("TRN2", target_bir_lowering=False)
    L = cfg.depth
    SEQ = cfg.seq
    NS = cfg.ns_tok
    NB = cfg.nsamp
    KEEP = cfg.keep
    poff, prows = prm_layout(L)
    STG = cfg.stages

    def din(name, shape, dt=F32):
        return nc.dram_tensor(name, list(shape), dt, kind="ExternalInput").ap()

    def dout(name, shape, dt=F32):
        return nc.dram_tensor(name, list(shape), dt, kind="ExternalOutput").ap()

    def dscr(name, shape, dt=BF16):
        return nc.dram_tensor(name, list(shape), dt, kind="Internal").ap()

    xp = din("xp", [SEQ, D])
    xs = din("xs", [NS, D])
    prm = din("prm", [prows, 128])
    ident_in = din("ident", [128, 128])
    wtab_in = din("wtab", [128, NM * ATT_H * 128])
    bdmask_in = din("bdmask", [128, 128])
    bdmask_s_in = din("bdmask_s", [128, 128])
    rowmask_in = din("rowmask", [128, 8])
    rsmask_in = din("rsmask", [128, TT])
    rsmask_s_in = din("rsmask_s", [128, 128])
    wsm_in = din("wsm", [128, NB * ATT_H * DEC_T])
    sconv_in = din("sconv", [L, NB, CW - 1, CONV_DIM])
    ck_in = din("ck", [L, NB, WIN, 256])
    cv_in = din("cv", [L, NB, WIN, 256])
    shg_in = din("shg", [L, NB, HG_H, 128, 128])
    w_g = [din("wg1", [L, D, FFN]), din("wg2", [L, D, FFN])]
    w_u = [din("wu1", [L, D, FFN]), din("wu2", [L, D, FFN])]
    w_d = [din("wd1", [L, FFN, D]), din("wd2", [L, FFN, D])]
    w_i = din("wi", [L, D, N_IN])
    w_o = din("wo", [L, D, D])

    yp = dout("yp", [SEQ, D])
    ys = dout("ys", [NS, D])
    o_cp = dout("o_cp", [L, CW - 1, CONV_DIM])
    o_cs = dout("o_cs", [L, NB, CW - 1, CONV_DIM])
    o_kp = dout("o_kp", [L, KEEP, 256])
    o_vp = dout("o_vp", [L, KEEP, 256])
    o_ks = dout("o_ks", [L, NS, 256])
    o_vs = dout("o_vs", [L, NS, 256])
    o_hp = dout("o_hp", [L, HG_H, 128, 128])
    o_hs = dout("o_hs", [L, NB, HG_H, 128, 128])

    sG = [dscr("sg%d" % f, [L, NFC, 128, NCH * 128]) for f in range(2)]
    sU = [dscr("su%d" % f, [L, NFC, 128, NCH * 128]) for f in range(2)]
    sD = [dscr("sd%d" % f, [L, NCH, 128, NFC * 128]) for f in range(2)]
    sI = dscr("si", [L, NIC, 128, NCH * 128])
    sO = dscr("so", [L, NCH, 128, NCH * 128])
    sK = dscr("sk", [L, 256, SEQ])
    sV = dscr("sv", [L, SEQ, ATT_H * 128])

    stack = contextlib.ExitStack()
    with stack:
        S = Sched(nc, stack)

        def sb(name, shape, dt=F32):
            return stack.enter_context(nc.sbuf_tensor(name, list(shape), dt))

        class B:
            def __init__(self, ap, tags):
                self.ap, self.tags = ap, list(tags)

        NSTG = 3
        with contextlib.ExitStack() as pstack:
            stg_f = [pstack.enter_context(nc.sbuf_tensor("stgf%d" % i, [128, NFC * 128], F32)) for i in range(NSTG)]
            stg_b = [pstack.enter_context(nc.sbuf_tensor("stgb%d" % i, [128, NFC * 128], BF16)) for i in range(NSTG)]
            cast_rr = [0]

            def convert(src2d, K, c, dst_unit, dtag):
                kc = K // 128
                i = cast_rr[0]
                cast_rr[0] += 1
                s = i % NSTG
                srcv = src2d[:, c * 128:(c + 1) * 128].rearrange("(kc p) j -> p kc j", p=128)
                S.dma("sp", stg_f[s][:, 0:kc * 128].rearrange("p (kc j) -> p kc j", j=128), srcv, writes=[("stgf", s)])
                eng = ("dve", "act", "pool")[i % 3]
                if eng == "act":
                    fn = lambda e, s=s, kc=kc: e.activation(out=stg_b[s][:, 0:kc * 128], in_=stg_f[s][:, 0:kc * 128], func=AF.Copy)
                else:
                    fn = lambda e, s=s, kc=kc: e.tensor_copy(out=stg_b[s][:, 0:kc * 128], in_=stg_f[s][:, 0:kc * 128])
                S.op(eng, fn, reads=[("stgf", s)], writes=[("stgb", s)])
                S.dma("pool", dst_unit, stg_b[s][:, 0:kc * 128], reads=[("stgb", s)], writes=[dtag])

            for l in range(L):
                for f in range(2):
                    if f == 1 and STG < 9:
                        continue
                    for c in range(NFC):
                        convert(w_g[f][l], D, c, sG[f][l, c], ("sG", f, l, c))
                        convert(w_u[f][l], D, c, sU[f][l, c], ("sU", f, l, c))
                    for c in range(NCH):
                        convert(w_d[f][l], FFN, c, sD[f][l, c], ("sD", f, l, c))
                if STG >= 2:
                    for c in range(NIC):
                        convert(w_i[l], D, c, sI[l, c], ("sI", l, c))
                    for c in range(NCH):
                        convert(w_o[l], D, c, sO[l, c], ("sO", l, c))
            S.barrier()

        ARK = 52
        arena = sb("arena", [128, ARK * 512], BF16)

        def av(name, off_kb, kb, dt=BF16, pat=None, **dims):
            e0 = int(round(off_kb * 512))
            ne = int(round(kb * 512))
            ap = arena[:, e0:e0 + ne]
            if dt == F32:
                ap = ap.bitcast(F32)
            if pat is not None:
                ap = ap.rearrange(pat, **dims)
            k0 = int(np.floor(off_kb + 1e-9))
            k1 = int(np.ceil(off_kb + kb - 1e-9))
            return B(ap, [("ar", k) for k in range(k0, k1)])

        xT = sb("xT", [128, NCH, TT])
        hT = sb("hT", [128, NCH, TT], BF16)
        ymix = sb("ymix", [128, NCH, TT], BF16)
        xTs = sb("xTs", [128, NCH, NS])
        hTs = sb("hTs", [128, NCH, 128], BF16)
        hids = sb("hids", [128, NFC, NS], BF16)
        ymixs = sb("ymixs", [128, NCH, NS], BF16)
        rstd = sb("rstd", [128, TT])
        sgb = [sb("sgb%d" % i, [128, TT]) for i in range(2)]
        prmT = sb("prmT", [128, prows])
        ident = sb("identf", [128, 128])
        identb = sb("identb", [128, 128], BF16)
        ones_b = sb("ones_b", [128, 128], BF16)
        eps_col = sb("eps_col", [128, 1])
        NB8, NB22 = 6, 2
        wb8 = [sb("wb8_%d" % i, [128, NCH * 128], BF16) for i in range(NB8)]
        wb22 = [sb("wb22_%d" % i, [128, NFC * 128], BF16) for i in range(NB22)]
        kwin = sb("kwin", [128, 2, WIN + TT], BF16)
        vwin = sb("vwin", [128, (WIN + TT) // 128, ATT_H, 128], BF16)
        wmask = sb("wmask", [128, NM, ATT_H, 128], BF16)
        Sst = sb("Sst", [128, L, HG_H, 128])
        utail = sb("utail", [128, L, 2, CW - 1], BF16)
        ubuf = sb("ubuf", [128, 2, CW - 1 + TT], BF16)
        ubufs = sb("ubufs", [128, 2, NB, CW - 1 + DEC_T], BF16)
        bdmask = sb("bdmask_t", [128, 128], BF16)
        bdmask_s = sb("bdmasks_t", [128, 128], BF16)
        rowmask = sb("rowmask_t", [128, 8])
        rsmask = sb("rsmask_t", [128, TT])
        rsmask_s = sb("rsmasks_t", [128, 128])
        wsm = sb("wsm_t", [128, NB, ATT_H, DEC_T], BF16)
        lbT = sb("lbT", [128, L * HG_H])
        omlT = sb("omlT", [128, L * HG_H])
        lbtmp = sb("lbtmp", [128, 2 * L * HG_H + 2 * HG_H])

        hidc = [av("hid", c, 1.0) for c in range(NFC)]
        finc = [av("fin", 2 * c, 2.0, F32) for c in range(NCH)]
        sqc = [av("sq", 36 + c, 1.0) for c in range(NCH)]
        iobuf = [av("io", 44 + 4 * i, 4.0, F32) for i in range(2)]

        psum = [stack.enter_context(nc.psum_tensor("ps%d" % i, [128, 512], F32)) for i in range(8)]
        ps_rr = [0]
        NPR = 6

        NPRv = [NPR]

        def ps_next():
            b = ps_rr[0] % NPRv[0]
            ps_rr[0] = (b + 1) % NPRv[0]
            return b

        def PT(b):
            return ("ps", b)

        S.dma("sp", ident[:], ident_in[:], writes=[("ident",)])
        S.op("dve", lambda e: e.tensor_copy(out=identb[:], in_=ident[:]), reads=[("ident",)], writes=[("identb",)])
        S.op("pool", lambda e: e.memset(ones_b[:], 1.0), writes=[("ones",)])
        S.op("pool", lambda e: e.memset(eps_col[:], EPS), writes=[("eps",)])
        S.op("pool", lambda e: e.memset(hTs[:], 0.0), writes=[("S", "h", c) for c in range(NCH)])
        S.op("pool", lambda e: e.memset(vwin[:], 1.0), writes=[("vwin", "hist"), ("vwin", "cur")])
        S.op("pool", lambda e: e.memset(Sst[:], 0.0), writes=[("Sst", l, h) for l in range(L) for h in range(HG_H)])
        S.op("pool", lambda e: e.memset(utail[:], 0.0), writes=[("utail", l) for l in range(L)])
        S.dma("sp", rowmask[:], rowmask_in[:], writes=[("rowmask",)])
        S.dma("sp", rsmask[:], rsmask_in[:], writes=[("rsmask",)])
        S.dma("sp", rsmask_s[:], rsmask_s_in[:], writes=[("rsmask_s",)])
        for blk in range(prows // 128):
            io = iobuf[blk % 2]
            S.dma("sp", io.ap[:, 0:128], prm[blk * 128:(blk + 1) * 128, :], writes=io.tags)
            b = ps_next()
            S.op("pe", lambda e, io=io, b=b: e.transpose(psum[b][:, 0:128], io.ap[:, 0:128], ident[:]),
                 reads=io.tags + [("ident",)], writes=[PT(b)])
            S.op("act", lambda e, b=b, blk=blk: e.activation(out=prmT[:, blk * 128:(blk + 1) * 128], in_=psum[b][:, 0:128], func=AF.Copy),
                 reads=[PT(b)], writes=[("prmT",)])
        for src, dst, tag in ((bdmask_in, bdmask, "bdmask"), (bdmask_s_in, bdmask_s, "bdmask_s")):
            io = iobuf[0]
            S.dma("sp", io.ap[:, 0:128], src[:], writes=io.tags)
            S.op("dve", lambda e, io=io, dst=dst: e.tensor_copy(out=dst[:], in_=io.ap[:, 0:128]), reads=io.tags, writes=[(tag,)])
        io = iobuf[1]
        nws = NB * ATT_H * DEC_T
        S.dma("sp", io.ap[:, 0:nws], wsm_in[:], writes=io.tags)
        S.op("dve", lambda e, io=io: e.tensor_copy(out=wsm[:].rearrange("p b h t -> p (b h t)"), in_=io.ap[:, 0:nws]), reads=io.tags, writes=[("wsm",)])
        if STG >= 3:
            wflat = wmask[:].rearrange("p m h q -> p (m h q)")
            tot = NM * ATT_H * 128
            for i, c0 in enumerate(range(0, tot, 1024)):
                io = iobuf[i % 2]
                n = min(1024, tot - c0)
                S.dma("sp", io.ap[:, 0:n], wtab_in[:, c0:c0 + n], writes=io.tags)
                eng = "dve" if i % 2 == 0 else "pool"
                S.op(eng, lambda e, io=io, c0=c0, n=n: e.tensor_copy(out=wflat[:, c0:c0 + n], in_=io.ap[:, 0:n]), reads=io.tags, writes=[("wmask",)])

        def pcol(name, idx):
            c = poff[name] + idx
            return prmT[:, c:c + 1]

        if STG >= 4:
            nlh = L * HG_H
            ex = lbtmp[:, 0:nlh]
            sm = lbtmp[:, nlh:2 * nlh]
            tot = lbtmp[:, 2 * nlh:2 * nlh + HG_H]
            rc = lbtmp[:, 2 * nlh + HG_H:2 * nlh + 2 * HG_H]
            r0 = poff["hlb"]
            S.op("act", lambda e: e.activation(out=ex, in_=prmT[:, r0:r0 + nlh], func=AF.Exp), reads=[("prmT",)], writes=[("lbtmp",)])
            S.op("dve", lambda e: e.tensor_copy(out=tot, in_=ex[:, 0:HG_H]), reads=[("lbtmp",)], writes=[("lbtmp",)])
            for l in range(1, L):
                S.op("dve", lambda e, l=l: e.tensor_tensor(out=tot, in0=tot, in1=ex[:, l * HG_H:(l + 1) * HG_H], op=ALU.add),
                     reads=[("lbtmp",)], writes=[("lbtmp",)])
            S.op("dve", lambda e: e.reciprocal(out=rc, in_=tot), reads=[("lbtmp",)], writes=[("lbtmp",)])
            for l in range(L):
                S.op("dve", lambda e, l=l: e.tensor_tensor(out=sm[:, l * HG_H:(l + 1) * HG_H], in0=ex[:, l * HG_H:(l + 1) * HG_H], in1=rc, op=ALU.mult),
                     reads=[("lbtmp",)], writes=[("lbtmp",)])
            S.op("dve", lambda e: e.memset(lbT[:, 0:HG_H], 0.0), writes=[("lbT",)])
            for l in range(1, L):
                S.op("dve", lambda e, l=l: e.tensor_tensor(out=lbT[:, l * HG_H:(l + 1) * HG_H], in0=lbT[:, (l - 1) * HG_H:l * HG_H],
                                                           in1=sm[:, l * HG_H:(l + 1) * HG_H], op=ALU.add),
                     reads=[("lbtmp",), ("lbT",)], writes=[("lbT",)])
            S.op("dve", lambda e: e.tensor_scalar(out=omlT[:], in0=lbT[:], scalar1=-1.0, scalar2=1.0, op0=ALU.mult, op1=ALU.add),
                 reads=[("lbT",)], writes=[("omlT",)])

        class WStream:
            def __init__(self):
                self.plan = []
                self.nload = 0
                self.ncons = 0
                self.cls_idx = {8: 0, 22: 0}
                self.slot_of = []
                self.prev_user = []
                self.slot_last = {}

            def add(self, uid, ap, cls, dtag):
                k = self.cls_idx[cls]
                self.cls_idx[cls] += 1
                nb = NB8 if cls == 8 else NB22
                slot = (cls, k % nb)
                self.prev_user.append(self.slot_last.get(slot, -1))
                self.slot_last[slot] = len(self.plan)
                self.slot_of.append(slot)
                self.plan.append((uid, ap, cls, dtag))

            def _buf(self, slot):
                cls, i = slot
                return (wb8 if cls == 8 else wb22)[i]

            def consume(self, uid):
                i = self.ncons
                assert self.plan[i][0] == uid, (self.plan[i][0], uid)
                while self.nload < len(self.plan) and self.nload <= i + 5 and (self.prev_user[self.nload] < 0 or self.prev_user[self.nload] <= i - 2):
                    j = self.nload
                    _, ap, cls, dtag = self.plan[j]
                    slot = self.slot_of[j]
                    S.dma("sp", self._buf(slot)[:], ap, reads=[dtag], writes=[("wb",) + slot])
                    self.nload += 1
                assert self.nload > i, (self.nload, i)
                self.ncons += 1
                slot = self.slot_of[i]
                return self._buf(slot), ("wb",) + slot

        WS = WStream()
        FM_CONV = [0, 1, 2, 3]
        FM_QKV = [4, 5, 6, 7, 8, 9]
        HG_ORDER = []
        for _h in range(HG_H):
            HG_ORDER += [18 + _h, 10 + _h, 14 + _h, 22 + _h]

        def plan_ffn(f, l):
            for c in range(NFC):
                WS.add(("g", f, l, c), sG[f][l, c], 8, ("sG", f, l, c))
                WS.add(("u", f, l, c), sU[f][l, c], 8, ("sU", f, l, c))
            for c in range(NCH):
                WS.add(("d", f, l, c), sD[f][l, c], 22, ("sD", f, l, c))

        def plan_layer(l):
            plan_ffn(0, l)
            if STG >= 2:
                for c in FM_CONV + FM_QKV + HG_ORDER:
                    WS.add(("i", l, c), sI[l, c], 8, ("sI", l, c))
                for c in range(NCH):
                    WS.add(("o", l, c), sO[l, c], 8, ("sO", l, c))
            if STG >= 9:
                plan_ffn(1, l)

        for j in range(cfg.ntile):
            for l in range(L):
                plan_layer(l)

        class Grp:
            pass

        gP = Grp()
        gP.name, gP.n, gP.x, gP.h, gP.ym = "P", TT, xT, hT, ymix
        gP.hd = [hc.ap for hc in hidc]
        gP.hdt = [hc.tags for hc in hidc]
        gS = Grp()
        gS.name, gS.n, gS.x, gS.h, gS.ym = "S", NS, xTs, hTs, ymixs
        gS.hd = [hids[:, c, :] for c in range(NFC)]
        gS.hdt = [[("S", "hd", c)] for c in range(NFC)]

        def xtag(g, c):
            return (g.name, "x", c)

        def htag(g, c):
            return (g.name, "h", c)

        def ytag(g, c):
            return (g.name, "ym", c)

        def sumsq_rstd(g, srcs, src_tags, nchunks, denom):
            n = g.n
            for c in range(nchunks):
                S.op("act", lambda e, c=c: e.activation(out=sqc[c].ap[:, 0:n], in_=srcs[c], func=AF.Square),
                     reads=src_tags[c], writes=sqc[c].tags)
            b = ps_next()

            def mm(e):
                ins = None
                for c in range(nchunks):
                    ins = e.matmul(psum[b][:, 0:n], lhsT=ones_b[:], rhs=sqc[c].ap[:, 0:n], start=(c == 0), stop=(c == nchunks - 1))
                return ins
            S.op("pe", mm, reads=sum([sqc[c].tags for c in range(nchunks)], []) + [("ones",)], writes=[PT(b)])
            S.op("act", lambda e: e.activation(out=rstd[:, 0:n], in_=psum[b][:, 0:n], func=AF.Ln, scale=1.0 / denom, bias=eps_col[:]),
                 reads=[PT(b), ("eps",)], writes=[("rstd",)])
            S.op("act", lambda e: e.activation(out=rstd[:, 0:n], in_=rstd[:, 0:n], func=AF.Exp, scale=-0.5),
                 reads=[("rstd",)], writes=[("rstd",)])

        def rmsnorm(g, gain_name, gain_idx0):
            n = g.n
            sumsq_rstd(g, [g.x[:, c, 0:n] for c in range(NCH)], [[xtag(g, c)] for c in range(NCH)], NCH, D)
            for c in range(NCH):
                S.op("dve", lambda e, c=c: e.scalar_tensor_tensor(out=g.h[:, c, 0:n], in0=g.x[:, c, 0:n],
                                                                  scalar=pcol(gain_name, gain_idx0 + c), in1=rstd[:, 0:n],
                                                                  op0=ALU.mult, op1=ALU.mult),
                     reads=[xtag(g, c), ("rstd",), ("prmT",)], writes=[htag(g, c)])

        def proj_fm(unit, utag, kch, rhs_fn, rhs_tags, n, b):
            def mm(e):
                ins = None
                for k in range(kch):
                    ins = e.matmul(psum[b][:, 0:n], lhsT=unit[:, k * 128:(k + 1) * 128], rhs=rhs_fn(k),
                                   start=(k == 0), stop=(k == kch - 1))
                return ins
            S.op("pe", mm, reads=[utag] + rhs_tags, writes=[PT(b)])

        def ffn(groups, f, l):
            sg_i = 0
            for c in range(NFC):
                ug, tg = WS.consume(("g", f, l, c))
                uu, tu = WS.consume(("u", f, l, c))
                for g in groups:
                    n = g.n
                    bg, bu = ps_next(), ps_next()
                    htags = [htag(g, k) for k in range(NCH)]
                    proj_fm(ug, tg, NCH, lambda k, g=g, n=n: g.h[:, k, 0:n], htags, n, bg)
                    proj_fm(uu, tu, NCH, lambda k, g=g, n=n: g.h[:, k, 0:n], htags, n, bu)
                    si = sg_i % 2
                    sg_i += 1
                    S.op("act", lambda e, si=si, bg=bg, n=n: e.activation(out=sgb[si][:, 0:n], in_=psum[bg][:, 0:n], func=AF.Silu),
                         reads=[PT(bg)], writes=[("sgb", si)])
                    S.op("dve", lambda e, si=si, bu=bu, n=n, g=g, c=c: e.tensor_tensor(out=g.hd[c][:, 0:n], in0=sgb[si][:, 0:n], in1=psum[bu][:, 0:n], op=ALU.mult),
                         reads=[("sgb", si), PT(bu)], writes=g.hdt[c])
            for oc in range(NCH):
                ud, td = WS.consume(("d", f, l, oc))
                for g in groups:
                    n = g.n
                    b = ps_next()
                    proj_fm(ud, td, NFC, lambda k, g=g, n=n: g.hd[k][:, 0:n], sum([g.hdt[k] for k in range(NFC)], []), n, b)
                    S.op("dve", lambda e, g=g, n=n, b=b, oc=oc: e.scalar_tensor_tensor(out=g.x[:, oc, 0:n], in0=psum[b][:, 0:n], scalar=0.5,
                                                                                      in1=g.x[:, oc, 0:n], op0=ALU.mult, op1=ALU.add),
                         reads=[PT(b), xtag(g, oc)], writes=[xtag(g, oc)])

        def load_tokens(g, src_rows, blocks):
            for bi, (r0, nr, c0) in enumerate(blocks):
                io = iobuf[bi % 2]
                S.dma("sp", io.ap[0:nr, :], src_rows[r0:r0 + nr, :], writes=io.tags)
                for c in range(NCH):
                    b = ps_next()
                    S.op("pe", lambda e, io=io, nr=nr, c=c, b=b: e.transpose(psum[b][:, 0:nr], io.ap[0:nr, c * 128:(c + 1) * 128], ident[0:nr, 0:nr]),
                         reads=io.tags + [("ident",)], writes=[PT(b)])
                    if c % 2 == 0:
                        S.op("act", lambda e, b=b, c=c, nr=nr, c0=c0: e.activation(out=g.x[:, c, c0:c0 + nr], in_=psum[b][:, 0:nr], func=AF.Copy),
                             reads=[PT(b)], writes=[xtag(g, c)])
                    else:
                        S.op("dve", lambda e, b=b, c=c, nr=nr, c0=c0: e.tensor_copy(out=g.x[:, c, c0:c0 + nr], in_=psum[b][:, 0:nr]),
                             reads=[PT(b)], writes=[xtag(g, c)])

        def final_norm_store(g, dst_rows, blocks):
            n = g.n
            sumsq_rstd(g, [g.x[:, c, 0:n] for c in range(NCH)], [[xtag(g, c)] for c in range(NCH)], NCH, D)
            for c in range(NCH):
                S.op("dve", lambda e, c=c: e.scalar_tensor_tensor(out=finc[c].ap[:, 0:n], in0=g.x[:, c, 0:n],
                                                                  scalar=pcol("lnf", c), in1=rstd[:, 0:n],
                                                                  op0=ALU.mult, op1=ALU.mult),
                     reads=[xtag(g, c), ("rstd",), ("prmT",)], writes=finc[c].tags)
            for bi, (r0, nr, c0) in enumerate(blocks):
                io = iobuf[bi % 2]
                for half in range(2):
                    b = ps_next()

                    def tr(e, half=half, b=b, nr=nr, c0=c0):
                        ins = None
                        for cc in range(4):
                            ins = e.transpose(psum[b][0:nr, cc * 128:(cc + 1) * 128], finc[half * 4 + cc].ap[:, c0:c0 + nr], ident[:])
                        return ins
                    S.op("pe", tr, reads=sum([finc[half * 4 + cc].tags for cc in range(4)], []) + [("ident",)], writes=[PT(b)])
                    if half == 0:
                        S.op("act", lambda e, b=b, nr=nr, io=io: e.activation(out=io.ap[0:nr, 0:512], in_=psum[b][0:nr, :], func=AF.Copy),
                             reads=[PT(b)], writes=io.tags)
                    else:
                        S.op("dve", lambda e, b=b, nr=nr, io=io: e.tensor_copy(out=io.ap[0:nr, 512:1024], in_=psum[b][0:nr, :]),
                             reads=[PT(b)], writes=io.tags)
                S.dma("pool", dst_rows[r0:r0 + nr, :], io.ap[0:nr, :], reads=io.tags, writes=[("out", g.name, r0)], is_output=True)

        def mk_ws(g):
            w = Grp()
            n = g.n
            if g is gP:
                w.cva = av("cva", 0, 4, F32, "p (c t) -> p c t", c=2)
                w.cvs = av("cvs", 4, 4, F32, "p (c t) -> p c t", c=2)
                w.cvy = av("cvy", 8, 4, F32, "p (c t) -> p c t", c=2)
                w.diag = av("diag", 14, 16, BF16, "p (w j) -> p w j", j=128)
                w.qz = av("qz", 30, 4, BF16, "p (h t) -> p h t", h=4)
                w.Eb = [av("Eb%d" % i, 34 + i, 1) for i in range(4)]
                w.Pb = [av("Pb%d" % i, 38 + i, 1) for i in range(4)]
                w.lnd = [av("lnd%d" % i, 42 + 0.5 * i, 0.5, F32) for i in range(2)]
                hp = Grp()
                for k, nm in enumerate(("q", "f", "lg", "kk", "eb", "en")):
                    setattr(hp, nm, av("h" + nm, 2 * k, 2, F32))
                w.hp = hp
                w.live = []
                for h in range(HG_H):
                    lv = Grp()
                    base = 12 + 5 * h
                    lv.qt = av("hqt%d" % h, base, 1)
                    lv.kt = av("hkt%d" % h, base + 1, 1)
                    lv.kh = av("hkh%d" % h, base + 2, 2, F32)
                    lv.gt = av("hgt%d" % h, base + 4, 1)
                    ebl = sb("ebl%d" % h, [128, n // HGC])
                    lv.ebl = B(ebl[:], [("ebl", h)])
                    emid = sb("emid%d" % h, [128, n // HGC])
                    lv.emid = B(emid[:], [("emid", h)])
                    w.live.append(lv)
                w.vtok = av("vtok", 32, 4, BF16, "p (b c) -> p b c", b=4)
                w.khm = [av("khm%d" % i, 36 + i, 1, BF16, "p (j d) -> p j d", j=4) for i in range(4)]
                w.Am = [av("Am%d" % i, 40 + 0.25 * i, 0.25) for i in range(4)]
                w.Sb = [[av("Sb%d_%d" % (h, i), 41 + 0.25 * (h * 5 + i), 0.25) for i in range(5)] for h in range(HG_H)]
                w.osq = av("osq", 46, 1)
                w.ot = av("ot", 47, 2, F32)
                w.Tb = []
                for h in range(HG_H):
                    tt_ = sb("Tb%d" % h, [128, 2, 128])
                    w.Tb.append([B(tt_[:, i, :], [("Tb", h, i)]) for i in range(2)])
            else:
                def t(nm, shape, dt=F32):
                    tt = sb("s_" + nm, shape, dt)
                    return B(tt[:], [("sw", nm)])
                w.cva = t("cva", [128, 2, n]); w.cvs = t("cvs", [128, 2, n]); w.cvy = t("cvy", [128, 2, n])
                w.diag = None
                w.qz = t("qz", [128, 4, n], BF16)
                w.Eb = [t("Eb%d" % i, [128, 64 + 16], BF16) for i in range(2)]
                w.Pb = [t("Pb%d" % i, [128, 64 + 16], BF16) for i in range(2)]
                w.lnd = [t("lnd%d" % i, [128, 16]) for i in range(2)]
                w.kvout = [av("kvo_s", 12, 2, F32)]
                hp = Grp()
                for nm in ("q", "f", "lg", "kk", "eb", "en"):
                    setattr(hp, nm, t("h" + nm, [128, n]))
                w.hp = hp
                w.live = []
                for h in range(HG_H):
                    lv = Grp()
                    lv.qt = t("hqt%d" % h, [128, n], BF16)
                    lv.kt = t("hkt%d" % h, [128, 128], BF16)
                    lv.kh = t("hkh%d" % h, [128, n])
                    lv.gt = t("hgt%d" % h, [128, n], BF16)
                    lv.ebl = t("ebl%d" % h, [128, n // DEC_T])
                    lv.emid = t("emid%d" % h, [128, n // DEC_T])
                    w.live.append(lv)
                w.vtok = t("vtok", [128, 1, 512], BF16)
                w.khm = [t("khm%d" % i, [128, 4, 128], BF16) for i in range(4)]
                w.Am = [t("Am%d" % i, [128, 128], BF16) for i in range(4)]
                w.Sb = [[t("Sb%d_%d" % (h, i), [128, 128], BF16) for i in range(5)] for h in range(HG_H)]
                w.osq = t("osq", [128, n], BF16)
                w.ot = t("ot", [128, n])
                w.shs = av("shs", 0, 8, F32, "p (b h v) -> p b h v", b=NB, h=HG_H)
                w.kTs = t("kTs", [128, 2, 128], BF16)
                w.vaug = t("vaug", [128, ATT_H, 128], BF16)
                w.ufp = t("ufp", [128, 256])
            return w

        wsP = mk_ws(gP)
        wsS = mk_ws(gS)
        gP.ws, gS.ws = wsP, wsS
        S.op("pool", lambda e: e.memset(wsP.qz.ap, 0.0), writes=wsP.qz.tags)
        S.op("pool", lambda e: e.memset(wsS.qz.ap, 0.0), writes=wsS.qz.tags)
        S.op("pool", lambda e: e.memset(wsS.vaug.ap, 1.0), writes=wsS.vaug.tags)
        S.op("pool", lambda e: e.memset(wsS.kTs.ap, 0.0), writes=wsS.kTs.tags)
        S.op("pool", lambda e: e.memset(wsS.vtok.ap, 0.0), writes=wsS.vtok.tags)
        for _i in range(4):
            S.op("pool", lambda e, _i=_i: e.memset(wsS.khm[_i].ap, 0.0), writes=wsS.khm[_i].tags)
            S.op("pool", lambda e, _i=_i: e.memset(wsS.live[_i].kt.ap, 0.0), writes=wsS.live[_i].kt.tags)

        acc_rr = [0]

        def acc_bank():
            b = 6 + acc_rr[0]
            acc_rr[0] ^= 1
            return b

        def win_chunk_fm(g, l, c, unit, utag):
            n = g.n
            b = ps_next()
            proj_fm(unit, utag, NCH, lambda k: g.h[:, k, 0:n], [htag(g, k) for k in range(NCH)], n, b)
            return b

        def win_chunk_tm(g, unit, utag):
            n = g.n
            bs = 128 if g is gP else n
            nblk = n // bs
            b = ps_next()

            def mm(e):
                ins = None
                for tb in range(nblk):
                    for k in range(NCH):
                        ins = e.matmul(psum[b][0:128, tb * 128:(tb + 1) * 128], lhsT=g.h[:, k, tb * bs:tb * bs + 128], rhs=unit[:, k * 128:(k + 1) * 128],
                                       start=(k == 0), stop=(k == NCH - 1))
                return ins
            S.op("pe", mm, reads=[utag] + [htag(g, k) for k in range(NCH)], writes=[PT(b)])
            return b, bs, nblk

        def conv_evac(g, c, b):
            n, w = g.n, g.ws
            if c < 2:
                S.op("act", lambda e: e.activation(out=w.cva.ap[:, c, 0:n], in_=psum[b][:, 0:n], func=AF.Copy), reads=[PT(b)], writes=w.cva.tags)
            else:
                S.op("act", lambda e: e.activation(out=w.cvs.ap[:, c - 2, 0:n], in_=psum[b][:, 0:n], func=AF.Sigmoid), reads=[PT(b)], writes=w.cvs.tags)

        def conv_glu(g):
            n, w = g.n, g.ws
            S.op("dve", lambda e: e.tensor_tensor(out=w.cva.ap[:, :, 0:n], in0=w.cva.ap[:, :, 0:n], in1=w.cvs.ap[:, :, 0:n], op=ALU.mult),
                 reads=w.cva.tags + w.cvs.tags, writes=w.cva.tags)

        def build_diag(l):
            w = wsP
            for wi in range(CW):
                for ch in range(2):
                    col = pcol("dww", l * CW * 2 + wi * 2 + ch)
                    if (wi * 2 + ch) % 2 == 0:
                        S.op("act", lambda e, wi=wi, ch=ch, col=col: e.activation(out=w.diag.ap[:, wi * 2 + ch, :], in_=identb[:], func=AF.Copy, scale=col),
                             reads=[("identb",), ("prmT",)], writes=w.diag.tags)
                    else:
                        S.op("dve", lambda e, wi=wi, ch=ch, col=col: e.tensor_scalar(out=w.diag.ap[:, wi * 2 + ch, :], in0=identb[:], scalar1=col, scalar2=None, op0=ALU.mult),
                             reads=[("identb",), ("prmT",)], writes=w.diag.tags)

        def conv_main(g, l, j):
            n, w = g.n, g.ws
            dg = wsP.diag
            if g is gP:
                S.op("pool", lambda e: e.tensor_copy(out=ubuf[:, :, 0:CW - 1], in_=utail[:, l, :, :]), reads=[("utail", l)], writes=[("ubuf",)])
                S.op("act", lambda e: e.activation(out=ubuf[:, :, CW - 1:CW - 1 + n], in_=w.cva.ap[:, :, 0:n], func=AF.Copy), reads=w.cva.tags, writes=[("ubuf",)])
                rhs = lambda ch, wi: ubuf[:, ch, wi:wi + n]
                outv = lambda b: psum[b][:, 0:n]
                utags = [("ubuf",)]
            else:
                for bb in range(NB):
                    io = iobuf[bb % 2]
                    S.dma("sp", io.ap[0:CW - 1, 0:256], sconv_in[l, bb], writes=io.tags)
                    for ch in range(2):
                        b = ps_next()
                        S.op("pe", lambda e, io=io, ch=ch, b=b: e.transpose(psum[b][:, 0:CW - 1], io.ap[0:CW - 1, ch * 128:(ch + 1) * 128], ident[0:CW - 1, 0:CW - 1]),
                             reads=io.tags + [("ident",)], writes=[PT(b)])
                        S.op("dve", lambda e, ch=ch, bb=bb, b=b: e.tensor_copy(out=ubufs[:, ch, bb, 0:CW - 1], in_=psum[b][:, 0:CW - 1]), reads=[PT(b)], writes=[("ubufs",)])
                    S.dma("pool", o_cs[l, bb, 0:CW - 1 - DEC_T, :], sconv_in[l, bb, DEC_T:CW - 1, :], writes=[("o_cs", l, bb, 0)], is_output=True)
                S.op("act", lambda e: e.activation(out=ubufs[:, :, :, CW - 1:CW - 1 + DEC_T], in_=w.cva.ap.rearrange("p c (b t) -> p c b t", t=DEC_T), func=AF.Copy),
                     reads=w.cva.tags, writes=[("ubufs",)])
                rhs = lambda ch, wi: ubufs[:, ch, :, wi:wi + DEC_T]
                outv = lambda b: psum[b][:, 0:n].rearrange("p (b t) -> p b t", t=DEC_T)
                utags = [("ubufs",)]
            banks = []
            for ch in range(2):
                b = ps_next()
                banks.append(b)

                def mm(e, ch=ch, b=b):
                    ins = None
                    for wi in range(CW):
                        ins = e.matmul(outv(b), lhsT=dg.ap[:, wi * 2 + ch, :], rhs=rhs(ch, wi), start=(wi == 0), stop=(wi == CW - 1))
                    return ins
                S.op("pe", mm, reads=dg.tags + utags, writes=[PT(b)])
                S.op("act", lambda e, ch=ch, b=b: e.activation(out=w.cvy.ap[:, ch, 0:n], in_=psum[b][:, 0:n], func=AF.Identity, bias=pcol("dwb", l * 2 + ch)),
                     reads=[PT(b), ("prmT",)], writes=w.cvy.tags)
            if g is gP:
                S.op("pool", lambda e: e.tensor_copy(out=utail[:, l, :, :], in_=ubuf[:, :, n:n + CW - 1]), reads=[("ubuf",)], writes=[("utail", l)])
            for ch in range(2):
                S.op("dve", lambda e, ch=ch: e.tensor_copy(out=sqc[ch].ap[:, 0:n], in_=w.cvy.ap[:, ch, 0:n]), reads=w.cvy.tags, writes=sqc[ch].tags)
            b = ps_next()

            def mm1(e):
                ins = None
                for ch in range(2):
                    ins = e.matmul(psum[b][:, 0:n], lhsT=ones_b[:], rhs=sqc[ch].ap[:, 0:n], start=(ch == 0), stop=(ch == 1))
                return ins
            S.op("pe", mm1, reads=sqc[0].tags + sqc[1].tags + [("ones",)], writes=[PT(b)])
            for ch in range(2):
                S.op("dve", lambda e, ch=ch: e.scalar_tensor_tensor(out=w.cvy.ap[:, ch, 0:n], in0=psum[b][:, 0:n], scalar=-1.0 / CONV_DIM,
                                                                   in1=w.cvy.ap[:, ch, 0:n], op0=ALU.mult, op1=ALU.add),
                     reads=[PT(b)] + w.cvy.tags, writes=w.cvy.tags)
            sumsq_rstd(g, [w.cvy.ap[:, ch, 0:n] for ch in range(2)], [w.cvy.tags, w.cvy.tags], 2, CONV_DIM)
            for ch in range(2):
                S.op("dve", lambda e, ch=ch: e.tensor_tensor(out=w.cvy.ap[:, ch, 0:n], in0=w.cvy.ap[:, ch, 0:n], in1=rstd[:, 0:n], op=ALU.mult),
                     reads=w.cvy.tags + [("rstd",)], writes=w.cvy.tags)
                S.op("act", lambda e, ch=ch: e.activation(out=g.ym[:, ch, 0:n], in_=w.cvy.ap[:, ch, 0:n], func=AF.Silu,
                                                          scale=pcol("clg", l * 2 + ch), bias=pcol("clb", l * 2 + ch)),
                     reads=w.cvy.tags + [("prmT",)], writes=[ytag(g, ch)])
            if g is gP and j == cfg.ntile - 1:
                io = iobuf[0]
                for ch in range(2):
                    b2 = ps_next()
                    S.op("pe", lambda e, ch=ch, b2=b2: e.transpose(psum[b2][0:CW - 1, 0:128], w.cva.ap[:, ch, n - (CW - 1):n], ident[:]),
                         reads=w.cva.tags + [("ident",)], writes=[PT(b2)])
                    S.op("dve", lambda e, ch=ch, b2=b2, io=io: e.tensor_copy(out=io.ap[0:CW - 1, ch * 128:(ch + 1) * 128], in_=psum[b2][0:CW - 1, 0:128]),
                         reads=[PT(b2)], writes=io.tags)
                S.dma("pool", o_cp[l], io.ap[0:CW - 1, 0:256], reads=io.tags, writes=[("o_cp", l)], is_output=True)
            if g is gS:
                for ch in range(2):
                    b2 = ps_next()
                    S.op("pe", lambda e, ch=ch, b2=b2: e.transpose(psum[b2][0:n, 0:128], w.cva.ap[:, ch, 0:n], ident[:]),
                         reads=w.cva.tags + [("ident",)], writes=[PT(b2)])
                    S.op("dve", lambda e, ch=ch, b2=b2: e.tensor_copy(out=w.ufp.ap[0:n, ch * 128:(ch + 1) * 128], in_=psum[b2][0:n, 0:128]),
                         reads=[PT(b2)], writes=w.ufp.tags)
                for bb in range(NB):
                    S.dma("pool", o_cs[l, bb, CW - 1 - DEC_T:CW - 1, :], w.ufp.ap[bb * DEC_T:(bb + 1) * DEC_T, :], reads=w.ufp.tags,
                          writes=[("o_cs", l, bb, 1)], is_output=True)

        kvst = av("kvst", 44, 8, F32, "p (b c) -> p b c", b=4)

        def att_q_evac(g, c, b):
            n, w = g.n, g.ws
            ch = c - 4
            for hp in range(2):
                h = 2 * ch + hp
                r0 = 64 * hp
                S.op("act", lambda e, h=h, r0=r0: e.activation(out=w.qz.ap[r0:r0 + 64, h, 0:n], in_=psum[b][r0:r0 + 64, 0:n], func=AF.Copy, scale=HD ** -0.5),
                     reads=[PT(b)], writes=w.qz.tags)

        def att_k_evac(g, l, j, c, b):
            n, w = g.n, g.ws
            ch = c - 6
            if g is gP:
                S.op("dve", lambda e: e.tensor_copy(out=kwin[:, ch, WIN:WIN + n], in_=psum[b][:, 0:n]), reads=[PT(b)], writes=[("kwin", "cur")])
                if ch == 1:
                    t0 = j * TT
                    S.dma("pool", sK[l].rearrange("(c p) t -> p c t", p=128)[:, :, t0:t0 + n], kwin[:, :, WIN:WIN + n], reads=[("kwin", "cur")], writes=[("sK", l, j)])
            else:
                S.op("dve", lambda e: e.tensor_copy(out=w.kTs.ap[:, ch, 0:n], in_=psum[b][:, 0:n]), reads=[PT(b)], writes=w.kTs.tags)

        def att_kv_tm(g, l, j, c, unit, utag):
            n, w = g.n, g.ws
            ci = c - 6
            b, bs, nblk = win_chunk_tm(g, unit, utag)
            t0 = j * TT
            if g is gP:
                S.op("act", lambda e: e.activation(out=kvst.ap[:, :, ci * 128:(ci + 1) * 128], in_=psum[b][:, :].rearrange("p (b c) -> p b c", c=128), func=AF.Copy),
                     reads=[PT(b)], writes=kvst.tags)
            else:
                kv = w.kvout[0]
                S.op("act", lambda e: e.activation(out=kv.ap[0:bs, ci * 128:(ci + 1) * 128], in_=psum[b][0:bs, 0:128], func=AF.Copy), reads=[PT(b)], writes=kv.tags)
            if c >= 8 and "kvcp" not in os.environ.get("DBG_SKIP", ""):
                for hh in range(2):
                    h = 2 * (c - 8) + hh
                    if g is gP:
                        S.op("act", lambda e, h=h, hh=hh: e.activation(out=vwin[:, WIN // 128:WIN // 128 + 4, h, 64 * hh:64 * hh + 64],
                                                                       in_=psum[b][:, :].rearrange("p (b c) -> p b c", c=128)[:, :, 64 * hh:64 * hh + 64], func=AF.Copy),
                             reads=[PT(b)], writes=[("vwin", "cur")])
                    else:
                        S.op("act", lambda e, h=h, hh=hh: e.activation(out=w.vaug.ap[0:bs, h, 64 * hh:64 * hh + 64], in_=psum[b][0:bs, 64 * hh:64 * hh + 64], func=AF.Copy),
                             reads=[PT(b)], writes=w.vaug.tags)
            if c == 9 and "kvdma" not in os.environ.get("DBG_SKIP", ""):
                if g is gP:
                    r0 = t0 - (SEQ - KEEP)
                    if r0 >= 0:
                        S.dma("pool", o_kp[l, r0:r0 + n, :].rearrange("(b p) c -> p b c", p=128), kvst.ap[:, :, 0:256], reads=kvst.tags, writes=[("o_kp", l, r0)], is_output=True)
                        S.dma("pool", o_vp[l, r0:r0 + n, :].rearrange("(b p) c -> p b c", p=128), kvst.ap[:, :, 256:512], reads=kvst.tags, writes=[("o_vp", l, r0)], is_output=True)
                    S.dma("pool", sV[l, t0:t0 + n, :].rearrange("(b p) c -> p b c", p=128),
                          vwin[:, WIN // 128:WIN // 128 + 4, :, :].rearrange("p b h d -> p b (h d)"), reads=[("vwin", "cur")], writes=[("sV", l, j)])
                else:
                    kv = w.kvout[0]
                    S.dma("pool", o_ks[l], kv.ap[0:n, 0:256], reads=kv.tags, writes=[("o_ks", l)], is_output=True)
                    S.dma("pool", o_vs[l], kv.ap[0:n, 256:512], reads=kv.tags, writes=[("o_vs", l)], is_output=True)

        def att_finish(g, w, bO, h, ych, cols, li):
            ch, hp = divmod(h, 2)
            orow = 64 * hp
            drow = 64 * (1 - hp)
            nq = cols[1] - cols[0]
            ld = w.lnd[li % 2]
            S.op("act", lambda e: e.activation(out=ld.ap[orow:orow + 64, 0:nq], in_=psum[bO][drow:drow + 64, 0:nq], func=AF.Ln),
                 reads=[PT(bO)], writes=ld.tags)
            S.op("act", lambda e: e.activation(out=ld.ap[orow:orow + 64, 0:nq], in_=ld.ap[orow:orow + 64, 0:nq], func=AF.Exp, scale=-1.0),
                 reads=ld.tags, writes=ld.tags)
            S.op("dve", lambda e: e.tensor_tensor(out=g.ym[orow:orow + 64, 2 + ch, cols[0]:cols[1]], in0=psum[bO][orow:orow + 64, 0:nq],
                                                  in1=ld.ap[orow:orow + 64, 0:nq], op=ALU.mult),
                 reads=[PT(bO)] + ld.tags, writes=[ytag(g, 2 + ch)])

        _wt = att_weight_table().astype(np.float64)
        MMAX = [max(m for m in range(NM) if np.any(_wt[:, m, h, :] >= 2.0 ** -134)) for h in range(ATT_H)]

        def att_prompt(l, j):
            g, w = gP, wsP
            t0 = j * TT
            work = []
            for qb in range(TT // 128):
                gb = j * (TT // 128) + qb
                nm = min(16, gb) + 1
                for h in range(ATT_H):
                    bO = acc_bank()
                    nmh = min(nm, MMAX[h] + 1)
                    for g0 in range(0, nmh, 4):
                        work.append(dict(qb=qb, h=h, grp=list(range(g0, min(nmh, g0 + 4))), first=(g0 == 0), last=(g0 + 4 >= nmh), bO=bO, nm=nmh))
            li = [0]

            def emit_S(i):
                wk = work[i]
                b = ps_next()
                wk["b"] = b
                qb, h, grp = wk["qb"], wk["h"], wk["grp"]
                ch = h // 2

                def mm(e):
                    ins = None
                    for gi, m in enumerate(grp):
                        wkb = 16 + qb - m
                        ins = e.matmul(psum[b][:, gi * 128:(gi + 1) * 128], lhsT=kwin[:, ch, wkb * 128:(wkb + 1) * 128],
                                       rhs=w.qz.ap[:, h, qb * 128:(qb + 1) * 128], start=True, stop=True)
                    return ins
                S.op("pe", mm, reads=[("kwin", "hist"), ("kwin", "cur")] + w.qz.tags, writes=[PT(b)])
                ng = len(grp)
                Eb, Pb = w.Eb[i % 4], w.Pb[i % 4]
                S.op("act", lambda e: e.activation(out=Eb.ap[:, 0:ng * 128], in_=psum[b][:, 0:ng * 128], func=AF.Exp), reads=[PT(b)], writes=Eb.tags)
                S.op("dve", lambda e: e.tensor_tensor(
                    out=Pb.ap[:, 0:ng * 128].rearrange("p (m q) -> p m q", q=128), in0=Eb.ap[:, 0:ng * 128].rearrange("p (m q) -> p m q", q=128),
                    in1=wmask[:, grp[0]:grp[0] + ng, h, :], op=ALU.mult), reads=Eb.tags + [("wmask",)], writes=Pb.tags)

            def emit_PV(i):
                wk = work[i]
                qb, h, grp, bO, nm = wk["qb"], wk["h"], wk["grp"], wk["bO"], wk["nm"]
                Pb = w.Pb[i % 4]

                def pv(e):
                    ins = None
                    for gi, m in enumerate(grp):
                        wkb = 16 + qb - m
                        ins = e.matmul(psum[bO][:, 0:128], lhsT=vwin[:, wkb, h, :], rhs=Pb.ap[:, gi * 128:(gi + 1) * 128],
                                       start=(wk["first"] and gi == 0), stop=(m == nm - 1))
                    return ins
                S.op("pe", pv, reads=Pb.tags + [("vwin", "hist"), ("vwin", "cur")], writes=[PT(bO)])
                if wk["last"]:
                    att_finish(g, w, bO, h, None, (qb * 128, (qb + 1) * 128), li[0])
                    li[0] += 1

            LA = 3
            for i in range(len(work) + LA):
                if i < len(work):
                    emit_S(i)
                if i - LA >= 0:
                    emit_PV(i - LA)

        def att_sample(l):
            g, w = gS, wsS
            n = g.n
            li = 0
            ei = 0
            for bb in range(NB):
                for q4 in range(4):
                    io = iobuf[q4 % 2]
                    S.dma("sp", io.ap[:, :].rearrange("p (b c) -> p b c", c=256), ck_in[l, bb, q4 * 512:(q4 + 1) * 512, :].rearrange("(b p) c -> p b c", p=128), writes=io.tags)
                    for ch in range(2):
                        b = ps_next()

                        def tr(e, io=io, ch=ch, b=b):
                            ins = None
                            for kb in range(4):
                                ins = e.transpose(psum[b][:, kb * 128:(kb + 1) * 128], io.ap[:, kb * 256 + ch * 128:kb * 256 + (ch + 1) * 128], ident[:])
                            return ins
                        S.op("pe", tr, reads=io.tags + [("ident",)], writes=[PT(b)])
                        if ch == 0:
                            S.op("act", lambda e, b=b, q4=q4, ch=ch: e.activation(out=kwin[:, ch, q4 * 512:(q4 + 1) * 512], in_=psum[b][:, :], func=AF.Copy),
                                 reads=[PT(b)], writes=[("kwin", "hist")])
                        else:
                            S.op("dve", lambda e, b=b, q4=q4, ch=ch: e.tensor_copy(out=kwin[:, ch, q4 * 512:(q4 + 1) * 512], in_=psum[b][:, :]),
                                 reads=[PT(b)], writes=[("kwin", "hist")])
                for q4 in range(4):
                    io = iobuf[q4 % 2]
                    S.dma("sp", io.ap[:, :].rearrange("p (b c) -> p b c", c=256), cv_in[l, bb, q4 * 512:(q4 + 1) * 512, :].rearrange("(b p) c -> p b c", p=128), writes=io.tags)
                    for par in range(2):
                        S.op("act", lambda e, io=io, q4=q4, par=par: e.activation(
                            out=vwin[:, q4 * 4:(q4 + 1) * 4, par::2, 64 * par:64 * par + 64],
                            in_=io.ap[:, :].rearrange("p (b h d) -> p b h d", b=4, d=64)[:, :, par::2, :], func=AF.Copy),
                            reads=io.tags, writes=[("vwin", "hist")])
                for h in range(ATT_H):
                    ch = h // 2
                    bO = acc_bank()
                    b = ps_next()

                    def mm(e, b=b, ch=ch, h=h, bb=bb):
                        ins = None
                        for m in range(1, min(16, MMAX[h]) + 1):
                            wkb = 16 - m
                            ins = e.matmul(psum[b][:, (m - 1) * DEC_T:m * DEC_T], lhsT=kwin[:, ch, wkb * 128:(wkb + 1) * 128],
                                           rhs=w.qz.ap[:, h, bb * DEC_T:(bb + 1) * DEC_T], start=True, stop=True)
                        ins = e.matmul(psum[b][0:128, 64:64 + DEC_T], lhsT=w.kTs.ap[:, ch, 0:128], rhs=w.qz.ap[:, h, bb * DEC_T:(bb + 1) * DEC_T], start=True, stop=True)
                        return ins
                    S.op("pe", mm, reads=[("kwin", "hist")] + w.kTs.tags + w.qz.tags, writes=[PT(b)])
                    Eb = w.Eb[ei % 2]
                    Pb = w.Pb[ei % 2]
                    ei += 1
                    mh = min(16, MMAX[h])
                    S.op("act", lambda e, b=b, Eb=Eb, mh=mh: e.activation(out=Eb.ap[:, 0:mh * DEC_T], in_=psum[b][:, 0:mh * DEC_T], func=AF.Exp), reads=[PT(b)], writes=Eb.tags)
                    S.op("act", lambda e, b=b, Eb=Eb: e.activation(out=Eb.ap[:, 64:64 + DEC_T], in_=psum[b][:, 64:64 + DEC_T], func=AF.Exp), reads=[PT(b)], writes=Eb.tags)
                    S.op("dve", lambda e, Eb=Eb, Pb=Pb, h=h, mh=mh: e.tensor_tensor(out=Pb.ap[:, 0:mh * DEC_T].rearrange("p (m q) -> p m q", q=DEC_T),
                                                                          in0=Eb.ap[:, 0:mh * DEC_T].rearrange("p (m q) -> p m q", q=DEC_T),
                                                                          in1=wmask[:, 1:mh + 1, h, 0:DEC_T], op=ALU.mult),
                         reads=Eb.tags + [("wmask",)], writes=Pb.tags)
                    S.op("dve", lambda e, Eb=Eb, Pb=Pb, h=h, bb=bb: e.tensor_tensor(out=Pb.ap[:, 64:64 + DEC_T], in0=Eb.ap[:, 64:64 + DEC_T],
                                                                                 in1=wsm[:, bb, h, :], op=ALU.mult),
                         reads=Eb.tags + [("wsm",)], writes=Pb.tags)

                    def pv(e, Pb=Pb, h=h, bO=bO):
                        ins = None
                        for m in range(1, min(16, MMAX[h]) + 1):
                            wkb = 16 - m
                            ins = e.matmul(psum[bO][:, 0:DEC_T], lhsT=vwin[:, wkb, h, :], rhs=Pb.ap[:, (m - 1) * DEC_T:m * DEC_T], start=(m == 1), stop=False)
                        ins = e.matmul(psum[bO][:, 0:DEC_T], lhsT=w.vaug.ap[:, h, :], rhs=Pb.ap[:, 64:64 + DEC_T], start=False, stop=True)
                        return ins
                    S.op("pe", pv, reads=Pb.tags + [("vwin", "hist")] + w.vaug.tags, writes=[PT(bO)])
                    att_finish(g, w, bO, h, None, (bb * DEC_T, (bb + 1) * DEC_T), li)
                    li += 1

        def hgrn_prep(g, l, h):
            n, w = g.n, g.ws
            hs, lv = w.hp, w.live[h]
            prompt = g is gP
            C = HGC if prompt else DEC_T
            nchunk = n // C
            rsm = rsmask if prompt else rsmask_s
            rsm_tag = ("rsmask",) if prompt else ("rsmask_s",)
            lbc = lbT[:, l * HG_H + h:l * HG_H + h + 1]
            omc = omlT[:, l * HG_H + h:l * HG_H + h + 1]
            A = lambda bf: bf.ap[:, 0:n]
            S.op("pool", lambda e: e.tensor_scalar(out=A(hs.f), in0=A(hs.f), scalar1=omc, scalar2=lbc, op0=ALU.mult, op1=ALU.add),
                 reads=hs.f.tags + [("lbT",), ("omlT",)], writes=hs.f.tags)
            S.op("act", lambda e: e.activation(out=A(hs.lg), in_=A(hs.f), func=AF.Ln), reads=hs.f.tags, writes=hs.lg.tags)
            S.op("pool", lambda e: e.tensor_scalar(out=A(hs.kk), in0=A(hs.f), scalar1=-1.0, scalar2=1.0, op0=ALU.mult, op1=ALU.add),
                 reads=hs.f.tags, writes=hs.kk.tags)
            S.op("dve", lambda e: e.tensor_tensor_scan(out=A(hs.f), data0=rsm[:, 0:n], data1=A(hs.lg), initial=0.0, op0=ALU.mult, op1=ALU.add),
                 reads=hs.lg.tags + [rsm_tag], writes=hs.f.tags)
            b3 = hs.f.ap[:, 0:n].rearrange("p (c t) -> p c t", t=C)
            MID = C // 2 - 1
            S.op("act", lambda e: e.activation(out=lv.ebl.ap[:, 0:nchunk], in_=b3[:, :, C - 1], func=AF.Exp), reads=hs.f.tags, writes=lv.ebl.tags)
            S.op("act", lambda e: e.activation(out=lv.emid.ap[:, 0:nchunk], in_=b3[:, :, MID], func=AF.Exp), reads=hs.f.tags, writes=lv.emid.tags)
            S.op("dve", lambda e: e.tensor_tensor(out=hs.lg.ap[:, 0:n].rearrange("p (c t) -> p c t", t=C),
                                                  in0=b3, in1=b3[:, :, MID:MID + 1].to_broadcast([128, nchunk, C]), op=ALU.subtract),
                 reads=hs.f.tags, writes=hs.lg.tags)
            S.op("act", lambda e: e.activation(out=A(hs.eb), in_=A(hs.lg), func=AF.Exp), reads=hs.lg.tags, writes=hs.eb.tags)
            S.op("act", lambda e: e.activation(out=A(hs.en), in_=A(hs.lg), func=AF.Exp, scale=-1.0), reads=hs.lg.tags, writes=hs.en.tags)
            S.op("dve", lambda e: e.tensor_tensor(out=A(lv.qt), in0=A(hs.q), in1=A(hs.eb), op=ALU.mult), reads=hs.q.tags + hs.eb.tags, writes=lv.qt.tags)
            S.op("dve", lambda e: e.tensor_tensor(out=A(lv.kt), in0=A(hs.kk), in1=A(hs.en), op=ALU.mult), reads=hs.kk.tags + hs.en.tags, writes=lv.kt.tags)
            S.op("dve", lambda e: e.tensor_tensor(out=hs.lg.ap[:, 0:n].rearrange("p (c t) -> p c t", t=C),
                                                  in0=b3[:, :, C - 1:C].to_broadcast([128, nchunk, C]), in1=b3, op=ALU.subtract),
                 reads=hs.f.tags + hs.eb.tags + hs.en.tags, writes=hs.lg.tags)
            S.op("act", lambda e: e.activation(out=A(hs.en), in_=A(hs.lg), func=AF.Exp), reads=hs.lg.tags, writes=hs.en.tags)
            S.op("dve", lambda e: e.tensor_tensor(out=A(lv.kh), in0=A(hs.kk), in1=A(hs.en), op=ALU.mult), reads=hs.kk.tags + hs.en.tags, writes=lv.kh.tags)

        def hgrn_chain(g, l, j):
            n, w = g.n, g.ws
            prompt = g is gP
            C = HGC if prompt else DEC_T
            bs = 128 if prompt else n
            nblk = n // bs
            NCB = bs // C
            bdm = bdmask if prompt else bdmask_s
            bdm_tag = ("bdmask",) if prompt else ("bdmask_s",)
            rm0 = 0 if prompt else 4
            vt_tags = w.vtok.tags
            ps_rr[0] = ps_rr[0] % 4
            npr_save = NPRv[0]
            NPRv[0] = 4
            bOT = [4 + h for h in range(HG_H)]
            ki = [0]
            if prompt:
                for h in range(HG_H):
                    S.op("pool", lambda e, h=h: e.tensor_copy(out=w.Tb[h][0].ap, in_=Sst[:, l, h, :]), reads=[("Sst", l, h)], writes=w.Tb[h][0].tags)
                    S.op("act", lambda e, h=h: e.activation(out=w.Sb[h][0].ap, in_=Sst[:, l, h, :], func=AF.Copy, scale=w.live[h].emid.ap[:, 0:1]),
                         reads=[("Sst", l, h)] + w.live[h].emid.tags, writes=w.Sb[h][0].tags)

            st1 = {}

            def stage1(tb, h):
                lv = w.live[h]
                c0 = tb * bs
                bA = ps_next()
                S.op("pe", lambda e: e.matmul(psum[bA][0:128, 0:bs], lhsT=lv.kt.ap[:, c0:c0 + 128], rhs=lv.qt.ap[:, c0:c0 + bs], start=True, stop=True),
                     reads=lv.kt.tags + lv.qt.tags, writes=[PT(bA)])
                Am = w.Am[ki[0] % 4]
                khm = w.khm[ki[0] % 4]
                ki[0] += 1
                S.op("dve", lambda e: e.tensor_tensor(out=Am.ap[:, 0:bs], in0=psum[bA][:, 0:bs], in1=bdm[:, 0:bs], op=ALU.mult),
                     reads=[PT(bA), bdm_tag], writes=Am.tags)
                bT = ps_next()
                S.op("pe", lambda e: e.transpose(psum[bT][0:bs, 0:128], lv.kh.ap[:, c0:c0 + bs], ident[:]),
                     reads=lv.kh.tags + [("ident",)], writes=[PT(bT)])
                for jj in range(NCB):
                    rc = rowmask[0:bs, rm0 + jj:rm0 + jj + 1]
                    S.op("act", lambda e, jj=jj, rc=rc: e.activation(out=khm.ap[0:bs, jj, :], in_=psum[bT][0:bs, 0:128], func=AF.Copy, scale=rc),
                         reads=[PT(bT), ("rowmask",)], writes=khm.tags)
                st1[(tb, h)] = (Am, khm)

            def stage1b(tb, h):
                lv = w.live[h]
                c0 = tb * bs
                Am, khm = st1[(tb, h)]
                bD = ps_next()

                def dS(e):
                    ins = None
                    for jj in range(NCB):
                        ins = e.matmul(psum[bD][:, jj * 128:(jj + 1) * 128], lhsT=khm.ap[:, jj, :], rhs=w.vtok.ap[:, tb, h * 128:(h + 1) * 128], start=True, stop=True)
                    return ins
                S.op("pe", dS, reads=khm.tags + vt_tags, writes=[PT(bD)])
                S.op("pe", lambda e: e.matmul(psum[bOT[h]][:, c0:c0 + bs], lhsT=w.vtok.ap[:, tb, h * 128:(h + 1) * 128], rhs=Am.ap[:, 0:bs], start=True, stop=False),
                     reads=Am.tags + vt_tags, writes=[PT(bOT[h])])
                for jj in range(NCB):
                    if prompt:
                        cidx = tb * NCB + jj
                        ecol = lv.ebl.ap[:, cidx:cidx + 1]
                        tin, tout = w.Tb[h][cidx % 2], w.Tb[h][(cidx + 1) % 2]
                        S.op("dve", lambda e, ecol=ecol, jj=jj, tin=tin, tout=tout: e.scalar_tensor_tensor(out=tout.ap, in0=tin.ap, scalar=ecol,
                                                                                      in1=psum[bD][:, jj * 128:(jj + 1) * 128], op0=ALU.mult, op1=ALU.add),
                             reads=tin.tags + [PT(bD)] + lv.ebl.tags, writes=tout.tags)
                        sbn = w.Sb[h][(cidx + 1) % 5]
                        if cidx + 1 < nblk * NCB:
                            mcol = lv.emid.ap[:, cidx + 1:cidx + 2]
                            S.op("act", lambda e, sbn=sbn, tout=tout, mcol=mcol: e.activation(out=sbn.ap, in_=tout.ap, func=AF.Copy, scale=mcol),
                                 reads=tout.tags + lv.emid.tags, writes=sbn.tags)
                    else:
                        sbc = w.Sb[h][jj]
                        S.op("act", lambda e, sbc=sbc, jj=jj: e.activation(out=sbc.ap, in_=w.shs.ap[:, jj, h, :], func=AF.Copy, scale=lv.emid.ap[:, jj:jj + 1]),
                             reads=w.shs.tags + lv.emid.tags, writes=sbc.tags)
                        ecol = lv.ebl.ap[:, jj:jj + 1]
                        S.op("dve", lambda e, ecol=ecol, jj=jj: e.scalar_tensor_tensor(out=w.shs.ap[:, jj, h, :], in0=w.shs.ap[:, jj, h, :], scalar=ecol,
                                                                                      in1=psum[bD][:, jj * 128:(jj + 1) * 128], op0=ALU.mult, op1=ALU.add),
                             reads=w.shs.tags + [PT(bD)] + lv.ebl.tags, writes=w.shs.tags)

            def stage2(tb, h):
                lv = w.live[h]
                c0 = tb * bs
                for jj in range(NCB):
                    q0 = c0 + jj * C
                    sbc = w.Sb[h][(tb * NCB + jj) % 5] if prompt else w.Sb[h][jj]
                    S.op("pe", lambda e, sbc=sbc, q0=q0, jj=jj: e.matmul(psum[bOT[h]][:, q0:q0 + C], lhsT=sbc.ap, rhs=lv.qt.ap[:, q0:q0 + C], start=False, stop=(jj == NCB - 1)),
                         reads=sbc.tags + lv.qt.tags, writes=[PT(bOT[h])])

            seq = [(tb, h) for tb in range(nblk) for h in range(HG_H)]
            for i in range(len(seq) + 2):
                if i < len(seq):
                    stage1(*seq[i])
                if 1 <= i <= len(seq):
                    stage1b(*seq[i - 1])
                if i >= 2:
                    stage2(*seq[i - 2])
            if prompt:
                nct = nblk * NCB
                for h in range(HG_H):
                    S.op("pool", lambda e, h=h: e.tensor_copy(out=Sst[:, l, h, :], in_=w.Tb[h][nct % 2].ap), reads=w.Tb[h][nct % 2].tags, writes=[("Sst", l, h)])
            for h in range(HG_H):
                lv = w.live[h]
                S.op("act", lambda e, h=h: e.activation(out=w.osq.ap[:, 0:n], in_=psum[bOT[h]][:, 0:n], func=AF.Square), reads=[PT(bOT[h])], writes=w.osq.tags)
                bN = ps_next()
                S.op("pe", lambda e, bN=bN: e.matmul(psum[bN][:, 0:n], lhsT=ones_b[:], rhs=w.osq.ap[:, 0:n], start=True, stop=True), reads=w.osq.tags + [("ones",)], writes=[PT(bN)])
                S.op("act", lambda e, bN=bN: e.activation(out=w.ot.ap[:, 0:n], in_=psum[bN][:, 0:n], func=AF.Ln, scale=1.0 / 128, bias=eps_col[:]), reads=[PT(bN), ("eps",)], writes=w.ot.tags)
                S.op("act", lambda e: e.activation(out=w.ot.ap[:, 0:n], in_=w.ot.ap[:, 0:n], func=AF.Exp, scale=-0.5), reads=w.ot.tags, writes=w.ot.tags)
                S.op("dve", lambda e, h=h: e.tensor_tensor(out=w.ot.ap[:, 0:n], in0=psum[bOT[h]][:, 0:n], in1=w.ot.ap[:, 0:n], op=ALU.mult), reads=[PT(bOT[h])] + w.ot.tags, writes=w.ot.tags)
                S.op("dve", lambda e, h=h, lv=lv: e.scalar_tensor_tensor(out=g.ym[:, 4 + h, 0:n], in0=w.ot.ap[:, 0:n], scalar=pcol("hgn", l), in1=lv.gt.ap[:, 0:n], op0=ALU.mult, op1=ALU.mult),
                     reads=w.ot.tags + lv.gt.tags + [("prmT",)], writes=[ytag(g, 4 + h)])
            NPRv[0] = npr_save

        def hgrn_vtok(g, h, unit, utag):
            w = g.ws
            b, bs, nblk = win_chunk_tm(g, unit, utag)
            if g is gP:
                S.op("act", lambda e: e.activation(out=w.vtok.ap[:, :, h * 128:(h + 1) * 128], in_=psum[b][:, :].rearrange("p (b c) -> p b c", c=128), func=AF.Copy),
                     reads=[PT(b)], writes=w.vtok.tags)
            else:
                S.op("act", lambda e: e.activation(out=w.vtok.ap[:, 0, h * 128:(h + 1) * 128], in_=psum[b][:, 0:128], func=AF.Copy),
                     reads=[PT(b)], writes=w.vtok.tags)

        def att_hist_load(l, j):
            t0 = j * TT
            avail = min(WIN, t0)
            if avail > 0:
                S.dma("sp", kwin[:, :, WIN - avail:WIN], sK[l].rearrange("(c p) t -> p c t", p=128)[:, :, t0 - avail:t0],
                      reads=[("sK", l, jj) for jj in range(j)], writes=[("kwin", "hist")])
                nb = avail // 128
                S.dma("sp", vwin[:, 16 - nb:16, :, :].rearrange("p b h d -> p b (h d)"),
                      sV[l, t0 - avail:t0, :].rearrange("(b p) c -> p b c", p=128),
                      reads=[("sV", l, jj) for jj in range(j)], writes=[("vwin", "hist")])

        def mix_layer(groups, l, j):
            if STG >= 3:
                att_hist_load(l, j)
            for g in groups:
                rmsnorm(g, "lnm", l * 8)
            for c in FM_CONV:
                u, t = WS.consume(("i", l, c))
                for g in groups:
                    conv_evac(g, c, win_chunk_fm(g, l, c, u, t))
            for g in groups:
                conv_glu(g)
            build_diag(l)
            if STG >= 3:
                for g in groups:
                    S.op("pool", lambda e, g=g: e.memset(g.ws.qz.ap, 0.0), writes=g.ws.qz.tags)
            for c in FM_QKV:
                u, t = WS.consume(("i", l, c))
                if STG < 3:
                    continue
                for g in groups:
                    if c in (4, 5):
                        att_q_evac(g, c, win_chunk_fm(g, l, c, u, t))
                    elif c in (6, 7):
                        att_k_evac(g, l, j, c, win_chunk_fm(g, l, c, u, t))
                    if c >= 6 and "kvtm" not in os.environ.get("DBG_SKIP", ""):
                        att_kv_tm(g, l, j, c, u, t)
            for g in groups:
                conv_main(g, l, j)
            if STG >= 3:
                if "attp" in os.environ.get("DBG_SKIP", ""):
                    for c in (2, 3):
                        S.op("pool", lambda e, c=c: e.memset(gP.ym[:, c, 0:gP.n], 0.0), writes=[ytag(gP, c)])
                else:
                    att_prompt(l, j)
                if gS in groups:
                    if os.environ.get("NO_ATT_S"):
                        for c in (2, 3):
                            S.op("pool", lambda e, c=c: e.memset(gS.ym[:, c, 0:gS.n], 0.0), writes=[ytag(gS, c)])
                    else:
                        att_sample(l)
            else:
                for g in groups:
                    for c in (2, 3):
                        S.op("pool", lambda e, g=g, c=c: e.memset(g.ym[:, c, 0:g.n], 0.0), writes=[ytag(g, c)])
            for h in range(HG_H):
                for c, key in ((18 + h, "v"), (10 + h, "q"), (14 + h, "f"), (22 + h, "g")):
                    u, t = WS.consume(("i", l, c))
                    if STG < 4:
                        continue
                    for g in groups:
                        if key == "v":
                            hgrn_vtok(g, h, u, t)
                        else:
                            bz = win_chunk_fm(g, l, c, u, t)
                            dst = {"q": g.ws.hp.q, "f": g.ws.hp.f, "g": g.ws.live[h].gt}[key]
                            fnc = AF.Sigmoid if key == "f" else AF.Silu
                            S.op("act", lambda e, g=g, dst=dst, bz=bz, fnc=fnc: e.activation(out=dst.ap[:, 0:g.n], in_=psum[bz][:, 0:g.n], func=fnc),
                                 reads=[PT(bz)], writes=dst.tags)
                if STG >= 4:
                    for g in groups:
                        hgrn_prep(g, l, h)
                else:
                    for g in groups:
                        S.op("pool", lambda e, g=g, h=h: e.memset(g.ym[:, 4 + h, 0:g.n], 0.0), writes=[ytag(g, 4 + h)])
            if STG >= 4:
                for g in groups:
                    if g is gS:
                        S.dma("sp", wsS.shs.ap, shg_in[l].rearrange("b h d v -> d b h v"), writes=wsS.shs.tags)
                    hgrn_chain(g, l, j)
            if STG >= 4:
                if j == cfg.ntile - 1:
                    S.dma("pool", o_hp[l].rearrange("h d v -> d h v"), Sst[:, l, :, :], reads=[("Sst", l, h) for h in range(HG_H)], writes=[("o_hp", l)], is_output=True)
                if gS in groups:
                    S.dma("pool", o_hs[l].rearrange("b h d v -> d b h v"), wsS.shs.ap, reads=wsS.shs.tags, writes=[("o_hs", l)], is_output=True)
            for oc in range(NCH):
                u, t = WS.consume(("o", l, oc))
                for g in groups:
                    n = g.n
                    b = ps_next()
                    proj_fm(u, t, NCH, lambda k, g=g, n=n: g.ym[:, k, 0:n], [ytag(g, k) for k in range(NCH)], n, b)
                    S.op("dve", lambda e, g=g, n=n, b=b, oc=oc: e.tensor_tensor(out=g.x[:, oc, 0:n], in0=psum[b][:, 0:n], in1=g.x[:, oc, 0:n], op=ALU.add),
                         reads=[PT(b), xtag(g, oc)], writes=[xtag(g, oc)])

        load_tokens(gS, xs, [(0, NS, 0)])
        for j in range(cfg.ntile):
            load_tokens(gP, xp, [(j * TT + 128 * b, 128, 128 * b) for b in range(TT // 128)])
            groups = [gP, gS] if j == 0 else [gP]
            for l in range(L):
                for g in groups:
                    rmsnorm(g, "ln1", l * 8)
                ffn(groups, 0, l)
                if STG >= 2:
                    mix_layer(groups, l, j)
                if STG >= 9:
                    for g in groups:
                        rmsnorm(g, "ln2", l * 8)
                    ffn(groups, 1, l)
            final_norm_store(gP, yp, [(j * TT + 128 * b, 128, 128 * b) for b in range(TT // 128)])
            if j == 0:
                final_norm_store(gS, ys, [(0, NS, 0)])

        if cfg.debug:
            dbg = dout("dbg_ym", [128, NCH * TT], BF16)
            S.dma("pool", dbg, ymix[:].rearrange("p c t -> p (c t)"), reads=[ytag(gP, c) for c in range(NCH)], writes=[("dbg", 0)], is_output=True)
            dbgs = dout("dbg_yms", [128, NCH * NS], BF16)
            S.dma("pool", dbgs, ymixs[:].rearrange("p c t -> p (c t)"), reads=[ytag(gS, c) for c in range(NCH)], writes=[("dbg", 1)], is_output=True)
        S.finish()
        with nc.Block() as block:
            S.replay(block)
        cfg.n_inst = dict(S.n_inst)
    return nc


def pack_prm(depth, ln1, lnm, ln2, lnf, dww, dwb, clg, clb, hlb, hgn):
    off, prows = prm_layout(depth)
    out = np.zeros((prows, 128), np.float32)

    def put(name, arr):
        a = np.ascontiguousarray(arr, dtype=np.float32).reshape(-1, 128)
        out[off[name]:off[name] + a.shape[0]] = a
    put("ln1", ln1); put("lnm", lnm); put("ln2", ln2); put("lnf", lnf)
    put("dww", dww); put("dwb", dwb); put("clg", clg); put("clb", clb); put("hlb", hlb); put("hgn", hgn)
    return out


_CACHE = {}


def run(cfg, inputs):
    key = (cfg.seq, cfg.depth, cfg.nsamp, cfg.n_cores, cfg.nseq, cfg.stages, cfg.debug)
    if key not in _CACHE:
        _CACHE[key] = build_program(cfg)
    nc = _CACHE[key]
    L = cfg.depth
    f32 = lambda a: np.ascontiguousarray(a, dtype=np.float32)
    prm = pack_prm(L, inputs["ln_ffn1"], inputs["ln_mix"], inputs["ln_ffn2"], inputs["ln_final"],
                   inputs["conv_dw_w"], inputs["conv_dw_b"], inputs["conv_ln_g"], inputs["conv_ln_b"],
                   inputs["hg_lower_bounds"], inputs["hg_norm_g"])
    shared = {
        "prm": prm,
        "wg1": f32(inputs["w_ffn1_gate"]), "wu1": f32(inputs["w_ffn1_up"]), "wd1": f32(inputs["w_ffn1_down"]),
        "wg2": f32(inputs["w_ffn2_gate"]), "wu2": f32(inputs["w_ffn2_up"]), "wd2": f32(inputs["w_ffn2_down"]),
        "wi": f32(inputs["w_in"]), "wo": f32(inputs["w_out"]),
    }
    shared.update(const_tables(cfg.nsamp))
    xpf = f32(inputs["x_prompt"])
    xsf = f32(inputs["x_sample"])
    sconv = f32(inputs["state_conv"])
    ck = f32(inputs["cache_k_win"]).reshape(L, -1, WIN, 256)
    cv = f32(inputs["cache_v_win"]).reshape(L, -1, WIN, 256)
    shg = f32(inputs["state_hgrn"])
    in_maps = []
    zero_seq = None
    nb = cfg.nsamp
    for c in range(cfg.n_cores):
        m = dict(shared)
        if c < cfg.nseq:
            m["xp"] = xpf[c]
        else:
            if zero_seq is None:
                zero_seq = np.zeros((cfg.seq, D), np.float32)
            m["xp"] = zero_seq
        sl = slice(c * nb, (c + 1) * nb)
        m["xs"] = np.ascontiguousarray(xsf[sl].reshape(cfg.ns_tok, D))
        m["sconv"] = np.ascontiguousarray(sconv[:, sl])
        m["ck"] = np.ascontiguousarray(ck[:, sl])
        m["cv"] = np.ascontiguousarray(cv[:, sl])
        m["shg"] = np.ascontiguousarray(shg[:, sl])
        in_maps.append(m)
    res = run_bass_kernel_spmd(nc, in_maps, core_ids=list(range(cfg.n_cores)))
    return res.results


def assemble(cfg, r):
    L, nb = cfg.depth, cfg.nsamp
    nsq = cfg.nseq
    y_prompt = np.stack([r[c]["yp"] for c in range(nsq)])
    y_sample = np.concatenate([r[c]["ys"].reshape(nb, DEC_T, D) for c in range(cfg.n_cores)], 0)
    cp = np.stack([r[c]["o_cp"] for c in range(nsq)], 1)
    cs = np.concatenate([r[c]["o_cs"] for c in range(cfg.n_cores)], 1)
    kp = np.stack([r[c]["o_kp"].reshape(L, cfg.keep, ATT_H, HD) for c in range(nsq)], 1)
    vp = np.stack([r[c]["o_vp"].reshape(L, cfg.keep, ATT_H, HD) for c in range(nsq)], 1)
    ks = np.concatenate([r[c]["o_ks"].reshape(L, nb, DEC_T, ATT_H, HD) for c in range(cfg.n_cores)], 1)
    vs = np.concatenate([r[c]["o_vs"].reshape(L, nb, DEC_T, ATT_H, HD) for c in range(cfg.n_cores)], 1)
    hp = np.stack([r[c]["o_hp"] for c in range(nsq)], 1)
    hs = np.concatenate([r[c]["o_hs"] for c in range(cfg.n_cores)], 1)
    outs = (y_prompt, y_sample, cp, cs, kp, vp, ks, vs, hp, hs)
    return tuple(np.ascontiguousarray(o, dtype=np.float32) for o in outs)


def kernel(**inputs):
    cfg = Cfg()
    r = run(cfg, inputs)
    return assemble(cfg, r)
```

```python
import contextlib
import os
import numpy as np
import concourse.bass as bass
import concourse.mybir as mybir
from concourse.bass_utils import run_bass_kernel_spmd

F32 = mybir.dt.float32
BF16 = mybir.dt.bfloat16
AF = mybir.ActivationFunctionType
ALU = mybir.AluOpType

D = 1024
NCH = 8
FFN = 2816
NFC = 22
N_IN = 3328
NIC = 26
CONV_DIM = 256
CW = 31
ATT_H = 4
HD = 64
HG_H = 4
HGC = 64
WIN = 2048
TT = 512
EPS = 1e-6
DEC_T = 4


class Sched:
    COMPUTE = ("pe", "act", "dve", "pool")

    def __init__(self, nc, stack, n_dma_sems=24):
        self.nc = nc
        self.streams = {e: [] for e in ("pe", "act", "dve", "pool", "sp")}
        self.sems = {}
        for e in self.COMPUTE:
            self.sems[e] = stack.enter_context(nc.semaphore("s_" + e))
        self.count = {e: 0 for e in self.COMPUTE}
        self.dma_sems = {}
        self.dma_val = {}
        self.dma_rr = {}
        for q in ("sp", "pool"):
            self.dma_sems[q] = []
            for i in range(n_dma_sems):
                key = "d_%s_%d" % (q, i)
                self.sems[key] = stack.enter_context(nc.semaphore(key))
                self.dma_sems[q].append(key)
                self.dma_val[key] = 0
            self.dma_rr[q] = 0
        self.waited = {}
        self.last_write = {}
        self.readers = {}
        self.out_tokens = []
        self.n_inst = {e: 0 for e in self.streams}

    def _need(self, eng, token, needs):
        if token is None:
            return
        key, val = token
        if self.waited.get((eng, key), 0) >= val:
            return
        if needs.get(key, 0) < val:
            needs[key] = val

    def _collect(self, eng, reads, writes):
        needs = {}
        for t in reads:
            self._need(eng, self.last_write.get(t), needs)
        for t in writes:
            self._need(eng, self.last_write.get(t), needs)
            for tok in self.readers.get(t, ()):
                self._need(eng, tok, needs)
        return needs

    def _emit_waits(self, eng, needs, is_dma=False):
        for key, val in needs.items():
            if key == eng and not is_dma:
                continue
            sem = self.sems[key]
            self.streams[eng].append(lambda e, sem=sem, val=val: e.wait_ge(sem, val))
            self.waited[(eng, key)] = val
            self.n_inst[eng] += 1

    def _commit(self, token, reads, writes):
        for t in reads:
            self.readers.setdefault(t, []).append(token)
        for t in writes:
            self.last_write[t] = token
            self.readers[t] = []

    def op(self, eng, fn, reads=(), writes=()):
        needs = self._collect(eng, reads, writes)
        if eng in needs:
            own = needs.pop(eng)
            val = own if eng != "pe" else 0
            if val > self.waited.get((eng, eng), 0):
                sem = self.sems[eng]
                self.streams[eng].append(lambda e, sem=sem, val=val: e.wait_ge(sem, val))
                self.waited[(eng, eng)] = val
        self._emit_waits(eng, needs)
        self.count[eng] += 1
        token = (eng, self.count[eng])
        sem = self.sems[eng]
        self.streams[eng].append(lambda e, fn=fn, sem=sem: fn(e).then_inc(sem, 1))
        self.n_inst[eng] += 1
        self._commit(token, reads, writes)
        return token

    def dma(self, q, out, in_, reads=(), writes=(), is_output=False):
        needs = self._collect(q, reads, writes)
        rr = self.dma_rr[q]
        self.dma_rr[q] = (rr + 1) % len(self.dma_sems[q])
        key = self.dma_sems[q][rr]
        prev = self.dma_val[key]
        if prev > 0 and self.waited.get((q, key), 0) < prev:
            needs[key] = max(needs.get(key, 0), prev)
        self._emit_waits(q, needs, is_dma=True)
        val = prev + 16
        self.dma_val[key] = val
        sem = self.sems[key]
        self.streams[q].append(lambda e, out=out, in_=in_, sem=sem: e.dma_start(out=out, in_=in_).then_inc(sem, 16))
        self.n_inst[q] += 1
        token = (key, val)
        self._commit(token, reads, writes)
        if is_output:
            self.out_tokens.append(token)
        return token

    def barrier(self):
        for eng in self.streams:
            for x in self.COMPUTE:
                if x != eng and self.count[x] > 0:
                    sem, val = self.sems[x], self.count[x]
                    self.streams[eng].append(lambda e, sem=sem, val=val: e.wait_ge(sem, val))
                    self.waited[(eng, x)] = val
            for key, val in self.dma_val.items():
                if val > 0:
                    sem = self.sems[key]
                    self.streams[eng].append(lambda e, sem=sem, val=val: e.wait_ge(sem, val))
                    self.waited[(eng, key)] = val
        self.last_write = {}
        self.readers = {}

    def finish(self):
        finals = {}
        for key, val in self.out_tokens:
            finals[key] = max(finals.get(key, 0), val)
        for key, val in finals.items():
            sem = self.sems[key]
            self.streams["sp"].append(lambda e, sem=sem, val=val: e.wait_ge(sem, val))

    def replay(self, block):
        nc = self.nc
        streams = self.streams

        @block.sync
        def _(e):
            for f in streams["sp"]:
                f(e)

        @block.tensor
        def _(e):
            for f in streams["pe"]:
                f(e)

        @block.scalar
        def _(e):
            for f in streams["act"]:
                f(e)

        @block.vector
        def _(e):
            for f in streams["dve"]:
                f(e)

        @block.gpsimd
        def _(e):
            for f in streams["pool"]:
                f(e)


class Cfg:
    def __init__(self, seq=8192, depth=4, nsamp=4, n_cores=8, nseq=2, stages=99):
        self.seq = seq
        self.depth = depth
        self.nsamp = nsamp
        self.n_cores = n_cores
        self.nseq = nseq
        self.stages = stages
        self.ntile = seq // TT
        self.ns_tok = nsamp * DEC_T
        self.keep = min(WIN, seq)
        self.debug = False


def prm_layout(depth):
    off = {}
    r = 0
    for name, rows in (("ln1", depth * 8), ("lnm", depth * 8), ("ln2", depth * 8), ("lnf", 8),
                       ("dww", depth * CW * 2), ("dwb", depth * 2), ("clg", depth * 2), ("clb", depth * 2),
                       ("hlb", depth * 4), ("hgn", depth)):
        off[name] = r
        r += rows
    return off, ((r + 127) // 128) * 128


NM = 17


def att_weight_table():
    k = np.arange(128)[:, None, None]
    m = np.arange(NM)[None, :, None]
    q = np.arange(128)[None, None, :]
    d = 128 * m + q - k
    mult = ((d >= 0) & (d <= 128)).astype(np.float64) + ((d >= 0) & (d <= 512) & (d % 4 == 0)) + ((d >= 0) & (d <= 2048) & (d % 16 == 0))
    out = np.zeros((128, NM, ATT_H, 128), np.float32)
    for h in range(ATT_H):
        slope = 2.0 ** (-8.0 * (h + 1) / ATT_H)
        out[:, :, h, :] = mult * np.exp(-slope * np.maximum(d, 0))
    return out


def const_tables(nsamp):
    c = {}
    c["ident"] = np.eye(128, dtype=np.float32)
    c["wtab"] = att_weight_table().reshape(128, NM * ATT_H * 128)
    p = np.arange(128)
    bd = ((p[:, None] // HGC) == (p[None, :] // HGC)) & (p[:, None] <= p[None, :])
    c["bdmask"] = bd.astype(np.float32)
    bds = np.zeros((128, 128), np.float32)
    ns = nsamp * DEC_T
    ps = np.arange(ns)
    bds[:ns, :ns] = (((ps[:, None] // DEC_T) == (ps[None, :] // DEC_T)) & (ps[:, None] <= ps[None, :]))
    c["bdmask_s"] = bds
    rm = np.zeros((128, 8), np.float32)
    for j in range(4):
        rm[:, j] = (p // HGC == j)
        rm[:ns, 4 + j] = (ps // DEC_T == j)
    c["rowmask"] = rm
    rs = np.ones((128, TT), np.float32)
    rs[:, ::HGC] = 0.0
    c["rsmask"] = rs
    rss = np.ones((128, 128), np.float32)
    rss[:, 0:ns:DEC_T] = 0.0
    c["rsmask_s"] = rss
    wsm = np.zeros((128, nsamp, ATT_H, DEC_T), np.float32)
    for kk in range(ns):
        kb, kt = divmod(kk, DEC_T)
        for t in range(DEC_T):
            if t >= kt:
                d = t - kt
                mult = 1 + (d % 4 == 0) + (d % 16 == 0)
                for h in range(ATT_H):
                    slope = 2.0 ** (-8.0 * (h + 1) / ATT_H)
                    wsm[kk, kb, h, t] = mult * np.exp(-slope * d)
    c["wsm"] = wsm.reshape(128, nsamp * ATT_H * DEC_T)
    return c


def build_program(cfg):
    nc = bass.Bass("TRN2", target_bir_lowering=False)
    L = cfg.depth
    SEQ = cfg.seq
    NS = cfg.ns_tok
    NB = cfg.nsamp
    KEEP = cfg.keep
    poff, prows = prm_layout(L)
    STG = cfg.stages

    def din(name, shape, dt=F32):
        return nc.dram_tensor(name, list(shape), dt, kind="ExternalInput").ap()

    def dout(name, shape, dt=F32):
        return nc.dram_tensor(name, list(shape), dt, kind="ExternalOutput").ap()

    def dscr(name, shape, dt=BF16):
        return nc.dram_tensor(name, list(shape), dt, kind="Internal").ap()

    xp = din("xp", [SEQ, D])
    xs = din("xs", [NS, D])
    prm = din("prm", [prows, 128])
    ident_in = din("ident", [128, 128])
    wtab_in = din("wtab", [128, NM * ATT_H * 128])
    bdmask_in = din("bdmask", [128, 128])
    bdmask_s_in = din("bdmask_s", [128, 128])
    rowmask_in = din("rowmask", [128, 8])
    rsmask_in = din("rsmask", [128, TT])
    rsmask_s_in = din("rsmask_s", [128, 128])
    wsm_in = din("wsm", [128, NB * ATT_H * DEC_T])
    sconv_in = din("sconv", [L, NB, CW - 1, CONV_DIM])
    ck_in = din("ck", [L, NB, WIN, 256])
    cv_in = din("cv", [L, NB, WIN, 256])
    shg_in = din("shg", [L, NB, HG_H, 128, 128])
    w_g = [din("wg1", [L, D, FFN]), din("wg2", [L, D, FFN])]
    w_u = [din("wu1", [L, D, FFN]), din("wu2", [L, D, FFN])]
    w_d = [din("wd1", [L, FFN, D]), din("wd2", [L, FFN, D])]
    w_i = din("wi", [L, D, N_IN])
    w_o = din("wo", [L, D, D])

    yp = dout("yp", [SEQ, D])
    ys = dout("ys", [NS, D])
    o_cp = dout("o_cp", [L, CW - 1, CONV_DIM])
    o_cs = dout("o_cs", [L, NB, CW - 1, CONV_DIM])
    o_kp = dout("o_kp", [L, KEEP, 256])
    o_vp = dout("o_vp", [L, KEEP, 256])
    o_ks = dout("o_ks", [L, NS, 256])
    o_vs = dout("o_vs", [L, NS, 256])
    o_hp = dout("o_hp", [L, HG_H, 128, 128])
    o_hs = dout("o_hs", [L, NB, HG_H, 128, 128])

    sG = [dscr("sg%d" % f, [L, NFC, 128, NCH * 128]) for f in range(2)]
    sU = [dscr("su%d" % f, [L, NFC, 128, NCH * 128]) for f in range(2)]
    sD = [dscr("sd%d" % f, [L, NCH, 128, NFC * 128]) for f in range(2)]
    sI = dscr("si", [L, NIC, 128, NCH * 128])
    sO = dscr("so", [L, NCH, 128, NCH * 128])
    sK = dscr("sk", [L, 256, SEQ])
    sV = dscr("sv", [L, SEQ, ATT_H * 128])

    stack = contextlib.ExitStack()
    with stack:
        S = Sched(nc, stack)

        def sb(name, shape, dt=F32):
            return stack.enter_context(nc.sbuf_tensor(name, list(shape), dt))

        class B:
            def __init__(self, ap, tags):
                self.ap, self.tags = ap, list(tags)

        NSTG = 3
        with contextlib.ExitStack() as pstack:
            stg_f = [pstack.enter_context(nc.sbuf_tensor("stgf%d" % i, [128, NFC * 128], F32)) for i in range(NSTG)]
            stg_b = [pstack.enter_context(nc.sbuf_tensor("stgb%d" % i, [128, NFC * 128], BF16)) for i in range(NSTG)]
            cast_rr = [0]

            def convert(src2d, K, c, dst_unit, dtag):
                kc = K // 128
                i = cast_rr[0]
                cast_rr[0] += 1
                s = i % NSTG
                srcv = src2d[:, c * 128:(c + 1) * 128].rearrange("(kc p) j -> p kc j", p=128)
                S.dma("sp", stg_f[s][:, 0:kc * 128].rearrange("p (kc j) -> p kc j", j=128), srcv, writes=[("stgf", s)])
                eng = ("dve", "act", "pool")[i % 3]
                if eng == "act":
                    fn = lambda e, s=s, kc=kc: e.activation(out=stg_b[s][:, 0:kc * 128], in_=stg_f[s][:, 0:kc * 128], func=AF.Copy)
                else:
                    fn = lambda e, s=s, kc=kc: e.tensor_copy(out=stg_b[s][:, 0:kc * 128], in_=stg_f[s][:, 0:kc * 128])
                S.op(eng, fn, reads=[("stgf", s)], writes=[("stgb", s)])
                S.dma("pool", dst_unit, stg_b[s][:, 0:kc * 128], reads=[("stgb", s)], writes=[dtag])

            for l in range(L):
                for f in range(2):
                    if f == 1 and STG < 9:
                        continue
                    for c in range(NFC):
                        convert(w_g[f][l], D, c, sG[f][l, c], ("sG", f, l, c))
                        convert(w_u[f][l], D, c, sU[f][l, c], ("sU", f, l, c))
                    for c in range(NCH):
                        convert(w_d[f][l], FFN, c, sD[f][l, c], ("sD", f, l, c))
                if STG >= 2:
                    for c in range(NIC):
                        convert(w_i[l], D, c, sI[l, c], ("sI", l, c))
                    for c in range(NCH):
                        convert(w_o[l], D, c, sO[l, c], ("sO", l, c))
            S.barrier()

        ARK = 52
        arena = sb("arena", [128, ARK * 512], BF16)

        def av(name, off_kb, kb, dt=BF16, pat=None, **dims):
            e0 = int(round(off_kb * 512))
            ne = int(round(kb * 512))
            ap = arena[:, e0:e0 + ne]
            if dt == F32:
                ap = ap.bitcast(F32)
            if pat is not None:
                ap = ap.rearrange(pat, **dims)
            k0 = int(np.floor(off_kb + 1e-9))
            k1 = int(np.ceil(off_kb + kb - 1e-9))
            return B(ap, [("ar", k) for k in range(k0, k1)])

        xT = sb("xT", [128, NCH, TT])
        hT = sb("hT", [128, NCH, TT], BF16)
        ymix = sb("ymix", [128, NCH, TT], BF16)
        xTs = sb("xTs", [128, NCH, NS])
        hTs = sb("hTs", [128, NCH, 128], BF16)
        hids = sb("hids", [128, NFC, NS], BF16)
        ymixs = sb("ymixs", [128, NCH, NS], BF16)
        rstd = sb("rstd", [128, TT])
        sgb = [sb("sgb%d" % i, [128, TT]) for i in range(2)]
        prmT = sb("prmT", [128, prows])
        ident = sb("identf", [128, 128])
        identb = sb("identb", [128, 128], BF16)
        ones_b = sb("ones_b", [128, 128], BF16)
        eps_col = sb("eps_col", [128, 1])
        NB8, NB22 = 6, 2
        wb8 = [sb("wb8_%d" % i, [128, NCH * 128], BF16) for i in range(NB8)]
        wb22 = [sb("wb22_%d" % i, [128, NFC * 128], BF16) for i in range(NB22)]
        kwin = sb("kwin", [128, 2, WIN + TT], BF16)
        vwin = sb("vwin", [128, (WIN + TT) // 128, ATT_H, 128], BF16)
        wmask = sb("wmask", [128, NM, ATT_H, 128], BF16)
        Sst = sb("Sst", [128, L, HG_H, 128])
        utail = sb("utail", [128, L, 2, CW - 1], BF16)
        ubuf = sb("ubuf", [128, 2, CW - 1 + TT], BF16)
        ubufs = sb("ubufs", [128, 2, NB, CW - 1 + DEC_T], BF16)
        bdmask = sb("bdmask_t", [128, 128], BF16)
        bdmask_s = sb("bdmasks_t", [128, 128], BF16)
        rowmask = sb("rowmask_t", [128, 8])
        rsmask = sb("rsmask_t", [128, TT])
        rsmask_s = sb("rsmasks_t", [128, 128])
        wsm = sb("wsm_t", [128, NB, ATT_H, DEC_T], BF16)
        lbT = sb("lbT", [128, L * HG_H])
        omlT = sb("omlT", [128, L * HG_H])
        lbtmp = sb("lbtmp", [128, 2 * L * HG_H + 2 * HG_H])

        hidc = [av("hid", c, 1.0) for c in range(NFC)]
        finc = [av("fin", 2 * c, 2.0, F32) for c in range(NCH)]
        sqc = [av("sq", 36 + c, 1.0) for c in range(NCH)]
        iobuf = [av("io", 44 + 4 * i, 4.0, F32) for i in range(2)]

        psum = [stack.enter_context(nc.psum_tensor("ps%d" % i, [128, 512], F32)) for i in range(8)]
        ps_rr = [0]
        NPR = 6

        NPRv = [NPR]

        def ps_next():
            b = ps_rr[0] % NPRv[0]
            ps_rr[0] = (b + 1) % NPRv[0]
            return b

        def PT(b):
            return ("ps", b)

        S.dma("sp", ident[:], ident_in[:], writes=[("ident",)])
        S.op("dve", lambda e: e.tensor_copy(out=identb[:], in_=ident[:]), reads=[("ident",)], writes=[("identb",)])
        S.op("pool", lambda e: e.memset(ones_b[:], 1.0), writes=[("ones",)])
        S.op("pool", lambda e: e.memset(eps_col[:], EPS), writes=[("eps",)])
        S.op("pool", lambda e: e.memset(hTs[:], 0.0), writes=[("S", "h", c) for c in range(NCH)])
        S.op("pool", lambda e: e.memset(vwin[:], 1.0), writes=[("vwin", "hist"), ("vwin", "cur")])
        S.op("pool", lambda e: e.memset(Sst[:], 0.0), writes=[("Sst", l, h) for l in range(L) for h in range(HG_H)])
        S.op("pool", lambda e: e.memset(utail[:], 0.0), writes=[("utail", l) for l in range(L)])
        S.dma("sp", rowmask[:], rowmask_in[:], writes=[("rowmask",)])
        S.dma("sp", rsmask[:], rsmask_in[:], writes=[("rsmask",)])
        S.dma("sp", rsmask_s[:], rsmask_s_in[:], writes=[("rsmask_s",)])
        for blk in range(prows // 128):
            io = iobuf[blk % 2]
            S.dma("sp", io.ap[:, 0:128], prm[blk * 128:(blk + 1) * 128, :], writes=io.tags)
            b = ps_next()
            S.op("pe", lambda e, io=io, b=b: e.transpose(psum[b][:, 0:128], io.ap[:, 0:128], ident[:]),
                 reads=io.tags + [("ident",)], writes=[PT(b)])
            S.op("act", lambda e, b=b, blk=blk: e.activation(out=prmT[:, blk * 128:(blk + 1) * 128], in_=psum[b][:, 0:128], func=AF.Copy),
                 reads=[PT(b)], writes=[("prmT",)])
        for src, dst, tag in ((bdmask_in, bdmask, "bdmask"), (bdmask_s_in, bdmask_s, "bdmask_s")):
            io = iobuf[0]
            S.dma("sp", io.ap[:, 0:128], src[:], writes=io.tags)
            S.op("dve", lambda e, io=io, dst=dst: e.tensor_copy(out=dst[:], in_=io.ap[:, 0:128]), reads=io.tags, writes=[(tag,)])
        io = iobuf[1]
        nws = NB * ATT_H * DEC_T
        S.dma("sp", io.ap[:, 0:nws], wsm_in[:], writes=io.tags)
        S.op("dve", lambda e, io=io: e.tensor_copy(out=wsm[:].rearrange("p b h t -> p (b h t)"), in_=io.ap[:, 0:nws]), reads=io.tags, writes=[("wsm",)])
        if STG >= 3:
            wflat = wmask[:].rearrange("p m h q -> p (m h q)")
            tot = NM * ATT_H * 128
            for i, c0 in enumerate(range(0, tot, 1024)):
                io = iobuf[i % 2]
                n = min(1024, tot - c0)
                S.dma("sp", io.ap[:, 0:n], wtab_in[:, c0:c0 + n], writes=io.tags)
                eng = "dve" if i % 2 == 0 else "pool"
                S.op(eng, lambda e, io=io, c0=c0, n=n: e.tensor_copy(out=wflat[:, c0:c0 + n], in_=io.ap[:, 0:n]), reads=io.tags, writes=[("wmask",)])

        def pcol(name, idx):
            c = poff[name] + idx
            return prmT[:, c:c + 1]

        if STG >= 4:
            nlh = L * HG_H
            ex = lbtmp[:, 0:nlh]
            sm = lbtmp[:, nlh:2 * nlh]
            tot = lbtmp[:, 2 * nlh:2 * nlh + HG_H]
            rc = lbtmp[:, 2 * nlh + HG_H:2 * nlh + 2 * HG_H]
            r0 = poff["hlb"]
            S.op("act", lambda e: e.activation(out=ex, in_=prmT[:, r0:r0 + nlh], func=AF.Exp), reads=[("prmT",)], writes=[("lbtmp",)])
            S.op("dve", lambda e: e.tensor_copy(out=tot, in_=ex[:, 0:HG_H]), reads=[("lbtmp",)], writes=[("lbtmp",)])
            for l in range(1, L):
                S.op("dve", lambda e, l=l: e.tensor_tensor(out=tot, in0=tot, in1=ex[:, l * HG_H:(l + 1) * HG_H], op=ALU.add),
                     reads=[("lbtmp",)], writes=[("lbtmp",)])
            S.op("dve", lambda e: e.reciprocal(out=rc, in_=tot), reads=[("lbtmp",)], writes=[("lbtmp",)])
            for l in range(L):
                S.op("dve", lambda e, l=l: e.tensor_tensor(out=sm[:, l * HG_H:(l + 1) * HG_H], in0=ex[:, l * HG_H:(l + 1) * HG_H], in1=rc, op=ALU.mult),
                     reads=[("lbtmp",)], writes=[("lbtmp",)])
            S.op("dve", lambda e: e.memset(lbT[:, 0:HG_H], 0.0), writes=[("lbT",)])
            for l in range(1, L):
                S.op("dve", lambda e, l=l: e.tensor_tensor(out=lbT[:, l * HG_H:(l + 1) * HG_H], in0=lbT[:, (l - 1) * HG_H:l * HG_H],
                                                           in1=sm[:, l * HG_H:(l + 1) * HG_H], op=ALU.add),
                     reads=[("lbtmp",), ("lbT",)], writes=[("lbT",)])
            S.op("dve", lambda e: e.tensor_scalar(out=omlT[:], in0=lbT[:], scalar1=-1.0, scalar2=1.0, op0=ALU.mult, op1=ALU.add),
                 reads=[("lbT",)], writes=[("omlT",)])

        class WStream:
            def __init__(self):
                self.plan = []
                self.nload = 0
                self.ncons = 0
                self.cls_idx = {8: 0, 22: 0}
                self.slot_of = []
                self.prev_user = []
                self.slot_last = {}

            def add(self, uid, ap, cls, dtag):
                k = self.cls_idx[cls]
                self.cls_idx[cls] += 1
                nb = NB8 if cls == 8 else NB22
                slot = (cls, k % nb)
                self.prev_user.append(self.slot_last.get(slot, -1))
                self.slot_last[slot] = len(self.plan)
                self.slot_of.append(slot)
                self.plan.append((uid, ap, cls, dtag))

            def _buf(self, slot):
                cls, i = slot
                return (wb8 if cls == 8 else wb22)[i]

            def consume(self, uid):
                i = self.ncons
                assert self.plan[i][0] == uid, (self.plan[i][0], uid)
                while self.nload < len(self.plan) and self.nload <= i + 5 and (self.prev_user[self.nload] < 0 or self.prev_user[self.nload] <= i - 2):
                    j = self.nload
                    _, ap, cls, dtag = self.plan[j]
                    slot = self.slot_of[j]
                    S.dma("sp", self._buf(slot)[:], ap, reads=[dtag], writes=[("wb",) + slot])
                    self.nload += 1
                assert self.nload > i, (self.nload, i)
                self.ncons += 1
                slot = self.slot_of[i]
                return self._buf(slot), ("wb",) + slot

        WS = WStream()
        FM_CONV = [0, 1, 2, 3]
        FM_QKV = [4, 5, 6, 7, 8, 9]
        HG_ORDER = []
        for _h in range(HG_H):
            HG_ORDER += [18 + _h, 10 + _h, 14 + _h, 22 + _h]

        def plan_ffn(f, l):
            for c in range(NFC):
                WS.add(("g", f, l, c), sG[f][l, c], 8, ("sG", f, l, c))
                WS.add(("u", f, l, c), sU[f][l, c], 8, ("sU", f, l, c))
            for c in range(NCH):
                WS.add(("d", f, l, c), sD[f][l, c], 22, ("sD", f, l, c))

        def plan_layer(l):
            plan_ffn(0, l)
            if STG >= 2:
                for c in FM_CONV + FM_QKV + HG_ORDER:
                    WS.add(("i", l, c), sI[l, c], 8, ("sI", l, c))
                for c in range(NCH):
                    WS.add(("o", l, c), sO[l, c], 8, ("sO", l, c))
            if STG >= 9:
                plan_ffn(1, l)

        for j in range(cfg.ntile):
            for l in range(L):
                plan_layer(l)

        class Grp:
            pass

        gP = Grp()
        gP.name, gP.n, gP.x, gP.h, gP.ym = "P", TT, xT, hT, ymix
        gP.hd = [hc.ap for hc in hidc]
        gP.hdt = [hc.tags for hc in hidc]
        gS = Grp()
        gS.name, gS.n, gS.x, gS.h, gS.ym = "S", NS, xTs, hTs, ymixs
        gS.hd = [hids[:, c, :] for c in range(NFC)]
        gS.hdt = [[("S", "hd", c)] for c in range(NFC)]

        def xtag(g, c):
            return (g.name, "x", c)

        def htag(g, c):
            return (g.name, "h", c)

        def ytag(g, c):
            return (g.name, "ym", c)

        def sumsq_rstd(g, srcs, src_tags, nchunks, denom):
            n = g.n
            for c in range(nchunks):
                S.op("act", lambda e, c=c: e.activation(out=sqc[c].ap[:, 0:n], in_=srcs[c], func=AF.Square),
                     reads=src_tags[c], writes=sqc[c].tags)
            b = ps_next()

            def mm(e):
                ins = None
                for c in range(nchunks):
                    ins = e.matmul(psum[b][:, 0:n], lhsT=ones_b[:], rhs=sqc[c].ap[:, 0:n], start=(c == 0), stop=(c == nchunks - 1))
                return ins
            S.op("pe", mm, reads=sum([sqc[c].tags for c in range(nchunks)], []) + [("ones",)], writes=[PT(b)])
            S.op("act", lambda e: e.activation(out=rstd[:, 0:n], in_=psum[b][:, 0:n], func=AF.Ln, scale=1.0 / denom, bias=eps_col[:]),
                 reads=[PT(b), ("eps",)], writes=[("rstd",)])
            S.op("act", lambda e: e.activation(out=rstd[:, 0:n], in_=rstd[:, 0:n], func=AF.Exp, scale=-0.5),
                 reads=[("rstd",)], writes=[("rstd",)])

        def rmsnorm(g, gain_name, gain_idx0):
            n = g.n
            sumsq_rstd(g, [g.x[:, c, 0:n] for c in range(NCH)], [[xtag(g, c)] for c in range(NCH)], NCH, D)
            for c in range(NCH):
                S.op("dve", lambda e, c=c: e.scalar_tensor_tensor(out=g.h[:, c, 0:n], in0=g.x[:, c, 0:n],
                                                                  scalar=pcol(gain_name, gain_idx0 + c), in1=rstd[:, 0:n],
                                                                  op0=ALU.mult, op1=ALU.mult),
                     reads=[xtag(g, c), ("rstd",), ("prmT",)], writes=[htag(g, c)])

        def proj_fm(unit, utag, kch, rhs_fn, rhs_tags, n, b):
            def mm(e):
                ins = None
                for k in range(kch):
                    ins = e.matmul(psum[b][:, 0:n], lhsT=unit[:, k * 128:(k + 1) * 128], rhs=rhs_fn(k),
                                   start=(k == 0), stop=(k == kch - 1))
                return ins
            S.op("pe", mm, reads=[utag] + rhs_tags, writes=[PT(b)])

        def ffn(groups, f, l):
            sg_i = 0
            for c in range(NFC):
                ug, tg = WS.consume(("g", f, l, c))
                uu, tu = WS.consume(("u", f, l, c))
                for g in groups:
                    n = g.n
                    bg, bu = ps_next(), ps_next()
                    htags = [htag(g, k) for k in range(NCH)]
                    proj_fm(ug, tg, NCH, lambda k, g=g, n=n: g.h[:, k, 0:n], htags, n, bg)
                    proj_fm(uu, tu, NCH, lambda k, g=g, n=n: g.h[:, k, 0:n], htags, n, bu)
                    si = sg_i % 2
                    sg_i += 1
                    S.op("act", lambda e, si=si, bg=bg, n=n: e.activation(out=sgb[si][:, 0:n], in_=psum[bg][:, 0:n], func=AF.Silu),
                         reads=[PT(bg)], writes=[("sgb", si)])
                    S.op("dve", lambda e, si=si, bu=bu, n=n, g=g, c=c: e.tensor_tensor(out=g.hd[c][:, 0:n], in0=sgb[si][:, 0:n], in1=psum[bu][:, 0:n], op=ALU.mult),
                         reads=[("sgb", si), PT(bu)], writes=g.hdt[c])
            for oc in range(NCH):
                ud, td = WS.consume(("d", f, l, oc))
                for g in groups:
                    n = g.n
                    b = ps_next()
                    proj_fm(ud, td, NFC, lambda k, g=g, n=n: g.hd[k][:, 0:n], sum([g.hdt[k] for k in range(NFC)], []), n, b)
                    S.op("dve", lambda e, g=g, n=n, b=b, oc=oc: e.scalar_tensor_tensor(out=g.x[:, oc, 0:n], in0=psum[b][:, 0:n], scalar=0.5,
                                                                                      in1=g.x[:, oc, 0:n], op0=ALU.mult, op1=ALU.add),
                         reads=[PT(b), xtag(g, oc)], writes=[xtag(g, oc)])

        def load_tokens(g, src_rows, blocks):
            for bi, (r0, nr, c0) in enumerate(blocks):
                io = iobuf[bi % 2]
                S.dma("sp", io.ap[0:nr, :], src_rows[r0:r0 + nr, :], writes=io.tags)
                for c in range(NCH):
                    b = ps_next()
                    S.op("pe", lambda e, io=io, nr=nr, c=c, b=b: e.transpose(psum[b][:, 0:nr], io.ap[0:nr, c * 128:(c + 1) * 128], ident[0:nr, 0:nr]),
                         reads=io.tags + [("ident",)], writes=[PT(b)])
                    if c % 2 == 0:
                        S.op("act", lambda e, b=b, c=c, nr=nr, c0=c0: e.activation(out=g.x[:, c, c0:c0 + nr], in_=psum[b][:, 0:nr], func=AF.Copy),
                             reads=[PT(b)], writes=[xtag(g, c)])
                    else:
                        S.op("dve", lambda e, b=b, c=c, nr=nr, c0=c0: e.tensor_copy(out=g.x[:, c, c0:c0 + nr], in_=psum[b][:, 0:nr]),
                             reads=[PT(b)], writes=[xtag(g, c)])

        def final_norm_store(g, dst_rows, blocks):
            n = g.n
            sumsq_rstd(g, [g.x[:, c, 0:n] for c in range(NCH)], [[xtag(g, c)] for c in range(NCH)], NCH, D)
            for c in range(NCH):
                S.op("dve", lambda e, c=c: e.scalar_tensor_tensor(out=finc[c].ap[:, 0:n], in0=g.x[:, c, 0:n],
                                                                  scalar=pcol("lnf", c), in1=rstd[:, 0:n],
                                                                  op0=ALU.mult, op1=ALU.mult),
                     reads=[xtag(g, c), ("rstd",), ("prmT",)], writes=finc[c].tags)
            for bi, (r0, nr, c0) in enumerate(blocks):
                io = iobuf[bi % 2]
                for half in range(2):
                    b = ps_next()

                    def tr(e, half=half, b=b, nr=nr, c0=c0):
                        ins = None
                        for cc in range(4):
                            ins = e.transpose(psum[b][0:nr, cc * 128:(cc + 1) * 128], finc[half * 4 + cc].ap[:, c0:c0 + nr], ident[:])
                        return ins
                    S.op("pe", tr, reads=sum([finc[half * 4 + cc].tags for cc in range(4)], []) + [("ident",)], writes=[PT(b)])
                    if half == 0:
                        S.op("act", lambda e, b=b, nr=nr, io=io: e.activation(out=io.ap[0:nr, 0:512], in_=psum[b][0:nr, :], func=AF.Copy),
                             reads=[PT(b)], writes=io.tags)
                    else:
                        S.op("dve", lambda e, b=b, nr=nr, io=io: e.tensor_copy(out=io.ap[0:nr, 512:1024], in_=psum[b][0:nr, :]),
                             reads=[PT(b)], writes=io.tags)
                S.dma("pool", dst_rows[r0:r0 + nr, :], io.ap[0:nr, :], reads=io.tags, writes=[("out", g.name, r0)], is_output=True)

        def mk_ws(g):
            w = Grp()
            n = g.n
            if g is gP:
                w.cva = av("cva", 0, 4, F32, "p (c t) -> p c t", c=2)
                w.cvs = av("cvs", 4, 4, F32, "p (c t) -> p c t", c=2)
                w.cvy = av("cvy", 8, 4, F32, "p (c t) -> p c t", c=2)
                w.diag = av("diag", 14, 16, BF16, "p (w j) -> p w j", j=128)
                w.qz = av("qz", 30, 4, BF16, "p (h t) -> p h t", h=4)
                w.Eb = [av("Eb%d" % i, 34 + i, 1) for i in range(4)]
                w.Pb = [av("Pb%d" % i, 38 + i, 1) for i in range(4)]
                w.lnd = [av("lnd%d" % i, 42 + 0.5 * i, 0.5, F32) for i in range(2)]
                hp = Grp()
                for k, nm in enumerate(("q", "f", "lg", "kk", "eb", "en")):
                    setattr(hp, nm, av("h" + nm, 2 * k, 2, F32))
                w.hp = hp
                w.live = []
                for h in range(HG_H):
                    lv = Grp()
                    base = 12 + 5 * h
                    lv.qt = av("hqt%d" % h, base, 1)
                    lv.kt = av("hkt%d" % h, base + 1, 1)
                    lv.kh = av("hkh%d" % h, base + 2, 2, F32)
                    lv.gt = av("hgt%d" % h, base + 4, 1)
                    ebl = sb("ebl%d" % h, [128, n // HGC])
                    lv.ebl = B(ebl[:], [("ebl", h)])
                    emid = sb("emid%d" % h, [128, n // HGC])
                    lv.emid = B(emid[:], [("emid", h)])
                    w.live.append(lv)
                w.vtok = av("vtok", 32, 4, BF16, "p (b c) -> p b c", b=4)
                w.khm = [av("khm%d" % i, 36 + i, 1, BF16, "p (j d) -> p j d", j=4) for i in range(4)]
                w.Am = [av("Am%d" % i, 40 + 0.25 * i, 0.25) for i in range(4)]
                w.Sb = [[av("Sb%d_%d" % (h, i), 41 + 0.25 * (h * 5 + i), 0.25) for i in range(5)] for h in range(HG_H)]
                w.osq = av("osq", 46, 1)
                w.ot = av("ot", 47, 2, F32)
                w.Tb = []
                for h in range(HG_H):
                    tt_ = sb("Tb%d" % h, [128, 2, 128])
                    w.Tb.append([B(tt_[:, i, :], [("Tb", h, i)]) for i in range(2)])
            else:
                def t(nm, shape, dt=F32):
                    tt = sb("s_" + nm, shape, dt)
                    return B(tt[:], [("sw", nm)])
                w.cva = t("cva", [128, 2, n]); w.cvs = t("cvs", [128, 2, n]); w.cvy = t("cvy", [128, 2, n])
                w.diag = None
                w.qz = t("qz", [128, 4, n], BF16)
                w.Eb = [t("Eb%d" % i, [128, 64 + 16], BF16) for i in range(2)]
                w.Pb = [t("Pb%d" % i, [128, 64 + 16], BF16) for i in range(2)]
                w.lnd = [t("lnd%d" % i, [128, 16]) for i in range(2)]
                w.kvout = [av("kvo_s", 12, 2, F32)]
                hp = Grp()
                for nm in ("q", "f", "lg", "kk", "eb", "en"):
                    setattr(hp, nm, t("h" + nm, [128, n]))
                w.hp = hp
                w.live = []
                for h in range(HG_H):
                    lv = Grp()
                    lv.qt = t("hqt%d" % h, [128, n], BF16)
                    lv.kt = t("hkt%d" % h, [128, 128], BF16)
                    lv.kh = t("hkh%d" % h, [128, n])
                    lv.gt = t("hgt%d" % h, [128, n], BF16)
                    lv.ebl = t("ebl%d" % h, [128, n // DEC_T])
                    lv.emid = t("emid%d" % h, [128, n // DEC_T])
                    w.live.append(lv)
                w.vtok = t("vtok", [128, 1, 512], BF16)
                w.khm = [t("khm%d" % i, [128, 4, 128], BF16) for i in range(4)]
                w.Am = [t("Am%d" % i, [128, 128], BF16) for i in range(4)]
                w.Sb = [[t("Sb%d_%d" % (h, i), [128, 128], BF16) for i in range(5)] for h in range(HG_H)]
                w.osq = t("osq", [128, n], BF16)
                w.ot = t("ot", [128, n])
                w.shs = av("shs", 0, 8, F32, "p (b h v) -> p b h v", b=NB, h=HG_H)
                w.kTs = t("kTs", [128, 2, 128], BF16)
                w.vaug = t("vaug", [128, ATT_H, 128], BF16)
                w.ufp = t("ufp", [128, 256])
            return w

        wsP = mk_ws(gP)
        wsS = mk_ws(gS)
        gP.ws, gS.ws = wsP, wsS
        S.op("pool", lambda e: e.memset(wsP.qz.ap, 0.0), writes=wsP.qz.tags)
        S.op("pool", lambda e: e.memset(wsS.qz.ap, 0.0), writes=wsS.qz.tags)
        S.op("pool", lambda e: e.memset(wsS.vaug.ap, 1.0), writes=wsS.vaug.tags)
        S.op("pool", lambda e: e.memset(wsS.kTs.ap, 0.0), writes=wsS.kTs.tags)
        S.op("pool", lambda e: e.memset(wsS.vtok.ap, 0.0), writes=wsS.vtok.tags)
        for _i in range(4):
            S.op("pool", lambda e, _i=_i: e.memset(wsS.khm[_i].ap, 0.0), writes=wsS.khm[_i].tags)
            S.op("pool", lambda e, _i=_i: e.memset(wsS.live[_i].kt.ap, 0.0), writes=wsS.live[_i].kt.tags)

        acc_rr = [0]

        def acc_bank():
            b = 6 + acc_rr[0]
            acc_rr[0] ^= 1
            return b

        def win_chunk_fm(g, l, c, unit, utag):
            n = g.n
            b = ps_next()
            proj_fm(unit, utag, NCH, lambda k: g.h[:, k, 0:n], [htag(g, k) for k in range(NCH)], n, b)
            return b

        def win_chunk_tm(g, unit, utag):
            n = g.n
            bs = 128 if g is gP else n
            nblk = n // bs
            b = ps_next()

            def mm(e):
                ins = None
                for tb in range(nblk):
                    for k in range(NCH):
                        ins = e.matmul(psum[b][0:128, tb * 128:(tb + 1) * 128], lhsT=g.h[:, k, tb * bs:tb * bs + 128], rhs=unit[:, k * 128:(k + 1) * 128],
                                       start=(k == 0), stop=(k == NCH - 1))
                return ins
            S.op("pe", mm, reads=[utag] + [htag(g, k) for k in range(NCH)], writes=[PT(b)])
            return b, bs, nblk

        def conv_evac(g, c, b):
            n, w = g.n, g.ws
            if c < 2:
                S.op("act", lambda e: e.activation(out=w.cva.ap[:, c, 0:n], in_=psum[b][:, 0:n], func=AF.Copy), reads=[PT(b)], writes=w.cva.tags)
            else:
                S.op("act", lambda e: e.activation(out=w.cvs.ap[:, c - 2, 0:n], in_=psum[b][:, 0:n], func=AF.Sigmoid), reads=[PT(b)], writes=w.cvs.tags)

        def conv_glu(g):
            n, w = g.n, g.ws
            S.op("dve", lambda e: e.tensor_tensor(out=w.cva.ap[:, :, 0:n], in0=w.cva.ap[:, :, 0:n], in1=w.cvs.ap[:, :, 0:n], op=ALU.mult),
                 reads=w.cva.tags + w.cvs.tags, writes=w.cva.tags)

        def build_diag(l):
            w = wsP
            for wi in range(CW):
                for ch in range(2):
                    col = pcol("dww", l * CW * 2 + wi * 2 + ch)
                    if False:
                        pass
                    else:
                        S.op("dve", lambda e, wi=wi, ch=ch, col=col: e.tensor_scalar(out=w.diag.ap[:, wi * 2 + ch, :], in0=identb[:], scalar1=col, scalar2=None, op0=ALU.mult),
                             reads=[("identb",), ("prmT",)], writes=w.diag.tags)

        def conv_main(g, l, j):
            n, w = g.n, g.ws
            dg = wsP.diag
            if g is gP:
                S.op("pool", lambda e: e.tensor_copy(out=ubuf[:, :, 0:CW - 1], in_=utail[:, l, :, :]), reads=[("utail", l)], writes=[("ubuf",)])
                S.op("act", lambda e: e.activation(out=ubuf[:, :, CW - 1:CW - 1 + n], in_=w.cva.ap[:, :, 0:n], func=AF.Copy), reads=w.cva.tags, writes=[("ubuf",)])
                rhs = lambda ch, wi: ubuf[:, ch, wi:wi + n]
                outv = lambda b: psum[b][:, 0:n]
                utags = [("ubuf",)]
            else:
                for bb in range(NB):
                    io = iobuf[bb % 2]
                    S.dma("sp", io.ap[0:CW - 1, 0:256], sconv_in[l, bb], writes=io.tags)
                    for ch in range(2):
                        b = ps_next()
                        S.op("pe", lambda e, io=io, ch=ch, b=b: e.transpose(psum[b][:, 0:CW - 1], io.ap[0:CW - 1, ch * 128:(ch + 1) * 128], ident[0:CW - 1, 0:CW - 1]),
                             reads=io.tags + [("ident",)], writes=[PT(b)])
                        S.op("dve", lambda e, ch=ch, bb=bb, b=b: e.tensor_copy(out=ubufs[:, ch, bb, 0:CW - 1], in_=psum[b][:, 0:CW - 1]), reads=[PT(b)], writes=[("ubufs",)])
                    S.dma("pool", o_cs[l, bb, 0:CW - 1 - DEC_T, :], sconv_in[l, bb, DEC_T:CW - 1, :], writes=[("o_cs", l, bb, 0)], is_output=True)
                S.op("act", lambda e: e.activation(out=ubufs[:, :, :, CW - 1:CW - 1 + DEC_T], in_=w.cva.ap.rearrange("p c (b t) -> p c b t", t=DEC_T), func=AF.Copy),
                     reads=w.cva.tags, writes=[("ubufs",)])
                rhs = lambda ch, wi: ubufs[:, ch, :, wi:wi + DEC_T]
                outv = lambda b: psum[b][:, 0:n].rearrange("p (b t) -> p b t", t=DEC_T)
                utags = [("ubufs",)]
            banks = []
            for ch in range(2):
                b = ps_next()
                banks.append(b)

                def mm(e, ch=ch, b=b):
                    ins = None
                    for wi in range(CW):
                        ins = e.matmul(outv(b), lhsT=dg.ap[:, wi * 2 + ch, :], rhs=rhs(ch, wi), start=(wi == 0), stop=(wi == CW - 1))
                    return ins
                S.op("pe", mm, reads=dg.tags + utags, writes=[PT(b)])
                S.op("act", lambda e, ch=ch, b=b: e.activation(out=w.cvy.ap[:, ch, 0:n], in_=psum[b][:, 0:n], func=AF.Identity, bias=pcol("dwb", l * 2 + ch)),
                     reads=[PT(b), ("prmT",)], writes=w.cvy.tags)
            if g is gP:
                S.op("pool", lambda e: e.tensor_copy(out=utail[:, l, :, :], in_=ubuf[:, :, n:n + CW - 1]), reads=[("ubuf",)], writes=[("utail", l)])
            for ch in range(2):
                S.op("dve", lambda e, ch=ch: e.tensor_copy(out=sqc[ch].ap[:, 0:n], in_=w.cvy.ap[:, ch, 0:n]), reads=w.cvy.tags, writes=sqc[ch].tags)
            b = ps_next()

            def mm1(e):
                ins = None
                for ch in range(2):
                    ins = e.matmul(psum[b][:, 0:n], lhsT=ones_b[:], rhs=sqc[ch].ap[:, 0:n], start=(ch == 0), stop=(ch == 1))
                return ins
            S.op("pe", mm1, reads=sqc[0].tags + sqc[1].tags + [("ones",)], writes=[PT(b)])
            for ch in range(2):
                S.op("dve", lambda e, ch=ch: e.scalar_tensor_tensor(out=w.cvy.ap[:, ch, 0:n], in0=psum[b][:, 0:n], scalar=-1.0 / CONV_DIM,
                                                                   in1=w.cvy.ap[:, ch, 0:n], op0=ALU.mult, op1=ALU.add),
                     reads=[PT(b)] + w.cvy.tags, writes=w.cvy.tags)
            sumsq_rstd(g, [w.cvy.ap[:, ch, 0:n] for ch in range(2)], [w.cvy.tags, w.cvy.tags], 2, CONV_DIM)
            for ch in range(2):
                S.op("dve", lambda e, ch=ch: e.tensor_tensor(out=w.cvy.ap[:, ch, 0:n], in0=w.cvy.ap[:, ch, 0:n], in1=rstd[:, 0:n], op=ALU.mult),
                     reads=w.cvy.tags + [("rstd",)], writes=w.cvy.tags)
                S.op("act", lambda e, ch=ch: e.activation(out=g.ym[:, ch, 0:n], in_=w.cvy.ap[:, ch, 0:n], func=AF.Silu,
                                                          scale=pcol("clg", l * 2 + ch), bias=pcol("clb", l * 2 + ch)),
                     reads=w.cvy.tags + [("prmT",)], writes=[ytag(g, ch)])
            if g is gP and j == cfg.ntile - 1:
                io = iobuf[0]
                for ch in range(2):
                    b2 = ps_next()
                    S.op("pe", lambda e, ch=ch, b2=b2: e.transpose(psum[b2][0:CW - 1, 0:128], w.cva.ap[:, ch, n - (CW - 1):n], ident[:]),
                         reads=w.cva.tags + [("ident",)], writes=[PT(b2)])
                    S.op("dve", lambda e, ch=ch, b2=b2, io=io: e.tensor_copy(out=io.ap[0:CW - 1, ch * 128:(ch + 1) * 128], in_=psum[b2][0:CW - 1, 0:128]),
                         reads=[PT(b2)], writes=io.tags)
                S.dma("pool", o_cp[l], io.ap[0:CW - 1, 0:256], reads=io.tags, writes=[("o_cp", l)], is_output=True)
            if g is gS:
                for ch in range(2):
                    b2 = ps_next()
                    S.op("pe", lambda e, ch=ch, b2=b2: e.transpose(psum[b2][0:n, 0:128], w.cva.ap[:, ch, 0:n], ident[:]),
                         reads=w.cva.tags + [("ident",)], writes=[PT(b2)])
                    S.op("dve", lambda e, ch=ch, b2=b2: e.tensor_copy(out=w.ufp.ap[0:n, ch * 128:(ch + 1) * 128], in_=psum[b2][0:n, 0:128]),
                         reads=[PT(b2)], writes=w.ufp.tags)
                for bb in range(NB):
                    S.dma("pool", o_cs[l, bb, CW - 1 - DEC_T:CW - 1, :], w.ufp.ap[bb * DEC_T:(bb + 1) * DEC_T, :], reads=w.ufp.tags,
                          writes=[("o_cs", l, bb, 1)], is_output=True)

        kvst = av("kvst", 44, 8, F32, "p (b c) -> p b c", b=4)

        def att_q_evac(g, c, b):
            n, w = g.n, g.ws
            ch = c - 4
            for hp in range(2):
                h = 2 * ch + hp
                r0 = 64 * hp
                S.op("act", lambda e, h=h, r0=r0: e.activation(out=w.qz.ap[r0:r0 + 64, h, 0:n], in_=psum[b][r0:r0 + 64, 0:n], func=AF.Copy, scale=HD ** -0.5),
                     reads=[PT(b)], writes=w.qz.tags)

        def att_k_evac(g, l, j, c, b):
            n, w = g.n, g.ws
            ch = c - 6
            if g is gP:
                S.op("dve", lambda e: e.tensor_copy(out=kwin[:, ch, WIN:WIN + n], in_=psum[b][:, 0:n]), reads=[PT(b)], writes=[("kwin", "cur")])
                if ch == 1:
                    t0 = j * TT
                    S.dma("pool", sK[l].rearrange("(c p) t -> p c t", p=128)[:, :, t0:t0 + n], kwin[:, :, WIN:WIN + n], reads=[("kwin", "cur")], writes=[("sK", l, j)])
            else:
                S.op("dve", lambda e: e.tensor_copy(out=w.kTs.ap[:, ch, 0:n], in_=psum[b][:, 0:n]), reads=[PT(b)], writes=w.kTs.tags)

        def att_kv_tm(g, l, j, c, unit, utag):
            n, w = g.n, g.ws
            ci = c - 6
            b, bs, nblk = win_chunk_tm(g, unit, utag)
            t0 = j * TT
            if g is gP:
                S.op("act", lambda e: e.activation(out=kvst.ap[:, :, ci * 128:(ci + 1) * 128], in_=psum[b][:, :].rearrange("p (b c) -> p b c", c=128), func=AF.Copy),
                     reads=[PT(b)], writes=kvst.tags)
            else:
                kv = w.kvout[0]
                S.op("act", lambda e: e.activation(out=kv.ap[0:bs, ci * 128:(ci + 1) * 128], in_=psum[b][0:bs, 0:128], func=AF.Copy), reads=[PT(b)], writes=kv.tags)
            if c >= 8 and "kvcp" not in os.environ.get("DBG_SKIP", ""):
                for hh in range(2):
                    h = 2 * (c - 8) + hh
                    if g is gP:
                        S.op("act", lambda e, h=h, hh=hh: e.activation(out=vwin[:, WIN // 128:WIN // 128 + 4, h, 64 * hh:64 * hh + 64],
                                                                       in_=psum[b][:, :].rearrange("p (b c) -> p b c", c=128)[:, :, 64 * hh:64 * hh + 64], func=AF.Copy),
                             reads=[PT(b)], writes=[("vwin", "cur")])
                    else:
                        S.op("act", lambda e, h=h, hh=hh: e.activation(out=w.vaug.ap[0:bs, h, 64 * hh:64 * hh + 64], in_=psum[b][0:bs, 64 * hh:64 * hh + 64], func=AF.Copy),
                             reads=[PT(b)], writes=w.vaug.tags)
            if c == 9 and "kvdma" not in os.environ.get("DBG_SKIP", ""):
                if g is gP:
                    r0 = t0 - (SEQ - KEEP)
                    if r0 >= 0:
                        S.dma("pool", o_kp[l, r0:r0 + n, :].rearrange("(b p) c -> p b c", p=128), kvst.ap[:, :, 0:256], reads=kvst.tags, writes=[("o_kp", l, r0)], is_output=True)
                        S.dma("pool", o_vp[l, r0:r0 + n, :].rearrange("(b p) c -> p b c", p=128), kvst.ap[:, :, 256:512], reads=kvst.tags, writes=[("o_vp", l, r0)], is_output=True)
                    S.dma("pool", sV[l, t0:t0 + n, :].rearrange("(b p) c -> p b c", p=128),
                          vwin[:, WIN // 128:WIN // 128 + 4, :, :].rearrange("p b h d -> p b (h d)"), reads=[("vwin", "cur")], writes=[("sV", l, j)])
                else:
                    kv = w.kvout[0]
                    S.dma("pool", o_ks[l], kv.ap[0:n, 0:256], reads=kv.tags, writes=[("o_ks", l)], is_output=True)
                    S.dma("pool", o_vs[l], kv.ap[0:n, 256:512], reads=kv.tags, writes=[("o_vs", l)], is_output=True)

        def att_finish(g, w, bO, h, ych, cols, li):
            ch, hp = divmod(h, 2)
            orow = 64 * hp
            drow = 64 * (1 - hp)
            nq = cols[1] - cols[0]
            ld = w.lnd[li % 2]
            S.op("act", lambda e: e.activation(out=ld.ap[orow:orow + 64, 0:nq], in_=psum[bO][drow:drow + 64, 0:nq], func=AF.Ln),
                 reads=[PT(bO)], writes=ld.tags)
            S.op("act", lambda e: e.activation(out=ld.ap[orow:orow + 64, 0:nq], in_=ld.ap[orow:orow + 64, 0:nq], func=AF.Exp, scale=-1.0),
                 reads=ld.tags, writes=ld.tags)
            S.op("dve", lambda e: e.tensor_tensor(out=g.ym[orow:orow + 64, 2 + ch, cols[0]:cols[1]], in0=psum[bO][orow:orow + 64, 0:nq],
                                                  in1=ld.ap[orow:orow + 64, 0:nq], op=ALU.mult),
                 reads=[PT(bO)] + ld.tags, writes=[ytag(g, 2 + ch)])

        _wt = att_weight_table().astype(np.float64)
        MMAX = [max(m for m in range(NM) if np.any(_wt[:, m, h, :] >= 2.0 ** -134)) for h in range(ATT_H)]

        def att_prompt(l, j):
            g, w = gP, wsP
            t0 = j * TT
            work = []
            for qb in range(TT // 128):
                gb = j * (TT // 128) + qb
                nm = min(16, gb) + 1
                for h in range(ATT_H):
                    bO = acc_bank()
                    nmh = min(nm, MMAX[h] + 1)
                    for g0 in range(0, nmh, 4):
                        work.append(dict(qb=qb, h=h, grp=list(range(g0, min(nmh, g0 + 4))), first=(g0 == 0), last=(g0 + 4 >= nmh), bO=bO, nm=nmh))
            li = [0]

            def emit_S(i):
                wk = work[i]
                b = ps_next()
                wk["b"] = b
                qb, h, grp = wk["qb"], wk["h"], wk["grp"]
                ch = h // 2

                def mm(e):
                    ins = None
                    for gi, m in enumerate(grp):
                        wkb = 16 + qb - m
                        ins = e.matmul(psum[b][:, gi * 128:(gi + 1) * 128], lhsT=kwin[:, ch, wkb * 128:(wkb + 1) * 128],
                                       rhs=w.qz.ap[:, h, qb * 128:(qb + 1) * 128], start=True, stop=True)
                    return ins
                S.op("pe", mm, reads=[("kwin", "hist"), ("kwin", "cur")] + w.qz.tags, writes=[PT(b)])
                ng = len(grp)
                Eb, Pb = w.Eb[i % 4], w.Pb[i % 4]
                S.op("act", lambda e: e.activation(out=Eb.ap[:, 0:ng * 128], in_=psum[b][:, 0:ng * 128], func=AF.Exp), reads=[PT(b)], writes=Eb.tags)
                S.op("dve", lambda e: e.tensor_tensor(
                    out=Pb.ap[:, 0:ng * 128].rearrange("p (m q) -> p m q", q=128), in0=Eb.ap[:, 0:ng * 128].rearrange("p (m q) -> p m q", q=128),
                    in1=wmask[:, grp[0]:grp[0] + ng, h, :], op=ALU.mult), reads=Eb.tags + [("wmask",)], writes=Pb.tags)

            def emit_PV(i):
                wk = work[i]
                qb, h, grp, bO, nm = wk["qb"], wk["h"], wk["grp"], wk["bO"], wk["nm"]
                Pb = w.Pb[i % 4]

                def pv(e):
                    ins = None
                    for gi, m in enumerate(grp):
                        wkb = 16 + qb - m
                        ins = e.matmul(psum[bO][:, 0:128], lhsT=vwin[:, wkb, h, :], rhs=Pb.ap[:, gi * 128:(gi + 1) * 128],
                                       start=(wk["first"] and gi == 0), stop=(m == nm - 1))
                    return ins
                S.op("pe", pv, reads=Pb.tags + [("vwin", "hist"), ("vwin", "cur")], writes=[PT(bO)])
                if wk["last"]:
                    att_finish(g, w, bO, h, None, (qb * 128, (qb + 1) * 128), li[0])
                    li[0] += 1

            LA = 3
            for i in range(len(work) + LA):
                if i < len(work):
                    emit_S(i)
                if i - LA >= 0:
                    emit_PV(i - LA)

        def att_sample(l):
            g, w = gS, wsS
            n = g.n
            li = 0
            ei = 0
            for bb in range(NB):
                for q4 in range(4):
                    io = iobuf[q4 % 2]
                    S.dma("sp", io.ap[:, :].rearrange("p (b c) -> p b c", c=256), ck_in[l, bb, q4 * 512:(q4 + 1) * 512, :].rearrange("(b p) c -> p b c", p=128), writes=io.tags)
                    for ch in range(2):
                        b = ps_next()

                        def tr(e, io=io, ch=ch, b=b):
                            ins = None
                            for kb in range(4):
                                ins = e.transpose(psum[b][:, kb * 128:(kb + 1) * 128], io.ap[:, kb * 256 + ch * 128:kb * 256 + (ch + 1) * 128], ident[:])
                            return ins
                        S.op("pe", tr, reads=io.tags + [("ident",)], writes=[PT(b)])
                        if ch == 0:
                            S.op("act", lambda e, b=b, q4=q4, ch=ch: e.activation(out=kwin[:, ch, q4 * 512:(q4 + 1) * 512], in_=psum[b][:, :], func=AF.Copy),
                                 reads=[PT(b)], writes=[("kwin", "hist")])
                        else:
                            S.op("dve", lambda e, b=b, q4=q4, ch=ch: e.tensor_copy(out=kwin[:, ch, q4 * 512:(q4 + 1) * 512], in_=psum[b][:, :]),
                                 reads=[PT(b)], writes=[("kwin", "hist")])
                for q4 in range(4):
                    io = iobuf[q4 % 2]
                    S.dma("sp", io.ap[:, :].rearrange("p (b c) -> p b c", c=256), cv_in[l, bb, q4 * 512:(q4 + 1) * 512, :].rearrange("(b p) c -> p b c", p=128), writes=io.tags)
                    for par in range(2):
                        S.op("act", lambda e, io=io, q4=q4, par=par: e.activation(
                            out=vwin[:, q4 * 4:(q4 + 1) * 4, par::2, 64 * par:64 * par + 64],
                            in_=io.ap[:, :].rearrange("p (b h d) -> p b h d", b=4, d=64)[:, :, par::2, :], func=AF.Copy),
                            reads=io.tags, writes=[("vwin", "hist")])
                for h in range(ATT_H):
                    ch = h // 2
                    bO = acc_bank()
                    b = ps_next()

                    def mm(e, b=b, ch=ch, h=h, bb=bb):
                        ins = None
                        for m in range(1, min(16, MMAX[h]) + 1):
                            wkb = 16 - m
                            ins = e.matmul(psum[b][:, (m - 1) * DEC_T:m * DEC_T], lhsT=kwin[:, ch, wkb * 128:(wkb + 1) * 128],
                                           rhs=w.qz.ap[:, h, bb * DEC_T:(bb + 1) * DEC_T], start=True, stop=True)
                        ins = e.matmul(psum[b][0:128, 64:64 + DEC_T], lhsT=w.kTs.ap[:, ch, 0:128], rhs=w.qz.ap[:, h, bb * DEC_T:(bb + 1) * DEC_T], start=True, stop=True)
                        return ins
                    S.op("pe", mm, reads=[("kwin", "hist")] + w.kTs.tags + w.qz.tags, writes=[PT(b)])
                    Eb = w.Eb[ei % 2]
                    Pb = w.Pb[ei % 2]
                    ei += 1
                    mh = min(16, MMAX[h])
                    S.op("act", lambda e, b=b, Eb=Eb, mh=mh: e.activation(out=Eb.ap[:, 0:mh * DEC_T], in_=psum[b][:, 0:mh * DEC_T], func=AF.Exp), reads=[PT(b)], writes=Eb.tags)
                    S.op("act", lambda e, b=b, Eb=Eb: e.activation(out=Eb.ap[:, 64:64 + DEC_T], in_=psum[b][:, 64:64 + DEC_T], func=AF.Exp), reads=[PT(b)], writes=Eb.tags)
                    S.op("dve", lambda e, Eb=Eb, Pb=Pb, h=h, mh=mh: e.tensor_tensor(out=Pb.ap[:, 0:mh * DEC_T].rearrange("p (m q) -> p m q", q=DEC_T),
                                                                          in0=Eb.ap[:, 0:mh * DEC_T].rearrange("p (m q) -> p m q", q=DEC_T),
                                                                          in1=wmask[:, 1:mh + 1, h, 0:DEC_T], op=ALU.mult),
                         reads=Eb.tags + [("wmask",)], writes=Pb.tags)
                    S.op("dve", lambda e, Eb=Eb, Pb=Pb, h=h, bb=bb: e.tensor_tensor(out=Pb.ap[:, 64:64 + DEC_T], in0=Eb.ap[:, 64:64 + DEC_T],
                                                                                 in1=wsm[:, bb, h, :], op=ALU.mult),
                         reads=Eb.tags + [("wsm",)], writes=Pb.tags)

                    def pv(e, Pb=Pb, h=h, bO=bO):
                        ins = None
                        for m in range(1, min(16, MMAX[h]) + 1):
                            wkb = 16 - m
                            ins = e.matmul(psum[bO][:, 0:DEC_T], lhsT=vwin[:, wkb, h, :], rhs=Pb.ap[:, (m - 1) * DEC_T:m * DEC_T], start=(m == 1), stop=False)
                        ins = e.matmul(psum[bO][:, 0:DEC_T], lhsT=w.vaug.ap[:, h, :], rhs=Pb.ap[:, 64:64 + DEC_T], start=False, stop=True)
                        return ins
                    S.op("pe", pv, reads=Pb.tags + [("vwin", "hist")] + w.vaug.tags, writes=[PT(bO)])
                    att_finish(g, w, bO, h, None, (bb * DEC_T, (bb + 1) * DEC_T), li)
                    li += 1

        def hgrn_prep(g, l, h):
            n, w = g.n, g.ws
            hs, lv = w.hp, w.live[h]
            prompt = g is gP
            C = HGC if prompt else DEC_T
            nchunk = n // C
            rsm = rsmask if prompt else rsmask_s
            rsm_tag = ("rsmask",) if prompt else ("rsmask_s",)
            lbc = lbT[:, l * HG_H + h:l * HG_H + h + 1]
            omc = omlT[:, l * HG_H + h:l * HG_H + h + 1]
            A = lambda bf: bf.ap[:, 0:n]
            S.op("pool", lambda e: e.tensor_scalar(out=A(hs.f), in0=A(hs.f), scalar1=omc, scalar2=lbc, op0=ALU.mult, op1=ALU.add),
                 reads=hs.f.tags + [("lbT",), ("omlT",)], writes=hs.f.tags)
            S.op("act", lambda e: e.activation(out=A(hs.lg), in_=A(hs.f), func=AF.Ln), reads=hs.f.tags, writes=hs.lg.tags)
            S.op("pool", lambda e: e.tensor_scalar(out=A(hs.kk), in0=A(hs.f), scalar1=-1.0, scalar2=1.0, op0=ALU.mult, op1=ALU.add),
                 reads=hs.f.tags, writes=hs.kk.tags)
            S.op("dve", lambda e: e.tensor_tensor_scan(out=A(hs.f), data0=rsm[:, 0:n], data1=A(hs.lg), initial=0.0, op0=ALU.mult, op1=ALU.add),
                 reads=hs.lg.tags + [rsm_tag], writes=hs.f.tags)
            b3 = hs.f.ap[:, 0:n].rearrange("p (c t) -> p c t", t=C)
            MID = C // 2 - 1
            S.op("act", lambda e: e.activation(out=lv.ebl.ap[:, 0:nchunk], in_=b3[:, :, C - 1], func=AF.Exp), reads=hs.f.tags, writes=lv.ebl.tags)
            S.op("act", lambda e: e.activation(out=lv.emid.ap[:, 0:nchunk], in_=b3[:, :, MID], func=AF.Exp), reads=hs.f.tags, writes=lv.emid.tags)
            S.op("dve", lambda e: e.tensor_tensor(out=hs.lg.ap[:, 0:n].rearrange("p (c t) -> p c t", t=C),
                                                  in0=b3, in1=b3[:, :, MID:MID + 1].to_broadcast([128, nchunk, C]), op=ALU.subtract),
                 reads=hs.f.tags, writes=hs.lg.tags)
            S.op("act", lambda e: e.activation(out=A(hs.eb), in_=A(hs.lg), func=AF.Exp), reads=hs.lg.tags, writes=hs.eb.tags)
            S.op("act", lambda e: e.activation(out=A(hs.en), in_=A(hs.lg), func=AF.Exp, scale=-1.0), reads=hs.lg.tags, writes=hs.en.tags)
            S.op("dve", lambda e: e.tensor_tensor(out=A(lv.qt), in0=A(hs.q), in1=A(hs.eb), op=ALU.mult), reads=hs.q.tags + hs.eb.tags, writes=lv.qt.tags)
            S.op("dve", lambda e: e.tensor_tensor(out=A(lv.kt), in0=A(hs.kk), in1=A(hs.en), op=ALU.mult), reads=hs.kk.tags + hs.en.tags, writes=lv.kt.tags)
            S.op("dve", lambda e: e.tensor_tensor(out=hs.lg.ap[:, 0:n].rearrange("p (c t) -> p c t", t=C),
                                                  in0=b3[:, :, C - 1:C].to_broadcast([128, nchunk, C]), in1=b3, op=ALU.subtract),
                 reads=hs.f.tags + hs.eb.tags + hs.en.tags, writes=hs.lg.tags)
            S.op("act", lambda e: e.activation(out=A(hs.en), in_=A(hs.lg), func=AF.Exp), reads=hs.lg.tags, writes=hs.en.tags)
            S.op("dve", lambda e: e.tensor_tensor(out=A(lv.kh), in0=A(hs.kk), in1=A(hs.en), op=ALU.mult), reads=hs.kk.tags + hs.en.tags, writes=lv.kh.tags)

        def hgrn_chain(g, l, j):
            n, w = g.n, g.ws
            prompt = g is gP
            C = HGC if prompt else DEC_T
            bs = 128 if prompt else n
            nblk = n // bs
            NCB = bs // C
            bdm = bdmask if prompt else bdmask_s
            bdm_tag = ("bdmask",) if prompt else ("bdmask_s",)
            rm0 = 0 if prompt else 4
            vt_tags = w.vtok.tags
            ps_rr[0] = ps_rr[0] % 4
            npr_save = NPRv[0]
            NPRv[0] = 4
            bOT = [4 + h for h in range(HG_H)]
            ki = [0]
            if prompt:
                for h in range(HG_H):
                    S.op("pool", lambda e, h=h: e.tensor_copy(out=w.Tb[h][0].ap, in_=Sst[:, l, h, :]), reads=[("Sst", l, h)], writes=w.Tb[h][0].tags)
                    S.op("act", lambda e, h=h: e.activation(out=w.Sb[h][0].ap, in_=Sst[:, l, h, :], func=AF.Copy, scale=w.live[h].emid.ap[:, 0:1]),
                         reads=[("Sst", l, h)] + w.live[h].emid.tags, writes=w.Sb[h][0].tags)

            st1 = {}

            def stage1(tb, h):
                lv = w.live[h]
                c0 = tb * bs
                bA = ps_next()
                S.op("pe", lambda e: e.matmul(psum[bA][0:128, 0:bs], lhsT=lv.kt.ap[:, c0:c0 + 128], rhs=lv.qt.ap[:, c0:c0 + bs], start=True, stop=True),
                     reads=lv.kt.tags + lv.qt.tags, writes=[PT(bA)])
                Am = w.Am[ki[0] % 4]
                khm = w.khm[ki[0] % 4]
                ki[0] += 1
                S.op("dve", lambda e: e.tensor_tensor(out=Am.ap[:, 0:bs], in0=psum[bA][:, 0:bs], in1=bdm[:, 0:bs], op=ALU.mult),
                     reads=[PT(bA), bdm_tag], writes=Am.tags)
                bT = ps_next()
                S.op("pe", lambda e: e.transpose(psum[bT][0:bs, 0:128], lv.kh.ap[:, c0:c0 + bs], ident[:]),
                     reads=lv.kh.tags + [("ident",)], writes=[PT(bT)])
                for jj in range(NCB):
                    rc = rowmask[0:bs, rm0 + jj:rm0 + jj + 1]
                    S.op("dve", lambda e, jj=jj, rc=rc: e.tensor_scalar(out=khm.ap[0:bs, jj, :], in0=psum[bT][0:bs, 0:128], scalar1=rc, scalar2=None, op0=ALU.mult),
                         reads=[PT(bT), ("rowmask",)], writes=khm.tags)
                st1[(tb, h)] = (Am, khm)

            def stage1b(tb, h):
                lv = w.live[h]
                c0 = tb * bs
                Am, khm = st1[(tb, h)]
                bD = ps_next()

                def dS(e):
                    ins = None
                    for jj in range(NCB):
                        ins = e.matmul(psum[bD][:, jj * 128:(jj + 1) * 128], lhsT=khm.ap[:, jj, :], rhs=w.vtok.ap[:, tb, h * 128:(h + 1) * 128], start=True, stop=True)
                    return ins
                S.op("pe", dS, reads=khm.tags + vt_tags, writes=[PT(bD)])
                S.op("pe", lambda e: e.matmul(psum[bOT[h]][:, c0:c0 + bs], lhsT=w.vtok.ap[:, tb, h * 128:(h + 1) * 128], rhs=Am.ap[:, 0:bs], start=True, stop=False),
                     reads=Am.tags + vt_tags, writes=[PT(bOT[h])])
                for jj in range(NCB):
                    if prompt:
                        cidx = tb * NCB + jj
                        ecol = lv.ebl.ap[:, cidx:cidx + 1]
                        tin, tout = w.Tb[h][cidx % 2], w.Tb[h][(cidx + 1) % 2]
                        S.op("dve", lambda e, ecol=ecol, jj=jj, tin=tin, tout=tout: e.scalar_tensor_tensor(out=tout.ap, in0=tin.ap, scalar=ecol,
                                                                                      in1=psum[bD][:, jj * 128:(jj + 1) * 128], op0=ALU.mult, op1=ALU.add),
                             reads=tin.tags + [PT(bD)] + lv.ebl.tags, writes=tout.tags)
                        sbn = w.Sb[h][(cidx + 1) % 5]
                        if cidx + 1 < nblk * NCB:
                            mcol = lv.emid.ap[:, cidx + 1:cidx + 2]
                            S.op("act", lambda e, sbn=sbn, tout=tout, mcol=mcol: e.activation(out=sbn.ap, in_=tout.ap, func=AF.Copy, scale=mcol),
                                 reads=tout.tags + lv.emid.tags, writes=sbn.tags)
                    else:
                        sbc = w.Sb[h][jj]
                        S.op("act", lambda e, sbc=sbc, jj=jj: e.activation(out=sbc.ap, in_=w.shs.ap[:, jj, h, :], func=AF.Copy, scale=lv.emid.ap[:, jj:jj + 1]),
                             reads=w.shs.tags + lv.emid.tags, writes=sbc.tags)
                        ecol = lv.ebl.ap[:, jj:jj + 1]
                        S.op("dve", lambda e, ecol=ecol, jj=jj: e.scalar_tensor_tensor(out=w.shs.ap[:, jj, h, :], in0=w.shs.ap[:, jj, h, :], scalar=ecol,
                                                                                      in1=psum[bD][:, jj * 128:(jj + 1) * 128], op0=ALU.mult, op1=ALU.add),
                             reads=w.shs.tags + [PT(bD)] + lv.ebl.tags, writes=w.shs.tags)

            def stage2(tb, h):
                lv = w.live[h]
                c0 = tb * bs
                for jj in range(NCB):
                    q0 = c0 + jj * C
                    sbc = w.Sb[h][(tb * NCB + jj) % 5] if prompt else w.Sb[h][jj]
                    S.op("pe", lambda e, sbc=sbc, q0=q0, jj=jj: e.matmul(psum[bOT[h]][:, q0:q0 + C], lhsT=sbc.ap, rhs=lv.qt.ap[:, q0:q0 + C], start=False, stop=(jj == NCB - 1)),
                         reads=sbc.tags + lv.qt.tags, writes=[PT(bOT[h])])

            seq = [(tb, h) for tb in range(nblk) for h in range(HG_H)]
            for i in range(len(seq) + 2):
                if i < len(seq):
                    stage1(*seq[i])
                if 1 <= i <= len(seq):
                    stage1b(*seq[i - 1])
                if i >= 2:
                    stage2(*seq[i - 2])
            if prompt:
                nct = nblk * NCB
                for h in range(HG_H):
                    S.op("pool", lambda e, h=h: e.tensor_copy(out=Sst[:, l, h, :], in_=w.Tb[h][nct % 2].ap), reads=w.Tb[h][nct % 2].tags, writes=[("Sst", l, h)])
            for h in range(HG_H):
                lv = w.live[h]
                S.op("act", lambda e, h=h: e.activation(out=w.osq.ap[:, 0:n], in_=psum[bOT[h]][:, 0:n], func=AF.Square), reads=[PT(bOT[h])], writes=w.osq.tags)
                bN = ps_next()
                S.op("pe", lambda e, bN=bN: e.matmul(psum[bN][:, 0:n], lhsT=ones_b[:], rhs=w.osq.ap[:, 0:n], start=True, stop=True), reads=w.osq.tags + [("ones",)], writes=[PT(bN)])
                S.op("act", lambda e, bN=bN: e.activation(out=w.ot.ap[:, 0:n], in_=psum[bN][:, 0:n], func=AF.Ln, scale=1.0 / 128, bias=eps_col[:]), reads=[PT(bN), ("eps",)], writes=w.ot.tags)
                S.op("act", lambda e: e.activation(out=w.ot.ap[:, 0:n], in_=w.ot.ap[:, 0:n], func=AF.Exp, scale=-0.5), reads=w.ot.tags, writes=w.ot.tags)
                S.op("dve", lambda e, h=h: e.tensor_tensor(out=w.ot.ap[:, 0:n], in0=psum[bOT[h]][:, 0:n], in1=w.ot.ap[:, 0:n], op=ALU.mult), reads=[PT(bOT[h])] + w.ot.tags, writes=w.ot.tags)
                S.op("dve", lambda e, h=h, lv=lv: e.scalar_tensor_tensor(out=g.ym[:, 4 + h, 0:n], in0=w.ot.ap[:, 0:n], scalar=pcol("hgn", l), in1=lv.gt.ap[:, 0:n], op0=ALU.mult, op1=ALU.mult),
                     reads=w.ot.tags + lv.gt.tags + [("prmT",)], writes=[ytag(g, 4 + h)])
            NPRv[0] = npr_save

        def hgrn_vtok(g, h, unit, utag):
            w = g.ws
            b, bs, nblk = win_chunk_tm(g, unit, utag)
            if g is gP:
                S.op("act", lambda e: e.activation(out=w.vtok.ap[:, :, h * 128:(h + 1) * 128], in_=psum[b][:, :].rearrange("p (b c) -> p b c", c=128), func=AF.Copy),
                     reads=[PT(b)], writes=w.vtok.tags)
            else:
                S.op("act", lambda e: e.activation(out=w.vtok.ap[:, 0, h * 128:(h + 1) * 128], in_=psum[b][:, 0:128], func=AF.Copy),
                     reads=[PT(b)], writes=w.vtok.tags)

        def att_hist_load(l, j):
            t0 = j * TT
            avail = min(WIN, t0)
            if avail > 0:
                S.dma("sp", kwin[:, :, WIN - avail:WIN], sK[l].rearrange("(c p) t -> p c t", p=128)[:, :, t0 - avail:t0],
                      reads=[("sK", l, jj) for jj in range(j)], writes=[("kwin", "hist")])
                nb = avail // 128
                S.dma("sp", vwin[:, 16 - nb:16, :, :].rearrange("p b h d -> p b (h d)"),
                      sV[l, t0 - avail:t0, :].rearrange("(b p) c -> p b c", p=128),
                      reads=[("sV", l, jj) for jj in range(j)], writes=[("vwin", "hist")])

        def mix_layer(groups, l, j):
            if STG >= 3:
                att_hist_load(l, j)
            for g in groups:
                rmsnorm(g, "lnm", l * 8)
            build_diag(l)
            for c in FM_CONV:
                u, t = WS.consume(("i", l, c))
                for g in groups:
                    conv_evac(g, c, win_chunk_fm(g, l, c, u, t))
            for g in groups:
                conv_glu(g)
            if STG >= 3:
                for g in groups:
                    S.op("pool", lambda e, g=g: e.memset(g.ws.qz.ap, 0.0), writes=g.ws.qz.tags)
            for c in FM_QKV:
                u, t = WS.consume(("i", l, c))
                if STG < 3:
                    continue
                for g in groups:
                    if c in (4, 5):
                        att_q_evac(g, c, win_chunk_fm(g, l, c, u, t))
                    elif c in (6, 7):
                        att_k_evac(g, l, j, c, win_chunk_fm(g, l, c, u, t))
                    if c >= 6 and "kvtm" not in os.environ.get("DBG_SKIP", ""):
                        att_kv_tm(g, l, j, c, u, t)
            for g in groups:
                conv_main(g, l, j)
            if STG >= 3:
                if "attp" in os.environ.get("DBG_SKIP", ""):
                    for c in (2, 3):
                        S.op("pool", lambda e, c=c: e.memset(gP.ym[:, c, 0:gP.n], 0.0), writes=[ytag(gP, c)])
                else:
                    att_prompt(l, j)
                if gS in groups:
                    if os.environ.get("NO_ATT_S"):
                        for c in (2, 3):
                            S.op("pool", lambda e, c=c: e.memset(gS.ym[:, c, 0:gS.n], 0.0), writes=[ytag(gS, c)])
                    else:
                        att_sample(l)
            else:
                for g in groups:
                    for c in (2, 3):
                        S.op("pool", lambda e, g=g, c=c: e.memset(g.ym[:, c, 0:g.n], 0.0), writes=[ytag(g, c)])
            for h in range(HG_H):
                for c, key in ((18 + h, "v"), (10 + h, "q"), (14 + h, "f"), (22 + h, "g")):
                    u, t = WS.consume(("i", l, c))
                    if STG < 4:
                        continue
                    for g in groups:
                        if key == "v":
                            hgrn_vtok(g, h, u, t)
                        else:
                            bz = win_chunk_fm(g, l, c, u, t)
                            dst = {"q": g.ws.hp.q, "f": g.ws.hp.f, "g": g.ws.live[h].gt}[key]
                            fnc = AF.Sigmoid if key == "f" else AF.Silu
                            S.op("act", lambda e, g=g, dst=dst, bz=bz, fnc=fnc: e.activation(out=dst.ap[:, 0:g.n], in_=psum[bz][:, 0:g.n], func=fnc),
                                 reads=[PT(bz)], writes=dst.tags)
                if STG >= 4:
                    for g in groups:
                        hgrn_prep(g, l, h)
                else:
                    for g in groups:
                        S.op("pool", lambda e, g=g, h=h: e.memset(g.ym[:, 4 + h, 0:g.n], 0.0), writes=[ytag(g, 4 + h)])
            if STG >= 4:
                for g in groups:
                    if g is gS:
                        S.dma("sp", wsS.shs.ap, shg_in[l].rearrange("b h d v -> d b h v"), writes=wsS.shs.tags)
                    hgrn_chain(g, l, j)
            if STG >= 4:
                if j == cfg.ntile - 1:
                    S.dma("pool", o_hp[l].rearrange("h d v -> d h v"), Sst[:, l, :, :], reads=[("Sst", l, h) for h in range(HG_H)], writes=[("o_hp", l)], is_output=True)
                if gS in groups:
                    S.dma("pool", o_hs[l].rearrange("b h d v -> d b h v"), wsS.shs.ap, reads=wsS.shs.tags, writes=[("o_hs", l)], is_output=True)
            for oc in range(NCH):
                u, t = WS.consume(("o", l, oc))
                for g in groups:
                    n = g.n
                    b = ps_next()
                    proj_fm(u, t, NCH, lambda k, g=g, n=n: g.ym[:, k, 0:n], [ytag(g, k) for k in range(NCH)], n, b)
                    S.op("dve", lambda e, g=g, n=n, b=b, oc=oc: e.tensor_tensor(out=g.x[:, oc, 0:n], in0=psum[b][:, 0:n], in1=g.x[:, oc, 0:n], op=ALU.add),
                         reads=[PT(b), xtag(g, oc)], writes=[xtag(g, oc)])

        load_tokens(gS, xs, [(0, NS, 0)])
        for j in range(cfg.ntile):
            load_tokens(gP, xp, [(j * TT + 128 * b, 128, 128 * b) for b in range(TT // 128)])
            groups = [gP, gS] if j == 0 else [gP]
            for l in range(L):
                for g in groups:
                    rmsnorm(g, "ln1", l * 8)
                ffn(groups, 0, l)
                if STG >= 2:
                    mix_layer(groups, l, j)
                if STG >= 9:
                    for g in groups:
                        rmsnorm(g, "ln2", l * 8)
                    ffn(groups, 1, l)
            final_norm_store(gP, yp, [(j * TT + 128 * b, 128, 128 * b) for b in range(TT // 128)])
            if j == 0:
                final_norm_store(gS, ys, [(0, NS, 0)])

        if cfg.debug:
            dbg = dout("dbg_ym", [128, NCH * TT], BF16)
            S.dma("pool", dbg, ymix[:].rearrange("p c t -> p (c t)"), reads=[ytag(gP, c) for c in range(NCH)], writes=[("dbg", 0)], is_output=True)
            dbgs = dout("dbg_yms", [128, NCH * NS], BF16)
            S.dma("pool", dbgs, ymixs[:].rearrange("p c t -> p (c t)"), reads=[ytag(gS, c) for c in range(NCH)], writes=[("dbg", 1)], is_output=True)
        S.finish()
        with nc.Block() as block:
            S.replay(block)
        cfg.n_inst = dict(S.n_inst)
    return nc


def pack_prm(depth, ln1, lnm, ln2, lnf, dww, dwb, clg, clb, hlb, hgn):
    off, prows = prm_layout(depth)
    out = np.zeros((prows, 128), np.float32)

    def put(name, arr):
        a = np.ascontiguousarray(arr, dtype=np.float32).reshape(-1, 128)
        out[off[name]:off[name] + a.shape[0]] = a
    put("ln1", ln1); put("lnm", lnm); put("ln2", ln2); put("lnf", lnf)
    put("dww", dww); put("dwb", dwb); put("clg", clg); put("clb", clb); put("hlb", hlb); put("hgn", hgn)
    return out


_CACHE = {}


def run(cfg, inputs):
    key = (cfg.seq, cfg.depth, cfg.nsamp, cfg.n_cores, cfg.nseq, cfg.stages, cfg.debug)
    if key not in _CACHE:
        _CACHE[key] = build_program(cfg)
    nc = _CACHE[key]
    L = cfg.depth
    f32 = lambda a: np.ascontiguousarray(a, dtype=np.float32)
    prm = pack_prm(L, inputs["ln_ffn1"], inputs["ln_mix"], inputs["ln_ffn2"], inputs["ln_final"],
                   inputs["conv_dw_w"], inputs["conv_dw_b"], inputs["conv_ln_g"], inputs["conv_ln_b"],
                   inputs["hg_lower_bounds"], inputs["hg_norm_g"])
    shared = {
        "prm": prm,
        "wg1": f32(inputs["w_ffn1_gate"]), "wu1": f32(inputs["w_ffn1_up"]), "wd1": f32(inputs["w_ffn1_down"]),
        "wg2": f32(inputs["w_ffn2_gate"]), "wu2": f32(inputs["w_ffn2_up"]), "wd2": f32(inputs["w_ffn2_down"]),
        "wi": f32(inputs["w_in"]), "wo": f32(inputs["w_out"]),
    }
    shared.update(const_tables(cfg.nsamp))
    xpf = f32(inputs["x_prompt"])
    xsf = f32(inputs["x_sample"])
    sconv = f32(inputs["state_conv"])
    ck = f32(inputs["cache_k_win"]).reshape(L, -1, WIN, 256)
    cv = f32(inputs["cache_v_win"]).reshape(L, -1, WIN, 256)
    shg = f32(inputs["state_hgrn"])
    in_maps = []
    zero_seq = None
    nb = cfg.nsamp
    for c in range(cfg.n_cores):
        m = dict(shared)
        if c < cfg.nseq:
            m["xp"] = xpf[c]
        else:
            if zero_seq is None:
                zero_seq = np.zeros((cfg.seq, D), np.float32)
            m["xp"] = zero_seq
        sl = slice(c * nb, (c + 1) * nb)
        m["xs"] = np.ascontiguousarray(xsf[sl].reshape(cfg.ns_tok, D))
        m["sconv"] = np.ascontiguousarray(sconv[:, sl])
        m["ck"] = np.ascontiguousarray(ck[:, sl])
        m["cv"] = np.ascontiguousarray(cv[:, sl])
        m["shg"] = np.ascontiguousarray(shg[:, sl])
        in_maps.append(m)
    res = run_bass_kernel_spmd(nc, in_maps, core_ids=list(range(cfg.n_cores)))
    return res.results


def assemble(cfg, r):
    L, nb = cfg.depth, cfg.nsamp
    nsq = cfg.nseq
    y_prompt = np.stack([r[c]["yp"] for c in range(nsq)])
    y_sample = np.concatenate([r[c]["ys"].reshape(nb, DEC_T, D) for c in range(cfg.n_cores)], 0)
    cp = np.stack([r[c]["o_cp"] for c in range(nsq)], 1)
    cs = np.concatenate([r[c]["o_cs"] for c in range(cfg.n_cores)], 1)
    kp = np.stack([r[c]["o_kp"].reshape(L, cfg.keep, ATT_H, HD) for c in range(nsq)], 1)
    vp = np.stack([r[c]["o_vp"].reshape(L, cfg.keep, ATT_H, HD) for c in range(nsq)], 1)
    ks = np.concatenate([r[c]["o_ks"].reshape(L, nb, DEC_T, ATT_H, HD) for c in range(cfg.n_cores)], 1)
    vs = np.concatenate([r[c]["o_vs"].reshape(L, nb, DEC_T, ATT_H, HD) for c in range(cfg.n_cores)], 1)
    hp = np.stack([r[c]["o_hp"] for c in range(nsq)], 1)
    hs = np.concatenate([r[c]["o_hs"] for c in range(cfg.n_cores)], 1)
    outs = (y_prompt, y_sample, cp, cs, kp, vp, ks, vs, hp, hs)
    return tuple(np.ascontiguousarray(o, dtype=np.float32) for o in outs)


def kernel(**inputs):
    cfg = Cfg()
    r = run(cfg, inputs)
    return assemble(cfg, r)
```

```python
import contextlib
import os
import numpy as np
import concourse.bass as bass
import concourse.mybir as mybir
from concourse.bass_utils import run_bass_kernel_spmd

F32 = mybir.dt.float32
BF16 = mybir.dt.bfloat16
AF = mybir.ActivationFunctionType
ALU = mybir.AluOpType

D = 1024
NCH = 8
FFN = 2816
NFC = 22
N_IN = 3328
NIC = 26
CONV_DIM = 256
CW = 31
ATT_H = 4
HD = 64
HG_H = 4
HGC = 64
WIN = 2048
TT = 512
EPS = 1e-6
DEC_T = 4


class Sched:
    COMPUTE = ("pe", "act", "dve", "pool")

    def __init__(self, nc, stack, n_dma_sems=24):
        self.nc = nc
        self.streams = {e: [] for e in ("pe", "act", "dve", "pool", "sp")}
        self.sems = {}
        for e in self.COMPUTE:
            self.sems[e] = stack.enter_context(nc.semaphore("s_" + e))
        self.count = {e: 0 for e in self.COMPUTE}
        self.dma_sems = {}
        self.dma_val = {}
        self.dma_rr = {}
        for q in ("sp", "pool"):
            self.dma_sems[q] = []
            for i in range(n_dma_sems):
                key = "d_%s_%d" % (q, i)
                self.sems[key] = stack.enter_context(nc.semaphore(key))
                self.dma_sems[q].append(key)
                self.dma_val[key] = 0
            self.dma_rr[q] = 0
        self.waited = {}
        self.last_write = {}
        self.readers = {}
        self.out_tokens = []
        self.n_inst = {e: 0 for e in self.streams}

    def _need(self, eng, token, needs):
        if token is None:
            return
        key, val = token
        if self.waited.get((eng, key), 0) >= val:
            return
        if needs.get(key, 0) < val:
            needs[key] = val

    def _collect(self, eng, reads, writes):
        needs = {}
        for t in reads:
            self._need(eng, self.last_write.get(t), needs)
        for t in writes:
            self._need(eng, self.last_write.get(t), needs)
            for tok in self.readers.get(t, ()):
                self._need(eng, tok, needs)
        return needs

    def _emit_waits(self, eng, needs, is_dma=False):
        for key, val in needs.items():
            if key == eng and not is_dma:
                continue
            sem = self.sems[key]
            self.streams[eng].append(lambda e, sem=sem, val=val: e.wait_ge(sem, val))
            self.waited[(eng, key)] = val
            self.n_inst[eng] += 1

    def _commit(self, token, reads, writes):
        for t in reads:
            self.readers.setdefault(t, []).append(token)
        for t in writes:
            self.last_write[t] = token
            self.readers[t] = []

    def op(self, eng, fn, reads=(), writes=()):
        needs = self._collect(eng, reads, writes)
        if eng in needs:
            own = needs.pop(eng)
            val = own if eng != "pe" else 0
            if val > self.waited.get((eng, eng), 0):
                sem = self.sems[eng]
                self.streams[eng].append(lambda e, sem=sem, val=val: e.wait_ge(sem, val))
                self.waited[(eng, eng)] = val
        self._emit_waits(eng, needs)
        self.count[eng] += 1
        token = (eng, self.count[eng])
        sem = self.sems[eng]
        self.streams[eng].append(lambda e, fn=fn, sem=sem: fn(e).then_inc(sem, 1))
        self.n_inst[eng] += 1
        self._commit(token, reads, writes)
        return token

    def dma(self, q, out, in_, reads=(), writes=(), is_output=False):
        needs = self._collect(q, reads, writes)
        rr = self.dma_rr[q]
        self.dma_rr[q] = (rr + 1) % len(self.dma_sems[q])
        key = self.dma_sems[q][rr]
        prev = self.dma_val[key]
        if prev > 0 and self.waited.get((q, key), 0) < prev:
            needs[key] = max(needs.get(key, 0), prev)
        self._emit_waits(q, needs, is_dma=True)
        val = prev + 16
        self.dma_val[key] = val
        sem = self.sems[key]
        self.streams[q].append(lambda e, out=out, in_=in_, sem=sem: e.dma_start(out=out, in_=in_).then_inc(sem, 16))
        self.n_inst[q] += 1
        token = (key, val)
        self._commit(token, reads, writes)
        if is_output:
            self.out_tokens.append(token)
        return token

    def barrier(self):
        for eng in self.streams:
            for x in self.COMPUTE:
                if x != eng and self.count[x] > 0:
                    sem, val = self.sems[x], self.count[x]
                    self.streams[eng].append(lambda e, sem=sem, val=val: e.wait_ge(sem, val))
                    self.waited[(eng, x)] = val
            for key, val in self.dma_val.items():
                if val > 0:
                    sem = self.sems[key]
                    self.streams[eng].append(lambda e, sem=sem, val=val: e.wait_ge(sem, val))
                    self.waited[(eng, key)] = val
        self.last_write = {}
        self.readers = {}

    def finish(self):
        finals = {}
        for key, val in self.out_tokens:
            finals[key] = max(finals.get(key, 0), val)
        for key, val in finals.items():
            sem = self.sems[key]
            self.streams["sp"].append(lambda e, sem=sem, val=val: e.wait_ge(sem, val))

    def replay(self, block):
        nc = self.nc
        streams = self.streams

        @block.sync
        def _(e):
            for f in streams["sp"]:
                f(e)

        @block.tensor
        def _(e):
            for f in streams["pe"]:
                f(e)

        @block.scalar
        def _(e):
            for f in streams["act"]:
                f(e)

        @block.vector
        def _(e):
            for f in streams["dve"]:
                f(e)

        @block.gpsimd
        def _(e):
            for f in streams["pool"]:
                f(e)


class Cfg:
    def __init__(self, seq=8192, depth=4, nsamp=4, n_cores=8, nseq=2, stages=99):
        self.seq = seq
        self.depth = depth
        self.nsamp = nsamp
        self.n_cores = n_cores
        self.nseq = nseq
        self.stages = stages
        self.ntile = seq // TT
        self.ns_tok = nsamp * DEC_T
        self.keep = min(WIN, seq)
        self.debug = False


def prm_layout(depth):
    off = {}
    r = 0
    for name, rows in (("ln1", depth * 8), ("lnm", depth * 8), ("ln2", depth * 8), ("lnf", 8),
                       ("dww", depth * CW * 2), ("dwb", depth * 2), ("clg", depth * 2), ("clb", depth * 2),
                       ("hlb", depth * 4), ("hgn", depth)):
        off[name] = r
        r += rows
    return off, ((r + 127) // 128) * 128


NM = 17


def att_weight_table():
    k = np.arange(128)[:, None, None]
    m = np.arange(NM)[None, :, None]
    q = np.arange(128)[None, None, :]
    d = 128 * m + q - k
    mult = ((d >= 0) & (d <= 128)).astype(np.float64) + ((d >= 0) & (d <= 512) & (d % 4 == 0)) + ((d >= 0) & (d <= 2048) & (d % 16 == 0))
    out = np.zeros((128, NM, ATT_H, 128), np.float32)
    for h in range(ATT_H):
        slope = 2.0 ** (-8.0 * (h + 1) / ATT_H)
        out[:, :, h, :] = mult * np.exp(-slope * np.maximum(d, 0))
    return out


def const_tables(nsamp):
    c = {}
    c["ident"] = np.eye(128, dtype=np.float32)
    c["wtab"] = att_weight_table().reshape(128, NM * ATT_H * 128)
    p = np.arange(128)
    bd = ((p[:, None] // HGC) == (p[None, :] // HGC)) & (p[:, None] <= p[None, :])
    c["bdmask"] = bd.astype(np.float32)
    bds = np.zeros((128, 128), np.float32)
    ns = nsamp * DEC_T
    ps = np.arange(ns)
    bds[:ns, :ns] = (((ps[:, None] // DEC_T) == (ps[None, :] // DEC_T)) & (ps[:, None] <= ps[None, :]))
    c["bdmask_s"] = bds
    rm = np.zeros((128, 8), np.float32)
    for j in range(4):
        rm[:, j] = (p // HGC == j)
        rm[:ns, 4 + j] = (ps // DEC_T == j)
    c["rowmask"] = rm
    rs = np.ones((128, TT), np.float32)
    rs[:, ::HGC] = 0.0
    c["rsmask"] = rs
    rss = np.ones((128, 128), np.float32)
    rss[:, 0:ns:DEC_T] = 0.0
    c["rsmask_s"] = rss
    wsm = np.zeros((128, nsamp, ATT_H, DEC_T), np.float32)
    for kk in range(ns):
        kb, kt = divmod(kk, DEC_T)
        for t in range(DEC_T):
            if t >= kt:
                d = t - kt
                mult = 1 + (d % 4 == 0) + (d % 16 == 0)
                for h in range(ATT_H):
                    slope = 2.0 ** (-8.0 * (h + 1) / ATT_H)
                    wsm[kk, kb, h, t] = mult * np.exp(-slope * d)
    c["wsm"] = wsm.reshape(128, nsamp * ATT_H * DEC_T)
    return c


def build_program(cfg):
    nc = bass.Bass("TRN2", target_bir_lowering=False)
    L = cfg.depth
    SEQ = cfg.seq
    NS = cfg.ns_tok
    NB = cfg.nsamp
    KEEP = cfg.keep
    poff, prows = prm_layout(L)
    STG = cfg.stages

    def din(name, shape, dt=F32):
        return nc.dram_tensor(name, list(shape), dt, kind="ExternalInput").ap()

    def dout(name, shape, dt=F32):
        return nc.dram_tensor(name, list(shape), dt, kind="ExternalOutput").ap()

    def dscr(name, shape, dt=BF16):
        return nc.dram_tensor(name, list(shape), dt, kind="Internal").ap()

    xp = din("xp", [SEQ, D])
    xs = din("xs", [NS, D])
    prm = din("prm", [prows, 128])
    ident_in = din("ident", [128, 128])
    wtab_in = din("wtab", [128, NM * ATT_H * 128])
    bdmask_in = din("bdmask", [128, 128])
    bdmask_s_in = din("bdmask_s", [128, 128])
    rowmask_in = din("rowmask", [128, 8])
    rsmask_in = din("rsmask", [128, TT])
    rsmask_s_in = din("rsmask_s", [128, 128])
    wsm_in = din("wsm", [128, NB * ATT_H * DEC_T])
    sconv_in = din("sconv", [L, NB, CW - 1, CONV_DIM])
    ck_in = din("ck", [L, NB, WIN, 256])
    cv_in = din("cv", [L, NB, WIN, 256])
    shg_in = din("shg", [L, NB, HG_H, 128, 128])
    w_g = [din("wg1", [L, D, FFN]), din("wg2", [L, D, FFN])]
    w_u = [din("wu1", [L, D, FFN]), din("wu2", [L, D, FFN])]
    w_d = [din("wd1", [L, FFN, D]), din("wd2", [L, FFN, D])]
    w_i = din("wi", [L, D, N_IN])
    w_o = din("wo", [L, D, D])

    yp = dout("yp", [SEQ, D])
    ys = dout("ys", [NS, D])
    o_cp = dout("o_cp", [L, CW - 1, CONV_DIM])
    o_cs = dout("o_cs", [L, NB, CW - 1, CONV_DIM])
    o_kp = dout("o_kp", [L, KEEP, 256])
    o_vp = dout("o_vp", [L, KEEP, 256])
    o_ks = dout("o_ks", [L, NS, 256])
    o_vs = dout("o_vs", [L, NS, 256])
    o_hp = dout("o_hp", [L, HG_H, 128, 128])
    o_hs = dout("o_hs", [L, NB, HG_H, 128, 128])

    sG = [dscr("sg%d" % f, [L, NFC, 128, NCH * 128]) for f in range(2)]
    sU = [dscr("su%d" % f, [L, NFC, 128, NCH * 128]) for f in range(2)]
    sD = [dscr("sd%d" % f, [L, NCH, 128, NFC * 128]) for f in range(2)]
    sI = dscr("si", [L, NIC, 128, NCH * 128])
    sO = dscr("so", [L, NCH, 128, NCH * 128])
    sK = dscr("sk", [L, 256, SEQ])
    sV = dscr("sv", [L, SEQ, ATT_H * 128])

    stack = contextlib.ExitStack()
    with stack:
        S = Sched(nc, stack)

        def sb(name, shape, dt=F32):
            return stack.enter_context(nc.sbuf_tensor(name, list(shape), dt))

        class B:
            def __init__(self, ap, tags):
                self.ap, self.tags = ap, list(tags)

        NSTG = 8
        with contextlib.ExitStack() as pstack:
            stg_f = [pstack.enter_context(nc.sbuf_tensor("stgf%d" % i, [128, NFC * 128], F32)) for i in range(NSTG)]
            stg_b = [pstack.enter_context(nc.sbuf_tensor("stgb%d" % i, [128, NFC * 128], BF16)) for i in range(NSTG)]
            cast_rr = [0]

            def convert(src2d, K, c, dst_unit, dtag):
                kc = K // 128
                i = cast_rr[0]
                cast_rr[0] += 1
                s = i % NSTG
                srcv = src2d[:, c * 128:(c + 1) * 128].rearrange("(kc p) j -> p kc j", p=128)
                S.dma("sp", stg_f[s][:, 0:kc * 128].rearrange("p (kc j) -> p kc j", j=128), srcv, writes=[("stgf", s)])
                eng = ("dve", "act", "pool")[i % 3]
                if eng == "act":
                    fn = lambda e, s=s, kc=kc: e.activation(out=stg_b[s][:, 0:kc * 128], in_=stg_f[s][:, 0:kc * 128], func=AF.Copy)
                else:
                    fn = lambda e, s=s, kc=kc: e.tensor_copy(out=stg_b[s][:, 0:kc * 128], in_=stg_f[s][:, 0:kc * 128])
                S.op(eng, fn, reads=[("stgf", s)], writes=[("stgb", s)])
                S.dma("pool", dst_unit, stg_b[s][:, 0:kc * 128], reads=[("stgb", s)], writes=[dtag])

            for l in range(L):
                for f in range(2):
                    if f == 1 and STG < 9:
                        continue
                    for c in range(NFC):
                        convert(w_g[f][l], D, c, sG[f][l, c], ("sG", f, l, c))
                        convert(w_u[f][l], D, c, sU[f][l, c], ("sU", f, l, c))
                    for c in range(NCH):
                        convert(w_d[f][l], FFN, c, sD[f][l, c], ("sD", f, l, c))
                if STG >= 2:
                    for c in range(NIC):
                        convert(w_i[l], D, c, sI[l, c], ("sI", l, c))
                    for c in range(NCH):
                        convert(w_o[l], D, c, sO[l, c], ("sO", l, c))
            S.barrier()

        ARK = 52
        arena = sb("arena", [128, ARK * 512], BF16)

        def av(name, off_kb, kb, dt=BF16, pat=None, **dims):
            e0 = int(round(off_kb * 512))
            ne = int(round(kb * 512))
            ap = arena[:, e0:e0 + ne]
            if dt == F32:
                ap = ap.bitcast(F32)
            if pat is not None:
                ap = ap.rearrange(pat, **dims)
            k0 = int(np.floor(off_kb + 1e-9))
            k1 = int(np.ceil(off_kb + kb - 1e-9))
            return B(ap, [("ar", k) for k in range(k0, k1)])

        xT = sb("xT", [128, NCH, TT])
        hT = sb("hT", [128, NCH, TT], BF16)
        ymix = sb("ymix", [128, NCH, TT], BF16)
        xTs = sb("xTs", [128, NCH, NS])
        hTs = sb("hTs", [128, NCH, 128], BF16)
        hids = sb("hids", [128, NFC, NS], BF16)
        ymixs = sb("ymixs", [128, NCH, NS], BF16)
        rstd = sb("rstd", [128, TT])
        sgb = [sb("sgb%d" % i, [128, TT]) for i in range(2)]
        prmT = sb("prmT", [128, prows])
        ident = sb("identf", [128, 128])
        identb = sb("identb", [128, 128], BF16)
        ones_b = sb("ones_b", [128, 128], BF16)
        eps_col = sb("eps_col", [128, 1])
        NB8, NB22 = 6, 2
        wb8 = [sb("wb8_%d" % i, [128, NCH * 128], BF16) for i in range(NB8)]
        wb22 = [sb("wb22_%d" % i, [128, NFC * 128], BF16) for i in range(NB22)]
        kwin = sb("kwin", [128, 2, WIN + TT], BF16)
        vwin = sb("vwin", [128, (WIN + TT) // 128, ATT_H, 128], BF16)
        wmask = sb("wmask", [128, NM, ATT_H, 128], BF16)
        Sst = sb("Sst", [128, L, HG_H, 128])
        utail = sb("utail", [128, L, 2, CW - 1], BF16)
        ubuf = sb("ubuf", [128, 2, CW - 1 + TT], BF16)
        ubufs = sb("ubufs", [128, 2, NB, CW - 1 + DEC_T], BF16)
        bdmask = sb("bdmask_t", [128, 128], BF16)
        bdmask_s = sb("bdmasks_t", [128, 128], BF16)
        rowmask = sb("rowmask_t", [128, 8])
        rsmask = sb("rsmask_t", [128, TT])
        rsmask_s = sb("rsmasks_t", [128, 128])
        wsm = sb("wsm_t", [128, NB, ATT_H, DEC_T], BF16)
        lbT = sb("lbT", [128, L * HG_H])
        omlT = sb("omlT", [128, L * HG_H])
        lbtmp = sb("lbtmp", [128, 2 * L * HG_H + 2 * HG_H])

        hidc = [av("hid", c, 1.0) for c in range(NFC)]
        finc = [av("fin", 2 * c, 2.0, F32) for c in range(NCH)]
        sqc = [av("sq", 36 + c, 1.0) for c in range(NCH)]
        iobuf = [av("io", 44 + 4 * i, 4.0, F32) for i in range(2)]

        psum = [stack.enter_context(nc.psum_tensor("ps%d" % i, [128, 512], F32)) for i in range(8)]
        ps_rr = [0]
        NPR = 6

        NPRv = [NPR]

        def ps_next():
            b = ps_rr[0] % NPRv[0]
            ps_rr[0] = (b + 1) % NPRv[0]
            return b

        def PT(b):
            return ("ps", b)

        S.dma("sp", ident[:], ident_in[:], writes=[("ident",)])
        S.op("dve", lambda e: e.tensor_copy(out=identb[:], in_=ident[:]), reads=[("ident",)], writes=[("identb",)])
        S.op("pool", lambda e: e.memset(ones_b[:], 1.0), writes=[("ones",)])
        S.op("pool", lambda e: e.memset(eps_col[:], EPS), writes=[("eps",)])
        S.op("pool", lambda e: e.memset(hTs[:], 0.0), writes=[("S", "h", c) for c in range(NCH)])
        S.op("pool", lambda e: e.memset(vwin[:], 1.0), writes=[("vwin", "hist"), ("vwin", "cur")])
        S.op("pool", lambda e: e.memset(Sst[:], 0.0), writes=[("Sst", l, h) for l in range(L) for h in range(HG_H)])
        S.op("pool", lambda e: e.memset(utail[:], 0.0), writes=[("utail", l) for l in range(L)])
        S.dma("sp", rowmask[:], rowmask_in[:], writes=[("rowmask",)])
        S.dma("sp", rsmask[:], rsmask_in[:], writes=[("rsmask",)])
        S.dma("sp", rsmask_s[:], rsmask_s_in[:], writes=[("rsmask_s",)])
        for blk in range(prows // 128):
            io = iobuf[blk % 2]
            S.dma("sp", io.ap[:, 0:128], prm[blk * 128:(blk + 1) * 128, :], writes=io.tags)
            b = ps_next()
            S.op("pe", lambda e, io=io, b=b: e.transpose(psum[b][:, 0:128], io.ap[:, 0:128], ident[:]),
                 reads=io.tags + [("ident",)], writes=[PT(b)])
            S.op("act", lambda e, b=b, blk=blk: e.activation(out=prmT[:, blk * 128:(blk + 1) * 128], in_=psum[b][:, 0:128], func=AF.Copy),
                 reads=[PT(b)], writes=[("prmT",)])
        for src, dst, tag in ((bdmask_in, bdmask, "bdmask"), (bdmask_s_in, bdmask_s, "bdmask_s")):
            io = iobuf[0]
            S.dma("sp", io.ap[:, 0:128], src[:], writes=io.tags)
            S.op("dve", lambda e, io=io, dst=dst: e.tensor_copy(out=dst[:], in_=io.ap[:, 0:128]), reads=io.tags, writes=[(tag,)])
        io = iobuf[1]
        nws = NB * ATT_H * DEC_T
        S.dma("sp", io.ap[:, 0:nws], wsm_in[:], writes=io.tags)
        S.op("dve", lambda e, io=io: e.tensor_copy(out=wsm[:].rearrange("p b h t -> p (b h t)"), in_=io.ap[:, 0:nws]), reads=io.tags, writes=[("wsm",)])
        if STG >= 3:
            wflat = wmask[:].rearrange("p m h q -> p (m h q)")
            tot = NM * ATT_H * 128
            for i, c0 in enumerate(range(0, tot, 1024)):
                io = iobuf[i % 2]
                n = min(1024, tot - c0)
                S.dma("sp", io.ap[:, 0:n], wtab_in[:, c0:c0 + n], writes=io.tags)
                eng = "dve" if i % 2 == 0 else "pool"
                S.op(eng, lambda e, io=io, c0=c0, n=n: e.tensor_copy(out=wflat[:, c0:c0 + n], in_=io.ap[:, 0:n]), reads=io.tags, writes=[("wmask",)])

        def pcol(name, idx):
            c = poff[name] + idx
            return prmT[:, c:c + 1]

        if STG >= 4:
            nlh = L * HG_H
            ex = lbtmp[:, 0:nlh]
            sm = lbtmp[:, nlh:2 * nlh]
            tot = lbtmp[:, 2 * nlh:2 * nlh + HG_H]
            rc = lbtmp[:, 2 * nlh + HG_H:2 * nlh + 2 * HG_H]
            r0 = poff["hlb"]
            S.op("act", lambda e: e.activation(out=ex, in_=prmT[:, r0:r0 + nlh], func=AF.Exp), reads=[("prmT",)], writes=[("lbtmp",)])
            S.op("dve", lambda e: e.tensor_copy(out=tot, in_=ex[:, 0:HG_H]), reads=[("lbtmp",)], writes=[("lbtmp",)])
            for l in range(1, L):
                S.op("dve", lambda e, l=l: e.tensor_tensor(out=tot, in0=tot, in1=ex[:, l * HG_H:(l + 1) * HG_H], op=ALU.add),
                     reads=[("lbtmp",)], writes=[("lbtmp",)])
            S.op("dve", lambda e: e.reciprocal(out=rc, in_=tot), reads=[("lbtmp",)], writes=[("lbtmp",)])
            for l in range(L):
                S.op("dve", lambda e, l=l: e.tensor_tensor(out=sm[:, l * HG_H:(l + 1) * HG_H], in0=ex[:, l * HG_H:(l + 1) * HG_H], in1=rc, op=ALU.mult),
                     reads=[("lbtmp",)], writes=[("lbtmp",)])
            S.op("dve", lambda e: e.memset(lbT[:, 0:HG_H], 0.0), writes=[("lbT",)])
            for l in range(1, L):
                S.op("dve", lambda e, l=l: e.tensor_tensor(out=lbT[:, l * HG_H:(l + 1) * HG_H], in0=lbT[:, (l - 1) * HG_H:l * HG_H],
                                                           in1=sm[:, l * HG_H:(l + 1) * HG_H], op=ALU.add),
                     reads=[("lbtmp",), ("lbT",)], writes=[("lbT",)])
            S.op("dve", lambda e: e.tensor_scalar(out=omlT[:], in0=lbT[:], scalar1=-1.0, scalar2=1.0, op0=ALU.mult, op1=ALU.add),
                 reads=[("lbT",)], writes=[("omlT",)])

        class WStream:
            def __init__(self):
                self.plan = []
                self.nload = 0
                self.ncons = 0
                self.cls_idx = {8: 0, 22: 0}
                self.slot_of = []
                self.prev_user = []
                self.slot_last = {}

            def add(self, uid, ap, cls, dtag):
                k = self.cls_idx[cls]
                self.cls_idx[cls] += 1
                nb = NB8 if cls == 8 else NB22
                slot = (cls, k % nb)
                self.prev_user.append(self.slot_last.get(slot, -1))
                self.slot_last[slot] = len(self.plan)
                self.slot_of.append(slot)
                self.plan.append((uid, ap, cls, dtag))

            def _buf(self, slot):
                cls, i = slot
                return (wb8 if cls == 8 else wb22)[i]

            def consume(self, uid):
                i = self.ncons
                assert self.plan[i][0] == uid, (self.plan[i][0], uid)
                while self.nload < len(self.plan) and self.nload <= i + 5 and (self.prev_user[self.nload] < 0 or self.prev_user[self.nload] <= i - 2):
                    j = self.nload
                    _, ap, cls, dtag = self.plan[j]
                    slot = self.slot_of[j]
                    S.dma("sp", self._buf(slot)[:], ap, reads=[dtag], writes=[("wb",) + slot])
                    self.nload += 1
                assert self.nload > i, (self.nload, i)
                self.ncons += 1
                slot = self.slot_of[i]
                return self._buf(slot), ("wb",) + slot

        WS = WStream()
        FM_CONV = [0, 1, 2, 3]
        FM_QKV = [4, 5, 6, 7, 8, 9]
        HG_ORDER = []
        for _h in range(HG_H):
            HG_ORDER += [18 + _h, 10 + _h, 14 + _h, 22 + _h]

        def plan_ffn(f, l):
            for c in range(NFC):
                WS.add(("g", f, l, c), sG[f][l, c], 8, ("sG", f, l, c))
                WS.add(("u", f, l, c), sU[f][l, c], 8, ("sU", f, l, c))
            for c in range(NCH):
                WS.add(("d", f, l, c), sD[f][l, c], 22, ("sD", f, l, c))

        def plan_layer(l):
            plan_ffn(0, l)
            if STG >= 2:
                for c in FM_CONV + FM_QKV + HG_ORDER:
                    WS.add(("i", l, c), sI[l, c], 8, ("sI", l, c))
                for c in range(NCH):
                    WS.add(("o", l, c), sO[l, c], 8, ("sO", l, c))
            if STG >= 9:
                plan_ffn(1, l)

        for j in range(cfg.ntile):
            for l in range(L):
                plan_layer(l)

        class Grp:
            pass

        gP = Grp()
        gP.name, gP.n, gP.x, gP.h, gP.ym = "P", TT, xT, hT, ymix
        gP.hd = [hc.ap for hc in hidc]
        gP.hdt = [hc.tags for hc in hidc]
        gS = Grp()
        gS.name, gS.n, gS.x, gS.h, gS.ym = "S", NS, xTs, hTs, ymixs
        gS.hd = [hids[:, c, :] for c in range(NFC)]
        gS.hdt = [[("S", "hd", c)] for c in range(NFC)]

        def xtag(g, c):
            return (g.name, "x", c)

        def htag(g, c):
            return (g.name, "h", c)

        def ytag(g, c):
            return (g.name, "ym", c)

        def sumsq_rstd(g, srcs, src_tags, nchunks, denom):
            n = g.n
            for c in range(nchunks):
                S.op("act", lambda e, c=c: e.activation(out=sqc[c].ap[:, 0:n], in_=srcs[c], func=AF.Square),
                     reads=src_tags[c], writes=sqc[c].tags)
            b = ps_next()

            def mm(e):
                ins = None
                for c in range(nchunks):
                    ins = e.matmul(psum[b][:, 0:n], lhsT=ones_b[:], rhs=sqc[c].ap[:, 0:n], start=(c == 0), stop=(c == nchunks - 1))
                return ins
            S.op("pe", mm, reads=sum([sqc[c].tags for c in range(nchunks)], []) + [("ones",)], writes=[PT(b)])
            S.op("act", lambda e: e.activation(out=rstd[:, 0:n], in_=psum[b][:, 0:n], func=AF.Ln, scale=1.0 / denom, bias=eps_col[:]),
                 reads=[PT(b), ("eps",)], writes=[("rstd",)])
            S.op("act", lambda e: e.activation(out=rstd[:, 0:n], in_=rstd[:, 0:n], func=AF.Exp, scale=-0.5),
                 reads=[("rstd",)], writes=[("rstd",)])

        def rmsnorm(g, gain_name, gain_idx0):
            n = g.n
            sumsq_rstd(g, [g.x[:, c, 0:n] for c in range(NCH)], [[xtag(g, c)] for c in range(NCH)], NCH, D)
            for c in range(NCH):
                S.op("dve", lambda e, c=c: e.scalar_tensor_tensor(out=g.h[:, c, 0:n], in0=g.x[:, c, 0:n],
                                                                  scalar=pcol(gain_name, gain_idx0 + c), in1=rstd[:, 0:n],
                                                                  op0=ALU.mult, op1=ALU.mult),
                     reads=[xtag(g, c), ("rstd",), ("prmT",)], writes=[htag(g, c)])

        def proj_fm(unit, utag, kch, rhs_fn, rhs_tags, n, b):
            def mm(e):
                ins = None
                for k in range(kch):
                    ins = e.matmul(psum[b][:, 0:n], lhsT=unit[:, k * 128:(k + 1) * 128], rhs=rhs_fn(k),
                                   start=(k == 0), stop=(k == kch - 1))
                return ins
            S.op("pe", mm, reads=[utag] + rhs_tags, writes=[PT(b)])

        def ffn(groups, f, l):
            sg_i = 0
            for c in range(NFC):
                ug, tg = WS.consume(("g", f, l, c))
                uu, tu = WS.consume(("u", f, l, c))
                for g in groups:
                    n = g.n
                    bg, bu = ps_next(), ps_next()
                    htags = [htag(g, k) for k in range(NCH)]
                    proj_fm(ug, tg, NCH, lambda k, g=g, n=n: g.h[:, k, 0:n], htags, n, bg)
                    proj_fm(uu, tu, NCH, lambda k, g=g, n=n: g.h[:, k, 0:n], htags, n, bu)
                    si = sg_i % 2
                    sg_i += 1
                    S.op("act", lambda e, si=si, bg=bg, n=n: e.activation(out=sgb[si][:, 0:n], in_=psum[bg][:, 0:n], func=AF.Silu),
                         reads=[PT(bg)], writes=[("sgb", si)])
                    S.op("dve", lambda e, si=si, bu=bu, n=n, g=g, c=c: e.tensor_tensor(out=g.hd[c][:, 0:n], in0=sgb[si][:, 0:n], in1=psum[bu][:, 0:n], op=ALU.mult),
                         reads=[("sgb", si), PT(bu)], writes=g.hdt[c])
            for oc in range(NCH):
                ud, td = WS.consume(("d", f, l, oc))
                for g in groups:
                    n = g.n
                    b = ps_next()
                    proj_fm(ud, td, NFC, lambda k, g=g, n=n: g.hd[k][:, 0:n], sum([g.hdt[k] for k in range(NFC)], []), n, b)
                    S.op("dve", lambda e, g=g, n=n, b=b, oc=oc: e.scalar_tensor_tensor(out=g.x[:, oc, 0:n], in0=psum[b][:, 0:n], scalar=0.5,
                                                                                      in1=g.x[:, oc, 0:n], op0=ALU.mult, op1=ALU.add),
                         reads=[PT(b), xtag(g, oc)], writes=[xtag(g, oc)])

        def load_tokens(g, src_rows, blocks):
            for bi, (r0, nr, c0) in enumerate(blocks):
                io = iobuf[bi % 2]
                S.dma("sp", io.ap[0:nr, :], src_rows[r0:r0 + nr, :], writes=io.tags)
                for c in range(NCH):
                    b = ps_next()
                    S.op("pe", lambda e, io=io, nr=nr, c=c, b=b: e.transpose(psum[b][:, 0:nr], io.ap[0:nr, c * 128:(c + 1) * 128], ident[0:nr, 0:nr]),
                         reads=io.tags + [("ident",)], writes=[PT(b)])
                    if c % 2 == 0:
                        S.op("act", lambda e, b=b, c=c, nr=nr, c0=c0: e.activation(out=g.x[:, c, c0:c0 + nr], in_=psum[b][:, 0:nr], func=AF.Copy),
                             reads=[PT(b)], writes=[xtag(g, c)])
                    else:
                        S.op("dve", lambda e, b=b, c=c, nr=nr, c0=c0: e.tensor_copy(out=g.x[:, c, c0:c0 + nr], in_=psum[b][:, 0:nr]),
                             reads=[PT(b)], writes=[xtag(g, c)])

        def final_norm_store(g, dst_rows, blocks):
            n = g.n
            sumsq_rstd(g, [g.x[:, c, 0:n] for c in range(NCH)], [[xtag(g, c)] for c in range(NCH)], NCH, D)
            for c in range(NCH):
                S.op("dve", lambda e, c=c: e.scalar_tensor_tensor(out=finc[c].ap[:, 0:n], in0=g.x[:, c, 0:n],
                                                                  scalar=pcol("lnf", c), in1=rstd[:, 0:n],
                                                                  op0=ALU.mult, op1=ALU.mult),
                     reads=[xtag(g, c), ("rstd",), ("prmT",)], writes=finc[c].tags)
            for bi, (r0, nr, c0) in enumerate(blocks):
                io = iobuf[bi % 2]
                for half in range(2):
                    b = ps_next()

                    def tr(e, half=half, b=b, nr=nr, c0=c0):
                        ins = None
                        for cc in range(4):
                            ins = e.transpose(psum[b][0:nr, cc * 128:(cc + 1) * 128], finc[half * 4 + cc].ap[:, c0:c0 + nr], ident[:])
                        return ins
                    S.op("pe", tr, reads=sum([finc[half * 4 + cc].tags for cc in range(4)], []) + [("ident",)], writes=[PT(b)])
                    if half == 0:
                        S.op("act", lambda e, b=b, nr=nr, io=io: e.activation(out=io.ap[0:nr, 0:512], in_=psum[b][0:nr, :], func=AF.Copy),
                             reads=[PT(b)], writes=io.tags)
                    else:
                        S.op("dve", lambda e, b=b, nr=nr, io=io: e.tensor_copy(out=io.ap[0:nr, 512:1024], in_=psum[b][0:nr, :]),
                             reads=[PT(b)], writes=io.tags)
                S.dma("pool", dst_rows[r0:r0 + nr, :], io.ap[0:nr, :], reads=io.tags, writes=[("out", g.name, r0)], is_output=True)

        def mk_ws(g):
            w = Grp()
            n = g.n
            if g is gP:
                w.cva = av("cva", 0, 4, F32, "p (c t) -> p c t", c=2)
                w.cvs = av("cvs", 4, 4, F32, "p (c t) -> p c t", c=2)
                w.cvy = av("cvy", 8, 4, F32, "p (c t) -> p c t", c=2)
                w.diag = av("diag", 14, 16, BF16, "p (w j) -> p w j", j=128)
                w.qz = av("qz", 30, 4, BF16, "p (h t) -> p h t", h=4)
                w.Eb = [av("Eb%d" % i, 34 + i, 1) for i in range(4)]
                w.Pb = [av("Pb%d" % i, 38 + i, 1) for i in range(4)]
                w.lnd = [av("lnd%d" % i, 42 + 0.5 * i, 0.5, F32) for i in range(2)]
                hp = Grp()
                for k, nm in enumerate(("q", "f", "lg", "kk", "eb", "en")):
                    setattr(hp, nm, av("h" + nm, 2 * k, 2, F32))
                w.hp = hp
                w.live = []
                for h in range(HG_H):
                    lv = Grp()
                    base = 12 + 5 * h
                    lv.qt = av("hqt%d" % h, base, 1)
                    lv.kt = av("hkt%d" % h, base + 1, 1)
                    lv.kh = av("hkh%d" % h, base + 2, 2, F32)
                    lv.gt = av("hgt%d" % h, base + 4, 1)
                    ebl = sb("ebl%d" % h, [128, n // HGC])
                    lv.ebl = B(ebl[:], [("ebl", h)])
                    emid = sb("emid%d" % h, [128, n // HGC])
                    lv.emid = B(emid[:], [("emid", h)])
                    w.live.append(lv)
                w.vtok = av("vtok", 32, 4, BF16, "p (b c) -> p b c", b=4)
                w.khm = [av("khm%d" % i, 36 + i, 1, BF16, "p (j d) -> p j d", j=4) for i in range(4)]
                w.Am = [av("Am%d" % i, 40 + 0.25 * i, 0.25) for i in range(4)]
                w.Sb = [[av("Sb%d_%d" % (h, i), 41 + 0.25 * (h * 5 + i), 0.25) for i in range(5)] for h in range(HG_H)]
                w.osq = av("osq", 46, 1)
                w.ot = av("ot", 47, 2, F32)
                w.Tb = []
                for h in range(HG_H):
                    tt_ = sb("Tb%d" % h, [128, 2, 128])
                    w.Tb.append([B(tt_[:, i, :], [("Tb", h, i)]) for i in range(2)])
            else:
                def t(nm, shape, dt=F32):
                    tt = sb("s_" + nm, shape, dt)
                    return B(tt[:], [("sw", nm)])
                w.cva = t("cva", [128, 2, n]); w.cvs = t("cvs", [128, 2, n]); w.cvy = t("cvy", [128, 2, n])
                w.diag = None
                w.qz = t("qz", [128, 4, n], BF16)
                w.Eb = [t("Eb%d" % i, [128, 64 + 16], BF16) for i in range(2)]
                w.Pb = [t("Pb%d" % i, [128, 64 + 16], BF16) for i in range(2)]
                w.lnd = [t("lnd%d" % i, [128, 16]) for i in range(2)]
                w.kvout = [av("kvo_s", 12, 2, F32)]
                hp = Grp()
                for nm in ("q", "f", "lg", "kk", "eb", "en"):
                    setattr(hp, nm, t("h" + nm, [128, n]))
                w.hp = hp
                w.live = []
                for h in range(HG_H):
                    lv = Grp()
                    lv.qt = t("hqt%d" % h, [128, n], BF16)
                    lv.kt = t("hkt%d" % h, [128, 128], BF16)
                    lv.kh = t("hkh%d" % h, [128, n])
                    lv.gt = t("hgt%d" % h, [128, n], BF16)
                    lv.ebl = t("ebl%d" % h, [128, n // DEC_T])
                    lv.emid = t("emid%d" % h, [128, n // DEC_T])
                    w.live.append(lv)
                w.vtok = t("vtok", [128, 1, 512], BF16)
                w.khm = [t("khm%d" % i, [128, 4, 128], BF16) for i in range(4)]
                w.Am = [t("Am%d" % i, [128, 128], BF16) for i in range(4)]
                w.Sb = [[t("Sb%d_%d" % (h, i), [128, 128], BF16) for i in range(5)] for h in range(HG_H)]
                w.osq = t("osq", [128, n], BF16)
                w.ot = t("ot", [128, n])
                w.shs = av("shs", 0, 8, F32, "p (b h v) -> p b h v", b=NB, h=HG_H)
                w.kTs = t("kTs", [128, 2, 128], BF16)
                w.vaug = t("vaug", [128, ATT_H, 128], BF16)
                w.ufp = t("ufp", [128, 256])
            return w

        wsP = mk_ws(gP)
        wsS = mk_ws(gS)
        gP.ws, gS.ws = wsP, wsS
        S.op("pool", lambda e: e.memset(wsP.qz.ap, 0.0), writes=wsP.qz.tags)
        S.op("pool", lambda e: e.memset(wsS.qz.ap, 0.0), writes=wsS.qz.tags)
        S.op("pool", lambda e: e.memset(wsS.vaug.ap, 1.0), writes=wsS.vaug.tags)
        S.op("pool", lambda e: e.memset(wsS.kTs.ap, 0.0), writes=wsS.kTs.tags)
        S.op("pool", lambda e: e.memset(wsS.vtok.ap, 0.0), writes=wsS.vtok.tags)
        for _i in range(4):
            S.op("pool", lambda e, _i=_i: e.memset(wsS.khm[_i].ap, 0.0), writes=wsS.khm[_i].tags)
            S.op("pool", lambda e, _i=_i: e.memset(wsS.live[_i].kt.ap, 0.0), writes=wsS.live[_i].kt.tags)

        acc_rr = [0]

        def acc_bank():
            b = 6 + acc_rr[0]
            acc_rr[0] ^= 1
            return b

        def win_chunk_fm(g, l, c, unit, utag):
            n = g.n
            b = ps_next()
            proj_fm(unit, utag, NCH, lambda k: g.h[:, k, 0:n], [htag(g, k) for k in range(NCH)], n, b)
            return b

        def win_chunk_tm(g, unit, utag):
            n = g.n
            bs = 128 if g is gP else n
            nblk = n // bs
            b = ps_next()

            def mm(e):
                ins = None
                for tb in range(nblk):
                    for k in range(NCH):
                        ins = e.matmul(psum[b][0:128, tb * 128:(tb + 1) * 128], lhsT=g.h[:, k, tb * bs:tb * bs + 128], rhs=unit[:, k * 128:(k + 1) * 128],
                                       start=(k == 0), stop=(k == NCH - 1))
                return ins
            S.op("pe", mm, reads=[utag] + [htag(g, k) for k in range(NCH)], writes=[PT(b)])
            return b, bs, nblk

        def conv_evac(g, c, b):
            n, w = g.n, g.ws
            if c < 2:
                S.op("act", lambda e: e.activation(out=w.cva.ap[:, c, 0:n], in_=psum[b][:, 0:n], func=AF.Copy), reads=[PT(b)], writes=w.cva.tags)
            else:
                S.op("act", lambda e: e.activation(out=w.cvs.ap[:, c - 2, 0:n], in_=psum[b][:, 0:n], func=AF.Sigmoid), reads=[PT(b)], writes=w.cvs.tags)

        def conv_glu(g):
            n, w = g.n, g.ws
            S.op("dve", lambda e: e.tensor_tensor(out=w.cva.ap[:, :, 0:n], in0=w.cva.ap[:, :, 0:n], in1=w.cvs.ap[:, :, 0:n], op=ALU.mult),
                 reads=w.cva.tags + w.cvs.tags, writes=w.cva.tags)

        def build_diag(l):
            w = wsP
            for wi in range(CW):
                for ch in range(2):
                    col = pcol("dww", l * CW * 2 + wi * 2 + ch)
                    if False:
                        pass
                    else:
                        S.op("dve", lambda e, wi=wi, ch=ch, col=col: e.tensor_scalar(out=w.diag.ap[:, wi * 2 + ch, :], in0=identb[:], scalar1=col, scalar2=None, op0=ALU.mult),
                             reads=[("identb",), ("prmT",)], writes=w.diag.tags)

        def conv_main(g, l, j):
            n, w = g.n, g.ws
            dg = wsP.diag
            if g is gP:
                S.op("pool", lambda e: e.tensor_copy(out=ubuf[:, :, 0:CW - 1], in_=utail[:, l, :, :]), reads=[("utail", l)], writes=[("ubuf",)])
                S.op("act", lambda e: e.activation(out=ubuf[:, :, CW - 1:CW - 1 + n], in_=w.cva.ap[:, :, 0:n], func=AF.Copy), reads=w.cva.tags, writes=[("ubuf",)])
                rhs = lambda ch, wi: ubuf[:, ch, wi:wi + n]
                outv = lambda b: psum[b][:, 0:n]
                utags = [("ubuf",)]
            else:
                for bb in range(NB):
                    io = iobuf[bb % 2]
                    S.dma("sp", io.ap[0:CW - 1, 0:256], sconv_in[l, bb], writes=io.tags)
                    for ch in range(2):
                        b = ps_next()
                        S.op("pe", lambda e, io=io, ch=ch, b=b: e.transpose(psum[b][:, 0:CW - 1], io.ap[0:CW - 1, ch * 128:(ch + 1) * 128], ident[0:CW - 1, 0:CW - 1]),
                             reads=io.tags + [("ident",)], writes=[PT(b)])
                        S.op("dve", lambda e, ch=ch, bb=bb, b=b: e.tensor_copy(out=ubufs[:, ch, bb, 0:CW - 1], in_=psum[b][:, 0:CW - 1]), reads=[PT(b)], writes=[("ubufs",)])
                    S.dma("pool", o_cs[l, bb, 0:CW - 1 - DEC_T, :], sconv_in[l, bb, DEC_T:CW - 1, :], writes=[("o_cs", l, bb, 0)], is_output=True)
                S.op("act", lambda e: e.activation(out=ubufs[:, :, :, CW - 1:CW - 1 + DEC_T], in_=w.cva.ap.rearrange("p c (b t) -> p c b t", t=DEC_T), func=AF.Copy),
                     reads=w.cva.tags, writes=[("ubufs",)])
                rhs = lambda ch, wi: ubufs[:, ch, :, wi:wi + DEC_T]
                outv = lambda b: psum[b][:, 0:n].rearrange("p (b t) -> p b t", t=DEC_T)
                utags = [("ubufs",)]
            banks = []
            for ch in range(2):
                b = ps_next()
                banks.append(b)

                def mm(e, ch=ch, b=b):
                    ins = None
                    for wi in range(CW):
                        ins = e.matmul(outv(b), lhsT=dg.ap[:, wi * 2 + ch, :], rhs=rhs(ch, wi), start=(wi == 0), stop=(wi == CW - 1))
                    return ins
                S.op("pe", mm, reads=dg.tags + utags, writes=[PT(b)])
                S.op("act", lambda e, ch=ch, b=b: e.activation(out=w.cvy.ap[:, ch, 0:n], in_=psum[b][:, 0:n], func=AF.Identity, bias=pcol("dwb", l * 2 + ch)),
                     reads=[PT(b), ("prmT",)], writes=w.cvy.tags)
            if g is gP:
                S.op("pool", lambda e: e.tensor_copy(out=utail[:, l, :, :], in_=ubuf[:, :, n:n + CW - 1]), reads=[("ubuf",)], writes=[("utail", l)])
            for ch in range(2):
                S.op("dve", lambda e, ch=ch: e.tensor_copy(out=sqc[ch].ap[:, 0:n], in_=w.cvy.ap[:, ch, 0:n]), reads=w.cvy.tags, writes=sqc[ch].tags)
            b = ps_next()

            def mm1(e):
                ins = None
                for ch in range(2):
                    ins = e.matmul(psum[b][:, 0:n], lhsT=ones_b[:], rhs=sqc[ch].ap[:, 0:n], start=(ch == 0), stop=(ch == 1))
                return ins
            S.op("pe", mm1, reads=sqc[0].tags + sqc[1].tags + [("ones",)], writes=[PT(b)])
            for ch in range(2):
                S.op("dve", lambda e, ch=ch: e.scalar_tensor_tensor(out=w.cvy.ap[:, ch, 0:n], in0=psum[b][:, 0:n], scalar=-1.0 / CONV_DIM,
                                                                   in1=w.cvy.ap[:, ch, 0:n], op0=ALU.mult, op1=ALU.add),
                     reads=[PT(b)] + w.cvy.tags, writes=w.cvy.tags)
            sumsq_rstd(g, [w.cvy.ap[:, ch, 0:n] for ch in range(2)], [w.cvy.tags, w.cvy.tags], 2, CONV_DIM)
            for ch in range(2):
                S.op("dve", lambda e, ch=ch: e.tensor_tensor(out=w.cvy.ap[:, ch, 0:n], in0=w.cvy.ap[:, ch, 0:n], in1=rstd[:, 0:n], op=ALU.mult),
                     reads=w.cvy.tags + [("rstd",)], writes=w.cvy.tags)
                S.op("act", lambda e, ch=ch: e.activation(out=g.ym[:, ch, 0:n], in_=w.cvy.ap[:, ch, 0:n], func=AF.Silu,
                                                          scale=pcol("clg", l * 2 + ch), bias=pcol("clb", l * 2 + ch)),
                     reads=w.cvy.tags + [("prmT",)], writes=[ytag(g, ch)])
            if g is gP and j == cfg.ntile - 1:
                io = iobuf[0]
                for ch in range(2):
                    b2 = ps_next()
                    S.op("pe", lambda e, ch=ch, b2=b2: e.transpose(psum[b2][0:CW - 1, 0:128], w.cva.ap[:, ch, n - (CW - 1):n], ident[:]),
                         reads=w.cva.tags + [("ident",)], writes=[PT(b2)])
                    S.op("dve", lambda e, ch=ch, b2=b2, io=io: e.tensor_copy(out=io.ap[0:CW - 1, ch * 128:(ch + 1) * 128], in_=psum[b2][0:CW - 1, 0:128]),
                         reads=[PT(b2)], writes=io.tags)
                S.dma("pool", o_cp[l], io.ap[0:CW - 1, 0:256], reads=io.tags, writes=[("o_cp", l)], is_output=True)
            if g is gS:
                for ch in range(2):
                    b2 = ps_next()
                    S.op("pe", lambda e, ch=ch, b2=b2: e.transpose(psum[b2][0:n, 0:128], w.cva.ap[:, ch, 0:n], ident[:]),
                         reads=w.cva.tags + [("ident",)], writes=[PT(b2)])
                    S.op("dve", lambda e, ch=ch, b2=b2: e.tensor_copy(out=w.ufp.ap[0:n, ch * 128:(ch + 1) * 128], in_=psum[b2][0:n, 0:128]),
                         reads=[PT(b2)], writes=w.ufp.tags)
                for bb in range(NB):
                    S.dma("pool", o_cs[l, bb, CW - 1 - DEC_T:CW - 1, :], w.ufp.ap[bb * DEC_T:(bb + 1) * DEC_T, :], reads=w.ufp.tags,
                          writes=[("o_cs", l, bb, 1)], is_output=True)

        kvst = av("kvst", 44, 8, F32, "p (b c) -> p b c", b=4)

        def att_q_evac(g, c, b):
            n, w = g.n, g.ws
            ch = c - 4
            for hp in range(2):
                h = 2 * ch + hp
                r0 = 64 * hp
                S.op("act", lambda e, h=h, r0=r0: e.activation(out=w.qz.ap[r0:r0 + 64, h, 0:n], in_=psum[b][r0:r0 + 64, 0:n], func=AF.Copy, scale=HD ** -0.5),
                     reads=[PT(b)], writes=w.qz.tags)

        def att_k_evac(g, l, j, c, b):
            n, w = g.n, g.ws
            ch = c - 6
            if g is gP:
                S.op("dve", lambda e: e.tensor_copy(out=kwin[:, ch, WIN:WIN + n], in_=psum[b][:, 0:n]), reads=[PT(b)], writes=[("kwin", "cur")])
                if ch == 1:
                    t0 = j * TT
                    S.dma("pool", sK[l].rearrange("(c p) t -> p c t", p=128)[:, :, t0:t0 + n], kwin[:, :, WIN:WIN + n], reads=[("kwin", "cur")], writes=[("sK", l, j)])
            else:
                S.op("dve", lambda e: e.tensor_copy(out=w.kTs.ap[:, ch, 0:n], in_=psum[b][:, 0:n]), reads=[PT(b)], writes=w.kTs.tags)

        def att_kv_tm(g, l, j, c, unit, utag):
            n, w = g.n, g.ws
            ci = c - 6
            b, bs, nblk = win_chunk_tm(g, unit, utag)
            t0 = j * TT
            if g is gP:
                S.op("act", lambda e: e.activation(out=kvst.ap[:, :, ci * 128:(ci + 1) * 128], in_=psum[b][:, :].rearrange("p (b c) -> p b c", c=128), func=AF.Copy),
                     reads=[PT(b)], writes=kvst.tags)
            else:
                kv = w.kvout[0]
                S.op("act", lambda e: e.activation(out=kv.ap[0:bs, ci * 128:(ci + 1) * 128], in_=psum[b][0:bs, 0:128], func=AF.Copy), reads=[PT(b)], writes=kv.tags)
            if c >= 8 and "kvcp" not in os.environ.get("DBG_SKIP", ""):
                for hh in range(2):
                    h = 2 * (c - 8) + hh
                    if g is gP:
                        S.op("act", lambda e, h=h, hh=hh: e.activation(out=vwin[:, WIN // 128:WIN // 128 + 4, h, 64 * hh:64 * hh + 64],
                                                                       in_=psum[b][:, :].rearrange("p (b c) -> p b c", c=128)[:, :, 64 * hh:64 * hh + 64], func=AF.Copy),
                             reads=[PT(b)], writes=[("vwin", "cur")])
                    else:
                        S.op("act", lambda e, h=h, hh=hh: e.activation(out=w.vaug.ap[0:bs, h, 64 * hh:64 * hh + 64], in_=psum[b][0:bs, 64 * hh:64 * hh + 64], func=AF.Copy),
                             reads=[PT(b)], writes=w.vaug.tags)
            if c == 9 and "kvdma" not in os.environ.get("DBG_SKIP", ""):
                if g is gP:
                    r0 = t0 - (SEQ - KEEP)
                    if r0 >= 0:
                        S.dma("pool", o_kp[l, r0:r0 + n, :].rearrange("(b p) c -> p b c", p=128), kvst.ap[:, :, 0:256], reads=kvst.tags, writes=[("o_kp", l, r0)], is_output=True)
                        S.dma("pool", o_vp[l, r0:r0 + n, :].rearrange("(b p) c -> p b c", p=128), kvst.ap[:, :, 256:512], reads=kvst.tags, writes=[("o_vp", l, r0)], is_output=True)
                    S.dma("pool", sV[l, t0:t0 + n, :].rearrange("(b p) c -> p b c", p=128),
                          vwin[:, WIN // 128:WIN // 128 + 4, :, :].rearrange("p b h d -> p b (h d)"), reads=[("vwin", "cur")], writes=[("sV", l, j)])
                else:
                    kv = w.kvout[0]
                    S.dma("pool", o_ks[l], kv.ap[0:n, 0:256], reads=kv.tags, writes=[("o_ks", l)], is_output=True)
                    S.dma("pool", o_vs[l], kv.ap[0:n, 256:512], reads=kv.tags, writes=[("o_vs", l)], is_output=True)

        def att_finish(g, w, bO, h, ych, cols, li):
            ch, hp = divmod(h, 2)
            orow = 64 * hp
            drow = 64 * (1 - hp)
            nq = cols[1] - cols[0]
            ld = w.lnd[li % 2]
            S.op("act", lambda e: e.activation(out=ld.ap[orow:orow + 64, 0:nq], in_=psum[bO][drow:drow + 64, 0:nq], func=AF.Ln),
                 reads=[PT(bO)], writes=ld.tags)
            S.op("act", lambda e: e.activation(out=ld.ap[orow:orow + 64, 0:nq], in_=ld.ap[orow:orow + 64, 0:nq], func=AF.Exp, scale=-1.0),
                 reads=ld.tags, writes=ld.tags)
            S.op("dve", lambda e: e.tensor_tensor(out=g.ym[orow:orow + 64, 2 + ch, cols[0]:cols[1]], in0=psum[bO][orow:orow + 64, 0:nq],
                                                  in1=ld.ap[orow:orow + 64, 0:nq], op=ALU.mult),
                 reads=[PT(bO)] + ld.tags, writes=[ytag(g, 2 + ch)])

        _wt = att_weight_table().astype(np.float64)
        MMAX = [max(m for m in range(NM) if np.any(_wt[:, m, h, :] >= 2.0 ** -134)) for h in range(ATT_H)]

        def att_prompt(l, j):
            g, w = gP, wsP
            t0 = j * TT
            work = []
            for qb in range(TT // 128):
                gb = j * (TT // 128) + qb
                nm = min(16, gb) + 1
                for h in range(ATT_H):
                    bO = acc_bank()
                    nmh = min(nm, MMAX[h] + 1)
                    for g0 in range(0, nmh, 4):
                        work.append(dict(qb=qb, h=h, grp=list(range(g0, min(nmh, g0 + 4))), first=(g0 == 0), last=(g0 + 4 >= nmh), bO=bO, nm=nmh))
            li = [0]

            def emit_S(i):
                wk = work[i]
                b = ps_next()
                wk["b"] = b
                qb, h, grp = wk["qb"], wk["h"], wk["grp"]
                ch = h // 2

                def mm(e):
                    ins = None
                    for gi, m in enumerate(grp):
                        wkb = 16 + qb - m
                        ins = e.matmul(psum[b][:, gi * 128:(gi + 1) * 128], lhsT=kwin[:, ch, wkb * 128:(wkb + 1) * 128],
                                       rhs=w.qz.ap[:, h, qb * 128:(qb + 1) * 128], start=True, stop=True)
                    return ins
                S.op("pe", mm, reads=[("kwin", "hist"), ("kwin", "cur")] + w.qz.tags, writes=[PT(b)])
                ng = len(grp)
                Eb, Pb = w.Eb[i % 4], w.Pb[i % 4]
                S.op("act", lambda e: e.activation(out=Eb.ap[:, 0:ng * 128], in_=psum[b][:, 0:ng * 128], func=AF.Exp), reads=[PT(b)], writes=Eb.tags)
                S.op("dve", lambda e: e.tensor_tensor(
                    out=Pb.ap[:, 0:ng * 128].rearrange("p (m q) -> p m q", q=128), in0=Eb.ap[:, 0:ng * 128].rearrange("p (m q) -> p m q", q=128),
                    in1=wmask[:, grp[0]:grp[0] + ng, h, :], op=ALU.mult), reads=Eb.tags + [("wmask",)], writes=Pb.tags)

            def emit_PV(i):
                wk = work[i]
                qb, h, grp, bO, nm = wk["qb"], wk["h"], wk["grp"], wk["bO"], wk["nm"]
                Pb = w.Pb[i % 4]

                def pv(e):
                    ins = None
                    for gi, m in enumerate(grp):
                        wkb = 16 + qb - m
                        ins = e.matmul(psum[bO][:, 0:128], lhsT=vwin[:, wkb, h, :], rhs=Pb.ap[:, gi * 128:(gi + 1) * 128],
                                       start=(wk["first"] and gi == 0), stop=(m == nm - 1))
                    return ins
                S.op("pe", pv, reads=Pb.tags + [("vwin", "hist"), ("vwin", "cur")], writes=[PT(bO)])
                if wk["last"]:
                    att_finish(g, w, bO, h, None, (qb * 128, (qb + 1) * 128), li[0])
                    li[0] += 1

            LA = 3
            for i in range(len(work) + LA):
                if i < len(work):
                    emit_S(i)
                if i - LA >= 0:
                    emit_PV(i - LA)

        def att_sample(l):
            g, w = gS, wsS
            n = g.n
            li = 0
            ei = 0
            for bb in range(NB):
                for q4 in range(4):
                    io = iobuf[q4 % 2]
                    S.dma("sp", io.ap[:, :].rearrange("p (b c) -> p b c", c=256), ck_in[l, bb, q4 * 512:(q4 + 1) * 512, :].rearrange("(b p) c -> p b c", p=128), writes=io.tags)
                    for ch in range(2):
                        b = ps_next()

                        def tr(e, io=io, ch=ch, b=b):
                            ins = None
                            for kb in range(4):
                                ins = e.transpose(psum[b][:, kb * 128:(kb + 1) * 128], io.ap[:, kb * 256 + ch * 128:kb * 256 + (ch + 1) * 128], ident[:])
                            return ins
                        S.op("pe", tr, reads=io.tags + [("ident",)], writes=[PT(b)])
                        if ch == 0:
                            S.op("act", lambda e, b=b, q4=q4, ch=ch: e.activation(out=kwin[:, ch, q4 * 512:(q4 + 1) * 512], in_=psum[b][:, :], func=AF.Copy),
                                 reads=[PT(b)], writes=[("kwin", "hist")])
                        else:
                            S.op("dve", lambda e, b=b, q4=q4, ch=ch: e.tensor_copy(out=kwin[:, ch, q4 * 512:(q4 + 1) * 512], in_=psum[b][:, :]),
                                 reads=[PT(b)], writes=[("kwin", "hist")])
                for q4 in range(4):
                    io = iobuf[q4 % 2]
                    S.dma("sp", io.ap[:, :].rearrange("p (b c) -> p b c", c=256), cv_in[l, bb, q4 * 512:(q4 + 1) * 512, :].rearrange("(b p) c -> p b c", p=128), writes=io.tags)
                    for par in range(2):
                        S.op("act", lambda e, io=io, q4=q4, par=par: e.activation(
                            out=vwin[:, q4 * 4:(q4 + 1) * 4, par::2, 64 * par:64 * par + 64],
                            in_=io.ap[:, :].rearrange("p (b h d) -> p b h d", b=4, d=64)[:, :, par::2, :], func=AF.Copy),
                            reads=io.tags, writes=[("vwin", "hist")])
                for h in range(ATT_H):
                    ch = h // 2
                    bO = acc_bank()
                    b = ps_next()

                    def mm(e, b=b, ch=ch, h=h, bb=bb):
                        ins = None
                        for m in range(1, min(16, MMAX[h]) + 1):
                            wkb = 16 - m
                            ins = e.matmul(psum[b][:, (m - 1) * DEC_T:m * DEC_T], lhsT=kwin[:, ch, wkb * 128:(wkb + 1) * 128],
                                           rhs=w.qz.ap[:, h, bb * DEC_T:(bb + 1) * DEC_T], start=True, stop=True)
                        ins = e.matmul(psum[b][0:128, 64:64 + DEC_T], lhsT=w.kTs.ap[:, ch, 0:128], rhs=w.qz.ap[:, h, bb * DEC_T:(bb + 1) * DEC_T], start=True, stop=True)
                        return ins
                    S.op("pe", mm, reads=[("kwin", "hist")] + w.kTs.tags + w.qz.tags, writes=[PT(b)])
                    Eb = w.Eb[ei % 2]
                    Pb = w.Pb[ei % 2]
                    ei += 1
                    mh = min(16, MMAX[h])
                    S.op("act", lambda e, b=b, Eb=Eb, mh=mh: e.activation(out=Eb.ap[:, 0:mh * DEC_T], in_=psum[b][:, 0:mh * DEC_T], func=AF.Exp), reads=[PT(b)], writes=Eb.tags)
                    S.op("act", lambda e, b=b, Eb=Eb: e.activation(out=Eb.ap[:, 64:64 + DEC_T], in_=psum[b][:, 64:64 + DEC_T], func=AF.Exp), reads=[PT(b)], writes=Eb.tags)
                    S.op("dve", lambda e, Eb=Eb, Pb=Pb, h=h, mh=mh: e.tensor_tensor(out=Pb.ap[:, 0:mh * DEC_T].rearrange("p (m q) -> p m q", q=DEC_T),
                                                                          in0=Eb.ap[:, 0:mh * DEC_T].rearrange("p (m q) -> p m q", q=DEC_T),
                                                                          in1=wmask[:, 1:mh + 1, h, 0:DEC_T], op=ALU.mult),
                         reads=Eb.tags + [("wmask",)], writes=Pb.tags)
                    S.op("dve", lambda e, Eb=Eb, Pb=Pb, h=h, bb=bb: e.tensor_tensor(out=Pb.ap[:, 64:64 + DEC_T], in0=Eb.ap[:, 64:64 + DEC_T],
                                                                                 in1=wsm[:, bb, h, :], op=ALU.mult),
                         reads=Eb.tags + [("wsm",)], writes=Pb.tags)

                    def pv(e, Pb=Pb, h=h, bO=bO):
                        ins = None
                        for m in range(1, min(16, MMAX[h]) + 1):
                            wkb = 16 - m
                            ins = e.matmul(psum[bO][:, 0:DEC_T], lhsT=vwin[:, wkb, h, :], rhs=Pb.ap[:, (m - 1) * DEC_T:m * DEC_T], start=(m == 1), stop=False)
                        ins = e.matmul(psum[bO][:, 0:DEC_T], lhsT=w.vaug.ap[:, h, :], rhs=Pb.ap[:, 64:64 + DEC_T], start=False, stop=True)
                        return ins
                    S.op("pe", pv, reads=Pb.tags + [("vwin", "hist")] + w.vaug.tags, writes=[PT(bO)])
                    att_finish(g, w, bO, h, None, (bb * DEC_T, (bb + 1) * DEC_T), li)
                    li += 1

        def hgrn_prep(g, l, h):
            n, w = g.n, g.ws
            hs, lv = w.hp, w.live[h]
            prompt = g is gP
            C = HGC if prompt else DEC_T
            nchunk = n // C
            rsm = rsmask if prompt else rsmask_s
            rsm_tag = ("rsmask",) if prompt else ("rsmask_s",)
            lbc = lbT[:, l * HG_H + h:l * HG_H + h + 1]
            omc = omlT[:, l * HG_H + h:l * HG_H + h + 1]
            A = lambda bf: bf.ap[:, 0:n]
            S.op("pool", lambda e: e.tensor_scalar(out=A(hs.f), in0=A(hs.f), scalar1=omc, scalar2=lbc, op0=ALU.mult, op1=ALU.add),
                 reads=hs.f.tags + [("lbT",), ("omlT",)], writes=hs.f.tags)
            S.op("act", lambda e: e.activation(out=A(hs.lg), in_=A(hs.f), func=AF.Ln), reads=hs.f.tags, writes=hs.lg.tags)
            S.op("pool", lambda e: e.tensor_scalar(out=A(hs.kk), in0=A(hs.f), scalar1=-1.0, scalar2=1.0, op0=ALU.mult, op1=ALU.add),
                 reads=hs.f.tags, writes=hs.kk.tags)
            S.op("dve", lambda e: e.tensor_tensor_scan(out=A(hs.f), data0=rsm[:, 0:n], data1=A(hs.lg), initial=0.0, op0=ALU.mult, op1=ALU.add),
                 reads=hs.lg.tags + [rsm_tag], writes=hs.f.tags)
            b3 = hs.f.ap[:, 0:n].rearrange("p (c t) -> p c t", t=C)
            MID = C // 2 - 1
            S.op("act", lambda e: e.activation(out=lv.ebl.ap[:, 0:nchunk], in_=b3[:, :, C - 1], func=AF.Exp), reads=hs.f.tags, writes=lv.ebl.tags)
            S.op("act", lambda e: e.activation(out=lv.emid.ap[:, 0:nchunk], in_=b3[:, :, MID], func=AF.Exp), reads=hs.f.tags, writes=lv.emid.tags)
            S.op("dve", lambda e: e.tensor_tensor(out=hs.lg.ap[:, 0:n].rearrange("p (c t) -> p c t", t=C),
                                                  in0=b3, in1=b3[:, :, MID:MID + 1].to_broadcast([128, nchunk, C]), op=ALU.subtract),
                 reads=hs.f.tags, writes=hs.lg.tags)
            S.op("act", lambda e: e.activation(out=A(hs.eb), in_=A(hs.lg), func=AF.Exp), reads=hs.lg.tags, writes=hs.eb.tags)
            S.op("act", lambda e: e.activation(out=A(hs.en), in_=A(hs.lg), func=AF.Exp, scale=-1.0), reads=hs.lg.tags, writes=hs.en.tags)
            S.op("dve", lambda e: e.tensor_tensor(out=A(lv.qt), in0=A(hs.q), in1=A(hs.eb), op=ALU.mult), reads=hs.q.tags + hs.eb.tags, writes=lv.qt.tags)
            S.op("dve", lambda e: e.tensor_tensor(out=A(lv.kt), in0=A(hs.kk), in1=A(hs.en), op=ALU.mult), reads=hs.kk.tags + hs.en.tags, writes=lv.kt.tags)
            S.op("dve", lambda e: e.tensor_tensor(out=hs.lg.ap[:, 0:n].rearrange("p (c t) -> p c t", t=C),
                                                  in0=b3[:, :, C - 1:C].to_broadcast([128, nchunk, C]), in1=b3, op=ALU.subtract),
                 reads=hs.f.tags + hs.eb.tags + hs.en.tags, writes=hs.lg.tags)
            S.op("act", lambda e: e.activation(out=A(hs.en), in_=A(hs.lg), func=AF.Exp), reads=hs.lg.tags, writes=hs.en.tags)
            S.op("dve", lambda e: e.tensor_tensor(out=A(lv.kh), in0=A(hs.kk), in1=A(hs.en), op=ALU.mult), reads=hs.kk.tags + hs.en.tags, writes=lv.kh.tags)

        def hgrn_chain(g, l, j):
            n, w = g.n, g.ws
            prompt = g is gP
            C = HGC if prompt else DEC_T
            bs = 128 if prompt else n
            nblk = n // bs
            NCB = bs // C
            bdm = bdmask if prompt else bdmask_s
            bdm_tag = ("bdmask",) if prompt else ("bdmask_s",)
            rm0 = 0 if prompt else 4
            vt_tags = w.vtok.tags
            ps_rr[0] = ps_rr[0] % 4
            npr_save = NPRv[0]
            NPRv[0] = 4
            bOT = [4 + h for h in range(HG_H)]
            ki = [0]
            if prompt:
                for h in range(HG_H):
                    S.op("pool", lambda e, h=h: e.tensor_copy(out=w.Tb[h][0].ap, in_=Sst[:, l, h, :]), reads=[("Sst", l, h)], writes=w.Tb[h][0].tags)
                    S.op("act", lambda e, h=h: e.activation(out=w.Sb[h][0].ap, in_=Sst[:, l, h, :], func=AF.Copy, scale=w.live[h].emid.ap[:, 0:1]),
                         reads=[("Sst", l, h)] + w.live[h].emid.tags, writes=w.Sb[h][0].tags)

            st1 = {}

            def stage1(tb, h):
                lv = w.live[h]
                c0 = tb * bs
                bA = ps_next()
                S.op("pe", lambda e: e.matmul(psum[bA][0:128, 0:bs], lhsT=lv.kt.ap[:, c0:c0 + 128], rhs=lv.qt.ap[:, c0:c0 + bs], start=True, stop=True),
                     reads=lv.kt.tags + lv.qt.tags, writes=[PT(bA)])
                Am = w.Am[ki[0] % 4]
                khm = w.khm[ki[0] % 4]
                ki[0] += 1
                S.op("dve", lambda e: e.tensor_tensor(out=Am.ap[:, 0:bs], in0=psum[bA][:, 0:bs], in1=bdm[:, 0:bs], op=ALU.mult),
                     reads=[PT(bA), bdm_tag], writes=Am.tags)
                bT = ps_next()
                S.op("pe", lambda e: e.transpose(psum[bT][0:bs, 0:128], lv.kh.ap[:, c0:c0 + bs], ident[:]),
                     reads=lv.kh.tags + [("ident",)], writes=[PT(bT)])
                for jj in range(NCB):
                    rc = rowmask[0:bs, rm0 + jj:rm0 + jj + 1]
                    S.op("dve", lambda e, jj=jj, rc=rc: e.tensor_scalar(out=khm.ap[0:bs, jj, :], in0=psum[bT][0:bs, 0:128], scalar1=rc, scalar2=None, op0=ALU.mult),
                         reads=[PT(bT), ("rowmask",)], writes=khm.tags)
                st1[(tb, h)] = (Am, khm)

            def stage1b(tb, h):
                lv = w.live[h]
                c0 = tb * bs
                Am, khm = st1[(tb, h)]
                bD = ps_next()

                def dS(e):
                    ins = None
                    for jj in range(NCB):
                        ins = e.matmul(psum[bD][:, jj * 128:(jj + 1) * 128], lhsT=khm.ap[:, jj, :], rhs=w.vtok.ap[:, tb, h * 128:(h + 1) * 128], start=True, stop=True)
                    return ins
                S.op("pe", dS, reads=khm.tags + vt_tags, writes=[PT(bD)])
                S.op("pe", lambda e: e.matmul(psum[bOT[h]][:, c0:c0 + bs], lhsT=w.vtok.ap[:, tb, h * 128:(h + 1) * 128], rhs=Am.ap[:, 0:bs], start=True, stop=False),
                     reads=Am.tags + vt_tags, writes=[PT(bOT[h])])
                for jj in range(NCB):
                    if prompt:
                        cidx = tb * NCB + jj
                        ecol = lv.ebl.ap[:, cidx:cidx + 1]
                        tin, tout = w.Tb[h][cidx % 2], w.Tb[h][(cidx + 1) % 2]
                        S.op("dve", lambda e, ecol=ecol, jj=jj, tin=tin, tout=tout: e.scalar_tensor_tensor(out=tout.ap, in0=tin.ap, scalar=ecol,
                                                                                      in1=psum[bD][:, jj * 128:(jj + 1) * 128], op0=ALU.mult, op1=ALU.add),
                             reads=tin.tags + [PT(bD)] + lv.ebl.tags, writes=tout.tags)
                        sbn = w.Sb[h][(cidx + 1) % 5]
                        if cidx + 1 < nblk * NCB:
                            mcol = lv.emid.ap[:, cidx + 1:cidx + 2]
                            S.op("act", lambda e, sbn=sbn, tout=tout, mcol=mcol: e.activation(out=sbn.ap, in_=tout.ap, func=AF.Copy, scale=mcol),
                                 reads=tout.tags + lv.emid.tags, writes=sbn.tags)
                    else:
                        sbc = w.Sb[h][jj]
                        S.op("act", lambda e, sbc=sbc, jj=jj: e.activation(out=sbc.ap, in_=w.shs.ap[:, jj, h, :], func=AF.Copy, scale=lv.emid.ap[:, jj:jj + 1]),
                             reads=w.shs.tags + lv.emid.tags, writes=sbc.tags)
                        ecol = lv.ebl.ap[:, jj:jj + 1]
                        S.op("dve", lambda e, ecol=ecol, jj=jj: e.scalar_tensor_tensor(out=w.shs.ap[:, jj, h, :], in0=w.shs.ap[:, jj, h, :], scalar=ecol,
                                                                                      in1=psum[bD][:, jj * 128:(jj + 1) * 128], op0=ALU.mult, op1=ALU.add),
                             reads=w.shs.tags + [PT(bD)] + lv.ebl.tags, writes=w.shs.tags)

            def stage2(tb, h):
                lv = w.live[h]
                c0 = tb * bs
                for jj in range(NCB):
                    q0 = c0 + jj * C
                    sbc = w.Sb[h][(tb * NCB + jj) % 5] if prompt else w.Sb[h][jj]
                    S.op("pe", lambda e, sbc=sbc, q0=q0, jj=jj: e.matmul(psum[bOT[h]][:, q0:q0 + C], lhsT=sbc.ap, rhs=lv.qt.ap[:, q0:q0 + C], start=False, stop=(jj == NCB - 1)),
                         reads=sbc.tags + lv.qt.tags, writes=[PT(bOT[h])])

            seq = [(tb, h) for tb in range(nblk) for h in range(HG_H)]
            for i in range(len(seq) + 2):
                if i < len(seq):
                    stage1(*seq[i])
                if 1 <= i <= len(seq):
                    stage1b(*seq[i - 1])
                if i >= 2:
                    stage2(*seq[i - 2])
            if prompt:
                nct = nblk * NCB
                for h in range(HG_H):
                    S.op("pool", lambda e, h=h: e.tensor_copy(out=Sst[:, l, h, :], in_=w.Tb[h][nct % 2].ap), reads=w.Tb[h][nct % 2].tags, writes=[("Sst", l, h)])
            for h in range(HG_H):
                lv = w.live[h]
                S.op("act", lambda e, h=h: e.activation(out=w.osq.ap[:, 0:n], in_=psum[bOT[h]][:, 0:n], func=AF.Square), reads=[PT(bOT[h])], writes=w.osq.tags)
                bN = ps_next()
                S.op("pe", lambda e, bN=bN: e.matmul(psum[bN][:, 0:n], lhsT=ones_b[:], rhs=w.osq.ap[:, 0:n], start=True, stop=True), reads=w.osq.tags + [("ones",)], writes=[PT(bN)])
                S.op("act", lambda e, bN=bN: e.activation(out=w.ot.ap[:, 0:n], in_=psum[bN][:, 0:n], func=AF.Ln, scale=1.0 / 128, bias=eps_col[:]), reads=[PT(bN), ("eps",)], writes=w.ot.tags)
                S.op("act", lambda e: e.activation(out=w.ot.ap[:, 0:n], in_=w.ot.ap[:, 0:n], func=AF.Exp, scale=-0.5), reads=w.ot.tags, writes=w.ot.tags)
                S.op("dve", lambda e, h=h: e.tensor_tensor(out=w.ot.ap[:, 0:n], in0=psum[bOT[h]][:, 0:n], in1=w.ot.ap[:, 0:n], op=ALU.mult), reads=[PT(bOT[h])] + w.ot.tags, writes=w.ot.tags)
                S.op("dve", lambda e, h=h, lv=lv: e.scalar_tensor_tensor(out=g.ym[:, 4 + h, 0:n], in0=w.ot.ap[:, 0:n], scalar=pcol("hgn", l), in1=lv.gt.ap[:, 0:n], op0=ALU.mult, op1=ALU.mult),
                     reads=w.ot.tags + lv.gt.tags + [("prmT",)], writes=[ytag(g, 4 + h)])
            NPRv[0] = npr_save

        def hgrn_vtok(g, h, unit, utag):
            w = g.ws
            b, bs, nblk = win_chunk_tm(g, unit, utag)
            if g is gP:
                S.op("act", lambda e: e.activation(out=w.vtok.ap[:, :, h * 128:(h + 1) * 128], in_=psum[b][:, :].rearrange("p (b c) -> p b c", c=128), func=AF.Copy),
                     reads=[PT(b)], writes=w.vtok.tags)
            else:
                S.op("act", lambda e: e.activation(out=w.vtok.ap[:, 0, h * 128:(h + 1) * 128], in_=psum[b][:, 0:128], func=AF.Copy),
                     reads=[PT(b)], writes=w.vtok.tags)

        def att_hist_load(l, j):
            t0 = j * TT
            avail = min(WIN, t0)
            if avail > 0:
                S.dma("sp", kwin[:, :, WIN - avail:WIN], sK[l].rearrange("(c p) t -> p c t", p=128)[:, :, t0 - avail:t0],
                      reads=[("sK", l, jj) for jj in range(j)], writes=[("kwin", "hist")])
                nb = avail // 128
                S.dma("sp", vwin[:, 16 - nb:16, :, :].rearrange("p b h d -> p b (h d)"),
                      sV[l, t0 - avail:t0, :].rearrange("(b p) c -> p b c", p=128),
                      reads=[("sV", l, jj) for jj in range(j)], writes=[("vwin", "hist")])

        def mix_layer(groups, l, j):
            if STG >= 3:
                att_hist_load(l, j)
            for g in groups:
                rmsnorm(g, "lnm", l * 8)
            build_diag(l)
            for c in FM_CONV:
                u, t = WS.consume(("i", l, c))
                for g in groups:
                    conv_evac(g, c, win_chunk_fm(g, l, c, u, t))
            for g in groups:
                conv_glu(g)
            if STG >= 3:
                for g in groups:
                    S.op("pool", lambda e, g=g: e.memset(g.ws.qz.ap, 0.0), writes=g.ws.qz.tags)
            for c in FM_QKV:
                u, t = WS.consume(("i", l, c))
                if STG < 3:
                    continue
                for g in groups:
                    if c in (4, 5):
                        att_q_evac(g, c, win_chunk_fm(g, l, c, u, t))
                    elif c in (6, 7):
                        att_k_evac(g, l, j, c, win_chunk_fm(g, l, c, u, t))
                    if c >= 6 and "kvtm" not in os.environ.get("DBG_SKIP", ""):
                        att_kv_tm(g, l, j, c, u, t)
            for g in groups:
                conv_main(g, l, j)
            if STG >= 3:
                if "attp" in os.environ.get("DBG_SKIP", ""):
                    for c in (2, 3):
                        S.op("pool", lambda e, c=c: e.memset(gP.ym[:, c, 0:gP.n], 0.0), writes=[ytag(gP, c)])
                else:
                    att_prompt(l, j)
                if gS in groups:
                    if os.environ.get("NO_ATT_S"):
                        for c in (2, 3):
                            S.op("pool", lambda e, c=c: e.memset(gS.ym[:, c, 0:gS.n], 0.0), writes=[ytag(gS, c)])
                    else:
                        att_sample(l)
            else:
                for g in groups:
                    for c in (2, 3):
                        S.op("pool", lambda e, g=g, c=c: e.memset(g.ym[:, c, 0:g.n], 0.0), writes=[ytag(g, c)])
            for h in range(HG_H):
                for c, key in ((18 + h, "v"), (10 + h, "q"), (14 + h, "f"), (22 + h, "g")):
                    u, t = WS.consume(("i", l, c))
                    if STG < 4:
                        continue
                    for g in groups:
                        if key == "v":
                            hgrn_vtok(g, h, u, t)
                        else:
                            bz = win_chunk_fm(g, l, c, u, t)
                            dst = {"q": g.ws.hp.q, "f": g.ws.hp.f, "g": g.ws.live[h].gt}[key]
                            fnc = AF.Sigmoid if key == "f" else AF.Silu
                            S.op("act", lambda e, g=g, dst=dst, bz=bz, fnc=fnc: e.activation(out=dst.ap[:, 0:g.n], in_=psum[bz][:, 0:g.n], func=fnc),
                                 reads=[PT(bz)], writes=dst.tags)
                if STG >= 4:
                    for g in groups:
                        hgrn_prep(g, l, h)
                else:
                    for g in groups:
                        S.op("pool", lambda e, g=g, h=h: e.memset(g.ym[:, 4 + h, 0:g.n], 0.0), writes=[ytag(g, 4 + h)])
            if STG >= 4:
                for g in groups:
                    if g is gS:
                        S.dma("sp", wsS.shs.ap, shg_in[l].rearrange("b h d v -> d b h v"), writes=wsS.shs.tags)
                    hgrn_chain(g, l, j)
            if STG >= 4:
                if j == cfg.ntile - 1:
                    S.dma("pool", o_hp[l].rearrange("h d v -> d h v"), Sst[:, l, :, :], reads=[("Sst", l, h) for h in range(HG_H)], writes=[("o_hp", l)], is_output=True)
                if gS in groups:
                    S.dma("pool", o_hs[l].rearrange("b h d v -> d b h v"), wsS.shs.ap, reads=wsS.shs.tags, writes=[("o_hs", l)], is_output=True)
            for oc in range(NCH):
                u, t = WS.consume(("o", l, oc))
                for g in groups:
                    n = g.n
                    b = ps_next()
                    proj_fm(u, t, NCH, lambda k, g=g, n=n: g.ym[:, k, 0:n], [ytag(g, k) for k in range(NCH)], n, b)
                    S.op("dve", lambda e, g=g, n=n, b=b, oc=oc: e.tensor_tensor(out=g.x[:, oc, 0:n], in0=psum[b][:, 0:n], in1=g.x[:, oc, 0:n], op=ALU.add),
                         reads=[PT(b), xtag(g, oc)], writes=[xtag(g, oc)])

        load_tokens(gS, xs, [(0, NS, 0)])
        for j in range(cfg.ntile):
            load_tokens(gP, xp, [(j * TT + 128 * b, 128, 128 * b) for b in range(TT // 128)])
            groups = [gP, gS] if j == 0 else [gP]
            for l in range(L):
                for g in groups:
                    rmsnorm(g, "ln1", l * 8)
                ffn(groups, 0, l)
                if STG >= 2:
                    mix_layer(groups, l, j)
                if STG >= 9:
                    for g in groups:
                        rmsnorm(g, "ln2", l * 8)
                    ffn(groups, 1, l)
            final_norm_store(gP, yp, [(j * TT + 128 * b, 128, 128 * b) for b in range(TT // 128)])
            if j == 0:
                final_norm_store(gS, ys, [(0, NS, 0)])

        if cfg.debug:
            dbg = dout("dbg_ym", [128, NCH * TT], BF16)
            S.dma("pool", dbg, ymix[:].rearrange("p c t -> p (c t)"), reads=[ytag(gP, c) for c in range(NCH)], writes=[("dbg", 0)], is_output=True)
            dbgs = dout("dbg_yms", [128, NCH * NS], BF16)
            S.dma("pool", dbgs, ymixs[:].rearrange("p c t -> p (c t)"), reads=[ytag(gS, c) for c in range(NCH)], writes=[("dbg", 1)], is_output=True)
        S.finish()
        with nc.Block() as block:
            S.replay(block)
        cfg.n_inst = dict(S.n_inst)
    return nc


def pack_prm(depth, ln1, lnm, ln2, lnf, dww, dwb, clg, clb, hlb, hgn):
    off, prows = prm_layout(depth)
    out = np.zeros((prows, 128), np.float32)

    def put(name, arr):
        a = np.ascontiguousarray(arr, dtype=np.float32).reshape(-1, 128)
        out[off[name]:off[name] + a.shape[0]] = a
    put("ln1", ln1); put("lnm", lnm); put("ln2", ln2); put("lnf", lnf)
    put("dww", dww); put("dwb", dwb); put("clg", clg); put("clb", clb); put("hlb", hlb); put("hgn", hgn)
    return out


_CACHE = {}


def run(cfg, inputs):
    key = (cfg.seq, cfg.depth, cfg.nsamp, cfg.n_cores, cfg.nseq, cfg.stages, cfg.debug)
    if key not in _CACHE:
        _CACHE[key] = build_program(cfg)
    nc = _CACHE[key]
    L = cfg.depth
    f32 = lambda a: np.ascontiguousarray(a, dtype=np.float32)
    prm = pack_prm(L, inputs["ln_ffn1"], inputs["ln_mix"], inputs["ln_ffn2"], inputs["ln_final"],
                   inputs["conv_dw_w"], inputs["conv_dw_b"], inputs["conv_ln_g"], inputs["conv_ln_b"],
                   inputs["hg_lower_bounds"], inputs["hg_norm_g"])
    shared = {
        "prm": prm,
        "wg1": f32(inputs["w_ffn1_gate"]), "wu1": f32(inputs["w_ffn1_up"]), "wd1": f32(inputs["w_ffn1_down"]),
        "wg2": f32(inputs["w_ffn2_gate"]), "wu2": f32(inputs["w_ffn2_up"]), "wd2": f32(inputs["w_ffn2_down"]),
        "wi": f32(inputs["w_in"]), "wo": f32(inputs["w_out"]),
    }
    shared.update(const_tables(cfg.nsamp))
    xpf = f32(inputs["x_prompt"])
    xsf = f32(inputs["x_sample"])
    sconv = f32(inputs["state_conv"])
    ck = f32(inputs["cache_k_win"]).reshape(L, -1, WIN, 256)
    cv = f32(inputs["cache_v_win"]).reshape(L, -1, WIN, 256)
    shg = f32(inputs["state_hgrn"])
    in_maps = []
    zero_seq = None
    nb = cfg.nsamp
    for c in range(cfg.n_cores):
        m = dict(shared)
        if c < cfg.nseq:
            m["xp"] = xpf[c]
        else:
            if zero_seq is None:
                zero_seq = np.zeros((cfg.seq, D), np.float32)
            m["xp"] = zero_seq
        sl = slice(c * nb, (c + 1) * nb)
        m["xs"] = np.ascontiguousarray(xsf[sl].reshape(cfg.ns_tok, D))
        m["sconv"] = np.ascontiguousarray(sconv[:, sl])
        m["ck"] = np.ascontiguousarray(ck[:, sl])
        m["cv"] = np.ascontiguousarray(cv[:, sl])
        m["shg"] = np.ascontiguousarray(shg[:, sl])
        in_maps.append(m)
    res = run_bass_kernel_spmd(nc, in_maps, core_ids=list(range(cfg.n_cores)))
    return res.results


def assemble(cfg, r):
    L, nb = cfg.depth, cfg.nsamp
    nsq = cfg.nseq
    y_prompt = np.stack([r[c]["yp"] for c in range(nsq)])
    y_sample = np.concatenate([r[c]["ys"].reshape(nb, DEC_T, D) for c in range(cfg.n_cores)], 0)
    cp = np.stack([r[c]["o_cp"] for c in range(nsq)], 1)
    cs = np.concatenate([r[c]["o_cs"] for c in range(cfg.n_cores)], 1)
    kp = np.stack([r[c]["o_kp"].reshape(L, cfg.keep, ATT_H, HD) for c in range(nsq)], 1)
    vp = np.stack([r[c]["o_vp"].reshape(L, cfg.keep, ATT_H, HD) for c in range(nsq)], 1)
    ks = np.concatenate([r[c]["o_ks"].reshape(L, nb, DEC_T, ATT_H, HD) for c in range(cfg.n_cores)], 1)
    vs = np.concatenate([r[c]["o_vs"].reshape(L, nb, DEC_T, ATT_H, HD) for c in range(cfg.n_cores)], 1)
    hp = np.stack([r[c]["o_hp"] for c in range(nsq)], 1)
    hs = np.concatenate([r[c]["o_hs"] for c in range(cfg.n_cores)], 1)
    outs = (y_prompt, y_sample, cp, cs, kp, vp, ks, vs, hp, hs)
    return tuple(np.ascontiguousarray(o, dtype=np.float32) for o in outs)


def kernel(**inputs):
    cfg = Cfg()
    r = run(cfg, inputs)
    return assemble(cfg, r)
```

```python
import contextlib
import numpy as np
import concourse.bass as bass
import concourse.mybir as mybir
from concourse.bass_utils import run_bass_kernel_spmd

F32 = mybir.dt.float32
BF16 = mybir.dt.bfloat16
AF = mybir.ActivationFunctionType
ALU = mybir.AluOpType

D = 1024
NCH = 8
FFN = 2816
NFC = 22
N_IN = 3328
NIC = 26
CONV_DIM = 256
CW = 31
ATT_H = 4
HD = 64
HG_H = 4
HGC = 64
WIN = 2048
TT = 512
EPS = 1e-6
DEC_T = 4


class Sched:
    COMPUTE = ("pe", "act", "dve", "pool")

    def __init__(self, nc, stack, n_dma_sems=24):
        self.nc = nc
        self.streams = {e: [] for e in ("pe", "act", "dve", "pool", "sp")}
        self.sems = {}
        for e in self.COMPUTE:
            self.sems[e] = stack.enter_context(nc.semaphore("s_" + e))
        self.count = {e: 0 for e in self.COMPUTE}
        self.dma_sems = {}
        self.dma_val = {}
        self.dma_rr = {}
        for q in ("sp", "pool"):
            self.dma_sems[q] = []
            for i in range(n_dma_sems):
                key = "d_%s_%d" % (q, i)
                self.sems[key] = stack.enter_context(nc.semaphore(key))
                self.dma_sems[q].append(key)
                self.dma_val[key] = 0
            self.dma_rr[q] = 0
        self.waited = {}
        self.last_write = {}
        self.readers = {}
        self.out_tokens = []
        self.n_inst = {e: 0 for e in self.streams}

    def _need(self, eng, token, needs):
        if token is None:
            return
        key, val = token
        if self.waited.get((eng, key), 0) >= val:
            return
        if needs.get(key, 0) < val:
            needs[key] = val

    def _collect(self, eng, reads, writes):
        needs = {}
        for t in reads:
            self._need(eng, self.last_write.get(t), needs)
        for t in writes:
            self._need(eng, self.last_write.get(t), needs)
            for tok in self.readers.get(t, ()):
                self._need(eng, tok, needs)
        return needs

    def _emit_waits(self, eng, needs, is_dma=False):
        for key, val in needs.items():
            if key == eng and not is_dma:
                continue
            sem = self.sems[key]
            self.streams[eng].append(lambda e, sem=sem, val=val: e.wait_ge(sem, val))
            self.waited[(eng, key)] = val
            self.n_inst[eng] += 1

    def _commit(self, token, reads, writes):
        for t in reads:
            self.readers.setdefault(t, []).append(token)
        for t in writes:
            self.last_write[t] = token
            self.readers[t] = []

    def op(self, eng, fn, reads=(), writes=()):
        needs = self._collect(eng, reads, writes)
        if eng in needs:
            own = needs.pop(eng)
            val = own if eng != "pe" else 0
            if val > self.waited.get((eng, eng), 0):
                sem = self.sems[eng]
                self.streams[eng].append(lambda e, sem=sem, val=val: e.wait_ge(sem, val))
                self.waited[(eng, eng)] = val
        self._emit_waits(eng, needs)
        self.count[eng] += 1
        token = (eng, self.count[eng])
        sem = self.sems[eng]
        self.streams[eng].append(lambda e, fn=fn, sem=sem: fn(e).then_inc(sem, 1))
        self.n_inst[eng] += 1
        self._commit(token, reads, writes)
        return token

    def dma(self, q, out, in_, reads=(), writes=(), is_output=False):
        needs = self._collect(q, reads, writes)
        rr = self.dma_rr[q]
        self.dma_rr[q] = (rr + 1) % len(self.dma_sems[q])
        key = self.dma_sems[q][rr]
        prev = self.dma_val[key]
        if prev > 0 and self.waited.get((q, key), 0) < prev:
            needs[key] = max(needs.get(key, 0), prev)
        self._emit_waits(q, needs, is_dma=True)
        val = prev + 16
        self.dma_val[key] = val
        sem = self.sems[key]
        self.streams[q].append(lambda e, out=out, in_=in_, sem=sem: e.dma_start(out=out, in_=in_).then_inc(sem, 16))
        self.n_inst[q] += 1
        token = (key, val)
        self._commit(token, reads, writes)
        if is_output:
            self.out_tokens.append(token)
        return token

    def barrier(self):
        for eng in self.streams:
            for x in self.COMPUTE:
                if x != eng and self.count[x] > 0:
                    sem, val = self.sems[x], self.count[x]
                    self.streams[eng].append(lambda e, sem=sem, val=val: e.wait_ge(sem, val))
                    self.waited[(eng, x)] = val
            for key, val in self.dma_val.items():
                if val > 0:
                    sem = self.sems[key]
                    self.streams[eng].append(lambda e, sem=sem, val=val: e.wait_ge(sem, val))
                    self.waited[(eng, key)] = val
        self.last_write = {}
        self.readers = {}

    def finish(self):
        finals = {}
        for key, val in self.out_tokens:
            finals[key] = max(finals.get(key, 0), val)
        for key, val in finals.items():
            sem = self.sems[key]
            self.streams["sp"].append(lambda e, sem=sem, val=val: e.wait_ge(sem, val))

    def replay(self, block):
        nc = self.nc
        streams = self.streams

        @block.sync
        def _(e):
            for f in streams["sp"]:
                f(e)

        @block.tensor
        def _(e):
            for f in streams["pe"]:
                f(e)

        @block.scalar
        def _(e):
            for f in streams["act"]:
                f(e)

        @block.vector
        def _(e):
            for f in streams["dve"]:
                f(e)

        @block.gpsimd
        def _(e):
            for f in streams["pool"]:
                f(e)


class Cfg:
    def __init__(self, seq=8192, depth=4, nsamp=4, n_cores=8, nseq=2, stages=99):
        self.seq = seq
        self.depth = depth
        self.nsamp = nsamp
        self.n_cores = n_cores
        self.nseq = nseq
        self.stages = stages
        self.ntile = seq // TT
        self.ns_tok = nsamp * DEC_T
        self.keep = min(WIN, seq)
        self.debug = False


def prm_layout(depth):
    off = {}
    r = 0
    for name, rows in (("ln1", depth * 8), ("lnm", depth * 8), ("ln2", depth * 8), ("lnf", 8),
                       ("dww", depth * CW * 2), ("dwb", depth * 2), ("clg", depth * 2), ("clb", depth * 2),
                       ("hlb", depth * 4), ("hgn", depth)):
        off[name] = r
        r += rows
    return off, ((r + 127) // 128) * 128


NM = 17


def att_weight_table():
    k = np.arange(128)[:, None, None]
    m = np.arange(NM)[None, :, None]
    q = np.arange(128)[None, None, :]
    d = 128 * m + q - k
    mult = ((d >= 0) & (d <= 128)).astype(np.float64) + ((d >= 0) & (d <= 512) & (d % 4 == 0)) + ((d >= 0) & (d <= 2048) & (d % 16 == 0))
    out = np.zeros((128, NM, ATT_H, 128), np.float32)
    for h in range(ATT_H):
        slope = 2.0 ** (-8.0 * (h + 1) / ATT_H)
        out[:, :, h, :] = mult * np.exp(-slope * np.maximum(d, 0))
    return out


def const_tables(nsamp):
    c = {}
    c["ident"] = np.eye(128, dtype=np.float32)
    c["wtab"] = att_weight_table().reshape(128, NM * ATT_H * 128)
    p = np.arange(128)
    bd = ((p[:, None] // HGC) == (p[None, :] // HGC)) & (p[:, None] <= p[None, :])
    c["bdmask"] = bd.astype(np.float32)
    bds = np.zeros((128, 128), np.float32)
    ns = nsamp * DEC_T
    ps = np.arange(ns)
    bds[:ns, :ns] = (((ps[:, None] // DEC_T) == (ps[None, :] // DEC_T)) & (ps[:, None] <= ps[None, :]))
    c["bdmask_s"] = bds
    rm = np.zeros((128, 8), np.float32)
    for j in range(4):
        rm[:, j] = (p // HGC == j)
        rm[:ns, 4 + j] = (ps // DEC_T == j)
    c["rowmask"] = rm
    rs = np.ones((128, TT), np.float32)
    rs[:, ::HGC] = 0.0
    c["rsmask"] = rs
    rss = np.ones((128, 128), np.float32)
    rss[:, 0:ns:DEC_T] = 0.0
    c["rsmask_s"] = rss
    wsm = np.zeros((128, nsamp, ATT_H, DEC_T), np.float32)
    for kk in range(ns):
        kb, kt = divmod(kk, DEC_T)
        for t in range(DEC_T):
            if t >= kt:
                d = t - kt
                mult = 1 + (d % 4 == 0) + (d % 16 == 0)
                for h in range(ATT_H):
                    slope = 2.0 ** (-8.0 * (h + 1) / ATT_H)
                    wsm[kk, kb, h, t] = mult * np.exp(-slope * d)
    c["wsm"] = wsm.reshape(128, nsamp * ATT_H * DEC_T)
    return c


def build_program(cfg):
    nc = bass.Bass("TRN2", target_bir_lowering=False)
    L = cfg.depth
    SEQ = cfg.seq
    NS = cfg.ns_tok
    NB = cfg.nsamp
    KEEP = cfg.keep
    poff, prows = prm_layout(L)
    STG = cfg.stages

    def din(name, shape, dt=F32):
        return nc.dram_tensor(name, list(shape), dt, kind="ExternalInput").ap()

    def dout(name, shape, dt=F32):
        return nc.dram_tensor(name, list(shape), dt, kind="ExternalOutput").ap()

    def dscr(name, shape, dt=BF16):
        return nc.dram_tensor(name, list(shape), dt, kind="Internal").ap()

    xp = din("xp", [SEQ, D])
    xs = din("xs", [NS, D])
    prm = din("prm", [prows, 128])
    ident_in = din("ident", [128, 128])
    wtab_in = din("wtab", [128, NM * ATT_H * 128])
    bdmask_in = din("bdmask", [128, 128])
    bdmask_s_in = din("bdmask_s", [128, 128])
    rowmask_in = din("rowmask", [128, 8])
    rsmask_in = din("rsmask", [128, TT])
    rsmask_s_in = din("rsmask_s", [128, 128])
    wsm_in = din("wsm", [128, NB * ATT_H * DEC_T])
    sconv_in = din("sconv", [L, NB, CW - 1, CONV_DIM])
    ck_in = din("ck", [L, NB, WIN, 256])
    cv_in = din("cv", [L, NB, WIN, 256])
    shg_in = din("shg", [L, NB, HG_H, 128, 128])
    w_g = [din("wg1", [L, D, FFN]), din("wg2", [L, D, FFN])]
    w_u = [din("wu1", [L, D, FFN]), din("wu2", [L, D, FFN])]
    w_d = [din("wd1", [L, FFN, D]), din("wd2", [L, FFN, D])]
    w_i = din("wi", [L, D, N_IN])
    w_o = din("wo", [L, D, D])

    yp = dout("yp", [SEQ, D])
    ys = dout("ys", [NS, D])
    o_cp = dout("o_cp", [L, CW - 1, CONV_DIM])
    o_cs = dout("o_cs", [L, NB, CW - 1, CONV_DIM])
    o_kp = dout("o_kp", [L, KEEP, 256])
    o_vp = dout("o_vp", [L, KEEP, 256])
    o_ks = dout("o_ks", [L, NS, 256])
    o_vs = dout("o_vs", [L, NS, 256])
    o_hp = dout("o_hp", [L, HG_H, 128, 128])
    o_hs = dout("o_hs", [L, NB, HG_H, 128, 128])

    sG = [dscr("sg%d" % f, [L, NFC, 128, NCH * 128]) for f in range(2)]
    sU = [dscr("su%d" % f, [L, NFC, 128, NCH * 128]) for f in range(2)]
    sD = [dscr("sd%d" % f, [L, NCH, 128, NFC * 128]) for f in range(2)]
    sI = dscr("si", [L, NIC, 128, NCH * 128])
    sO = dscr("so", [L, NCH, 128, NCH * 128])
    sK = dscr("sk", [L, 256, SEQ])
    sV = dscr("sv", [L, SEQ, ATT_H * 128])

    stack = contextlib.ExitStack()
    with stack:
        S = Sched(nc, stack)

        def sb(name, shape, dt=F32):
            return stack.enter_context(nc.sbuf_tensor(name, list(shape), dt))

        class B:
            def __init__(self, ap, tags):
                self.ap, self.tags = ap, list(tags)

        NSTG = 8
        with contextlib.ExitStack() as pstack:
            stg_f = [pstack.enter_context(nc.sbuf_tensor("stgf%d" % i, [128, NFC * 128], F32)) for i in range(NSTG)]
            stg_b = [pstack.enter_context(nc.sbuf_tensor("stgb%d" % i, [128, NFC * 128], BF16)) for i in range(NSTG)]
            cast_rr = [0]

            def convert(src2d, K, c, dst_unit, dtag):
                kc = K // 128
                i = cast_rr[0]
                cast_rr[0] += 1
                s = i % NSTG
                srcv = src2d[:, c * 128:(c + 1) * 128].rearrange("(kc p) j -> p kc j", p=128)
                S.dma("sp", stg_f[s][:, 0:kc * 128].rearrange("p (kc j) -> p kc j", j=128), srcv, writes=[("stgf", s)])
                eng = ("dve", "act", "pool")[i % 3]
                if eng == "act":
                    fn = lambda e, s=s, kc=kc: e.activation(out=stg_b[s][:, 0:kc * 128], in_=stg_f[s][:, 0:kc * 128], func=AF.Copy)
                else:
                    fn = lambda e, s=s, kc=kc: e.tensor_copy(out=stg_b[s][:, 0:kc * 128], in_=stg_f[s][:, 0:kc * 128])
                S.op(eng, fn, reads=[("stgf", s)], writes=[("stgb", s)])
                S.dma("pool", dst_unit, stg_b[s][:, 0:kc * 128], reads=[("stgb", s)], writes=[dtag])

            for l in range(L):
                for f in range(2):
                    if f == 1 and STG < 9:
                        continue
                    for c in range(NFC):
                        convert(w_g[f][l], D, c, sG[f][l, c], ("sG", f, l, c))
                        convert(w_u[f][l], D, c, sU[f][l, c], ("sU", f, l, c))
                    for c in range(NCH):
                        convert(w_d[f][l], FFN, c, sD[f][l, c], ("sD", f, l, c))
                if STG >= 2:
                    for c in range(NIC):
                        convert(w_i[l], D, c, sI[l, c], ("sI", l, c))
                    for c in range(NCH):
                        convert(w_o[l], D, c, sO[l, c], ("sO", l, c))
            S.barrier()

        ARK = 52
        arena = sb("arena", [128, ARK * 512], BF16)

        def av(name, off_kb, kb, dt=BF16, pat=None, **dims):
            e0 = int(round(off_kb * 512))
            ne = int(round(kb * 512))
            ap = arena[:, e0:e0 + ne]
            if dt == F32:
                ap = ap.bitcast(F32)
            if pat is not None:
                ap = ap.rearrange(pat, **dims)
            k0 = int(np.floor(off_kb + 1e-9))
            k1 = int(np.ceil(off_kb + kb - 1e-9))
            return B(ap, [("ar", k) for k in range(k0, k1)])

        xT = sb("xT", [128, NCH, TT])
        hT = sb("hT", [128, NCH, TT], BF16)
        ymix = sb("ymix", [128, NCH, TT], BF16)
        xTs = sb("xTs", [128, NCH, NS])
        hTs = sb("hTs", [128, NCH, 128], BF16)
        hids = sb("hids", [128, NFC, NS], BF16)
        ymixs = sb("ymixs", [128, NCH, NS], BF16)
        rstd = sb("rstd", [128, TT])
        sgb = [sb("sgb%d" % i, [128, TT]) for i in range(2)]
        prmT = sb("prmT", [128, prows])
        ident = sb("identf", [128, 128])
        identb = sb("identb", [128, 128], BF16)
        ones_b = sb("ones_b", [128, 128], BF16)
        eps_col = sb("eps_col", [128, 1])
        NB8, NB22 = 6, 2
        wb8 = [sb("wb8_%d" % i, [128, NCH * 128], BF16) for i in range(NB8)]
        wb22 = [sb("wb22_%d" % i, [128, NFC * 128], BF16) for i in range(NB22)]
        kwin = sb("kwin", [128, 2, WIN + TT], BF16)
        vwin = sb("vwin", [128, (WIN + TT) // 128, ATT_H, 128], BF16)
        wmask = sb("wmask", [128, NM, ATT_H, 128], BF16)
        Sst = sb("Sst", [128, L, HG_H, 128])
        utail = sb("utail", [128, L, 2, CW - 1], BF16)
        ubuf = sb("ubuf", [128, 2, CW - 1 + TT], BF16)
        ubufs = sb("ubufs", [128, 2, NB, CW - 1 + DEC_T], BF16)
        bdmask = sb("bdmask_t", [128, 128], BF16)
        bdmask_s = sb("bdmasks_t", [128, 128], BF16)
        rowmask = sb("rowmask_t", [128, 8])
        rsmask = sb("rsmask_t", [128, TT])
        rsmask_s = sb("rsmasks_t", [128, 128])
        wsm = sb("wsm_t", [128, NB, ATT_H, DEC_T], BF16)
        lbT = sb("lbT", [128, L * HG_H])
        omlT = sb("omlT", [128, L * HG_H])
        lbtmp = sb("lbtmp", [128, 2 * L * HG_H + 2 * HG_H])

        hidc = [av("hid", c, 1.0) for c in range(NFC)]
        finc = [av("fin", 2 * c, 2.0, F32) for c in range(NCH)]
        sqc = [av("sq", 36 + c, 1.0) for c in range(NCH)]
        iobuf = [av("io", 44 + 4 * i, 4.0, F32) for i in range(2)]

        psum = [stack.enter_context(nc.psum_tensor("ps%d" % i, [128, 512], F32)) for i in range(8)]
        ps_rr = [0]
        NPR = 6

        NPRv = [NPR]

        def ps_next():
            b = ps_rr[0] % NPRv[0]
            ps_rr[0] = (b + 1) % NPRv[0]
            return b

        def PT(b):
            return ("ps", b)

        S.dma("sp", ident[:], ident_in[:], writes=[("ident",)])
        S.op("dve", lambda e: e.tensor_copy(out=identb[:], in_=ident[:]), reads=[("ident",)], writes=[("identb",)])
        S.op("pool", lambda e: e.memset(ones_b[:], 1.0), writes=[("ones",)])
        S.op("pool", lambda e: e.memset(eps_col[:], EPS), writes=[("eps",)])
        S.op("pool", lambda e: e.memset(hTs[:], 0.0), writes=[("S", "h", c) for c in range(NCH)])
        S.op("pool", lambda e: e.memset(vwin[:], 1.0), writes=[("vwin", "hist"), ("vwin", "cur")])
        S.op("pool", lambda e: e.memset(Sst[:], 0.0), writes=[("Sst", l, h) for l in range(L) for h in range(HG_H)])
        S.op("pool", lambda e: e.memset(utail[:], 0.0), writes=[("utail", l) for l in range(L)])
        S.dma("sp", rowmask[:], rowmask_in[:], writes=[("rowmask",)])
        S.dma("sp", rsmask[:], rsmask_in[:], writes=[("rsmask",)])
        S.dma("sp", rsmask_s[:], rsmask_s_in[:], writes=[("rsmask_s",)])
        for blk in range(prows // 128):
            io = iobuf[blk % 2]
            S.dma("sp", io.ap[:, 0:128], prm[blk * 128:(blk + 1) * 128, :], writes=io.tags)
            b = ps_next()
            S.op("pe", lambda e, io=io, b=b: e.transpose(psum[b][:, 0:128], io.ap[:, 0:128], ident[:]),
                 reads=io.tags + [("ident",)], writes=[PT(b)])
            S.op("act", lambda e, b=b, blk=blk: e.activation(out=prmT[:, blk * 128:(blk + 1) * 128], in_=psum[b][:, 0:128], func=AF.Copy),
                 reads=[PT(b)], writes=[("prmT",)])
        for src, dst, tag in ((bdmask_in, bdmask, "bdmask"), (bdmask_s_in, bdmask_s, "bdmask_s")):
            io = iobuf[0]
            S.dma("sp", io.ap[:, 0:128], src[:], writes=io.tags)
            S.op("dve", lambda e, io=io, dst=dst: e.tensor_copy(out=dst[:], in_=io.ap[:, 0:128]), reads=io.tags, writes=[(tag,)])
        io = iobuf[1]
        nws = NB * ATT_H * DEC_T
        S.dma("sp", io.ap[:, 0:nws], wsm_in[:], writes=io.tags)
        S.op("dve", lambda e, io=io: e.tensor_copy(out=wsm[:].rearrange("p b h t -> p (b h t)"), in_=io.ap[:, 0:nws]), reads=io.tags, writes=[("wsm",)])
        if STG >= 3:
            wflat = wmask[:].rearrange("p m h q -> p (m h q)")
            tot = NM * ATT_H * 128
            for i, c0 in enumerate(range(0, tot, 1024)):
                io = iobuf[i % 2]
                n = min(1024, tot - c0)
                S.dma("sp", io.ap[:, 0:n], wtab_in[:, c0:c0 + n], writes=io.tags)
                eng = "dve" if i % 2 == 0 else "pool"
                S.op(eng, lambda e, io=io, c0=c0, n=n: e.tensor_copy(out=wflat[:, c0:c0 + n], in_=io.ap[:, 0:n]), reads=io.tags, writes=[("wmask",)])

        def pcol(name, idx):
            c = poff[name] + idx
            return prmT[:, c:c + 1]

        if STG >= 4:
            nlh = L * HG_H
            ex = lbtmp[:, 0:nlh]
            sm = lbtmp[:, nlh:2 * nlh]
            tot = lbtmp[:, 2 * nlh:2 * nlh + HG_H]
            rc = lbtmp[:, 2 * nlh + HG_H:2 * nlh + 2 * HG_H]
            r0 = poff["hlb"]
            S.op("act", lambda e: e.activation(out=ex, in_=prmT[:, r0:r0 + nlh], func=AF.Exp), reads=[("prmT",)], writes=[("lbtmp",)])
            S.op("dve", lambda e: e.tensor_copy(out=tot, in_=ex[:, 0:HG_H]), reads=[("lbtmp",)], writes=[("lbtmp",)])
            for l in range(1, L):
                S.op("dve", lambda e, l=l: e.tensor_tensor(out=tot, in0=tot, in1=ex[:, l * HG_H:(l + 1) * HG_H], op=ALU.add),
                     reads=[("lbtmp",)], writes=[("lbtmp",)])
            S.op("dve", lambda e: e.reciprocal(out=rc, in_=tot), reads=[("lbtmp",)], writes=[("lbtmp",)])
            for l in range(L):
                S.op("dve", lambda e, l=l: e.tensor_tensor(out=sm[:, l * HG_H:(l + 1) * HG_H], in0=ex[:, l * HG_H:(l + 1) * HG_H], in1=rc, op=ALU.mult),
                     reads=[("lbtmp",)], writes=[("lbtmp",)])
            S.op("dve", lambda e: e.memset(lbT[:, 0:HG_H], 0.0), writes=[("lbT",)])
            for l in range(1, L):
                S.op("dve", lambda e, l=l: e.tensor_tensor(out=lbT[:, l * HG_H:(l + 1) * HG_H], in0=lbT[:, (l - 1) * HG_H:l * HG_H],
                                                           in1=sm[:, l * HG_H:(l + 1) * HG_H], op=ALU.add),
                     reads=[("lbtmp",), ("lbT",)], writes=[("lbT",)])
            S.op("dve", lambda e: e.tensor_scalar(out=omlT[:], in0=lbT[:], scalar1=-1.0, scalar2=1.0, op0=ALU.mult, op1=ALU.add),
                 reads=[("lbT",)], writes=[("omlT",)])

        class WStream:
            def __init__(self):
                self.plan = []
                self.nload = 0
                self.ncons = 0
                self.cls_idx = {8: 0, 22: 0}
                self.slot_of = []
                self.prev_user = []
                self.slot_last = {}

            def add(self, uid, ap, cls, dtag):
                k = self.cls_idx[cls]
                self.cls_idx[cls] += 1
                nb = NB8 if cls == 8 else NB22
                slot = (cls, k % nb)
                self.prev_user.append(self.slot_last.get(slot, -1))
                self.slot_last[slot] = len(self.plan)
                self.slot_of.append(slot)
                self.plan.append((uid, ap, cls, dtag))

            def _buf(self, slot):
                cls, i = slot
                return (wb8 if cls == 8 else wb22)[i]

            def consume(self, uid):
                i = self.ncons
                assert self.plan[i][0] == uid, (self.plan[i][0], uid)
                while self.nload < len(self.plan) and self.nload <= i + 5 and (self.prev_user[self.nload] < 0 or self.prev_user[self.nload] <= i - 2):
                    j = self.nload
                    _, ap, cls, dtag = self.plan[j]
                    slot = self.slot_of[j]
                    S.dma("sp", self._buf(slot)[:], ap, reads=[dtag], writes=[("wb",) + slot])
                    self.nload += 1
                assert self.nload > i, (self.nload, i)
                self.ncons += 1
                slot = self.slot_of[i]
                return self._buf(slot), ("wb",) + slot

        WS = WStream()
        FM_CONV = [0, 1, 2, 3]
        FM_QKV = [4, 5, 6, 7, 8, 9]
        HG_ORDER = []
        for _h in range(HG_H):
            HG_ORDER += [18 + _h, 10 + _h, 14 + _h, 22 + _h]

        def plan_ffn(f, l):
            for c in range(NFC):
                WS.add(("g", f, l, c), sG[f][l, c], 8, ("sG", f, l, c))
                WS.add(("u", f, l, c), sU[f][l, c], 8, ("sU", f, l, c))
            for c in range(NCH):
                WS.add(("d", f, l, c), sD[f][l, c], 22, ("sD", f, l, c))

        def plan_layer(l):
            plan_ffn(0, l)
            if STG >= 2:
                for c in FM_CONV + FM_QKV + HG_ORDER:
                    WS.add(("i", l, c), sI[l, c], 8, ("sI", l, c))
                for c in range(NCH):
                    WS.add(("o", l, c), sO[l, c], 8, ("sO", l, c))
            if STG >= 9:
                plan_ffn(1, l)

        for j in range(cfg.ntile):
            for l in range(L):
                plan_layer(l)

        class Grp:
            pass

        gP = Grp()
        gP.name, gP.n, gP.x, gP.h, gP.ym = "P", TT, xT, hT, ymix
        gP.hd = [hc.ap for hc in hidc]
        gP.hdt = [hc.tags for hc in hidc]
        gS = Grp()
        gS.name, gS.n, gS.x, gS.h, gS.ym = "S", NS, xTs, hTs, ymixs
        gS.hd = [hids[:, c, :] for c in range(NFC)]
        gS.hdt = [[("S", "hd", c)] for c in range(NFC)]

        def xtag(g, c):
            return (g.name, "x", c)

        def htag(g, c):
            return (g.name, "h", c)

        def ytag(g, c):
            return (g.name, "ym", c)

        def sumsq_rstd(g, srcs, src_tags, nchunks, denom):
            n = g.n
            for c in range(nchunks):
                S.op("act", lambda e, c=c: e.activation(out=sqc[c].ap[:, 0:n], in_=srcs[c], func=AF.Square),
                     reads=src_tags[c], writes=sqc[c].tags)
            b = ps_next()
            for c in range(nchunks):
                S.op("pe", lambda e, c=c: e.matmul(psum[b][:, 0:n], lhsT=ones_b[:], rhs=sqc[c].ap[:, 0:n], start=(c == 0), stop=(c == nchunks - 1)),
                     reads=sqc[c].tags + [("ones",)], writes=[PT(b)])
            S.op("act", lambda e: e.activation(out=rstd[:, 0:n], in_=psum[b][:, 0:n], func=AF.Ln, scale=1.0 / denom, bias=eps_col[:]),
                 reads=[PT(b), ("eps",)], writes=[("rstd",)])
            S.op("act", lambda e: e.activation(out=rstd[:, 0:n], in_=rstd[:, 0:n], func=AF.Exp, scale=-0.5),
                 reads=[("rstd",)], writes=[("rstd",)])

        def rmsnorm(g, gain_name, gain_idx0):
            n = g.n
            sumsq_rstd(g, [g.x[:, c, 0:n] for c in range(NCH)], [[xtag(g, c)] for c in range(NCH)], NCH, D)
            for c in range(NCH):
                S.op("dve", lambda e, c=c: e.scalar_tensor_tensor(out=g.h[:, c, 0:n], in0=g.x[:, c, 0:n],
                                                                  scalar=pcol(gain_name, gain_idx0 + c), in1=rstd[:, 0:n],
                                                                  op0=ALU.mult, op1=ALU.mult),
                     reads=[xtag(g, c), ("rstd",), ("prmT",)], writes=[htag(g, c)])

        def proj_fm(unit, utag, kch, rhs_fn, rhs_tags, n, b):
            def mm(e):
                ins = None
                for k in range(kch):
                    ins = e.matmul(psum[b][:, 0:n], lhsT=unit[:, k * 128:(k + 1) * 128], rhs=rhs_fn(k),
                                   start=(k == 0), stop=(k == kch - 1))
                return ins
            S.op("pe", mm, reads=[utag] + rhs_tags, writes=[PT(b)])

        def ffn(groups, f, l):
            sg_i = 0
            for c in range(NFC):
                ug, tg = WS.consume(("g", f, l, c))
                uu, tu = WS.consume(("u", f, l, c))
                for g in groups:
                    n = g.n
                    bg, bu = ps_next(), ps_next()
                    htags = [htag(g, k) for k in range(NCH)]
                    proj_fm(ug, tg, NCH, lambda k, g=g, n=n: g.h[:, k, 0:n], htags, n, bg)
                    proj_fm(uu, tu, NCH, lambda k, g=g, n=n: g.h[:, k, 0:n], htags, n, bu)
                    si = sg_i % 2
                    sg_i += 1
                    S.op("act", lambda e, si=si, bg=bg, n=n: e.activation(out=sgb[si][:, 0:n], in_=psum[bg][:, 0:n], func=AF.Silu),
                         reads=[PT(bg)], writes=[("sgb", si)])
                    S.op("dve", lambda e, si=si, bu=bu, n=n, g=g, c=c: e.tensor_tensor(out=g.hd[c][:, 0:n], in0=sgb[si][:, 0:n], in1=psum[bu][:, 0:n], op=ALU.mult),
                         reads=[("sgb", si), PT(bu)], writes=g.hdt[c])
            for oc in range(NCH):
                ud, td = WS.consume(("d", f, l, oc))
                for g in groups:
                    n = g.n
                    b = ps_next()
                    proj_fm(ud, td, NFC, lambda k, g=g, n=n: g.hd[k][:, 0:n], sum([g.hdt[k] for k in range(NFC)], []), n, b)
                    S.op("dve", lambda e, g=g, n=n, b=b, oc=oc: e.scalar_tensor_tensor(out=g.x[:, oc, 0:n], in0=psum[b][:, 0:n], scalar=0.5,
                                                                                      in1=g.x[:, oc, 0:n], op0=ALU.mult, op1=ALU.add),
                         reads=[PT(b), xtag(g, oc)], writes=[xtag(g, oc)])

        def load_tokens(g, src_rows, blocks):
            for bi, (r0, nr, c0) in enumerate(blocks):
                io = iobuf[bi % 2]
                S.dma("sp", io.ap[0:nr, :], src_rows[r0:r0 + nr, :], writes=io.tags)
                for c in range(NCH):
                    b = ps_next()
                    S.op("pe", lambda e, io=io, nr=nr, c=c, b=b: e.transpose(psum[b][:, 0:nr], io.ap[0:nr, c * 128:(c + 1) * 128], ident[0:nr, 0:nr]),
                         reads=io.tags + [("ident",)], writes=[PT(b)])
                    if c % 2 == 0:
                        S.op("act", lambda e, b=b, c=c, nr=nr, c0=c0: e.activation(out=g.x[:, c, c0:c0 + nr], in_=psum[b][:, 0:nr], func=AF.Copy),
                             reads=[PT(b)], writes=[xtag(g, c)])
                    else:
                        S.op("dve", lambda e, b=b, c=c, nr=nr, c0=c0: e.tensor_copy(out=g.x[:, c, c0:c0 + nr], in_=psum[b][:, 0:nr]),
                             reads=[PT(b)], writes=[xtag(g, c)])

        def final_norm_store(g, dst_rows, blocks):
            n = g.n
            sumsq_rstd(g, [g.x[:, c, 0:n] for c in range(NCH)], [[xtag(g, c)] for c in range(NCH)], NCH, D)
            for c in range(NCH):
                S.op("dve", lambda e, c=c: e.scalar_tensor_tensor(out=finc[c].ap[:, 0:n], in0=g.x[:, c, 0:n],
                                                                  scalar=pcol("lnf", c), in1=rstd[:, 0:n],
                                                                  op0=ALU.mult, op1=ALU.mult),
                     reads=[xtag(g, c), ("rstd",), ("prmT",)], writes=finc[c].tags)
            for bi, (r0, nr, c0) in enumerate(blocks):
                io = iobuf[bi % 2]
                for half in range(2):
                    b = ps_next()

                    def tr(e, half=half, b=b, nr=nr, c0=c0):
                        ins = None
                        for cc in range(4):
                            ins = e.transpose(psum[b][0:nr, cc * 128:(cc + 1) * 128], finc[half * 4 + cc].ap[:, c0:c0 + nr], ident[:])
                        return ins
                    S.op("pe", tr, reads=sum([finc[half * 4 + cc].tags for cc in range(4)], []) + [("ident",)], writes=[PT(b)])
                    if half == 0:
                        S.op("act", lambda e, b=b, nr=nr, io=io: e.activation(out=io.ap[0:nr, 0:512], in_=psum[b][0:nr, :], func=AF.Copy),
                             reads=[PT(b)], writes=io.tags)
                    else:
                        S.op("dve", lambda e, b=b, nr=nr, io=io: e.tensor_copy(out=io.ap[0:nr, 512:1024], in_=psum[b][0:nr, :]),
                             reads=[PT(b)], writes=io.tags)
                S.dma("pool", dst_rows[r0:r0 + nr, :], io.ap[0:nr, :], reads=io.tags, writes=[("out", g.name, r0)], is_output=True)

        def mk_ws(g):
            w = Grp()
            n = g.n
            if g is gP:
                w.cva = av("cva", 0, 4, F32, "p (c t) -> p c t", c=2)
                w.cvs = av("cvs", 4, 4, F32, "p (c t) -> p c t", c=2)
                w.cvy = av("cvy", 8, 4, F32, "p (c t) -> p c t", c=2)
                w.diag = av("diag", 14, 16, BF16, "p (w j) -> p w j", j=128)
                w.qz = av("qz", 30, 4, BF16, "p (h t) -> p h t", h=4)
                w.Eb = [av("Eb%d" % i, 34 + i, 1) for i in range(4)]
                w.Pb = [av("Pb%d" % i, 38 + i, 1) for i in range(4)]
                w.lnd = [av("lnd%d" % i, 42 + 0.5 * i, 0.5, F32) for i in range(2)]
                hp = Grp()
                for k, nm in enumerate(("q", "f", "lg", "kk", "eb", "en")):
                    setattr(hp, nm, av("h" + nm, 2 * k, 2, F32))
                w.hp = hp
                w.live = []
                for h in range(HG_H):
                    lv = Grp()
                    base = 12 + 5 * h
                    lv.qt = av("hqt%d" % h, base, 1)
                    lv.kt = av("hkt%d" % h, base + 1, 1)
                    lv.kh = av("hkh%d" % h, base + 2, 2, F32)
                    lv.gt = av("hgt%d" % h, base + 4, 1)
                    ebl = sb("ebl%d" % h, [128, n // HGC])
                    lv.ebl = B(ebl[:], [("ebl", h)])
                    emid = sb("emid%d" % h, [128, n // HGC])
                    lv.emid = B(emid[:], [("emid", h)])
                    w.live.append(lv)
                w.vtok = av("vtok", 32, 4, BF16, "p (b c) -> p b c", b=4)
                w.khm = [av("khm%d" % i, 36 + i, 1, BF16, "p (j d) -> p j d", j=4) for i in range(4)]
                w.Am = [av("Am%d" % i, 40 + 0.25 * i, 0.25) for i in range(4)]
                w.Sb = [[av("Sb%d_%d" % (h, i), 41 + 0.25 * (h * 5 + i), 0.25) for i in range(5)] for h in range(HG_H)]
                w.osq = [av("osq%d" % h, 3 * h + 2, 1) for h in range(HG_H)]
                w.ot = [av("ot%d" % h, 3 * h, 2, F32) for h in range(HG_H)]
                w.Tb = []
                for h in range(HG_H):
                    tt_ = sb("Tb%d" % h, [128, 2, 128])
                    w.Tb.append([B(tt_[:, i, :], [("Tb", h, i)]) for i in range(2)])
            else:
                def t(nm, shape, dt=F32):
                    tt = sb("s_" + nm, shape, dt)
                    return B(tt[:], [("sw", nm)])
                w.cva = t("cva", [128, 2, n]); w.cvs = t("cvs", [128, 2, n]); w.cvy = t("cvy", [128, 2, n])
                w.diag = None
                w.qz = t("qz", [128, 4, n], BF16)
                w.Eb = [t("Eb%d" % i, [128, 64 + 16], BF16) for i in range(2)]
                w.Pb = [t("Pb%d" % i, [128, 64 + 16], BF16) for i in range(2)]
                w.lnd = [t("lnd%d" % i, [128, 16]) for i in range(2)]
                w.kvout = [av("kvo_s", 12, 2, F32)]
                hp = Grp()
                for nm in ("q", "f", "lg", "kk", "eb", "en"):
                    setattr(hp, nm, t("h" + nm, [128, n]))
                w.hp = hp
                w.live = []
                for h in range(HG_H):
                    lv = Grp()
                    lv.qt = t("hqt%d" % h, [128, n], BF16)
                    lv.kt = t("hkt%d" % h, [128, 128], BF16)
                    lv.kh = t("hkh%d" % h, [128, n])
                    lv.gt = t("hgt%d" % h, [128, n], BF16)
                    lv.ebl = t("ebl%d" % h, [128, n // DEC_T])
                    lv.emid = t("emid%d" % h, [128, n // DEC_T])
                    w.live.append(lv)
                w.vtok = t("vtok", [128, 1, 512], BF16)
                w.khm = [t("khm%d" % i, [128, 4, 128], BF16) for i in range(4)]
                w.Am = [t("Am%d" % i, [128, 128], BF16) for i in range(4)]
                w.Sb = [[t("Sb%d_%d" % (h, i), [128, 128], BF16) for i in range(5)] for h in range(HG_H)]
                w.osq = [t("osq%d" % h, [128, n], BF16) for h in range(HG_H)]
                w.ot = [t("ot%d" % h, [128, n]) for h in range(HG_H)]
                w.shs = av("shs", 0, 8, F32, "p (b h v) -> p b h v", b=NB, h=HG_H)
                w.kTs = t("kTs", [128, 2, 128], BF16)
                w.vaug = t("vaug", [128, ATT_H, 128], BF16)
                w.ufp = t("ufp", [128, 256])
            return w

        wsP = mk_ws(gP)
        wsS = mk_ws(gS)
        gP.ws, gS.ws = wsP, wsS
        S.op("pool", lambda e: e.memset(wsP.qz.ap, 0.0), writes=wsP.qz.tags)
        S.op("pool", lambda e: e.memset(wsS.qz.ap, 0.0), writes=wsS.qz.tags)
        S.op("pool", lambda e: e.memset(wsS.vaug.ap, 1.0), writes=wsS.vaug.tags)
        S.op("pool", lambda e: e.memset(wsS.kTs.ap, 0.0), writes=wsS.kTs.tags)
        S.op("pool", lambda e: e.memset(wsS.vtok.ap, 0.0), writes=wsS.vtok.tags)
        for _i in range(4):
            S.op("pool", lambda e, _i=_i: e.memset(wsS.khm[_i].ap, 0.0), writes=wsS.khm[_i].tags)
            S.op("pool", lambda e, _i=_i: e.memset(wsS.live[_i].kt.ap, 0.0), writes=wsS.live[_i].kt.tags)

        acc_rr = [0]

        def acc_bank():
            b = 6 + acc_rr[0]
            acc_rr[0] ^= 1
            return b

        def win_chunk_fm(g, l, c, unit, utag):
            n = g.n
            b = ps_next()
            proj_fm(unit, utag, NCH, lambda k: g.h[:, k, 0:n], [htag(g, k) for k in range(NCH)], n, b)
            return b

        def win_chunk_tm(g, unit, utag):
            n = g.n
            bs = 128 if g is gP else n
            nblk = n // bs
            b = ps_next()

            def mm(e):
                ins = None
                for tb in range(nblk):
                    for k in range(NCH):
                        ins = e.matmul(psum[b][0:128, tb * 128:(tb + 1) * 128], lhsT=g.h[:, k, tb * bs:tb * bs + 128], rhs=unit[:, k * 128:(k + 1) * 128],
                                       start=(k == 0), stop=(k == NCH - 1))
                return ins
            S.op("pe", mm, reads=[utag] + [htag(g, k) for k in range(NCH)], writes=[PT(b)])
            return b, bs, nblk

        def conv_evac(g, c, b):
            n, w = g.n, g.ws
            if c < 2:
                S.op("act", lambda e: e.activation(out=w.cva.ap[:, c, 0:n], in_=psum[b][:, 0:n], func=AF.Copy), reads=[PT(b)], writes=w.cva.tags)
            else:
                S.op("act", lambda e: e.activation(out=w.cvs.ap[:, c - 2, 0:n], in_=psum[b][:, 0:n], func=AF.Sigmoid), reads=[PT(b)], writes=w.cvs.tags)

        def conv_glu(g):
            n, w = g.n, g.ws
            S.op("dve", lambda e: e.tensor_tensor(out=w.cva.ap[:, :, 0:n], in0=w.cva.ap[:, :, 0:n], in1=w.cvs.ap[:, :, 0:n], op=ALU.mult),
                 reads=w.cva.tags + w.cvs.tags, writes=w.cva.tags)

        def build_diag(l):
            w = wsP
            for wi in range(CW):
                for ch in range(2):
                    col = pcol("dww", l * CW * 2 + wi * 2 + ch)
                    if False:
                        pass
                    else:
                        S.op("dve", lambda e, wi=wi, ch=ch, col=col: e.tensor_scalar(out=w.diag.ap[:, wi * 2 + ch, :], in0=identb[:], scalar1=col, scalar2=None, op0=ALU.mult),
                             reads=[("identb",), ("prmT",)], writes=w.diag.tags)

        def conv_main(g, l, j):
            n, w = g.n, g.ws
            dg = wsP.diag
            if g is gP:
                S.op("pool", lambda e: e.tensor_copy(out=ubuf[:, :, 0:CW - 1], in_=utail[:, l, :, :]), reads=[("utail", l)], writes=[("ubuf",)])
                S.op("act", lambda e: e.activation(out=ubuf[:, :, CW - 1:CW - 1 + n], in_=w.cva.ap[:, :, 0:n], func=AF.Copy), reads=w.cva.tags, writes=[("ubuf",)])
                rhs = lambda ch, wi: ubuf[:, ch, wi:wi + n]
                outv = lambda b: psum[b][:, 0:n]
                utags = [("ubuf",)]
            else:
                for bb in range(NB):
                    io = iobuf[bb % 2]
                    S.dma("sp", io.ap[0:CW - 1, 0:256], sconv_in[l, bb], writes=io.tags)
                    for ch in range(2):
                        b = ps_next()
                        S.op("pe", lambda e, io=io, ch=ch, b=b: e.transpose(psum[b][:, 0:CW - 1], io.ap[0:CW - 1, ch * 128:(ch + 1) * 128], ident[0:CW - 1, 0:CW - 1]),
                             reads=io.tags + [("ident",)], writes=[PT(b)])
                        S.op("dve", lambda e, ch=ch, bb=bb, b=b: e.tensor_copy(out=ubufs[:, ch, bb, 0:CW - 1], in_=psum[b][:, 0:CW - 1]), reads=[PT(b)], writes=[("ubufs",)])
                    S.dma("pool", o_cs[l, bb, 0:CW - 1 - DEC_T, :], sconv_in[l, bb, DEC_T:CW - 1, :], writes=[("o_cs", l, bb, 0)], is_output=True)
                S.op("act", lambda e: e.activation(out=ubufs[:, :, :, CW - 1:CW - 1 + DEC_T], in_=w.cva.ap.rearrange("p c (b t) -> p c b t", t=DEC_T), func=AF.Copy),
                     reads=w.cva.tags, writes=[("ubufs",)])
                rhs = lambda ch, wi: ubufs[:, ch, :, wi:wi + DEC_T]
                outv = lambda b: psum[b][:, 0:n].rearrange("p (b t) -> p b t", t=DEC_T)
                utags = [("ubufs",)]
            banks = []
            for ch in range(2):
                b = ps_next()
                banks.append(b)

                def mm(e, ch=ch, b=b):
                    ins = None
                    for wi in range(CW):
                        ins = e.matmul(outv(b), lhsT=dg.ap[:, wi * 2 + ch, :], rhs=rhs(ch, wi), start=(wi == 0), stop=(wi == CW - 1))
                    return ins
                S.op("pe", mm, reads=dg.tags + utags, writes=[PT(b)])
                S.op("act", lambda e, ch=ch, b=b: e.activation(out=w.cvy.ap[:, ch, 0:n], in_=psum[b][:, 0:n], func=AF.Identity, bias=pcol("dwb", l * 2 + ch)),
                     reads=[PT(b), ("prmT",)], writes=w.cvy.tags)
            if g is gP:
                S.op("pool", lambda e: e.tensor_copy(out=utail[:, l, :, :], in_=ubuf[:, :, n:n + CW - 1]), reads=[("ubuf",)], writes=[("utail", l)])
            for ch in range(2):
                S.op("dve", lambda e, ch=ch: e.tensor_copy(out=sqc[ch].ap[:, 0:n], in_=w.cvy.ap[:, ch, 0:n]), reads=w.cvy.tags, writes=sqc[ch].tags)
            b = ps_next()

            def mm1(e):
                ins = None
                for ch in range(2):
                    ins = e.matmul(psum[b][:, 0:n], lhsT=ones_b[:], rhs=sqc[ch].ap[:, 0:n], start=(ch == 0), stop=(ch == 1))
                return ins
            S.op("pe", mm1, reads=sqc[0].tags + sqc[1].tags + [("ones",)], writes=[PT(b)])
            for ch in range(2):
                S.op("dve", lambda e, ch=ch: e.scalar_tensor_tensor(out=w.cvy.ap[:, ch, 0:n], in0=psum[b][:, 0:n], scalar=-1.0 / CONV_DIM,
                                                                   in1=w.cvy.ap[:, ch, 0:n], op0=ALU.mult, op1=ALU.add),
                     reads=[PT(b)] + w.cvy.tags, writes=w.cvy.tags)
            sumsq_rstd(g, [w.cvy.ap[:, ch, 0:n] for ch in range(2)], [w.cvy.tags, w.cvy.tags], 2, CONV_DIM)
            for ch in range(2):
                S.op("dve", lambda e, ch=ch: e.tensor_tensor(out=w.cvy.ap[:, ch, 0:n], in0=w.cvy.ap[:, ch, 0:n], in1=rstd[:, 0:n], op=ALU.mult),
                     reads=w.cvy.tags + [("rstd",)], writes=w.cvy.tags)
                S.op("act", lambda e, ch=ch: e.activation(out=g.ym[:, ch, 0:n], in_=w.cvy.ap[:, ch, 0:n], func=AF.Silu,
                                                          scale=pcol("clg", l * 2 + ch), bias=pcol("clb", l * 2 + ch)),
                     reads=w.cvy.tags + [("prmT",)], writes=[ytag(g, ch)])
            if g is gP and j == cfg.ntile - 1:
                io = iobuf[0]
                for ch in range(2):
                    b2 = ps_next()
                    S.op("pe", lambda e, ch=ch, b2=b2: e.transpose(psum[b2][0:CW - 1, 0:128], w.cva.ap[:, ch, n - (CW - 1):n], ident[:]),
                         reads=w.cva.tags + [("ident",)], writes=[PT(b2)])
                    S.op("dve", lambda e, ch=ch, b2=b2, io=io: e.tensor_copy(out=io.ap[0:CW - 1, ch * 128:(ch + 1) * 128], in_=psum[b2][0:CW - 1, 0:128]),
                         reads=[PT(b2)], writes=io.tags)
                S.dma("pool", o_cp[l], io.ap[0:CW - 1, 0:256], reads=io.tags, writes=[("o_cp", l)], is_output=True)
            if g is gS:
                for ch in range(2):
                    b2 = ps_next()
                    S.op("pe", lambda e, ch=ch, b2=b2: e.transpose(psum[b2][0:n, 0:128], w.cva.ap[:, ch, 0:n], ident[:]),
                         reads=w.cva.tags + [("ident",)], writes=[PT(b2)])
                    S.op("dve", lambda e, ch=ch, b2=b2: e.tensor_copy(out=w.ufp.ap[0:n, ch * 128:(ch + 1) * 128], in_=psum[b2][0:n, 0:128]),
                         reads=[PT(b2)], writes=w.ufp.tags)
                for bb in range(NB):
                    S.dma("pool", o_cs[l, bb, CW - 1 - DEC_T:CW - 1, :], w.ufp.ap[bb * DEC_T:(bb + 1) * DEC_T, :], reads=w.ufp.tags,
                          writes=[("o_cs", l, bb, 1)], is_output=True)

        kvst = av("kvst", 44, 8, F32, "p (b c) -> p b c", b=4)

        def att_q_evac(g, c, b):
            n, w = g.n, g.ws
            ch = c - 4
            for hp in range(2):
                h = 2 * ch + hp
                r0 = 64 * hp
                S.op("act", lambda e, h=h, r0=r0: e.activation(out=w.qz.ap[r0:r0 + 64, h, 0:n], in_=psum[b][r0:r0 + 64, 0:n], func=AF.Copy, scale=HD ** -0.5),
                     reads=[PT(b)], writes=w.qz.tags)

        def att_k_evac(g, l, j, c, b):
            n, w = g.n, g.ws
            ch = c - 6
            if g is gP:
                S.op("dve", lambda e: e.tensor_copy(out=kwin[:, ch, WIN:WIN + n], in_=psum[b][:, 0:n]), reads=[PT(b)], writes=[("kwin", "cur")])
                if ch == 1:
                    t0 = j * TT
                    S.dma("pool", sK[l].rearrange("(c p) t -> p c t", p=128)[:, :, t0:t0 + n], kwin[:, :, WIN:WIN + n], reads=[("kwin", "cur")], writes=[("sK", l, j)])
            else:
                S.op("dve", lambda e: e.tensor_copy(out=w.kTs.ap[:, ch, 0:n], in_=psum[b][:, 0:n]), reads=[PT(b)], writes=w.kTs.tags)

        def att_kv_tm(g, l, j, c, unit, utag):
            n, w = g.n, g.ws
            ci = c - 6
            b, bs, nblk = win_chunk_tm(g, unit, utag)
            t0 = j * TT
            if g is gP:
                S.op("act", lambda e: e.activation(out=kvst.ap[:, :, ci * 128:(ci + 1) * 128], in_=psum[b][:, :].rearrange("p (b c) -> p b c", c=128), func=AF.Copy),
                     reads=[PT(b)], writes=kvst.tags)
            else:
                kv = w.kvout[0]
                S.op("act", lambda e: e.activation(out=kv.ap[0:bs, ci * 128:(ci + 1) * 128], in_=psum[b][0:bs, 0:128], func=AF.Copy), reads=[PT(b)], writes=kv.tags)
            if c >= 8:
                for hh in range(2):
                    h = 2 * (c - 8) + hh
                    if g is gP:
                        S.op("act", lambda e, h=h, hh=hh: e.activation(out=vwin[:, WIN // 128:WIN // 128 + 4, h, 64 * hh:64 * hh + 64],
                                                                       in_=psum[b][:, :].rearrange("p (b c) -> p b c", c=128)[:, :, 64 * hh:64 * hh + 64], func=AF.Copy),
                             reads=[PT(b)], writes=[("vwin", "cur")])
                    else:
                        S.op("act", lambda e, h=h, hh=hh: e.activation(out=w.vaug.ap[0:bs, h, 64 * hh:64 * hh + 64], in_=psum[b][0:bs, 64 * hh:64 * hh + 64], func=AF.Copy),
                             reads=[PT(b)], writes=w.vaug.tags)
            if c == 9:
                if g is gP:
                    r0 = t0 - (SEQ - KEEP)
                    if r0 >= 0:
                        S.dma("pool", o_kp[l, r0:r0 + n, :].rearrange("(b p) c -> p b c", p=128), kvst.ap[:, :, 0:256], reads=kvst.tags, writes=[("o_kp", l, r0)], is_output=True)
                        S.dma("pool", o_vp[l, r0:r0 + n, :].rearrange("(b p) c -> p b c", p=128), kvst.ap[:, :, 256:512], reads=kvst.tags, writes=[("o_vp", l, r0)], is_output=True)
                    S.dma("pool", sV[l, t0:t0 + n, :].rearrange("(b p) c -> p b c", p=128),
                          vwin[:, WIN // 128:WIN // 128 + 4, :, :].rearrange("p b h d -> p b (h d)"), reads=[("vwin", "cur")], writes=[("sV", l, j)])
                else:
                    kv = w.kvout[0]
                    S.dma("pool", o_ks[l], kv.ap[0:n, 0:256], reads=kv.tags, writes=[("o_ks", l)], is_output=True)
                    S.dma("pool", o_vs[l], kv.ap[0:n, 256:512], reads=kv.tags, writes=[("o_vs", l)], is_output=True)

        def att_finish(g, w, bO, h, ych, cols, li):
            ch, hp = divmod(h, 2)
            orow = 64 * hp
            drow = 64 * (1 - hp)
            nq = cols[1] - cols[0]
            ld = w.lnd[li % 2]
            S.op("act", lambda e: e.activation(out=ld.ap[orow:orow + 64, 0:nq], in_=psum[bO][drow:drow + 64, 0:nq], func=AF.Ln),
                 reads=[PT(bO)], writes=ld.tags)
            S.op("act", lambda e: e.activation(out=ld.ap[orow:orow + 64, 0:nq], in_=ld.ap[orow:orow + 64, 0:nq], func=AF.Exp, scale=-1.0),
                 reads=ld.tags, writes=ld.tags)
            S.op("dve", lambda e: e.tensor_tensor(out=g.ym[orow:orow + 64, 2 + ch, cols[0]:cols[1]], in0=psum[bO][orow:orow + 64, 0:nq],
                                                  in1=ld.ap[orow:orow + 64, 0:nq], op=ALU.mult),
                 reads=[PT(bO)] + ld.tags, writes=[ytag(g, 2 + ch)])

        _wt = att_weight_table().astype(np.float64)
        MMAX = [max(m for m in range(NM) if np.any(_wt[:, m, h, :] >= 2.0 ** -134)) for h in range(ATT_H)]

        def att_prompt(l, j):
            g, w = gP, wsP
            t0 = j * TT
            work = []
            for qb in range(TT // 128):
                gb = j * (TT // 128) + qb
                nm = min(16, gb) + 1
                for h in range(ATT_H):
                    bO = acc_bank()
                    nmh = min(nm, MMAX[h] + 1)
                    for g0 in range(0, nmh, 4):
                        work.append(dict(qb=qb, h=h, grp=list(range(g0, min(nmh, g0 + 4))), first=(g0 == 0), last=(g0 + 4 >= nmh), bO=bO, nm=nmh))
            li = [0]

            def emit_S(i):
                wk = work[i]
                b = ps_next()
                wk["b"] = b
                qb, h, grp = wk["qb"], wk["h"], wk["grp"]
                ch = h // 2

                def mm(e):
                    ins = None
                    for gi, m in enumerate(grp):
                        wkb = 16 + qb - m
                        ins = e.matmul(psum[b][:, gi * 128:(gi + 1) * 128], lhsT=kwin[:, ch, wkb * 128:(wkb + 1) * 128],
                                       rhs=w.qz.ap[:, h, qb * 128:(qb + 1) * 128], start=True, stop=True)
                    return ins
                S.op("pe", mm, reads=[("kwin", "hist"), ("kwin", "cur")] + w.qz.tags, writes=[PT(b)])
                ng = len(grp)
                Eb, Pb = w.Eb[i % 4], w.Pb[i % 4]
                S.op("act", lambda e: e.activation(out=Eb.ap[:, 0:ng * 128], in_=psum[b][:, 0:ng * 128], func=AF.Exp), reads=[PT(b)], writes=Eb.tags)
                S.op("dve", lambda e: e.tensor_tensor(
                    out=Pb.ap[:, 0:ng * 128].rearrange("p (m q) -> p m q", q=128), in0=Eb.ap[:, 0:ng * 128].rearrange("p (m q) -> p m q", q=128),
                    in1=wmask[:, grp[0]:grp[0] + ng, h, :], op=ALU.mult), reads=Eb.tags + [("wmask",)], writes=Pb.tags)

            def emit_PV(i):
                wk = work[i]
                qb, h, grp, bO, nm = wk["qb"], wk["h"], wk["grp"], wk["bO"], wk["nm"]
                Pb = w.Pb[i % 4]

                def pv(e):
                    ins = None
                    for gi, m in enumerate(grp):
                        wkb = 16 + qb - m
                        ins = e.matmul(psum[bO][:, 0:128], lhsT=vwin[:, wkb, h, :], rhs=Pb.ap[:, gi * 128:(gi + 1) * 128],
                                       start=(wk["first"] and gi == 0), stop=(m == nm - 1))
                    return ins
                S.op("pe", pv, reads=Pb.tags + [("vwin", "hist"), ("vwin", "cur")], writes=[PT(bO)])
                if wk["last"]:
                    att_finish(g, w, bO, h, None, (qb * 128, (qb + 1) * 128), li[0])
                    li[0] += 1

            LA = 3
            for i in range(len(work) + LA):
                if i < len(work):
                    emit_S(i)
                if i - LA >= 0:
                    emit_PV(i - LA)

        def att_sample(l):
            g, w = gS, wsS
            n = g.n
            li = 0
            ei = 0
            for bb in range(NB):
                for q4 in range(4):
                    io = iobuf[q4 % 2]
                    S.dma("sp", io.ap[:, :].rearrange("p (b c) -> p b c", c=256), ck_in[l, bb, q4 * 512:(q4 + 1) * 512, :].rearrange("(b p) c -> p b c", p=128), writes=io.tags)
                    for ch in range(2):
                        b = ps_next()

                        def tr(e, io=io, ch=ch, b=b):
                            ins = None
                            for kb in range(4):
                                ins = e.transpose(psum[b][:, kb * 128:(kb + 1) * 128], io.ap[:, kb * 256 + ch * 128:kb * 256 + (ch + 1) * 128], ident[:])
                            return ins
                        S.op("pe", tr, reads=io.tags + [("ident",)], writes=[PT(b)])
                        if ch == 0:
                            S.op("act", lambda e, b=b, q4=q4, ch=ch: e.activation(out=kwin[:, ch, q4 * 512:(q4 + 1) * 512], in_=psum[b][:, :], func=AF.Copy),
                                 reads=[PT(b)], writes=[("kwin", "hist")])
                        else:
                            S.op("dve", lambda e, b=b, q4=q4, ch=ch: e.tensor_copy(out=kwin[:, ch, q4 * 512:(q4 + 1) * 512], in_=psum[b][:, :]),
                                 reads=[PT(b)], writes=[("kwin", "hist")])
                for q4 in range(4):
                    io = iobuf[q4 % 2]
                    S.dma("sp", io.ap[:, :].rearrange("p (b c) -> p b c", c=256), cv_in[l, bb, q4 * 512:(q4 + 1) * 512, :].rearrange("(b p) c -> p b c", p=128), writes=io.tags)
                    for par in range(2):
                        S.op("act", lambda e, io=io, q4=q4, par=par: e.activation(
                            out=vwin[:, q4 * 4:(q4 + 1) * 4, par::2, 64 * par:64 * par + 64],
                            in_=io.ap[:, :].rearrange("p (b h d) -> p b h d", b=4, d=64)[:, :, par::2, :], func=AF.Copy),
                            reads=io.tags, writes=[("vwin", "hist")])
                for h in range(ATT_H):
                    ch = h // 2
                    bO = acc_bank()
                    b = ps_next()

                    def mm(e, b=b, ch=ch, h=h, bb=bb):
                        ins = None
                        for m in range(1, min(16, MMAX[h]) + 1):
                            wkb = 16 - m
                            ins = e.matmul(psum[b][:, (m - 1) * DEC_T:m * DEC_T], lhsT=kwin[:, ch, wkb * 128:(wkb + 1) * 128],
                                           rhs=w.qz.ap[:, h, bb * DEC_T:(bb + 1) * DEC_T], start=True, stop=True)
                        ins = e.matmul(psum[b][0:128, 64:64 + DEC_T], lhsT=w.kTs.ap[:, ch, 0:128], rhs=w.qz.ap[:, h, bb * DEC_T:(bb + 1) * DEC_T], start=True, stop=True)
                        return ins
                    S.op("pe", mm, reads=[("kwin", "hist")] + w.kTs.tags + w.qz.tags, writes=[PT(b)])
                    Eb = w.Eb[ei % 2]
                    Pb = w.Pb[ei % 2]
                    ei += 1
                    mh = min(16, MMAX[h])
                    S.op("act", lambda e, b=b, Eb=Eb, mh=mh: e.activation(out=Eb.ap[:, 0:mh * DEC_T], in_=psum[b][:, 0:mh * DEC_T], func=AF.Exp), reads=[PT(b)], writes=Eb.tags)
                    S.op("act", lambda e, b=b, Eb=Eb: e.activation(out=Eb.ap[:, 64:64 + DEC_T], in_=psum[b][:, 64:64 + DEC_T], func=AF.Exp), reads=[PT(b)], writes=Eb.tags)
                    S.op("dve", lambda e, Eb=Eb, Pb=Pb, h=h, mh=mh: e.tensor_tensor(out=Pb.ap[:, 0:mh * DEC_T].rearrange("p (m q) -> p m q", q=DEC_T),
                                                                          in0=Eb.ap[:, 0:mh * DEC_T].rearrange("p (m q) -> p m q", q=DEC_T),
                                                                          in1=wmask[:, 1:mh + 1, h, 0:DEC_T], op=ALU.mult),
                         reads=Eb.tags + [("wmask",)], writes=Pb.tags)
                    S.op("dve", lambda e, Eb=Eb, Pb=Pb, h=h, bb=bb: e.tensor_tensor(out=Pb.ap[:, 64:64 + DEC_T], in0=Eb.ap[:, 64:64 + DEC_T],
                                                                                 in1=wsm[:, bb, h, :], op=ALU.mult),
                         reads=Eb.tags + [("wsm",)], writes=Pb.tags)

                    def pv(e, Pb=Pb, h=h, bO=bO):
                        ins = None
                        for m in range(1, min(16, MMAX[h]) + 1):
                            wkb = 16 - m
                            ins = e.matmul(psum[bO][:, 0:DEC_T], lhsT=vwin[:, wkb, h, :], rhs=Pb.ap[:, (m - 1) * DEC_T:m * DEC_T], start=(m == 1), stop=False)
                        ins = e.matmul(psum[bO][:, 0:DEC_T], lhsT=w.vaug.ap[:, h, :], rhs=Pb.ap[:, 64:64 + DEC_T], start=False, stop=True)
                        return ins
                    S.op("pe", pv, reads=Pb.tags + [("vwin", "hist")] + w.vaug.tags, writes=[PT(bO)])
                    att_finish(g, w, bO, h, None, (bb * DEC_T, (bb + 1) * DEC_T), li)
                    li += 1

        def hgrn_prep(g, l, h):
            n, w = g.n, g.ws
            hs, lv = w.hp, w.live[h]
            prompt = g is gP
            C = HGC if prompt else DEC_T
            nchunk = n // C
            rsm = rsmask if prompt else rsmask_s
            rsm_tag = ("rsmask",) if prompt else ("rsmask_s",)
            lbc = lbT[:, l * HG_H + h:l * HG_H + h + 1]
            omc = omlT[:, l * HG_H + h:l * HG_H + h + 1]
            A = lambda bf: bf.ap[:, 0:n]
            S.op("pool", lambda e: e.tensor_scalar(out=A(hs.f), in0=A(hs.f), scalar1=omc, scalar2=lbc, op0=ALU.mult, op1=ALU.add),
                 reads=hs.f.tags + [("lbT",), ("omlT",)], writes=hs.f.tags)
            S.op("act", lambda e: e.activation(out=A(hs.lg), in_=A(hs.f), func=AF.Ln), reads=hs.f.tags, writes=hs.lg.tags)
            S.op("pool", lambda e: e.tensor_scalar(out=A(hs.kk), in0=A(hs.f), scalar1=-1.0, scalar2=1.0, op0=ALU.mult, op1=ALU.add),
                 reads=hs.f.tags, writes=hs.kk.tags)
            S.op("dve", lambda e: e.tensor_tensor_scan(out=A(hs.f), data0=rsm[:, 0:n], data1=A(hs.lg), initial=0.0, op0=ALU.mult, op1=ALU.add),
                 reads=hs.lg.tags + [rsm_tag], writes=hs.f.tags)
            b3 = hs.f.ap[:, 0:n].rearrange("p (c t) -> p c t", t=C)
            MID = C // 2 - 1
            S.op("act", lambda e: e.activation(out=lv.ebl.ap[:, 0:nchunk], in_=b3[:, :, C - 1], func=AF.Exp), reads=hs.f.tags, writes=lv.ebl.tags)
            S.op("act", lambda e: e.activation(out=lv.emid.ap[:, 0:nchunk], in_=b3[:, :, MID], func=AF.Exp), reads=hs.f.tags, writes=lv.emid.tags)
            S.op("dve", lambda e: e.tensor_tensor(out=hs.lg.ap[:, 0:n].rearrange("p (c t) -> p c t", t=C),
                                                  in0=b3, in1=b3[:, :, MID:MID + 1].to_broadcast([128, nchunk, C]), op=ALU.subtract),
                 reads=hs.f.tags, writes=hs.lg.tags)
            S.op("act", lambda e: e.activation(out=A(hs.eb), in_=A(hs.lg), func=AF.Exp), reads=hs.lg.tags, writes=hs.eb.tags)
            S.op("act", lambda e: e.activation(out=A(hs.en), in_=A(hs.lg), func=AF.Exp, scale=-1.0), reads=hs.lg.tags, writes=hs.en.tags)
            S.op("dve", lambda e: e.tensor_tensor(out=A(lv.qt), in0=A(hs.q), in1=A(hs.eb), op=ALU.mult), reads=hs.q.tags + hs.eb.tags, writes=lv.qt.tags)
            S.op("dve", lambda e: e.tensor_tensor(out=A(lv.kt), in0=A(hs.kk), in1=A(hs.en), op=ALU.mult), reads=hs.kk.tags + hs.en.tags, writes=lv.kt.tags)
            S.op("dve", lambda e: e.tensor_tensor(out=hs.lg.ap[:, 0:n].rearrange("p (c t) -> p c t", t=C),
                                                  in0=b3[:, :, C - 1:C].to_broadcast([128, nchunk, C]), in1=b3, op=ALU.subtract),
                 reads=hs.f.tags + hs.eb.tags + hs.en.tags, writes=hs.lg.tags)
            S.op("act", lambda e: e.activation(out=A(hs.en), in_=A(hs.lg), func=AF.Exp), reads=hs.lg.tags, writes=hs.en.tags)
            S.op("dve", lambda e: e.tensor_tensor(out=A(lv.kh), in0=A(hs.kk), in1=A(hs.en), op=ALU.mult), reads=hs.kk.tags + hs.en.tags, writes=lv.kh.tags)

        def hgrn_chain(g, l, j):
            n, w = g.n, g.ws
            prompt = g is gP
            C = HGC if prompt else DEC_T
            bs = 128 if prompt else n
            nblk = n // bs
            NCB = bs // C
            bdm = bdmask if prompt else bdmask_s
            bdm_tag = ("bdmask",) if prompt else ("bdmask_s",)
            rm0 = 0 if prompt else 4
            vt_tags = w.vtok.tags
            ps_rr[0] = ps_rr[0] % 4
            npr_save = NPRv[0]
            NPRv[0] = 4
            bOT = [4 + h for h in range(HG_H)]
            ki = [0]
            if prompt:
                for h in range(HG_H):
                    S.op("pool", lambda e, h=h: e.tensor_copy(out=w.Tb[h][0].ap, in_=Sst[:, l, h, :]), reads=[("Sst", l, h)], writes=w.Tb[h][0].tags)
                    S.op("act", lambda e, h=h: e.activation(out=w.Sb[h][0].ap, in_=Sst[:, l, h, :], func=AF.Copy, scale=w.live[h].emid.ap[:, 0:1]),
                         reads=[("Sst", l, h)] + w.live[h].emid.tags, writes=w.Sb[h][0].tags)

            st1 = {}

            def stage1(tb, h):
                lv = w.live[h]
                c0 = tb * bs
                bA = ps_next()
                S.op("pe", lambda e: e.matmul(psum[bA][0:128, 0:bs], lhsT=lv.kt.ap[:, c0:c0 + 128], rhs=lv.qt.ap[:, c0:c0 + bs], start=True, stop=True),
                     reads=lv.kt.tags + lv.qt.tags, writes=[PT(bA)])
                Am = w.Am[ki[0] % 4]
                khm = w.khm[ki[0] % 4]
                ki[0] += 1
                S.op("dve", lambda e: e.tensor_tensor(out=Am.ap[:, 0:bs], in0=psum[bA][:, 0:bs], in1=bdm[:, 0:bs], op=ALU.mult),
                     reads=[PT(bA), bdm_tag], writes=Am.tags)
                bT = ps_next()
                S.op("pe", lambda e: e.transpose(psum[bT][0:bs, 0:128], lv.kh.ap[:, c0:c0 + bs], ident[:]),
                     reads=lv.kh.tags + [("ident",)], writes=[PT(bT)])
                for jj in range(NCB):
                    rc = rowmask[0:bs, rm0 + jj:rm0 + jj + 1]
                    S.op("dve", lambda e, jj=jj, rc=rc: e.tensor_scalar(out=khm.ap[0:bs, jj, :], in0=psum[bT][0:bs, 0:128], scalar1=rc, scalar2=None, op0=ALU.mult),
                         reads=[PT(bT), ("rowmask",)], writes=khm.tags)
                st1[(tb, h)] = (Am, khm)

            def stage1b(tb, h):
                lv = w.live[h]
                c0 = tb * bs
                Am, khm = st1[(tb, h)]
                bD = ps_next()

                def dS(e):
                    ins = None
                    for jj in range(NCB):
                        ins = e.matmul(psum[bD][:, jj * 128:(jj + 1) * 128], lhsT=khm.ap[:, jj, :], rhs=w.vtok.ap[:, tb, h * 128:(h + 1) * 128], start=True, stop=True)
                    return ins
                S.op("pe", dS, reads=khm.tags + vt_tags, writes=[PT(bD)])
                S.op("pe", lambda e: e.matmul(psum[bOT[h]][:, c0:c0 + bs], lhsT=w.vtok.ap[:, tb, h * 128:(h + 1) * 128], rhs=Am.ap[:, 0:bs], start=True, stop=False),
                     reads=Am.tags + vt_tags, writes=[PT(bOT[h])])
                for jj in range(NCB):
                    if prompt:
                        cidx = tb * NCB + jj
                        ecol = lv.ebl.ap[:, cidx:cidx + 1]
                        tin, tout = w.Tb[h][cidx % 2], w.Tb[h][(cidx + 1) % 2]
                        S.op("dve", lambda e, ecol=ecol, jj=jj, tin=tin, tout=tout: e.scalar_tensor_tensor(out=tout.ap, in0=tin.ap, scalar=ecol,
                                                                                      in1=psum[bD][:, jj * 128:(jj + 1) * 128], op0=ALU.mult, op1=ALU.add),
                             reads=tin.tags + [PT(bD)] + lv.ebl.tags, writes=tout.tags)
                        sbn = w.Sb[h][(cidx + 1) % 5]
                        if cidx + 1 < nblk * NCB:
                            mcol = lv.emid.ap[:, cidx + 1:cidx + 2]
                            S.op("act", lambda e, sbn=sbn, tout=tout, mcol=mcol: e.activation(out=sbn.ap, in_=tout.ap, func=AF.Copy, scale=mcol),
                                 reads=tout.tags + lv.emid.tags, writes=sbn.tags)
                    else:
                        sbc = w.Sb[h][jj]
                        S.op("act", lambda e, sbc=sbc, jj=jj: e.activation(out=sbc.ap, in_=w.shs.ap[:, jj, h, :], func=AF.Copy, scale=lv.emid.ap[:, jj:jj + 1]),
                             reads=w.shs.tags + lv.emid.tags, writes=sbc.tags)
                        ecol = lv.ebl.ap[:, jj:jj + 1]
                        S.op("dve", lambda e, ecol=ecol, jj=jj: e.scalar_tensor_tensor(out=w.shs.ap[:, jj, h, :], in0=w.shs.ap[:, jj, h, :], scalar=ecol,
                                                                                      in1=psum[bD][:, jj * 128:(jj + 1) * 128], op0=ALU.mult, op1=ALU.add),
                             reads=w.shs.tags + [PT(bD)] + lv.ebl.tags, writes=w.shs.tags)

            def stage2(tb, h):
                lv = w.live[h]
                c0 = tb * bs
                for jj in range(NCB):
                    q0 = c0 + jj * C
                    sbc = w.Sb[h][(tb * NCB + jj) % 5] if prompt else w.Sb[h][jj]
                    S.op("pe", lambda e, sbc=sbc, q0=q0, jj=jj: e.matmul(psum[bOT[h]][:, q0:q0 + C], lhsT=sbc.ap, rhs=lv.qt.ap[:, q0:q0 + C], start=False, stop=(jj == NCB - 1)),
                         reads=sbc.tags + lv.qt.tags, writes=[PT(bOT[h])])

            seq = [(tb, h) for tb in range(nblk) for h in range(HG_H)]
            for i in range(len(seq) + 2):
                if i < len(seq):
                    stage1(*seq[i])
                if 1 <= i <= len(seq):
                    stage1b(*seq[i - 1])
                if i >= 2:
                    stage2(*seq[i - 2])
            if prompt:
                nct = nblk * NCB
                for h in range(HG_H):
                    S.op("pool", lambda e, h=h: e.tensor_copy(out=Sst[:, l, h, :], in_=w.Tb[h][nct % 2].ap), reads=w.Tb[h][nct % 2].tags, writes=[("Sst", l, h)])
            for h in range(HG_H):
                lv = w.live[h]
                osq, ot = w.osq[h], w.ot[h]
                S.op("act", lambda e, h=h, osq=osq: e.activation(out=osq.ap[:, 0:n], in_=psum[bOT[h]][:, 0:n], func=AF.Square), reads=[PT(bOT[h])], writes=osq.tags)
                bN = ps_next()
                S.op("pe", lambda e, bN=bN, osq=osq: e.matmul(psum[bN][:, 0:n], lhsT=ones_b[:], rhs=osq.ap[:, 0:n], start=True, stop=True), reads=osq.tags + [("ones",)], writes=[PT(bN)])
                S.op("act", lambda e, bN=bN, ot=ot: e.activation(out=ot.ap[:, 0:n], in_=psum[bN][:, 0:n], func=AF.Ln, scale=1.0 / 128, bias=eps_col[:]), reads=[PT(bN), ("eps",)], writes=ot.tags)
                S.op("act", lambda e, ot=ot: e.activation(out=ot.ap[:, 0:n], in_=ot.ap[:, 0:n], func=AF.Exp, scale=-0.5), reads=ot.tags, writes=ot.tags)
                S.op("dve", lambda e, h=h, ot=ot: e.tensor_tensor(out=ot.ap[:, 0:n], in0=psum[bOT[h]][:, 0:n], in1=ot.ap[:, 0:n], op=ALU.mult), reads=[PT(bOT[h])] + ot.tags, writes=ot.tags)
                S.op("dve", lambda e, h=h, lv=lv, ot=ot: e.scalar_tensor_tensor(out=g.ym[:, 4 + h, 0:n], in0=ot.ap[:, 0:n], scalar=pcol("hgn", l), in1=lv.gt.ap[:, 0:n], op0=ALU.mult, op1=ALU.mult),
                     reads=ot.tags + lv.gt.tags + [("prmT",)], writes=[ytag(g, 4 + h)])
            NPRv[0] = npr_save

        def hgrn_vtok(g, h, unit, utag):
            w = g.ws
            b, bs, nblk = win_chunk_tm(g, unit, utag)
            if g is gP:
                S.op("act", lambda e: e.activation(out=w.vtok.ap[:, :, h * 128:(h + 1) * 128], in_=psum[b][:, :].rearrange("p (b c) -> p b c", c=128), func=AF.Copy),
                     reads=[PT(b)], writes=w.vtok.tags)
            else:
                S.op("act", lambda e: e.activation(out=w.vtok.ap[:, 0, h * 128:(h + 1) * 128], in_=psum[b][:, 0:128], func=AF.Copy),
                     reads=[PT(b)], writes=w.vtok.tags)

        def att_hist_load(l, j):
            t0 = j * TT
            avail = min(WIN, t0)
            if avail > 0:
                S.dma("sp", kwin[:, :, WIN - avail:WIN], sK[l].rearrange("(c p) t -> p c t", p=128)[:, :, t0 - avail:t0],
                      reads=[("sK", l, jj) for jj in range(j)], writes=[("kwin", "hist")])
                nb = avail // 128
                S.dma("sp", vwin[:, 16 - nb:16, :, :].rearrange("p b h d -> p b (h d)"),
                      sV[l, t0 - avail:t0, :].rearrange("(b p) c -> p b c", p=128),
                      reads=[("sV", l, jj) for jj in range(j)], writes=[("vwin", "hist")])

        def mix_layer(groups, l, j):
            if STG >= 3:
                att_hist_load(l, j)
            for g in groups:
                rmsnorm(g, "lnm", l * 8)
            build_diag(l)
            for c in FM_CONV:
                u, t = WS.consume(("i", l, c))
                for g in groups:
                    conv_evac(g, c, win_chunk_fm(g, l, c, u, t))
            for g in groups:
                conv_glu(g)
            if STG >= 3:
                for g in groups:
                    S.op("pool", lambda e, g=g: e.memset(g.ws.qz.ap, 0.0), writes=g.ws.qz.tags)
            for c in FM_QKV:
                u, t = WS.consume(("i", l, c))
                if STG < 3:
                    continue
                for g in groups:
                    if c in (4, 5):
                        att_q_evac(g, c, win_chunk_fm(g, l, c, u, t))
                    elif c in (6, 7):
                        att_k_evac(g, l, j, c, win_chunk_fm(g, l, c, u, t))
                    if c >= 6:
                        att_kv_tm(g, l, j, c, u, t)
            for g in groups:
                conv_main(g, l, j)
            if STG >= 3:
                att_prompt(l, j)
                if gS in groups:
                    att_sample(l)
            else:
                for g in groups:
                    for c in (2, 3):
                        S.op("pool", lambda e, g=g, c=c: e.memset(g.ym[:, c, 0:g.n], 0.0), writes=[ytag(g, c)])
            for h in range(HG_H):
                for c, key in ((18 + h, "v"), (10 + h, "q"), (14 + h, "f"), (22 + h, "g")):
                    u, t = WS.consume(("i", l, c))
                    if STG < 4:
                        continue
                    for g in groups:
                        if key == "v":
                            hgrn_vtok(g, h, u, t)
                        else:
                            bz = win_chunk_fm(g, l, c, u, t)
                            dst = {"q": g.ws.hp.q, "f": g.ws.hp.f, "g": g.ws.live[h].gt}[key]
                            fnc = AF.Sigmoid if key == "f" else AF.Silu
                            S.op("act", lambda e, g=g, dst=dst, bz=bz, fnc=fnc: e.activation(out=dst.ap[:, 0:g.n], in_=psum[bz][:, 0:g.n], func=fnc),
                                 reads=[PT(bz)], writes=dst.tags)
                if STG >= 4:
                    for g in groups:
                        hgrn_prep(g, l, h)
                else:
                    for g in groups:
                        S.op("pool", lambda e, g=g, h=h: e.memset(g.ym[:, 4 + h, 0:g.n], 0.0), writes=[ytag(g, 4 + h)])
            if STG >= 4:
                for g in groups:
                    if g is gS:
                        S.dma("sp", wsS.shs.ap, shg_in[l].rearrange("b h d v -> d b h v"), writes=wsS.shs.tags)
                    hgrn_chain(g, l, j)
            if STG >= 4:
                if j == cfg.ntile - 1:
                    S.dma("pool", o_hp[l].rearrange("h d v -> d h v"), Sst[:, l, :, :], reads=[("Sst", l, h) for h in range(HG_H)], writes=[("o_hp", l)], is_output=True)
                if gS in groups:
                    S.dma("pool", o_hs[l].rearrange("b h d v -> d b h v"), wsS.shs.ap, reads=wsS.shs.tags, writes=[("o_hs", l)], is_output=True)
            for oc in range(NCH):
                u, t = WS.consume(("o", l, oc))
                for g in groups:
                    n = g.n
                    b = ps_next()
                    proj_fm(u, t, NCH, lambda k, g=g, n=n: g.ym[:, k, 0:n], [ytag(g, k) for k in range(NCH)], n, b)
                    S.op("dve", lambda e, g=g, n=n, b=b, oc=oc: e.tensor_tensor(out=g.x[:, oc, 0:n], in0=psum[b][:, 0:n], in1=g.x[:, oc, 0:n], op=ALU.add),
                         reads=[PT(b), xtag(g, oc)], writes=[xtag(g, oc)])

        load_tokens(gS, xs, [(0, NS, 0)])
        for j in range(cfg.ntile):
            load_tokens(gP, xp, [(j * TT + 128 * b, 128, 128 * b) for b in range(TT // 128)])
            groups = [gP, gS] if j == 0 else [gP]
            for l in range(L):
                for g in groups:
                    rmsnorm(g, "ln1", l * 8)
                ffn(groups, 0, l)
                if STG >= 2:
                    mix_layer(groups, l, j)
                if STG >= 9:
                    for g in groups:
                        rmsnorm(g, "ln2", l * 8)
                    ffn(groups, 1, l)
            final_norm_store(gP, yp, [(j * TT + 128 * b, 128, 128 * b) for b in range(TT // 128)])
            if j == 0:
                final_norm_store(gS, ys, [(0, NS, 0)])

        if cfg.debug:
            dbg = dout("dbg_ym", [128, NCH * TT], BF16)
            S.dma("pool", dbg, ymix[:].rearrange("p c t -> p (c t)"), reads=[ytag(gP, c) for c in range(NCH)], writes=[("dbg", 0)], is_output=True)
            dbgs = dout("dbg_yms", [128, NCH * NS], BF16)
            S.dma("pool", dbgs, ymixs[:].rearrange("p c t -> p (c t)"), reads=[ytag(gS, c) for c in range(NCH)], writes=[("dbg", 1)], is_output=True)
        S.finish()
        with nc.Block() as block:
            S.replay(block)
        cfg.n_inst = dict(S.n_inst)
    return nc


def pack_prm(depth, ln1, lnm, ln2, lnf, dww, dwb, clg, clb, hlb, hgn):
    off, prows = prm_layout(depth)
    out = np.zeros((prows, 128), np.float32)

    def put(name, arr):
        a = np.ascontiguousarray(arr, dtype=np.float32).reshape(-1, 128)
        out[off[name]:off[name] + a.shape[0]] = a
    put("ln1", ln1); put("lnm", lnm); put("ln2", ln2); put("lnf", lnf)
    put("dww", dww); put("dwb", dwb); put("clg", clg); put("clb", clb); put("hlb", hlb); put("hgn", hgn)
    return out


_CACHE = {}


def run(cfg, inputs):
    key = (cfg.seq, cfg.depth, cfg.nsamp, cfg.n_cores, cfg.nseq, cfg.stages, cfg.debug)
    if key not in _CACHE:
        _CACHE[key] = build_program(cfg)
    nc = _CACHE[key]
    L = cfg.depth
    f32 = lambda a: np.ascontiguousarray(a, dtype=np.float32)
    prm = pack_prm(L, inputs["ln_ffn1"], inputs["ln_mix"], inputs["ln_ffn2"], inputs["ln_final"],
                   inputs["conv_dw_w"], inputs["conv_dw_b"], inputs["conv_ln_g"], inputs["conv_ln_b"],
                   inputs["hg_lower_bounds"], inputs["hg_norm_g"])
    shared = {
        "prm": prm,
        "wg1": f32(inputs["w_ffn1_gate"]), "wu1": f32(inputs["w_ffn1_up"]), "wd1": f32(inputs["w_ffn1_down"]),
        "wg2": f32(inputs["w_ffn2_gate"]), "wu2": f32(inputs["w_ffn2_up"]), "wd2": f32(inputs["w_ffn2_down"]),
        "wi": f32(inputs["w_in"]), "wo": f32(inputs["w_out"]),
    }
    shared.update(const_tables(cfg.nsamp))
    xpf = f32(inputs["x_prompt"])
    xsf = f32(inputs["x_sample"])
    sconv = f32(inputs["state_conv"])
    ck = f32(inputs["cache_k_win"]).reshape(L, -1, WIN, 256)
    cv = f32(inputs["cache_v_win"]).reshape(L, -1, WIN, 256)
    shg = f32(inputs["state_hgrn"])
    in_maps = []
    zero_seq = None
    nb = cfg.nsamp
    for c in range(cfg.n_cores):
        m = dict(shared)
        if c < cfg.nseq:
            m["xp"] = xpf[c]
        else:
            if zero_seq is None:
                zero_seq = np.zeros((cfg.seq, D), np.float32)
            m["xp"] = zero_seq
        sl = slice(c * nb, (c + 1) * nb)
        m["xs"] = np.ascontiguousarray(xsf[sl].reshape(cfg.ns_tok, D))
        m["sconv"] = np.ascontiguousarray(sconv[:, sl])
        m["ck"] = np.ascontiguousarray(ck[:, sl])
        m["cv"] = np.ascontiguousarray(cv[:, sl])
        m["shg"] = np.ascontiguousarray(shg[:, sl])
        in_maps.append(m)
    res = run_bass_kernel_spmd(nc, in_maps, core_ids=list(range(cfg.n_cores)))
    return res.results


def assemble(cfg, r):
    L, nb = cfg.depth, cfg.nsamp
    nsq = cfg.nseq
    y_prompt = np.stack([r[c]["yp"] for c in range(nsq)])
    y_sample = np.concatenate([r[c]["ys"].reshape(nb, DEC_T, D) for c in range(cfg.n_cores)], 0)
    cp = np.stack([r[c]["o_cp"] for c in range(nsq)], 1)
    cs = np.concatenate([r[c]["o_cs"] for c in range(cfg.n_cores)], 1)
    kp = np.stack([r[c]["o_kp"].reshape(L, cfg.keep, ATT_H, HD) for c in range(nsq)], 1)
    vp = np.stack([r[c]["o_vp"].reshape(L, cfg.keep, ATT_H, HD) for c in range(nsq)], 1)
    ks = np.concatenate([r[c]["o_ks"].reshape(L, nb, DEC_T, ATT_H, HD) for c in range(cfg.n_cores)], 1)
    vs = np.concatenate([r[c]["o_vs"].reshape(L, nb, DEC_T, ATT_H, HD) for c in range(cfg.n_cores)], 1)
    hp = np.stack([r[c]["o_hp"] for c in range(nsq)], 1)
    hs = np.concatenate([r[c]["o_hs"] for c in range(cfg.n_cores)], 1)
    outs = (y_prompt, y_sample, cp, cs, kp, vp, ks, vs, hp, hs)
    return tuple(np.ascontiguousarray(o, dtype=np.float32) for o in outs)


def kernel(**inputs):
    cfg = Cfg()
    r = run(cfg, inputs)
    return assemble(cfg, r)
```

```python
import contextlib
import numpy as np
import concourse.bass as bass
import concourse.mybir as mybir
from concourse.bass_utils import run_bass_kernel_spmd

F32 = mybir.dt.float32
BF16 = mybir.dt.bfloat16
AF = mybir.ActivationFunctionType
ALU = mybir.AluOpType

D = 1024
NCH = 8
FFN = 2816
NFC = 22
N_IN = 3328
NIC = 26
CONV_DIM = 256
CW = 31
ATT_H = 4
HD = 64
HG_H = 4
HGC = 64
WIN = 2048
TT = 512
EPS = 1e-6
DEC_T = 4


class Sched:
    COMPUTE = ("pe", "act", "dve", "pool")

    def __init__(self, nc, stack, n_dma_sems=24):
        self.nc = nc
        self.streams = {e: [] for e in ("pe", "act", "dve", "pool", "sp")}
        self.sems = {}
        for e in self.COMPUTE:
            self.sems[e] = stack.enter_context(nc.semaphore("s_" + e))
        self.count = {e: 0 for e in self.COMPUTE}
        self.dma_sems = {}
        self.dma_val = {}
        self.dma_rr = {}
        for q in ("sp", "pool"):
            self.dma_sems[q] = []
            for i in range(n_dma_sems):
                key = "d_%s_%d" % (q, i)
                self.sems[key] = stack.enter_context(nc.semaphore(key))
                self.dma_sems[q].append(key)
                self.dma_val[key] = 0
            self.dma_rr[q] = 0
        self.waited = {}
        self.last_write = {}
        self.readers = {}
        self.out_tokens = []
        self.n_inst = {e: 0 for e in self.streams}

    def _need(self, eng, token, needs):
        if token is None:
            return
        key, val = token
        if self.waited.get((eng, key), 0) >= val:
            return
        if needs.get(key, 0) < val:
            needs[key] = val

    def _collect(self, eng, reads, writes):
        needs = {}
        for t in reads:
            self._need(eng, self.last_write.get(t), needs)
        for t in writes:
            self._need(eng, self.last_write.get(t), needs)
            for tok in self.readers.get(t, ()):
                self._need(eng, tok, needs)
        return needs

    def _emit_waits(self, eng, needs, is_dma=False):
        for key, val in needs.items():
            if key == eng and not is_dma:
                continue
            sem = self.sems[key]
            self.streams[eng].append(lambda e, sem=sem, val=val: e.wait_ge(sem, val))
            self.waited[(eng, key)] = val
            self.n_inst[eng] += 1

    def _commit(self, token, reads, writes):
        for t in reads:
            self.readers.setdefault(t, []).append(token)
        for t in writes:
            self.last_write[t] = token
            self.readers[t] = []

    def op(self, eng, fn, reads=(), writes=()):
        needs = self._collect(eng, reads, writes)
        if eng in needs:
            own = needs.pop(eng)
            val = own if eng != "pe" else 0
            if val > self.waited.get((eng, eng), 0):
                sem = self.sems[eng]
                self.streams[eng].append(lambda e, sem=sem, val=val: e.wait_ge(sem, val))
                self.waited[(eng, eng)] = val
        self._emit_waits(eng, needs)
        self.count[eng] += 1
        token = (eng, self.count[eng])
        sem = self.sems[eng]
        self.streams[eng].append(lambda e, fn=fn, sem=sem: fn(e).then_inc(sem, 1))
        self.n_inst[eng] += 1
        self._commit(token, reads, writes)
        return token

    def dma(self, q, out, in_, reads=(), writes=(), is_output=False):
        needs = self._collect(q, reads, writes)
        rr = self.dma_rr[q]
        self.dma_rr[q] = (rr + 1) % len(self.dma_sems[q])
        key = self.dma_sems[q][rr]
        prev = self.dma_val[key]
        if prev > 0 and self.waited.get((q, key), 0) < prev:
            needs[key] = max(needs.get(key, 0), prev)
        self._emit_waits(q, needs, is_dma=True)
        val = prev + 16
        self.dma_val[key] = val
        sem = self.sems[key]
        self.streams[q].append(lambda e, out=out, in_=in_, sem=sem: e.dma_start(out=out, in_=in_).then_inc(sem, 16))
        self.n_inst[q] += 1
        token = (key, val)
        self._commit(token, reads, writes)
        if is_output:
            self.out_tokens.append(token)
        return token

    def barrier(self):
        for eng in self.streams:
            for x in self.COMPUTE:
                if x != eng and self.count[x] > 0:
                    sem, val = self.sems[x], self.count[x]
                    self.streams[eng].append(lambda e, sem=sem, val=val: e.wait_ge(sem, val))
                    self.waited[(eng, x)] = val
            for key, val in self.dma_val.items():
                if val > 0:
                    sem = self.sems[key]
                    self.streams[eng].append(lambda e, sem=sem, val=val: e.wait_ge(sem, val))
                    self.waited[(eng, key)] = val
        self.last_write = {}
        self.readers = {}

    def finish(self):
        finals = {}
        for key, val in self.out_tokens:
            finals[key] = max(finals.get(key, 0), val)
        for key, val in finals.items():
            sem = self.sems[key]
            self.streams["sp"].append(lambda e, sem=sem, val=val: e.wait_ge(sem, val))

    def replay(self, block):
        nc = self.nc
        streams = self.streams

        @block.sync
        def _(e):
            for f in streams["sp"]:
                f(e)

        @block.tensor
        def _(e):
            for f in streams["pe"]:
                f(e)

        @block.scalar
        def _(e):
            for f in streams["act"]:
                f(e)

        @block.vector
        def _(e):
            for f in streams["dve"]:
                f(e)

        @block.gpsimd
        def _(e):
            for f in streams["pool"]:
                f(e)


class Cfg:
    def __init__(self, seq=8192, depth=4, nsamp=4, n_cores=8, nseq=2, stages=99):
        self.seq = seq
        self.depth = depth
        self.nsamp = nsamp
        self.n_cores = n_cores
        self.nseq = nseq
        self.stages = stages
        self.ntile = seq // TT
        self.ns_tok = nsamp * DEC_T
        self.keep = min(WIN, seq)
        self.debug = False


def prm_layout(depth):
    off = {}
    r = 0
    for name, rows in (("ln1", depth * 8), ("lnm", depth * 8), ("ln2", depth * 8), ("lnf", 8),
                       ("dww", depth * CW * 2), ("dwb", depth * 2), ("clg", depth * 2), ("clb", depth * 2),
                       ("hlb", depth * 4), ("hgn", depth)):
        off[name] = r
        r += rows
    return off, ((r + 127) // 128) * 128


NM = 17


def att_weight_table():
    k = np.arange(128)[:, None, None]
    m = np.arange(NM)[None, :, None]
    q = np.arange(128)[None, None, :]
    d = 128 * m + q - k
    mult = ((d >= 0) & (d <= 128)).astype(np.float64) + ((d >= 0) & (d <= 512) & (d % 4 == 0)) + ((d >= 0) & (d <= 2048) & (d % 16 == 0))
    out = np.zeros((128, NM, ATT_H, 128), np.float32)
    for h in range(ATT_H):
        slope = 2.0 ** (-8.0 * (h + 1) / ATT_H)
        out[:, :, h, :] = mult * np.exp(-slope * np.maximum(d, 0))
    return out


def const_tables(nsamp):
    c = {}
    c["ident"] = np.eye(128, dtype=np.float32)
    c["wtab"] = att_weight_table().reshape(128, NM * ATT_H * 128)
    p = np.arange(128)
    bd = ((p[:, None] // HGC) == (p[None, :] // HGC)) & (p[:, None] <= p[None, :])
    c["bdmask"] = bd.astype(np.float32)
    bds = np.zeros((128, 128), np.float32)
    ns = nsamp * DEC_T
    ps = np.arange(ns)
    bds[:ns, :ns] = (((ps[:, None] // DEC_T) == (ps[None, :] // DEC_T)) & (ps[:, None] <= ps[None, :]))
    c["bdmask_s"] = bds
    rm = np.zeros((128, 8), np.float32)
    for j in range(4):
        rm[:, j] = (p // HGC == j)
        rm[:ns, 4 + j] = (ps // DEC_T == j)
    c["rowmask"] = rm
    rs = np.ones((128, TT), np.float32)
    rs[:, ::HGC] = 0.0
    c["rsmask"] = rs
    rss = np.ones((128, 128), np.float32)
    rss[:, 0:ns:DEC_T] = 0.0
    c["rsmask_s"] = rss
    wsm = np.zeros((128, nsamp, ATT_H, DEC_T), np.float32)
    for kk in range(ns):
        kb, kt = divmod(kk, DEC_T)
        for t in range(DEC_T):
            if t >= kt:
                d = t - kt
                mult = 1 + (d % 4 == 0) + (d % 16 == 0)
                for h in range(ATT_H):
                    slope = 2.0 ** (-8.0 * (h + 1) / ATT_H)
                    wsm[kk, kb, h, t] = mult * np.exp(-slope * d)
    c["wsm"] = wsm.reshape(128, nsamp * ATT_H * DEC_T)
    return c


def build_program(cfg):
    nc = bass.Bass("TRN2", target_bir_lowering=False)
    L = cfg.depth
    SEQ = cfg.seq
    NS = cfg.ns_tok
    NB = cfg.nsamp
    KEEP = cfg.keep
    poff, prows = prm_layout(L)
    STG = cfg.stages

    def din(name, shape, dt=F32):
        return nc.dram_tensor(name, list(shape), dt, kind="ExternalInput").ap()

    def dout(name, shape, dt=F32):
        return nc.dram_tensor(name, list(shape), dt, kind="ExternalOutput").ap()

    def dscr(name, shape, dt=BF16):
        return nc.dram_tensor(name, list(shape), dt, kind="Internal").ap()

    xp = din("xp", [SEQ, D])
    xs = din("xs", [NS, D])
    prm = din("prm", [prows, 128])
    ident_in = din("ident", [128, 128])
    wtab_in = din("wtab", [128, NM * ATT_H * 128])
    bdmask_in = din("bdmask", [128, 128])
    bdmask_s_in = din("bdmask_s", [128, 128])
    rowmask_in = din("rowmask", [128, 8])
    rsmask_in = din("rsmask", [128, TT])
    rsmask_s_in = din("rsmask_s", [128, 128])
    wsm_in = din("wsm", [128, NB * ATT_H * DEC_T])
    sconv_in = din("sconv", [L, NB, CW - 1, CONV_DIM])
    ck_in = din("ck", [L, NB, WIN, 256])
    cv_in = din("cv", [L, NB, WIN, 256])
    shg_in = din("shg", [L, NB, HG_H, 128, 128])
    w_g = [din("wg1", [L, D, FFN]), din("wg2", [L, D, FFN])]
    w_u = [din("wu1", [L, D, FFN]), din("wu2", [L, D, FFN])]
    w_d = [din("wd1", [L, FFN, D]), din("wd2", [L, FFN, D])]
    w_i = din("wi", [L, D, N_IN])
    w_o = din("wo", [L, D, D])

    yp = dout("yp", [SEQ, D])
    ys = dout("ys", [NS, D])
    o_cp = dout("o_cp", [L, CW - 1, CONV_DIM])
    o_cs = dout("o_cs", [L, NB, CW - 1, CONV_DIM])
    o_kp = dout("o_kp", [L, KEEP, 256])
    o_vp = dout("o_vp", [L, KEEP, 256])
    o_ks = dout("o_ks", [L, NS, 256])
    o_vs = dout("o_vs", [L, NS, 256])
    o_hp = dout("o_hp", [L, HG_H, 128, 128])
    o_hs = dout("o_hs", [L, NB, HG_H, 128, 128])

    sG = [dscr("sg%d" % f, [L, NFC, 128, NCH * 128]) for f in range(2)]
    sU = [dscr("su%d" % f, [L, NFC, 128, NCH * 128]) for f in range(2)]
    sD = [dscr("sd%d" % f, [L, NCH, 128, NFC * 128]) for f in range(2)]
    sI = dscr("si", [L, NIC, 128, NCH * 128])
    sO = dscr("so", [L, NCH, 128, NCH * 128])
    sK = dscr("sk", [L, 256, SEQ])
    sV = dscr("sv", [L, SEQ, ATT_H * 128])

    stack = contextlib.ExitStack()
    with stack:
        S = Sched(nc, stack)

        def sb(name, shape, dt=F32):
            return stack.enter_context(nc.sbuf_tensor(name, list(shape), dt))

        class B:
            def __init__(self, ap, tags):
                self.ap, self.tags = ap, list(tags)

        NSTG = 8
        with contextlib.ExitStack() as pstack:
            stg_f = [pstack.enter_context(nc.sbuf_tensor("stgf%d" % i, [128, NFC * 128], F32)) for i in range(NSTG)]
            stg_b = [pstack.enter_context(nc.sbuf_tensor("stgb%d" % i, [128, NFC * 128], BF16)) for i in range(NSTG)]
            cast_rr = [0]

            def convert(src2d, K, c, dst_unit, dtag):
                kc = K // 128
                i = cast_rr[0]
                cast_rr[0] += 1
                s = i % NSTG
                srcv = src2d[:, c * 128:(c + 1) * 128].rearrange("(kc p) j -> p kc j", p=128)
                S.dma("sp", stg_f[s][:, 0:kc * 128].rearrange("p (kc j) -> p kc j", j=128), srcv, writes=[("stgf", s)])
                eng = ("dve", "act", "pool")[i % 3]
                if eng == "act":
                    fn = lambda e, s=s, kc=kc: e.activation(out=stg_b[s][:, 0:kc * 128], in_=stg_f[s][:, 0:kc * 128], func=AF.Copy)
                else:
                    fn = lambda e, s=s, kc=kc: e.tensor_copy(out=stg_b[s][:, 0:kc * 128], in_=stg_f[s][:, 0:kc * 128])
                S.op(eng, fn, reads=[("stgf", s)], writes=[("stgb", s)])
                S.dma("pool", dst_unit, stg_b[s][:, 0:kc * 128], reads=[("stgb", s)], writes=[dtag])

            for l in range(L):
                for f in range(2):
                    if f == 1 and STG < 9:
                        continue
                    for c in range(NFC):
                        convert(w_g[f][l], D, c, sG[f][l, c], ("sG", f, l, c))
                        convert(w_u[f][l], D, c, sU[f][l, c], ("sU", f, l, c))
                    for c in range(NCH):
                        convert(w_d[f][l], FFN, c, sD[f][l, c], ("sD", f, l, c))
                if STG >= 2:
                    for c in range(NIC):
                        convert(w_i[l], D, c, sI[l, c], ("sI", l, c))
                    for c in range(NCH):
                        convert(w_o[l], D, c, sO[l, c], ("sO", l, c))
            S.barrier()

        ARK = 52
        arena = sb("arena", [128, ARK * 512], BF16)

        def av(name, off_kb, kb, dt=BF16, pat=None, **dims):
            e0 = int(round(off_kb * 512))
            ne = int(round(kb * 512))
            ap = arena[:, e0:e0 + ne]
            if dt == F32:
                ap = ap.bitcast(F32)
            if pat is not None:
                ap = ap.rearrange(pat, **dims)
            k0 = int(np.floor(off_kb + 1e-9))
            k1 = int(np.ceil(off_kb + kb - 1e-9))
            return B(ap, [("ar", k) for k in range(k0, k1)])

        xT = sb("xT", [128, NCH, TT])
        hT = sb("hT", [128, NCH, TT], BF16)
        ymix = sb("ymix", [128, NCH, TT], BF16)
        xTs = sb("xTs", [128, NCH, NS])
        hTs = sb("hTs", [128, NCH, 128], BF16)
        hids = sb("hids", [128, NFC, NS], BF16)
        ymixs = sb("ymixs", [128, NCH, NS], BF16)
        rstd = sb("rstd", [128, TT])
        sgb = [sb("sgb%d" % i, [128, TT]) for i in range(2)]
        prmT = sb("prmT", [128, prows])
        ident = sb("identf", [128, 128])
        identb = sb("identb", [128, 128], BF16)
        ones_b = sb("ones_b", [128, 128], BF16)
        eps_col = sb("eps_col", [128, 1])
        NB8, NB22 = 6, 2
        wb8 = [sb("wb8_%d" % i, [128, NCH * 128], BF16) for i in range(NB8)]
        wb22 = [sb("wb22_%d" % i, [128, NFC * 128], BF16) for i in range(NB22)]
        kwin = sb("kwin", [128, 2, WIN + TT], BF16)
        vwin = sb("vwin", [128, (WIN + TT) // 128, ATT_H, 128], BF16)
        wmask = sb("wmask", [128, NM, ATT_H, 128], BF16)
        Sst = sb("Sst", [128, L, HG_H, 128])
        utail = sb("utail", [128, L, 2, CW - 1], BF16)
        ubuf = sb("ubuf", [128, 2, CW - 1 + TT], BF16)
        ubufs = sb("ubufs", [128, 2, NB, CW - 1 + DEC_T], BF16)
        bdmask = sb("bdmask_t", [128, 128], BF16)
        bdmask_s = sb("bdmasks_t", [128, 128], BF16)
        rowmask = sb("rowmask_t", [128, 8])
        rsmask = sb("rsmask_t", [128, TT])
        rsmask_s = sb("rsmasks_t", [128, 128])
        wsm = sb("wsm_t", [128, NB, ATT_H, DEC_T], BF16)
        lbT = sb("lbT", [128, L * HG_H])
        omlT = sb("omlT", [128, L * HG_H])
        lbtmp = sb("lbtmp", [128, 2 * L * HG_H + 2 * HG_H])

        hidc = [av("hid", c, 1.0) for c in range(NFC)]
        finc = [av("fin", 2 * c, 2.0, F32) for c in range(NCH)]
        sqc = [av("sq", 36 + c, 1.0) for c in range(NCH)]
        iobuf = [av("io", 44 + 4 * i, 4.0, F32) for i in range(2)]

        psum = [stack.enter_context(nc.psum_tensor("ps%d" % i, [128, 512], F32)) for i in range(8)]
        ps_rr = [0]
        NPR = 6

        NPRv = [NPR]

        def ps_next():
            b = ps_rr[0] % NPRv[0]
            ps_rr[0] = (b + 1) % NPRv[0]
            return b

        def PT(b):
            return ("ps", b)

        S.dma("sp", ident[:], ident_in[:], writes=[("ident",)])
        S.op("dve", lambda e: e.tensor_copy(out=identb[:], in_=ident[:]), reads=[("ident",)], writes=[("identb",)])
        S.op("pool", lambda e: e.memset(ones_b[:], 1.0), writes=[("ones",)])
        S.op("pool", lambda e: e.memset(eps_col[:], EPS), writes=[("eps",)])
        S.op("pool", lambda e: e.memset(hTs[:], 0.0), writes=[("S", "h", c) for c in range(NCH)])
        S.op("pool", lambda e: e.memset(vwin[:], 1.0), writes=[("vwin", "hist"), ("vwin", "cur")])
        S.op("pool", lambda e: e.memset(Sst[:], 0.0), writes=[("Sst", l, h) for l in range(L) for h in range(HG_H)])
        S.op("pool", lambda e: e.memset(utail[:], 0.0), writes=[("utail", l) for l in range(L)])
        S.dma("sp", rowmask[:], rowmask_in[:], writes=[("rowmask",)])
        S.dma("sp", rsmask[:], rsmask_in[:], writes=[("rsmask",)])
        S.dma("sp", rsmask_s[:], rsmask_s_in[:], writes=[("rsmask_s",)])
        for blk in range(prows // 128):
            io = iobuf[blk % 2]
            S.dma("sp", io.ap[:, 0:128], prm[blk * 128:(blk + 1) * 128, :], writes=io.tags)
            b = ps_next()
            S.op("pe", lambda e, io=io, b=b: e.transpose(psum[b][:, 0:128], io.ap[:, 0:128], ident[:]),
                 reads=io.tags + [("ident",)], writes=[PT(b)])
            S.op("act", lambda e, b=b, blk=blk: e.activation(out=prmT[:, blk * 128:(blk + 1) * 128], in_=psum[b][:, 0:128], func=AF.Copy),
                 reads=[PT(b)], writes=[("prmT",)])
        for src, dst, tag in ((bdmask_in, bdmask, "bdmask"), (bdmask_s_in, bdmask_s, "bdmask_s")):
            io = iobuf[0]
            S.dma("sp", io.ap[:, 0:128], src[:], writes=io.tags)
            S.op("dve", lambda e, io=io, dst=dst: e.tensor_copy(out=dst[:], in_=io.ap[:, 0:128]), reads=io.tags, writes=[(tag,)])
        io = iobuf[1]
        nws = NB * ATT_H * DEC_T
        S.dma("sp", io.ap[:, 0:nws], wsm_in[:], writes=io.tags)
        S.op("dve", lambda e, io=io: e.tensor_copy(out=wsm[:].rearrange("p b h t -> p (b h t)"), in_=io.ap[:, 0:nws]), reads=io.tags, writes=[("wsm",)])
        if STG >= 3:
            wflat = wmask[:].rearrange("p m h q -> p (m h q)")
            tot = NM * ATT_H * 128
            for i, c0 in enumerate(range(0, tot, 1024)):
                io = iobuf[i % 2]
                n = min(1024, tot - c0)
                S.dma("sp", io.ap[:, 0:n], wtab_in[:, c0:c0 + n], writes=io.tags)
                eng = "dve" if i % 2 == 0 else "pool"
                S.op(eng, lambda e, io=io, c0=c0, n=n: e.tensor_copy(out=wflat[:, c0:c0 + n], in_=io.ap[:, 0:n]), reads=io.tags, writes=[("wmask",)])

        def pcol(name, idx):
            c = poff[name] + idx
            return prmT[:, c:c + 1]

        if STG >= 4:
            nlh = L * HG_H
            ex = lbtmp[:, 0:nlh]
            sm = lbtmp[:, nlh:2 * nlh]
            tot = lbtmp[:, 2 * nlh:2 * nlh + HG_H]
            rc = lbtmp[:, 2 * nlh + HG_H:2 * nlh + 2 * HG_H]
            r0 = poff["hlb"]
            S.op("act", lambda e: e.activation(out=ex, in_=prmT[:, r0:r0 + nlh], func=AF.Exp), reads=[("prmT",)], writes=[("lbtmp",)])
            S.op("dve", lambda e: e.tensor_copy(out=tot, in_=ex[:, 0:HG_H]), reads=[("lbtmp",)], writes=[("lbtmp",)])
            for l in range(1, L):
                S.op("dve", lambda e, l=l: e.tensor_tensor(out=tot, in0=tot, in1=ex[:, l * HG_H:(l + 1) * HG_H], op=ALU.add),
                     reads=[("lbtmp",)], writes=[("lbtmp",)])
            S.op("dve", lambda e: e.reciprocal(out=rc, in_=tot), reads=[("lbtmp",)], writes=[("lbtmp",)])
            for l in range(L):
                S.op("dve", lambda e, l=l: e.tensor_tensor(out=sm[:, l * HG_H:(l + 1) * HG_H], in0=ex[:, l * HG_H:(l + 1) * HG_H], in1=rc, op=ALU.mult),
                     reads=[("lbtmp",)], writes=[("lbtmp",)])
            S.op("dve", lambda e: e.memset(lbT[:, 0:HG_H], 0.0), writes=[("lbT",)])
            for l in range(1, L):
                S.op("dve", lambda e, l=l: e.tensor_tensor(out=lbT[:, l * HG_H:(l + 1) * HG_H], in0=lbT[:, (l - 1) * HG_H:l * HG_H],
                                                           in1=sm[:, l * HG_H:(l + 1) * HG_H], op=ALU.add),
                     reads=[("lbtmp",), ("lbT",)], writes=[("lbT",)])
            S.op("dve", lambda e: e.tensor_scalar(out=omlT[:], in0=lbT[:], scalar1=-1.0, scalar2=1.0, op0=ALU.mult, op1=ALU.add),
                 reads=[("lbT",)], writes=[("omlT",)])

        class WStream:
            def __init__(self):
                self.plan = []
                self.nload = 0
                self.ncons = 0
                self.cls_idx = {8: 0, 22: 0}
                self.slot_of = []
                self.prev_user = []
                self.slot_last = {}

            def add(self, uid, ap, cls, dtag):
                k = self.cls_idx[cls]
                self.cls_idx[cls] += 1
                nb = NB8 if cls == 8 else NB22
                slot = (cls, k % nb)
                self.prev_user.append(self.slot_last.get(slot, -1))
                self.slot_last[slot] = len(self.plan)
                self.slot_of.append(slot)
                self.plan.append((uid, ap, cls, dtag))

            def _buf(self, slot):
                cls, i = slot
                return (wb8 if cls == 8 else wb22)[i]

            def consume(self, uid):
                i = self.ncons
                assert self.plan[i][0] == uid, (self.plan[i][0], uid)
                while self.nload < len(self.plan) and self.nload <= i + 5 and (self.prev_user[self.nload] < 0 or self.prev_user[self.nload] <= i - 2):
                    j = self.nload
                    _, ap, cls, dtag = self.plan[j]
                    slot = self.slot_of[j]
                    S.dma("sp", self._buf(slot)[:], ap, reads=[dtag], writes=[("wb",) + slot])
                    self.nload += 1
                assert self.nload > i, (self.nload, i)
                self.ncons += 1
                slot = self.slot_of[i]
                return self._buf(slot), ("wb",) + slot

        WS = WStream()
        FM_CONV = [0, 1, 2, 3]
        FM_QKV = [4, 5, 6, 7, 8, 9]
        HG_ORDER = []
        for _h in range(HG_H):
            HG_ORDER += [18 + _h, 10 + _h, 14 + _h, 22 + _h]

        def plan_ffn(f, l):
            for c in range(NFC):
                WS.add(("g", f, l, c), sG[f][l, c], 8, ("sG", f, l, c))
                WS.add(("u", f, l, c), sU[f][l, c], 8, ("sU", f, l, c))
            for c in range(NCH):
                WS.add(("d", f, l, c), sD[f][l, c], 22, ("sD", f, l, c))

        def plan_layer(l):
            plan_ffn(0, l)
            if STG >= 2:
                for c in FM_CONV + FM_QKV + HG_ORDER:
                    WS.add(("i", l, c), sI[l, c], 8, ("sI", l, c))
                for c in range(NCH):
                    WS.add(("o", l, c), sO[l, c], 8, ("sO", l, c))
            if STG >= 9:
                plan_ffn(1, l)

        for j in range(cfg.ntile):
            for l in range(L):
                plan_layer(l)

        class Grp:
            pass

        gP = Grp()
        gP.name, gP.n, gP.x, gP.h, gP.ym = "P", TT, xT, hT, ymix
        gP.hd = [hc.ap for hc in hidc]
        gP.hdt = [hc.tags for hc in hidc]
        gS = Grp()
        gS.name, gS.n, gS.x, gS.h, gS.ym = "S", NS, xTs, hTs, ymixs
        gS.hd = [hids[:, c, :] for c in range(NFC)]
        gS.hdt = [[("S", "hd", c)] for c in range(NFC)]

        def xtag(g, c):
            return (g.name, "x", c)

        def htag(g, c):
            return (g.name, "h", c)

        def ytag(g, c):
            return (g.name, "ym", c)

        def sumsq_rstd(g, srcs, src_tags, nchunks, denom):
            n = g.n
            for c in range(nchunks):
                S.op("act", lambda e, c=c: e.activation(out=sqc[c].ap[:, 0:n], in_=srcs[c], func=AF.Square),
                     reads=src_tags[c], writes=sqc[c].tags)
            b = ps_next()
            for c in range(nchunks):
                S.op("pe", lambda e, c=c: e.matmul(psum[b][:, 0:n], lhsT=ones_b[:], rhs=sqc[c].ap[:, 0:n], start=(c == 0), stop=(c == nchunks - 1)),
                     reads=sqc[c].tags + [("ones",)], writes=[PT(b)])
            S.op("act", lambda e: e.activation(out=rstd[:, 0:n], in_=psum[b][:, 0:n], func=AF.Ln, scale=1.0 / denom, bias=eps_col[:]),
                 reads=[PT(b), ("eps",)], writes=[("rstd",)])
            S.op("act", lambda e: e.activation(out=rstd[:, 0:n], in_=rstd[:, 0:n], func=AF.Exp, scale=-0.5),
                 reads=[("rstd",)], writes=[("rstd",)])

        def rmsnorm(g, gain_name, gain_idx0):
            n = g.n
            sumsq_rstd(g, [g.x[:, c, 0:n] for c in range(NCH)], [[xtag(g, c)] for c in range(NCH)], NCH, D)
            for c in range(NCH):
                S.op("dve", lambda e, c=c: e.scalar_tensor_tensor(out=g.h[:, c, 0:n], in0=g.x[:, c, 0:n],
                                                                  scalar=pcol(gain_name, gain_idx0 + c), in1=rstd[:, 0:n],
                                                                  op0=ALU.mult, op1=ALU.mult),
                     reads=[xtag(g, c), ("rstd",), ("prmT",)], writes=[htag(g, c)])

        def proj_fm(unit, utag, kch, rhs_fn, rhs_tags, n, b):
            def mm(e):
                ins = None
                for k in range(kch):
                    ins = e.matmul(psum[b][:, 0:n], lhsT=unit[:, k * 128:(k + 1) * 128], rhs=rhs_fn(k),
                                   start=(k == 0), stop=(k == kch - 1))
                return ins
            S.op("pe", mm, reads=[utag] + rhs_tags, writes=[PT(b)])

        def ffn(groups, f, l):
            sg_i = 0
            for c in range(NFC):
                ug, tg = WS.consume(("g", f, l, c))
                uu, tu = WS.consume(("u", f, l, c))
                for g in groups:
                    n = g.n
                    bg, bu = ps_next(), ps_next()
                    htags = [htag(g, k) for k in range(NCH)]
                    proj_fm(ug, tg, NCH, lambda k, g=g, n=n: g.h[:, k, 0:n], htags, n, bg)
                    proj_fm(uu, tu, NCH, lambda k, g=g, n=n: g.h[:, k, 0:n], htags, n, bu)
                    si = sg_i % 2
                    sg_i += 1
                    S.op("act", lambda e, si=si, bg=bg, n=n: e.activation(out=sgb[si][:, 0:n], in_=psum[bg][:, 0:n], func=AF.Silu),
                         reads=[PT(bg)], writes=[("sgb", si)])
                    S.op("dve", lambda e, si=si, bu=bu, n=n, g=g, c=c: e.tensor_tensor(out=g.hd[c][:, 0:n], in0=sgb[si][:, 0:n], in1=psum[bu][:, 0:n], op=ALU.mult),
                         reads=[("sgb", si), PT(bu)], writes=g.hdt[c])
            for oc in range(NCH):
                ud, td = WS.consume(("d", f, l, oc))
                for g in groups:
                    n = g.n
                    b = ps_next()
                    proj_fm(ud, td, NFC, lambda k, g=g, n=n: g.hd[k][:, 0:n], sum([g.hdt[k] for k in range(NFC)], []), n, b)
                    S.op("dve", lambda e, g=g, n=n, b=b, oc=oc: e.scalar_tensor_tensor(out=g.x[:, oc, 0:n], in0=psum[b][:, 0:n], scalar=0.5,
                                                                                      in1=g.x[:, oc, 0:n], op0=ALU.mult, op1=ALU.add),
                         reads=[PT(b), xtag(g, oc)], writes=[xtag(g, oc)])

        def load_tokens(g, src_rows, blocks):
            for bi, (r0, nr, c0) in enumerate(blocks):
                io = iobuf[bi % 2]
                S.dma("sp", io.ap[0:nr, :], src_rows[r0:r0 + nr, :], writes=io.tags)
                for c in range(NCH):
                    b = ps_next()
                    S.op("pe", lambda e, io=io, nr=nr, c=c, b=b: e.transpose(psum[b][:, 0:nr], io.ap[0:nr, c * 128:(c + 1) * 128], ident[0:nr, 0:nr]),
                         reads=io.tags + [("ident",)], writes=[PT(b)])
                    if c % 2 == 0:
                        S.op("act", lambda e, b=b, c=c, nr=nr, c0=c0: e.activation(out=g.x[:, c, c0:c0 + nr], in_=psum[b][:, 0:nr], func=AF.Copy),
                             reads=[PT(b)], writes=[xtag(g, c)])
                    else:
                        S.op("dve", lambda e, b=b, c=c, nr=nr, c0=c0: e.tensor_copy(out=g.x[:, c, c0:c0 + nr], in_=psum[b][:, 0:nr]),
                             reads=[PT(b)], writes=[xtag(g, c)])

        def final_norm_store(g, dst_rows, blocks):
            n = g.n
            sumsq_rstd(g, [g.x[:, c, 0:n] for c in range(NCH)], [[xtag(g, c)] for c in range(NCH)], NCH, D)
            for c in range(NCH):
                S.op("dve", lambda e, c=c: e.scalar_tensor_tensor(out=finc[c].ap[:, 0:n], in0=g.x[:, c, 0:n],
                                                                  scalar=pcol("lnf", c), in1=rstd[:, 0:n],
                                                                  op0=ALU.mult, op1=ALU.mult),
                     reads=[xtag(g, c), ("rstd",), ("prmT",)], writes=finc[c].tags)
            for bi, (r0, nr, c0) in enumerate(blocks):
                io = iobuf[bi % 2]
                for half in range(2):
                    b = ps_next()

                    def tr(e, half=half, b=b, nr=nr, c0=c0):
                        ins = None
                        for cc in range(4):
                            ins = e.transpose(psum[b][0:nr, cc * 128:(cc + 1) * 128], finc[half * 4 + cc].ap[:, c0:c0 + nr], ident[:])
                        return ins
                    S.op("pe", tr, reads=sum([finc[half * 4 + cc].tags for cc in range(4)], []) + [("ident",)], writes=[PT(b)])
                    if half == 0:
                        S.op("act", lambda e, b=b, nr=nr, io=io: e.activation(out=io.ap[0:nr, 0:512], in_=psum[b][0:nr, :], func=AF.Copy),
                             reads=[PT(b)], writes=io.tags)
                    else:
                        S.op("dve", lambda e, b=b, nr=nr, io=io: e.tensor_copy(out=io.ap[0:nr, 512:1024], in_=psum[b][0:nr, :]),
                             reads=[PT(b)], writes=io.tags)
                S.dma("pool", dst_rows[r0:r0 + nr, :], io.ap[0:nr, :], reads=io.tags, writes=[("out", g.name, r0)], is_output=True)

        def mk_ws(g):
            w = Grp()
            n = g.n
            if g is gP:
                w.cva = av("cva", 0, 4, F32, "p (c t) -> p c t", c=2)
                w.cvs = av("cvs", 4, 4, F32, "p (c t) -> p c t", c=2)
                w.cvy = av("cvy", 8, 4, F32, "p (c t) -> p c t", c=2)
                w.diag = av("diag", 14, 16, BF16, "p (w j) -> p w j", j=128)
                w.qz = av("qz", 30, 4, BF16, "p (h t) -> p h t", h=4)
                w.Eb = [av("Eb%d" % i, 34 + i, 1) for i in range(4)]
                w.Pb = [av("Pb%d" % i, 38 + i, 1) for i in range(4)]
                w.lnd = [av("lnd%d" % i, 42 + 0.5 * i, 0.5, F32) for i in range(2)]
                hp = Grp()
                for k, nm in enumerate(("q", "f", "lg", "kk", "eb", "en")):
                    setattr(hp, nm, av("h" + nm, 2 * k, 2, F32))
                w.hp = hp
                w.live = []
                for h in range(HG_H):
                    lv = Grp()
                    base = 12 + 5 * h
                    lv.qt = av("hqt%d" % h, base, 1)
                    lv.kt = av("hkt%d" % h, base + 1, 1)
                    lv.kh = av("hkh%d" % h, base + 2, 2, F32)
                    lv.gt = av("hgt%d" % h, base + 4, 1)
                    ebl = sb("ebl%d" % h, [128, n // HGC])
                    lv.ebl = B(ebl[:], [("ebl", h)])
                    emid = sb("emid%d" % h, [128, n // HGC])
                    lv.emid = B(emid[:], [("emid", h)])
                    w.live.append(lv)
                w.vtok = av("vtok", 32, 4, BF16, "p (b c) -> p b c", b=4)
                w.khm = [av("khm%d" % i, 36 + i, 1, BF16, "p (j d) -> p j d", j=4) for i in range(4)]
                w.Am = [av("Am%d" % i, 40 + 0.25 * i, 0.25) for i in range(4)]
                w.Sb = [[av("Sb%d_%d" % (h, i), 41 + 0.25 * (h * 5 + i), 0.25) for i in range(5)] for h in range(HG_H)]
                w.osq = [av("osq%d" % h, 3 * h + 2, 1) for h in range(HG_H)]
                w.ot = [av("ot%d" % h, 3 * h, 2, F32) for h in range(HG_H)]
                w.Tb = []
                for h in range(HG_H):
                    tt_ = sb("Tb%d" % h, [128, 2, 128])
                    w.Tb.append([B(tt_[:, i, :], [("Tb", h, i)]) for i in range(2)])
            else:
                def t(nm, shape, dt=F32):
                    tt = sb("s_" + nm, shape, dt)
                    return B(tt[:], [("sw", nm)])
                w.cva = t("cva", [128, 2, n]); w.cvs = t("cvs", [128, 2, n]); w.cvy = t("cvy", [128, 2, n])
                w.diag = None
                w.qz = t("qz", [128, 4, n], BF16)
                w.Eb = [t("Eb%d" % i, [128, 64 + 16], BF16) for i in range(2)]
                w.Pb = [t("Pb%d" % i, [128, 64 + 16], BF16) for i in range(2)]
                w.lnd = [t("lnd%d" % i, [128, 16]) for i in range(2)]
                w.kvout = [av("kvo_s", 12, 2, F32)]
                hp = Grp()
                for nm in ("q", "f", "lg", "kk", "eb", "en"):
                    setattr(hp, nm, t("h" + nm, [128, n]))
                w.hp = hp
                w.live = []
                for h in range(HG_H):
                    lv = Grp()
                    lv.qt = t("hqt%d" % h, [128, n], BF16)
                    lv.kt = t("hkt%d" % h, [128, 128], BF16)
                    lv.kh = t("hkh%d" % h, [128, n])
                    lv.gt = t("hgt%d" % h, [128, n], BF16)
                    lv.ebl = t("ebl%d" % h, [128, n // DEC_T])
                    lv.emid = t("emid%d" % h, [128, n // DEC_T])
                    w.live.append(lv)
                w.vtok = t("vtok", [128, 1, 512], BF16)
                w.khm = [t("khm%d" % i, [128, 4, 128], BF16) for i in range(4)]
                w.Am = [t("Am%d" % i, [128, 128], BF16) for i in range(4)]
                w.Sb = [[t("Sb%d_%d" % (h, i), [128, 128], BF16) for i in range(5)] for h in range(HG_H)]
                w.osq = [t("osq%d" % h, [128, n], BF16) for h in range(HG_H)]
                w.ot = [t("ot%d" % h, [128, n]) for h in range(HG_H)]
                w.shs = av("shs", 0, 8, F32, "p (b h v) -> p b h v", b=NB, h=HG_H)
                w.kTs = t("kTs", [128, 2, 128], BF16)
                w.vaug = t("vaug", [128, ATT_H, 128], BF16)
                w.ufp = t("ufp", [128, 256])
            return w

        wsP = mk_ws(gP)
        wsS = mk_ws(gS)
        gP.ws, gS.ws = wsP, wsS
        S.op("pool", lambda e: e.memset(wsP.qz.ap, 0.0), writes=wsP.qz.tags)
        S.op("pool", lambda e: e.memset(wsS.qz.ap, 0.0), writes=wsS.qz.tags)
        S.op("pool", lambda e: e.memset(wsS.vaug.ap, 1.0), writes=wsS.vaug.tags)
        S.op("pool", lambda e: e.memset(wsS.kTs.ap, 0.0), writes=wsS.kTs.tags)
        S.op("pool", lambda e: e.memset(wsS.vtok.ap, 0.0), writes=wsS.vtok.tags)
        for _i in range(4):
            S.op("pool", lambda e, _i=_i: e.memset(wsS.khm[_i].ap, 0.0), writes=wsS.khm[_i].tags)
            S.op("pool", lambda e, _i=_i: e.memset(wsS.live[_i].kt.ap, 0.0), writes=wsS.live[_i].kt.tags)

        acc_rr = [0]

        def acc_bank():
            b = 6 + acc_rr[0]
            acc_rr[0] ^= 1
            return b

        def win_chunk_fm(g, l, c, unit, utag):
            n = g.n
            b = ps_next()
            proj_fm(unit, utag, NCH, lambda k: g.h[:, k, 0:n], [htag(g, k) for k in range(NCH)], n, b)
            return b

        def win_chunk_tm(g, unit, utag):
            n = g.n
            bs = 128 if g is gP else n
            nblk = n // bs
            b = ps_next()

            def mm(e):
                ins = None
                for tb in range(nblk):
                    for k in range(NCH):
                        ins = e.matmul(psum[b][0:128, tb * 128:(tb + 1) * 128], lhsT=g.h[:, k, tb * bs:tb * bs + 128], rhs=unit[:, k * 128:(k + 1) * 128],
                                       start=(k == 0), stop=(k == NCH - 1))
                return ins
            S.op("pe", mm, reads=[utag] + [htag(g, k) for k in range(NCH)], writes=[PT(b)])
            return b, bs, nblk

        def conv_evac(g, c, b):
            n, w = g.n, g.ws
            if c < 2:
                S.op("act", lambda e: e.activation(out=w.cva.ap[:, c, 0:n], in_=psum[b][:, 0:n], func=AF.Copy), reads=[PT(b)], writes=w.cva.tags)
            else:
                S.op("act", lambda e: e.activation(out=w.cvs.ap[:, c - 2, 0:n], in_=psum[b][:, 0:n], func=AF.Sigmoid), reads=[PT(b)], writes=w.cvs.tags)

        def conv_glu(g):
            n, w = g.n, g.ws
            S.op("dve", lambda e: e.tensor_tensor(out=w.cva.ap[:, :, 0:n], in0=w.cva.ap[:, :, 0:n], in1=w.cvs.ap[:, :, 0:n], op=ALU.mult),
                 reads=w.cva.tags + w.cvs.tags, writes=w.cva.tags)

        def build_diag(l):
            w = wsP
            for wi in range(CW):
                for ch in range(2):
                    col = pcol("dww", l * CW * 2 + wi * 2 + ch)
                    if False:
                        pass
                    else:
                        S.op("dve", lambda e, wi=wi, ch=ch, col=col: e.tensor_scalar(out=w.diag.ap[:, wi * 2 + ch, :], in0=identb[:], scalar1=col, scalar2=None, op0=ALU.mult),
                             reads=[("identb",), ("prmT",)], writes=w.diag.tags)

        def conv_main(g, l, j):
            n, w = g.n, g.ws
            dg = wsP.diag
            if g is gP:
                S.op("pool", lambda e: e.tensor_copy(out=ubuf[:, :, 0:CW - 1], in_=utail[:, l, :, :]), reads=[("utail", l)], writes=[("ubuf",)])
                S.op("act", lambda e: e.activation(out=ubuf[:, :, CW - 1:CW - 1 + n], in_=w.cva.ap[:, :, 0:n], func=AF.Copy), reads=w.cva.tags, writes=[("ubuf",)])
                rhs = lambda ch, wi: ubuf[:, ch, wi:wi + n]
                outv = lambda b: psum[b][:, 0:n]
                utags = [("ubuf",)]
            else:
                for bb in range(NB):
                    io = iobuf[bb % 2]
                    S.dma("sp", io.ap[0:CW - 1, 0:256], sconv_in[l, bb], writes=io.tags)
                    for ch in range(2):
                        b = ps_next()
                        S.op("pe", lambda e, io=io, ch=ch, b=b: e.transpose(psum[b][:, 0:CW - 1], io.ap[0:CW - 1, ch * 128:(ch + 1) * 128], ident[0:CW - 1, 0:CW - 1]),
                             reads=io.tags + [("ident",)], writes=[PT(b)])
                        S.op("dve", lambda e, ch=ch, bb=bb, b=b: e.tensor_copy(out=ubufs[:, ch, bb, 0:CW - 1], in_=psum[b][:, 0:CW - 1]), reads=[PT(b)], writes=[("ubufs",)])
                    S.dma("pool", o_cs[l, bb, 0:CW - 1 - DEC_T, :], sconv_in[l, bb, DEC_T:CW - 1, :], writes=[("o_cs", l, bb, 0)], is_output=True)
                S.op("act", lambda e: e.activation(out=ubufs[:, :, :, CW - 1:CW - 1 + DEC_T], in_=w.cva.ap.rearrange("p c (b t) -> p c b t", t=DEC_T), func=AF.Copy),
                     reads=w.cva.tags, writes=[("ubufs",)])
                rhs = lambda ch, wi: ubufs[:, ch, :, wi:wi + DEC_T]
                outv = lambda b: psum[b][:, 0:n].rearrange("p (b t) -> p b t", t=DEC_T)
                utags = [("ubufs",)]
            banks = []
            for ch in range(2):
                b = ps_next()
                banks.append(b)

                def mm(e, ch=ch, b=b):
                    ins = None
                    for wi in range(CW):
                        ins = e.matmul(outv(b), lhsT=dg.ap[:, wi * 2 + ch, :], rhs=rhs(ch, wi), start=(wi == 0), stop=(wi == CW - 1))
                    return ins
                S.op("pe", mm, reads=dg.tags + utags, writes=[PT(b)])
                S.op("act", lambda e, ch=ch, b=b: e.activation(out=w.cvy.ap[:, ch, 0:n], in_=psum[b][:, 0:n], func=AF.Identity, bias=pcol("dwb", l * 2 + ch)),
                     reads=[PT(b), ("prmT",)], writes=w.cvy.tags)
            if g is gP:
                S.op("pool", lambda e: e.tensor_copy(out=utail[:, l, :, :], in_=ubuf[:, :, n:n + CW - 1]), reads=[("ubuf",)], writes=[("utail", l)])
            for ch in range(2):
                S.op("dve", lambda e, ch=ch: e.tensor_copy(out=sqc[ch].ap[:, 0:n], in_=w.cvy.ap[:, ch, 0:n]), reads=w.cvy.tags, writes=sqc[ch].tags)
            b = ps_next()

            def mm1(e):
                ins = None
                for ch in range(2):
                    ins = e.matmul(psum[b][:, 0:n], lhsT=ones_b[:], rhs=sqc[ch].ap[:, 0:n], start=(ch == 0), stop=(ch == 1))
                return ins
            S.op("pe", mm1, reads=sqc[0].tags + sqc[1].tags + [("ones",)], writes=[PT(b)])
            for ch in range(2):
                S.op("dve", lambda e, ch=ch: e.scalar_tensor_tensor(out=w.cvy.ap[:, ch, 0:n], in0=psum[b][:, 0:n], scalar=-1.0 / CONV_DIM,
                                                                   in1=w.cvy.ap[:, ch, 0:n], op0=ALU.mult, op1=ALU.add),
                     reads=[PT(b)] + w.cvy.tags, writes=w.cvy.tags)
            sumsq_rstd(g, [w.cvy.ap[:, ch, 0:n] for ch in range(2)], [w.cvy.tags, w.cvy.tags], 2, CONV_DIM)
            for ch in range(2):
                S.op("dve", lambda e, ch=ch: e.tensor_tensor(out=w.cvy.ap[:, ch, 0:n], in0=w.cvy.ap[:, ch, 0:n], in1=rstd[:, 0:n], op=ALU.mult),
                     reads=w.cvy.tags + [("rstd",)], writes=w.cvy.tags)
                S.op("act", lambda e, ch=ch: e.activation(out=g.ym[:, ch, 0:n], in_=w.cvy.ap[:, ch, 0:n], func=AF.Silu,
                                                          scale=pcol("clg", l * 2 + ch), bias=pcol("clb", l * 2 + ch)),
                     reads=w.cvy.tags + [("prmT",)], writes=[ytag(g, ch)])
            if g is gP and j == cfg.ntile - 1:
                io = iobuf[0]
                for ch in range(2):
                    b2 = ps_next()
                    S.op("pe", lambda e, ch=ch, b2=b2: e.transpose(psum[b2][0:CW - 1, 0:128], w.cva.ap[:, ch, n - (CW - 1):n], ident[:]),
                         reads=w.cva.tags + [("ident",)], writes=[PT(b2)])
                    S.op("dve", lambda e, ch=ch, b2=b2, io=io: e.tensor_copy(out=io.ap[0:CW - 1, ch * 128:(ch + 1) * 128], in_=psum[b2][0:CW - 1, 0:128]),
                         reads=[PT(b2)], writes=io.tags)
                S.dma("pool", o_cp[l], io.ap[0:CW - 1, 0:256], reads=io.tags, writes=[("o_cp", l)], is_output=True)
            if g is gS:
                for ch in range(2):
                    b2 = ps_next()
                    S.op("pe", lambda e, ch=ch, b2=b2: e.transpose(psum[b2][0:n, 0:128], w.cva.ap[:, ch, 0:n], ident[:]),
                         reads=w.cva.tags + [("ident",)], writes=[PT(b2)])
                    S.op("dve", lambda e, ch=ch, b2=b2: e.tensor_copy(out=w.ufp.ap[0:n, ch * 128:(ch + 1) * 128], in_=psum[b2][0:n, 0:128]),
                         reads=[PT(b2)], writes=w.ufp.tags)
                for bb in range(NB):
                    S.dma("pool", o_cs[l, bb, CW - 1 - DEC_T:CW - 1, :], w.ufp.ap[bb * DEC_T:(bb + 1) * DEC_T, :], reads=w.ufp.tags,
                          writes=[("o_cs", l, bb, 1)], is_output=True)

        kvst = av("kvst", 44, 8, F32, "p (b c) -> p b c", b=4)

        def att_q_evac(g, c, b):
            n, w = g.n, g.ws
            ch = c - 4
            for hp in range(2):
                h = 2 * ch + hp
                r0 = 64 * hp
                S.op("act", lambda e, h=h, r0=r0: e.activation(out=w.qz.ap[r0:r0 + 64, h, 0:n], in_=psum[b][r0:r0 + 64, 0:n], func=AF.Copy, scale=HD ** -0.5),
                     reads=[PT(b)], writes=w.qz.tags)

        def att_k_evac(g, l, j, c, b):
            n, w = g.n, g.ws
            ch = c - 6
            if g is gP:
                S.op("dve", lambda e: e.tensor_copy(out=kwin[:, ch, WIN:WIN + n], in_=psum[b][:, 0:n]), reads=[PT(b)], writes=[("kwin", "cur")])
                if ch == 1:
                    t0 = j * TT
                    S.dma("pool", sK[l].rearrange("(c p) t -> p c t", p=128)[:, :, t0:t0 + n], kwin[:, :, WIN:WIN + n], reads=[("kwin", "cur")], writes=[("sK", l, j)])
            else:
                S.op("dve", lambda e: e.tensor_copy(out=w.kTs.ap[:, ch, 0:n], in_=psum[b][:, 0:n]), reads=[PT(b)], writes=w.kTs.tags)

        def att_kv_tm(g, l, j, c, unit, utag):
            n, w = g.n, g.ws
            ci = c - 6
            b, bs, nblk = win_chunk_tm(g, unit, utag)
            t0 = j * TT
            if g is gP:
                S.op("act", lambda e: e.activation(out=kvst.ap[:, :, ci * 128:(ci + 1) * 128], in_=psum[b][:, :].rearrange("p (b c) -> p b c", c=128), func=AF.Copy),
                     reads=[PT(b)], writes=kvst.tags)
            else:
                kv = w.kvout[0]
                S.op("act", lambda e: e.activation(out=kv.ap[0:bs, ci * 128:(ci + 1) * 128], in_=psum[b][0:bs, 0:128], func=AF.Copy), reads=[PT(b)], writes=kv.tags)
            if c >= 8:
                for hh in range(2):
                    h = 2 * (c - 8) + hh
                    if g is gP:
                        S.op("act", lambda e, h=h, hh=hh: e.activation(out=vwin[:, WIN // 128:WIN // 128 + 4, h, 64 * hh:64 * hh + 64],
                                                                       in_=psum[b][:, :].rearrange("p (b c) -> p b c", c=128)[:, :, 64 * hh:64 * hh + 64], func=AF.Copy),
                             reads=[PT(b)], writes=[("vwin", "cur")])
                    else:
                        S.op("act", lambda e, h=h, hh=hh: e.activation(out=w.vaug.ap[0:bs, h, 64 * hh:64 * hh + 64], in_=psum[b][0:bs, 64 * hh:64 * hh + 64], func=AF.Copy),
                             reads=[PT(b)], writes=w.vaug.tags)
            if c == 9:
                if g is gP:
                    r0 = t0 - (SEQ - KEEP)
                    if r0 >= 0:
                        S.dma("pool", o_kp[l, r0:r0 + n, :].rearrange("(b p) c -> p b c", p=128), kvst.ap[:, :, 0:256], reads=kvst.tags, writes=[("o_kp", l, r0)], is_output=True)
                        S.dma("pool", o_vp[l, r0:r0 + n, :].rearrange("(b p) c -> p b c", p=128), kvst.ap[:, :, 256:512], reads=kvst.tags, writes=[("o_vp", l, r0)], is_output=True)
                    S.dma("pool", sV[l, t0:t0 + n, :].rearrange("(b p) c -> p b c", p=128),
                          vwin[:, WIN // 128:WIN // 128 + 4, :, :].rearrange("p b h d -> p b (h d)"), reads=[("vwin", "cur")], writes=[("sV", l, j)])
                else:
                    kv = w.kvout[0]
                    S.dma("pool", o_ks[l], kv.ap[0:n, 0:256], reads=kv.tags, writes=[("o_ks", l)], is_output=True)
                    S.dma("pool", o_vs[l], kv.ap[0:n, 256:512], reads=kv.tags, writes=[("o_vs", l)], is_output=True)

        def att_finish(g, w, bO, h, ych, cols, li):
            ch, hp = divmod(h, 2)
            orow = 64 * hp
            drow = 64 * (1 - hp)
            nq = cols[1] - cols[0]
            ld = w.lnd[li % 2]
            S.op("act", lambda e: e.activation(out=ld.ap[orow:orow + 64, 0:nq], in_=psum[bO][drow:drow + 64, 0:nq], func=AF.Ln),
                 reads=[PT(bO)], writes=ld.tags)
            S.op("act", lambda e: e.activation(out=ld.ap[orow:orow + 64, 0:nq], in_=ld.ap[orow:orow + 64, 0:nq], func=AF.Exp, scale=-1.0),
                 reads=ld.tags, writes=ld.tags)
            S.op("dve", lambda e: e.tensor_tensor(out=g.ym[orow:orow + 64, 2 + ch, cols[0]:cols[1]], in0=psum[bO][orow:orow + 64, 0:nq],
                                                  in1=ld.ap[orow:orow + 64, 0:nq], op=ALU.mult),
                 reads=[PT(bO)] + ld.tags, writes=[ytag(g, 2 + ch)])

        _wt = att_weight_table().astype(np.float64)
        MMAX = [max(m for m in range(NM) if np.any(_wt[:, m, h, :] >= 2.0 ** -134)) for h in range(ATT_H)]

        def att_prompt(l, j):
            g, w = gP, wsP
            t0 = j * TT
            work = []
            for qb in range(TT // 128):
                gb = j * (TT // 128) + qb
                nm = min(16, gb) + 1
                for h in range(ATT_H):
                    bO = acc_bank()
                    nmh = min(nm, MMAX[h] + 1)
                    for g0 in range(0, nmh, 4):
                        work.append(dict(qb=qb, h=h, grp=list(range(g0, min(nmh, g0 + 4))), first=(g0 == 0), last=(g0 + 4 >= nmh), bO=bO, nm=nmh))
            li = [0]

            def emit_S(i):
                wk = work[i]
                b = ps_next()
                wk["b"] = b
                qb, h, grp = wk["qb"], wk["h"], wk["grp"]
                ch = h // 2

                def mm(e):
                    ins = None
                    for gi, m in enumerate(grp):
                        wkb = 16 + qb - m
                        ins = e.matmul(psum[b][:, gi * 128:(gi + 1) * 128], lhsT=kwin[:, ch, wkb * 128:(wkb + 1) * 128],
                                       rhs=w.qz.ap[:, h, qb * 128:(qb + 1) * 128], start=True, stop=True)
                    return ins
                S.op("pe", mm, reads=[("kwin", "hist"), ("kwin", "cur")] + w.qz.tags, writes=[PT(b)])
                ng = len(grp)
                Eb, Pb = w.Eb[i % 4], w.Pb[i % 4]
                S.op("act", lambda e: e.activation(out=Eb.ap[:, 0:ng * 128], in_=psum[b][:, 0:ng * 128], func=AF.Exp), reads=[PT(b)], writes=Eb.tags)
                S.op("dve", lambda e: e.tensor_tensor(
                    out=Pb.ap[:, 0:ng * 128].rearrange("p (m q) -> p m q", q=128), in0=Eb.ap[:, 0:ng * 128].rearrange("p (m q) -> p m q", q=128),
                    in1=wmask[:, grp[0]:grp[0] + ng, h, :], op=ALU.mult), reads=Eb.tags + [("wmask",)], writes=Pb.tags)

            def emit_PV(i):
                wk = work[i]
                qb, h, grp, bO, nm = wk["qb"], wk["h"], wk["grp"], wk["bO"], wk["nm"]
                Pb = w.Pb[i % 4]

                def pv(e):
                    ins = None
                    for gi, m in enumerate(grp):
                        wkb = 16 + qb - m
                        ins = e.matmul(psum[bO][:, 0:128], lhsT=vwin[:, wkb, h, :], rhs=Pb.ap[:, gi * 128:(gi + 1) * 128],
                                       start=(wk["first"] and gi == 0), stop=(m == nm - 1))
                    return ins
                S.op("pe", pv, reads=Pb.tags + [("vwin", "hist"), ("vwin", "cur")], writes=[PT(bO)])
                if wk["last"]:
                    att_finish(g, w, bO, h, None, (qb * 128, (qb + 1) * 128), li[0])
                    li[0] += 1

            LA = 3
            for i in range(len(work) + LA):
                if i < len(work):
                    emit_S(i)
                if i - LA >= 0:
                    emit_PV(i - LA)

        def att_sample(l):
            g, w = gS, wsS
            n = g.n
            li = 0
            ei = 0
            for bb in range(NB):
                for q4 in range(4):
                    io = iobuf[q4 % 2]
                    S.dma("sp", io.ap[:, :].rearrange("p (b c) -> p b c", c=256), ck_in[l, bb, q4 * 512:(q4 + 1) * 512, :].rearrange("(b p) c -> p b c", p=128), writes=io.tags)
                    for ch in range(2):
                        b = ps_next()

                        def tr(e, io=io, ch=ch, b=b):
                            ins = None
                            for kb in range(4):
                                ins = e.transpose(psum[b][:, kb * 128:(kb + 1) * 128], io.ap[:, kb * 256 + ch * 128:kb * 256 + (ch + 1) * 128], ident[:])
                            return ins
                        S.op("pe", tr, reads=io.tags + [("ident",)], writes=[PT(b)])
                        if ch == 0:
                            S.op("act", lambda e, b=b, q4=q4, ch=ch: e.activation(out=kwin[:, ch, q4 * 512:(q4 + 1) * 512], in_=psum[b][:, :], func=AF.Copy),
                                 reads=[PT(b)], writes=[("kwin", "hist")])
                        else:
                            S.op("dve", lambda e, b=b, q4=q4, ch=ch: e.tensor_copy(out=kwin[:, ch, q4 * 512:(q4 + 1) * 512], in_=psum[b][:, :]),
                                 reads=[PT(b)], writes=[("kwin", "hist")])
                for q4 in range(4):
                    io = iobuf[q4 % 2]
                    S.dma("sp", io.ap[:, :].rearrange("p (b c) -> p b c", c=256), cv_in[l, bb, q4 * 512:(q4 + 1) * 512, :].rearrange("(b p) c -> p b c", p=128), writes=io.tags)
                    for par in range(2):
                        S.op("act", lambda e, io=io, q4=q4, par=par: e.activation(
                            out=vwin[:, q4 * 4:(q4 + 1) * 4, par::2, 64 * par:64 * par + 64],
                            in_=io.ap[:, :].rearrange("p (b h d) -> p b h d", b=4, d=64)[:, :, par::2, :], func=AF.Copy),
                            reads=io.tags, writes=[("vwin", "hist")])
                for h in range(ATT_H):
                    ch = h // 2
                    bO = acc_bank()
                    b = ps_next()

                    def mm(e, b=b, ch=ch, h=h, bb=bb):
                        ins = None
                        for m in range(1, min(16, MMAX[h]) + 1):
                            wkb = 16 - m
                            ins = e.matmul(psum[b][:, (m - 1) * DEC_T:m * DEC_T], lhsT=kwin[:, ch, wkb * 128:(wkb + 1) * 128],
                                           rhs=w.qz.ap[:, h, bb * DEC_T:(bb + 1) * DEC_T], start=True, stop=True)
                        ins = e.matmul(psum[b][0:128, 64:64 + DEC_T], lhsT=w.kTs.ap[:, ch, 0:128], rhs=w.qz.ap[:, h, bb * DEC_T:(bb + 1) * DEC_T], start=True, stop=True)
                        return ins
                    S.op("pe", mm, reads=[("kwin", "hist")] + w.kTs.tags + w.qz.tags, writes=[PT(b)])
                    Eb = w.Eb[ei % 2]
                    Pb = w.Pb[ei % 2]
                    ei += 1
                    mh = min(16, MMAX[h])
                    S.op("act", lambda e, b=b, Eb=Eb, mh=mh: e.activation(out=Eb.ap[:, 0:mh * DEC_T], in_=psum[b][:, 0:mh * DEC_T], func=AF.Exp), reads=[PT(b)], writes=Eb.tags)
                    S.op("act", lambda e, b=b, Eb=Eb: e.activation(out=Eb.ap[:, 64:64 + DEC_T], in_=psum[b][:, 64:64 + DEC_T], func=AF.Exp), reads=[PT(b)], writes=Eb.tags)
                    S.op("dve", lambda e, Eb=Eb, Pb=Pb, h=h, mh=mh: e.tensor_tensor(out=Pb.ap[:, 0:mh * DEC_T].rearrange("p (m q) -> p m q", q=DEC_T),
                                                                          in0=Eb.ap[:, 0:mh * DEC_T].rearrange("p (m q) -> p m q", q=DEC_T),
                                                                          in1=wmask[:, 1:mh + 1, h, 0:DEC_T], op=ALU.mult),
                         reads=Eb.tags + [("wmask",)], writes=Pb.tags)
                    S.op("dve", lambda e, Eb=Eb, Pb=Pb, h=h, bb=bb: e.tensor_tensor(out=Pb.ap[:, 64:64 + DEC_T], in0=Eb.ap[:, 64:64 + DEC_T],
                                                                                 in1=wsm[:, bb, h, :], op=ALU.mult),
                         reads=Eb.tags + [("wsm",)], writes=Pb.tags)

                    def pv(e, Pb=Pb, h=h, bO=bO):
                        ins = None
                        for m in range(1, min(16, MMAX[h]) + 1):
                            wkb = 16 - m
                            ins = e.matmul(psum[bO][:, 0:DEC_T], lhsT=vwin[:, wkb, h, :], rhs=Pb.ap[:, (m - 1) * DEC_T:m * DEC_T], start=(m == 1), stop=False)
                        ins = e.matmul(psum[bO][:, 0:DEC_T], lhsT=w.vaug.ap[:, h, :], rhs=Pb.ap[:, 64:64 + DEC_T], start=False, stop=True)
                        return ins
                    S.op("pe", pv, reads=Pb.tags + [("vwin", "hist")] + w.vaug.tags, writes=[PT(bO)])
                    att_finish(g, w, bO, h, None, (bb * DEC_T, (bb + 1) * DEC_T), li)
                    li += 1

        def hgrn_prep(g, l, h):
            n, w = g.n, g.ws
            hs, lv = w.hp, w.live[h]
            prompt = g is gP
            C = HGC if prompt else DEC_T
            nchunk = n // C
            rsm = rsmask if prompt else rsmask_s
            rsm_tag = ("rsmask",) if prompt else ("rsmask_s",)
            lbc = lbT[:, l * HG_H + h:l * HG_H + h + 1]
            omc = omlT[:, l * HG_H + h:l * HG_H + h + 1]
            A = lambda bf: bf.ap[:, 0:n]
            S.op("pool", lambda e: e.tensor_scalar(out=A(hs.f), in0=A(hs.f), scalar1=omc, scalar2=lbc, op0=ALU.mult, op1=ALU.add),
                 reads=hs.f.tags + [("lbT",), ("omlT",)], writes=hs.f.tags)
            S.op("act", lambda e: e.activation(out=A(hs.lg), in_=A(hs.f), func=AF.Ln), reads=hs.f.tags, writes=hs.lg.tags)
            S.op("pool", lambda e: e.tensor_scalar(out=A(hs.kk), in0=A(hs.f), scalar1=-1.0, scalar2=1.0, op0=ALU.mult, op1=ALU.add),
                 reads=hs.f.tags, writes=hs.kk.tags)
            S.op("dve", lambda e: e.tensor_tensor_scan(out=A(hs.f), data0=rsm[:, 0:n], data1=A(hs.lg), initial=0.0, op0=ALU.mult, op1=ALU.add),
                 reads=hs.lg.tags + [rsm_tag], writes=hs.f.tags)
            b3 = hs.f.ap[:, 0:n].rearrange("p (c t) -> p c t", t=C)
            MID = C // 2 - 1
            S.op("act", lambda e: e.activation(out=lv.ebl.ap[:, 0:nchunk], in_=b3[:, :, C - 1], func=AF.Exp), reads=hs.f.tags, writes=lv.ebl.tags)
            S.op("act", lambda e: e.activation(out=lv.emid.ap[:, 0:nchunk], in_=b3[:, :, MID], func=AF.Exp), reads=hs.f.tags, writes=lv.emid.tags)
            S.op("dve", lambda e: e.tensor_tensor(out=hs.lg.ap[:, 0:n].rearrange("p (c t) -> p c t", t=C),
                                                  in0=b3, in1=b3[:, :, MID:MID + 1].to_broadcast([128, nchunk, C]), op=ALU.subtract),
                 reads=hs.f.tags, writes=hs.lg.tags)
            S.op("act", lambda e: e.activation(out=A(hs.eb), in_=A(hs.lg), func=AF.Exp), reads=hs.lg.tags, writes=hs.eb.tags)
            S.op("act", lambda e: e.activation(out=A(hs.en), in_=A(hs.lg), func=AF.Exp, scale=-1.0), reads=hs.lg.tags, writes=hs.en.tags)
            S.op("dve", lambda e: e.tensor_tensor(out=A(lv.qt), in0=A(hs.q), in1=A(hs.eb), op=ALU.mult), reads=hs.q.tags + hs.eb.tags, writes=lv.qt.tags)
            S.op("dve", lambda e: e.tensor_tensor(out=A(lv.kt), in0=A(hs.kk), in1=A(hs.en), op=ALU.mult), reads=hs.kk.tags + hs.en.tags, writes=lv.kt.tags)
            S.op("dve", lambda e: e.tensor_tensor(out=hs.lg.ap[:, 0:n].rearrange("p (c t) -> p c t", t=C),
                                                  in0=b3[:, :, C - 1:C].to_broadcast([128, nchunk, C]), in1=b3, op=ALU.subtract),
                 reads=hs.f.tags + hs.eb.tags + hs.en.tags, writes=hs.lg.tags)
            S.op("act", lambda e: e.activation(out=A(hs.en), in_=A(hs.lg), func=AF.Exp), reads=hs.lg.tags, writes=hs.en.tags)
            S.op("dve", lambda e: e.tensor_tensor(out=A(lv.kh), in0=A(hs.kk), in1=A(hs.en), op=ALU.mult), reads=hs.kk.tags + hs.en.tags, writes=lv.kh.tags)

        def hgrn_chain(g, l, j):
            n, w = g.n, g.ws
            prompt = g is gP
            C = HGC if prompt else DEC_T
            bs = 128 if prompt else n
            nblk = n // bs
            NCB = bs // C
            bdm = bdmask if prompt else bdmask_s
            bdm_tag = ("bdmask",) if prompt else ("bdmask_s",)
            rm0 = 0 if prompt else 4
            vt_tags = w.vtok.tags
            ps_rr[0] = ps_rr[0] % 4
            npr_save = NPRv[0]
            NPRv[0] = 4
            bOT = [4 + h for h in range(HG_H)]
            ki = [0]
            if prompt:
                for h in range(HG_H):
                    S.op("pool", lambda e, h=h: e.tensor_copy(out=w.Tb[h][0].ap, in_=Sst[:, l, h, :]), reads=[("Sst", l, h)], writes=w.Tb[h][0].tags)
                    S.op("act", lambda e, h=h: e.activation(out=w.Sb[h][0].ap, in_=Sst[:, l, h, :], func=AF.Copy, scale=w.live[h].emid.ap[:, 0:1]),
                         reads=[("Sst", l, h)] + w.live[h].emid.tags, writes=w.Sb[h][0].tags)

            st1 = {}

            def stage1(tb, h):
                lv = w.live[h]
                c0 = tb * bs
                bA = ps_next()
                S.op("pe", lambda e: e.matmul(psum[bA][0:128, 0:bs], lhsT=lv.kt.ap[:, c0:c0 + 128], rhs=lv.qt.ap[:, c0:c0 + bs], start=True, stop=True),
                     reads=lv.kt.tags + lv.qt.tags, writes=[PT(bA)])
                Am = w.Am[ki[0] % 4]
                khm = w.khm[ki[0] % 4]
                ki[0] += 1
                S.op("dve", lambda e: e.tensor_tensor(out=Am.ap[:, 0:bs], in0=psum[bA][:, 0:bs], in1=bdm[:, 0:bs], op=ALU.mult),
                     reads=[PT(bA), bdm_tag], writes=Am.tags)
                bT = ps_next()
                S.op("pe", lambda e: e.transpose(psum[bT][0:bs, 0:128], lv.kh.ap[:, c0:c0 + bs], ident[:]),
                     reads=lv.kh.tags + [("ident",)], writes=[PT(bT)])
                for jj in range(NCB):
                    rc = rowmask[0:bs, rm0 + jj:rm0 + jj + 1]
                    S.op("dve", lambda e, jj=jj, rc=rc: e.tensor_scalar(out=khm.ap[0:bs, jj, :], in0=psum[bT][0:bs, 0:128], scalar1=rc, scalar2=None, op0=ALU.mult),
                         reads=[PT(bT), ("rowmask",)], writes=khm.tags)
                st1[(tb, h)] = (Am, khm)

            def stage1b(tb, h):
                lv = w.live[h]
                c0 = tb * bs
                Am, khm = st1[(tb, h)]
                bD = ps_next()

                def dS(e):
                    ins = None
                    for jj in range(NCB):
                        ins = e.matmul(psum[bD][:, jj * 128:(jj + 1) * 128], lhsT=khm.ap[:, jj, :], rhs=w.vtok.ap[:, tb, h * 128:(h + 1) * 128], start=True, stop=True)
                    return ins
                S.op("pe", dS, reads=khm.tags + vt_tags, writes=[PT(bD)])
                S.op("pe", lambda e: e.matmul(psum[bOT[h]][:, c0:c0 + bs], lhsT=w.vtok.ap[:, tb, h * 128:(h + 1) * 128], rhs=Am.ap[:, 0:bs], start=True, stop=False),
                     reads=Am.tags + vt_tags, writes=[PT(bOT[h])])
                for jj in range(NCB):
                    if prompt:
                        cidx = tb * NCB + jj
                        ecol = lv.ebl.ap[:, cidx:cidx + 1]
                        tin, tout = w.Tb[h][cidx % 2], w.Tb[h][(cidx + 1) % 2]
                        S.op("dve", lambda e, ecol=ecol, jj=jj, tin=tin, tout=tout: e.scalar_tensor_tensor(out=tout.ap, in0=tin.ap, scalar=ecol,
                                                                                      in1=psum[bD][:, jj * 128:(jj + 1) * 128], op0=ALU.mult, op1=ALU.add),
                             reads=tin.tags + [PT(bD)] + lv.ebl.tags, writes=tout.tags)
                        sbn = w.Sb[h][(cidx + 1) % 5]
                        if cidx + 1 < nblk * NCB:
                            mcol = lv.emid.ap[:, cidx + 1:cidx + 2]
                            S.op("act", lambda e, sbn=sbn, tout=tout, mcol=mcol: e.activation(out=sbn.ap, in_=tout.ap, func=AF.Copy, scale=mcol),
                                 reads=tout.tags + lv.emid.tags, writes=sbn.tags)
                    else:
                        sbc = w.Sb[h][jj]
                        S.op("act", lambda e, sbc=sbc, jj=jj: e.activation(out=sbc.ap, in_=w.shs.ap[:, jj, h, :], func=AF.Copy, scale=lv.emid.ap[:, jj:jj + 1]),
                             reads=w.shs.tags + lv.emid.tags, writes=sbc.tags)
                        ecol = lv.ebl.ap[:, jj:jj + 1]
                        S.op("dve", lambda e, ecol=ecol, jj=jj: e.scalar_tensor_tensor(out=w.shs.ap[:, jj, h, :], in0=w.shs.ap[:, jj, h, :], scalar=ecol,
                                                                                      in1=psum[bD][:, jj * 128:(jj + 1) * 128], op0=ALU.mult, op1=ALU.add),
                             reads=w.shs.tags + [PT(bD)] + lv.ebl.tags, writes=w.shs.tags)

            def stage2(tb, h):
                lv = w.live[h]
                c0 = tb * bs
                for jj in range(NCB):
                    q0 = c0 + jj * C
                    sbc = w.Sb[h][(tb * NCB + jj) % 5] if prompt else w.Sb[h][jj]
                    S.op("pe", lambda e, sbc=sbc, q0=q0, jj=jj: e.matmul(psum[bOT[h]][:, q0:q0 + C], lhsT=sbc.ap, rhs=lv.qt.ap[:, q0:q0 + C], start=False, stop=(jj == NCB - 1)),
                         reads=sbc.tags + lv.qt.tags, writes=[PT(bOT[h])])

            seq = [(tb, h) for tb in range(nblk) for h in range(HG_H)]
            for i in range(len(seq) + 4):
                if i < len(seq):
                    stage1(*seq[i])
                if 2 <= i < len(seq) + 2:
                    stage1b(*seq[i - 2])
                if i >= 4:
                    stage2(*seq[i - 4])
            if prompt:
                nct = nblk * NCB
                for h in range(HG_H):
                    S.op("pool", lambda e, h=h: e.tensor_copy(out=Sst[:, l, h, :], in_=w.Tb[h][nct % 2].ap), reads=w.Tb[h][nct % 2].tags, writes=[("Sst", l, h)])
            for h in range(HG_H):
                lv = w.live[h]
                osq, ot = w.osq[h], w.ot[h]
                S.op("act", lambda e, h=h, osq=osq: e.activation(out=osq.ap[:, 0:n], in_=psum[bOT[h]][:, 0:n], func=AF.Square), reads=[PT(bOT[h])], writes=osq.tags)
                bN = ps_next()
                S.op("pe", lambda e, bN=bN, osq=osq: e.matmul(psum[bN][:, 0:n], lhsT=ones_b[:], rhs=osq.ap[:, 0:n], start=True, stop=True), reads=osq.tags + [("ones",)], writes=[PT(bN)])
                S.op("act", lambda e, bN=bN, ot=ot: e.activation(out=ot.ap[:, 0:n], in_=psum[bN][:, 0:n], func=AF.Ln, scale=1.0 / 128, bias=eps_col[:]), reads=[PT(bN), ("eps",)], writes=ot.tags)
                S.op("act", lambda e, ot=ot: e.activation(out=ot.ap[:, 0:n], in_=ot.ap[:, 0:n], func=AF.Exp, scale=-0.5), reads=ot.tags, writes=ot.tags)
                S.op("dve", lambda e, h=h, ot=ot: e.tensor_tensor(out=ot.ap[:, 0:n], in0=psum[bOT[h]][:, 0:n], in1=ot.ap[:, 0:n], op=ALU.mult), reads=[PT(bOT[h])] + ot.tags, writes=ot.tags)
                S.op("dve", lambda e, h=h, lv=lv, ot=ot: e.scalar_tensor_tensor(out=g.ym[:, 4 + h, 0:n], in0=ot.ap[:, 0:n], scalar=pcol("hgn", l), in1=lv.gt.ap[:, 0:n], op0=ALU.mult, op1=ALU.mult),
                     reads=ot.tags + lv.gt.tags + [("prmT",)], writes=[ytag(g, 4 + h)])
            NPRv[0] = npr_save

        def hgrn_vtok(g, h, unit, utag):
            w = g.ws
            b, bs, nblk = win_chunk_tm(g, unit, utag)
            if g is gP:
                S.op("act", lambda e: e.activation(out=w.vtok.ap[:, :, h * 128:(h + 1) * 128], in_=psum[b][:, :].rearrange("p (b c) -> p b c", c=128), func=AF.Copy),
                     reads=[PT(b)], writes=w.vtok.tags)
            else:
                S.op("act", lambda e: e.activation(out=w.vtok.ap[:, 0, h * 128:(h + 1) * 128], in_=psum[b][:, 0:128], func=AF.Copy),
                     reads=[PT(b)], writes=w.vtok.tags)

        def att_hist_load(l, j):
            t0 = j * TT
            avail = min(WIN, t0)
            if avail > 0:
                S.dma("sp", kwin[:, :, WIN - avail:WIN], sK[l].rearrange("(c p) t -> p c t", p=128)[:, :, t0 - avail:t0],
                      reads=[("sK", l, jj) for jj in range(j)], writes=[("kwin", "hist")])
                nb = avail // 128
                S.dma("sp", vwin[:, 16 - nb:16, :, :].rearrange("p b h d -> p b (h d)"),
                      sV[l, t0 - avail:t0, :].rearrange("(b p) c -> p b c", p=128),
                      reads=[("sV", l, jj) for jj in range(j)], writes=[("vwin", "hist")])

        def mix_layer(groups, l, j):
            if STG >= 3:
                att_hist_load(l, j)
            for g in groups:
                rmsnorm(g, "lnm", l * 8)
            build_diag(l)
            for c in FM_CONV:
                u, t = WS.consume(("i", l, c))
                for g in groups:
                    conv_evac(g, c, win_chunk_fm(g, l, c, u, t))
            for g in groups:
                conv_glu(g)
            if STG >= 3:
                for g in groups:
                    S.op("pool", lambda e, g=g: e.memset(g.ws.qz.ap, 0.0), writes=g.ws.qz.tags)
            for c in FM_QKV:
                u, t = WS.consume(("i", l, c))
                if STG < 3:
                    continue
                for g in groups:
                    if c in (4, 5):
                        att_q_evac(g, c, win_chunk_fm(g, l, c, u, t))
                    elif c in (6, 7):
                        att_k_evac(g, l, j, c, win_chunk_fm(g, l, c, u, t))
                    if c >= 6:
                        att_kv_tm(g, l, j, c, u, t)
            for g in groups:
                conv_main(g, l, j)
            if STG >= 3:
                att_prompt(l, j)
                if gS in groups:
                    att_sample(l)
            else:
                for g in groups:
                    for c in (2, 3):
                        S.op("pool", lambda e, g=g, c=c: e.memset(g.ym[:, c, 0:g.n], 0.0), writes=[ytag(g, c)])
            for h in range(HG_H):
                for c, key in ((18 + h, "v"), (10 + h, "q"), (14 + h, "f"), (22 + h, "g")):
                    u, t = WS.consume(("i", l, c))
                    if STG < 4:
                        continue
                    for g in groups:
                        if key == "v":
                            hgrn_vtok(g, h, u, t)
                        else:
                            bz = win_chunk_fm(g, l, c, u, t)
                            dst = {"q": g.ws.hp.q, "f": g.ws.hp.f, "g": g.ws.live[h].gt}[key]
                            fnc = AF.Sigmoid if key == "f" else AF.Silu
                            S.op("act", lambda e, g=g, dst=dst, bz=bz, fnc=fnc: e.activation(out=dst.ap[:, 0:g.n], in_=psum[bz][:, 0:g.n], func=fnc),
                                 reads=[PT(bz)], writes=dst.tags)
                if STG >= 4:
                    for g in groups:
                        hgrn_prep(g, l, h)
                else:
                    for g in groups:
                        S.op("pool", lambda e, g=g, h=h: e.memset(g.ym[:, 4 + h, 0:g.n], 0.0), writes=[ytag(g, 4 + h)])
            if STG >= 4:
                for g in groups:
                    if g is gS:
                        S.dma("sp", wsS.shs.ap, shg_in[l].rearrange("b h d v -> d b h v"), writes=wsS.shs.tags)
                    hgrn_chain(g, l, j)
            if STG >= 4:
                if j == cfg.ntile - 1:
                    S.dma("pool", o_hp[l].rearrange("h d v -> d h v"), Sst[:, l, :, :], reads=[("Sst", l, h) for h in range(HG_H)], writes=[("o_hp", l)], is_output=True)
                if gS in groups:
                    S.dma("pool", o_hs[l].rearrange("b h d v -> d b h v"), wsS.shs.ap, reads=wsS.shs.tags, writes=[("o_hs", l)], is_output=True)
            for oc in range(NCH):
                u, t = WS.consume(("o", l, oc))
                for g in groups:
                    n = g.n
                    b = ps_next()
                    proj_fm(u, t, NCH, lambda k, g=g, n=n: g.ym[:, k, 0:n], [ytag(g, k) for k in range(NCH)], n, b)
                    S.op("dve", lambda e, g=g, n=n, b=b, oc=oc: e.tensor_tensor(out=g.x[:, oc, 0:n], in0=psum[b][:, 0:n], in1=g.x[:, oc, 0:n], op=ALU.add),
                         reads=[PT(b), xtag(g, oc)], writes=[xtag(g, oc)])

        load_tokens(gS, xs, [(0, NS, 0)])
        for j in range(cfg.ntile):
            load_tokens(gP, xp, [(j * TT + 128 * b, 128, 128 * b) for b in range(TT // 128)])
            groups = [gP, gS] if j == 0 else [gP]
            for l in range(L):
                for g in groups:
                    rmsnorm(g, "ln1", l * 8)
                ffn(groups, 0, l)
                if STG >= 2:
                    mix_layer(groups, l, j)
                if STG >= 9:
                    for g in groups:
                        rmsnorm(g, "ln2", l * 8)
                    ffn(groups, 1, l)
            final_norm_store(gP, yp, [(j * TT + 128 * b, 128, 128 * b) for b in range(TT // 128)])
            if j == 0:
                final_norm_store(gS, ys, [(0, NS, 0)])

        if cfg.debug:
            dbg = dout("dbg_ym", [128, NCH * TT], BF16)
            S.dma("pool", dbg, ymix[:].rearrange("p c t -> p (c t)"), reads=[ytag(gP, c) for c in range(NCH)], writes=[("dbg", 0)], is_output=True)
            dbgs = dout("dbg_yms", [128, NCH * NS], BF16)
            S.dma("pool", dbgs, ymixs[:].rearrange("p c t -> p (c t)"), reads=[ytag(gS, c) for c in range(NCH)], writes=[("dbg", 1)], is_output=True)
        S.finish()
        with nc.Block() as block:
            S.replay(block)
        cfg.n_inst = dict(S.n_inst)
    return nc


def pack_prm(depth, ln1, lnm, ln2, lnf, dww, dwb, clg, clb, hlb, hgn):
    off, prows = prm_layout(depth)
    out = np.zeros((prows, 128), np.float32)

    def put(name, arr):
        a = np.ascontiguousarray(arr, dtype=np.float32).reshape(-1, 128)
        out[off[name]:off[name] + a.shape[0]] = a
    put("ln1", ln1); put("lnm", lnm); put("ln2", ln2); put("lnf", lnf)
    put("dww", dww); put("dwb", dwb); put("clg", clg); put("clb", clb); put("hlb", hlb); put("hgn", hgn)
    return out


_CACHE = {}


def run(cfg, inputs):
    key = (cfg.seq, cfg.depth, cfg.nsamp, cfg.n_cores, cfg.nseq, cfg.stages, cfg.debug)
    if key not in _CACHE:
        _CACHE[key] = build_program(cfg)
    nc = _CACHE[key]
    L = cfg.depth
    f32 = lambda a: np.ascontiguousarray(a, dtype=np.float32)
    prm = pack_prm(L, inputs["ln_ffn1"], inputs["ln_mix"], inputs["ln_ffn2"], inputs["ln_final"],
                   inputs["conv_dw_w"], inputs["conv_dw_b"], inputs["conv_ln_g"], inputs["conv_ln_b"],
                   inputs["hg_lower_bounds"], inputs["hg_norm_g"])
    shared = {
        "prm": prm,
        "wg1": f32(inputs["w_ffn1_gate"]), "wu1": f32(inputs["w_ffn1_up"]), "wd1": f32(inputs["w_ffn1_down"]),
        "wg2": f32(inputs["w_ffn2_gate"]), "wu2": f32(inputs["w_ffn2_up"]), "wd2": f32(inputs["w_ffn2_down"]),
        "wi": f32(inputs["w_in"]), "wo": f32(inputs["w_out"]),
    }
    shared.update(const_tables(cfg.nsamp))
    xpf = f32(inputs["x_prompt"])
    xsf = f32(inputs["x_sample"])
    sconv = f32(inputs["state_conv"])
    ck = f32(inputs["cache_k_win"]).reshape(L, -1, WIN, 256)
    cv = f32(inputs["cache_v_win"]).reshape(L, -1, WIN, 256)
    shg = f32(inputs["state_hgrn"])
    in_maps = []
    zero_seq = None
    nb = cfg.nsamp
    for c in range(cfg.n_cores):
        m = dict(shared)
        if c < cfg.nseq:
            m["xp"] = xpf[c]
        else:
            if zero_seq is None:
                zero_seq = np.zeros((cfg.seq, D), np.float32)
            m["xp"] = zero_seq
        sl = slice(c * nb, (c + 1) * nb)
        m["xs"] = np.ascontiguousarray(xsf[sl].reshape(cfg.ns_tok, D))
        m["sconv"] = np.ascontiguousarray(sconv[:, sl])
        m["ck"] = np.ascontiguousarray(ck[:, sl])
        m["cv"] = np.ascontiguousarray(cv[:, sl])
        m["shg"] = np.ascontiguousarray(shg[:, sl])
        in_maps.append(m)
    res = run_bass_kernel_spmd(nc, in_maps, core_ids=list(range(cfg.n_cores)))
    return res.results


def assemble(cfg, r):
    L, nb = cfg.depth, cfg.nsamp
    nsq = cfg.nseq
    y_prompt = np.stack([r[c]["yp"] for c in range(nsq)])
    y_sample = np.concatenate([r[c]["ys"].reshape(nb, DEC_T, D) for c in range(cfg.n_cores)], 0)
    cp = np.stack([r[c]["o_cp"] for c in range(nsq)], 1)
    cs = np.concatenate([r[c]["o_cs"] for c in range(cfg.n_cores)], 1)
    kp = np.stack([r[c]["o_kp"].reshape(L, cfg.keep, ATT_H, HD) for c in range(nsq)], 1)
    vp = np.stack([r[c]["o_vp"].reshape(L, cfg.keep, ATT_H, HD) for c in range(nsq)], 1)
    ks = np.concatenate([r[c]["o_ks"].reshape(L, nb, DEC_T, ATT_H, HD) for c in range(cfg.n_cores)], 1)
    vs = np.concatenate([r[c]["o_vs"].reshape(L, nb, DEC_T, ATT_H, HD) for c in range(cfg.n_cores)], 1)
    hp = np.stack([r[c]["o_hp"] for c in range(nsq)], 1)
    hs = np.concatenate([r[c]["o_hs"] for c in range(cfg.n_cores)], 1)
    outs = (y_prompt, y_sample, cp, cs, kp, vp, ks, vs, hp, hs)
    return tuple(np.ascontiguousarray(o, dtype=np.float32) for o in outs)


def kernel(**inputs):
    cfg = Cfg()
    r = run(cfg, inputs)
    return assemble(cfg, r)
```
